# Optimizing a Trainium2 kernel written in Bass

```python
import jax, jax.numpy as jnp
from jax import lax
import numpy as np

D_MODEL = 2048
BATCH = 2
SEQ = 4096
DEPTH = 4

GRID_W = 64
CTX_LEN = 256
HEAD_DIM = 64
BLOCK = 128
WINDOW = 128
ROPE_THETA = 10000.0
A_HEADS = D_MODEL // 256
A_KV_HEADS = A_HEADS // 4
C_HEADS = D_MODEL // 256
C_KV_HEADS = C_HEADS // 4
B_HEADS = D_MODEL // 128
B_WIDTH = B_HEADS * HEAD_DIM
LORA_DECAY = 96
LORA_ICLR = 96
LORA_GATE = 256
D_FF = 4 * D_MODEL
A_Q_W = A_HEADS * HEAD_DIM
A_KV_W = A_KV_HEADS * HEAD_DIM
C_Q_W = C_HEADS * HEAD_DIM
C_KV_W = C_KV_HEADS * HEAD_DIM
A_IN = A_Q_W + 2 * A_KV_W
B_IN = 3 * B_WIDTH + 2 * LORA_DECAY + 2 * LORA_ICLR + LORA_GATE
C_IN = C_Q_W + 2 * C_KV_W
N_IN = A_IN + B_IN + C_IN
MIX_W = A_Q_W + B_WIDTH + C_Q_W
B_SPLITS = (B_WIDTH, 2 * B_WIDTH, 3 * B_WIDTH, 3 * B_WIDTH + LORA_DECAY, 3 * B_WIDTH + 2 * LORA_DECAY, 3 * B_WIDTH + 2 * LORA_DECAY + LORA_ICLR, 3 * B_WIDTH + 2 * LORA_DECAY + 2 * LORA_ICLR)
NORM_EPS = 1e-6
GN_EPS = 64e-5
NEG_INF = -1e30

kernel_name = "hybrid_parallel_heads_dit_block"


def rms_norm(x, g):
    xf = x.astype(jnp.float32)
    y = xf * lax.rsqrt(jnp.mean(xf * xf, -1, keepdims=True) + NORM_EPS)
    return (y * g.astype(jnp.float32)).astype(x.dtype)


def axial_rope_tables(length):
    rows = length // GRID_W
    row = jnp.broadcast_to(jnp.arange(rows)[:, None], (rows, GRID_W)).reshape(-1)
    col = jnp.broadcast_to(jnp.arange(GRID_W)[None, :], (rows, GRID_W)).reshape(-1)
    n_freq = HEAD_DIM // 4
    inv = ROPE_THETA ** (-jnp.arange(n_freq, dtype=jnp.float32) / n_freq)
    ang = jnp.concatenate([row[:, None].astype(jnp.float32) * inv, col[:, None].astype(jnp.float32) * inv], -1)
    return jnp.cos(ang), jnp.sin(ang)


def apply_rope(x, cos, sin):
    half = HEAD_DIM // 2
    xf = x.astype(jnp.float32)
    x1, x2 = xf[..., :half], xf[..., half:]
    c, s = cos[:, None, :], sin[:, None, :]
    return jnp.concatenate([x1 * c - x2 * s, x1 * s + x2 * c], -1).astype(x.dtype)


def joint_softmax(logits, sink=None):
    sizes = [s.shape[-1] for s in logits]
    parts = list(logits)
    if sink is not None:
        parts.append(jnp.broadcast_to(sink.astype(jnp.float32), logits[0].shape[:-1] + (1,)))
    p = jax.nn.softmax(jnp.concatenate(parts, -1), axis=-1)
    out, start = [], 0
    for n in sizes:
        out.append(p[..., start:start + n])
        start += n
    return out


def attn_heads(p, n_heads, n_kv, q_g, k_g):
    bsz, t = p.shape[:2]
    q, k, v = jnp.split(p, (n_heads * HEAD_DIM, (n_heads + n_kv) * HEAD_DIM), -1)
    q = rms_norm(q.reshape(bsz, t, n_heads, HEAD_DIM), q_g)
    k = rms_norm(k.reshape(bsz, t, n_kv, HEAD_DIM), k_g)
    return q, k, v.reshape(bsz, t, n_kv, HEAD_DIM)


def window_attention(q, k, v, kc, vc, sink):
    bsz, t, h, dh = q.shape
    hk = k.shape[2]
    nb = t // BLOCK
    scale = dh ** -0.5
    qb = q.reshape(bsz, nb, BLOCK, hk, h // hk, dh)
    pad = ((0, 0), (BLOCK, BLOCK), (0, 0), (0, 0))
    kp, vp = jnp.pad(k, pad), jnp.pad(v, pad)
    idx = jnp.arange(nb)[:, None] * BLOCK + jnp.arange(3 * BLOCK)[None, :]
    kb, vb = kp[:, idx], vp[:, idx]
    qpos = jnp.arange(nb)[:, None] * BLOCK + jnp.arange(BLOCK)[None, :]
    kpos = idx - BLOCK
    valid = (jnp.abs(qpos[:, :, None] - kpos[:, None, :]) <= WINDOW) & (kpos[:, None, :] >= 0) & (kpos[:, None, :] < t)
    s_lat = jnp.einsum('bnqhgd,bnkhd->bnhgqk', qb, kb).astype(jnp.float32) * scale
    s_lat = jnp.where(valid[None, :, None, None], s_lat, NEG_INF)
    s_ctx = jnp.einsum('bnqhgd,bchd->bnhgqc', qb, kc).astype(jnp.float32) * scale
    p_lat, p_ctx = joint_softmax([s_lat, s_ctx], sink)
    o = (jnp.einsum('bnhgqk,bnkhd->bnqhgd', p_lat.astype(v.dtype), vb)
         + jnp.einsum('bnhgqc,bchd->bnqhgd', p_ctx.astype(vc.dtype), vc))
    return o.reshape(bsz, t, h * dh)


def global_attention(q, k, v, kc, vc):
    bsz, t, h, dh = q.shape
    hk = k.shape[2]
    nb = t // BLOCK
    scale = dh ** -0.5
    qb = jnp.moveaxis(q.reshape(bsz, nb, BLOCK, hk, h // hk, dh), 1, 0)

    def one_block(qblk):
        s_lat = jnp.einsum('bqhgd,bkhd->bhgqk', qblk, k).astype(jnp.float32) * scale
        s_ctx = jnp.einsum('bqhgd,bchd->bhgqc', qblk, kc).astype(jnp.float32) * scale
        p_lat, p_ctx = joint_softmax([s_lat, s_ctx])
        return (jnp.einsum('bhgqk,bkhd->bqhgd', p_lat.astype(v.dtype), v)
                + jnp.einsum('bhgqc,bchd->bqhgd', p_ctx.astype(vc.dtype), vc))

    o = lax.map(one_block, qb)
    return jnp.moveaxis(o, 0, 1).reshape(bsz, t, h * dh)


def context_attention(q, k, v, sink):
    bsz, cn, h, dh = q.shape
    hk = k.shape[2]
    qg = q.reshape(bsz, cn, hk, h // hk, dh)
    s = jnp.einsum('bqhgd,bkhd->bhgqk', qg, k).astype(jnp.float32) * dh ** -0.5
    (p,) = joint_softmax([s], sink)
    o = jnp.einsum('bhgqk,bkhd->bqhgd', p.astype(v.dtype), v)
    return o.reshape(bsz, cn, h * dh)


def centred_shift(p, mu_prev, mu_next):
    prev = jnp.pad(p, ((0, 0), (1, 0), (0, 0)))[:, :-1]
    nxt = jnp.pad(p, ((0, 0), (0, 1), (0, 0)))[:, 1:]
    return p + mu_prev * (prev - p) + mu_next * (nxt - p)


def rwkv_streams(p, w0, w_up, a0, a_up, g_up, k_k, k_a):
    p = p.astype(jnp.float32)
    r, k, v, wd_f, wd_b, ad_f, ad_b, gd = jnp.split(p, B_SPLITS, -1)
    bsz, t = p.shape[:2]
    heads = lambda z: z.reshape(bsz, t, B_HEADS, HEAD_DIM)
    kk = heads(k * k_k)
    kk = kk / jnp.maximum(jnp.sqrt(jnp.sum(kk * kk, -1, keepdims=True)), 1e-12)
    decay, keys, iclr = [], [], []
    for d, (wd, ad) in enumerate(((wd_f, ad_f), (wd_b, ad_b))):
        w_log = -jax.nn.softplus(-(w0[d] + jnp.tanh(wd) @ w_up[d])) - 0.5
        a = jax.nn.sigmoid(a0[d] + ad @ a_up[d])
        decay.append(heads(jnp.exp(-jnp.exp(w_log))))
        keys.append(heads(k * (1.0 + (a - 1.0) * k_a)))
        iclr.append(heads(a))
    g = jax.nn.sigmoid(gd) @ g_up
    return dict(r=heads(r), v=heads(v), kk=kk, decay=decay, k=keys, a=iclr, g=g)


def wkv7_scan(state0, s, d, reverse):
    tm = lambda z: jnp.swapaxes(z, 0, 1)
    xs = (tm(s['r']), tm(s['decay'][d]), tm(s['k'][d]), tm(s['v']), tm(s['kk']), tm(s['a'][d]))

    def step(S, inp):
        r_t, w_t, k_t, v_t, kk_t, a_t = inp
        sa = jnp.einsum('bhvk,bhk->bhv', S, kk_t)
        S = S * w_t[:, :, None, :] - sa[..., None] * (kk_t * a_t)[:, :, None, :] + v_t[..., None] * k_t[:, :, None, :]
        return S, jnp.einsum('bhvk,bhk->bhv', S, r_t)

    s_fin, y = lax.scan(step, state0, xs, reverse=reverse)
    return s_fin, jnp.swapaxes(y, 0, 1)


def rwkv_readout(ys, s, r_k, gn_g, gn_b):
    y = ys[0] + ys[1]
    bsz, t = y.shape[:2]
    mu = jnp.mean(y, -1, keepdims=True)
    var = jnp.mean(jnp.square(y - mu), -1, keepdims=True)
    yn = ((y - mu) * lax.rsqrt(var + GN_EPS)).reshape(bsz, t, B_WIDTH) * gn_g + gn_b
    rk = r_k.reshape(B_HEADS, HEAD_DIM)
    bonus = jnp.sum(s['r'] * (s['k'][0] + s['k'][1]) * rk, -1, keepdims=True) * s['v']
    return (yn + bonus.reshape(bsz, t, B_WIDTH)) * s['g']


def rwkv_mixer(p_c, p_l, mu, w0, w_up, a0, a_up, g_up, k_k, k_a, r_k, gn_g, gn_b, need_ctx_out):
    s_c = rwkv_streams(centred_shift(p_c, mu[0], mu[1]), w0, w_up, a0, a_up, g_up, k_k, k_a)
    s_l = rwkv_streams(centred_shift(p_l, mu[0], mu[1]), w0, w_up, a0, a_up, g_up, k_k, k_a)
    state0 = jnp.zeros((p_l.shape[0], B_HEADS, HEAD_DIM, HEAD_DIM), jnp.float32)
    ys_c, ys_l = [], []
    for d in range(2):
        st_c, y_c = wkv7_scan(state0, s_c, d, d == 1)
        _, y_l = wkv7_scan(st_c, s_l, d, d == 1)
        ys_c.append(y_c)
        ys_l.append(y_l)
    out_l = rwkv_readout(ys_l, s_l, r_k, gn_g, gn_b).astype(p_l.dtype)
    out_c = rwkv_readout(ys_c, s_c, r_k, gn_g, gn_b).astype(p_c.dtype) if need_ctx_out else None
    return out_c, out_l


def sqrelu_mlp(u, w1, w2):
    return jnp.square(jax.nn.relu(u @ w1)) @ w2


def setup_inputs(seed: int = 0) -> dict:
    key = jax.random.key(seed)
    ks = jax.random.split(key, 32)
    f32 = jnp.float32
    nrm = lambda k, shape, s: jax.random.normal(k, shape, f32) * s
    return {
        'x': nrm(ks[0], (BATCH, SEQ, D_MODEL), 1.0),
        'c': nrm(ks[1], (BATCH, D_MODEL), 1.0),
        'ctx': nrm(ks[2], (BATCH, CTX_LEN, D_MODEL), 1.0),
        'c_ctx': nrm(ks[3], (D_MODEL,), 1.0),
        'ada_w': nrm(ks[4], (DEPTH, D_MODEL, 6 * D_MODEL), 0.5 * D_MODEL ** -0.5),
        'ada_b': nrm(ks[5], (DEPTH, 6 * D_MODEL), 0.02),
        'norm1_g': 1.0 + nrm(ks[6], (DEPTH, D_MODEL), 0.02),
        'norm2_g': 1.0 + nrm(ks[7], (DEPTH, D_MODEL), 0.02),
        'w_in': nrm(ks[8], (DEPTH, D_MODEL, N_IN), D_MODEL ** -0.5),
        'a_q_norm': 1.0 + nrm(ks[9], (DEPTH, HEAD_DIM), 0.02),
        'a_k_norm': 1.0 + nrm(ks[10], (DEPTH, HEAD_DIM), 0.02),
        'a_sink': nrm(ks[11], (DEPTH, A_HEADS), 0.5),
        'c_q_norm': 1.0 + nrm(ks[12], (DEPTH, HEAD_DIM), 0.02),
        'c_k_norm': 1.0 + nrm(ks[13], (DEPTH, HEAD_DIM), 0.02),
        'shift_mu': jax.random.uniform(ks[14], (DEPTH, 2, B_IN), f32, 0.0, 0.5),
        'decay_w0': jax.random.uniform(ks[15], (DEPTH, 2, B_WIDTH), f32, -4.0, 1.0),
        'decay_up': nrm(ks[16], (DEPTH, 2, LORA_DECAY, B_WIDTH), 0.5 * LORA_DECAY ** -0.5),
        'iclr_a0': nrm(ks[17], (DEPTH, 2, B_WIDTH), 0.5),
        'iclr_up': nrm(ks[18], (DEPTH, 2, LORA_ICLR, B_WIDTH), 0.5 * LORA_ICLR ** -0.5),
        'gate_up': nrm(ks[19], (DEPTH, LORA_GATE, B_WIDTH), LORA_GATE ** -0.5),
        'k_k': 0.85 + nrm(ks[20], (DEPTH, B_WIDTH), 0.02),
        'k_a': 1.0 + nrm(ks[21], (DEPTH, B_WIDTH), 0.02),
        'r_k': nrm(ks[22], (DEPTH, B_WIDTH), 0.1),
        'gn_g': 1.0 + nrm(ks[23], (DEPTH, B_WIDTH), 0.02),
        'gn_b': nrm(ks[24], (DEPTH, B_WIDTH), 0.02),
        'w_out': nrm(ks[25], (DEPTH, MIX_W, D_MODEL), MIX_W ** -0.5),
        'mlp_w1': nrm(ks[26], (DEPTH, D_MODEL, D_FF), D_MODEL ** -0.5),
        'mlp_w2': nrm(ks[27], (DEPTH, D_FF, D_MODEL), D_FF ** -0.5),
    }


def reference(x, c, ctx, c_ctx, ada_w, ada_b, norm1_g, norm2_g, w_in, a_q_norm, a_k_norm, a_sink, c_q_norm, c_k_norm, shift_mu, decay_w0, decay_up, iclr_a0, iclr_up, gate_up, k_k, k_a, r_k, gn_g, gn_b, w_out, mlp_w1, mlp_w2):
    cos, sin = axial_rope_tables(x.shape[1])
    h_lat, h_ctx = x, ctx
    for l in range(DEPTH):
        last = l == DEPTH - 1
        mod_lat = jax.nn.silu(c) @ ada_w[l] + ada_b[l]
        mod_ctx = jax.nn.silu(c_ctx) @ ada_w[l] + ada_b[l]
        sh1_l, sc1_l, g1_l, sh2_l, sc2_l, g2_l = jnp.split(mod_lat[:, None, :], 6, -1)
        sh1_c, sc1_c, g1_c, sh2_c, sc2_c, g2_c = jnp.split(mod_ctx, 6, -1)

        u_lat = rms_norm(h_lat, norm1_g[l]) * (1.0 + sc1_l) + sh1_l
        u_ctx = rms_norm(h_ctx, norm1_g[l]) * (1.0 + sc1_c) + sh1_c
        pA_l, pB_l, pC_l = jnp.split(u_lat @ w_in[l], (A_IN, A_IN + B_IN), -1)
        pA_c, pB_c, pC_c = jnp.split(u_ctx @ w_in[l], (A_IN, A_IN + B_IN), -1)

        qA_l, kA_l, vA_l = attn_heads(pA_l, A_HEADS, A_KV_HEADS, a_q_norm[l], a_k_norm[l])
        qA_l, kA_l = apply_rope(qA_l, cos, sin), apply_rope(kA_l, cos, sin)
        qA_c, kA_c, vA_c = attn_heads(pA_c, A_HEADS, A_KV_HEADS, a_q_norm[l], a_k_norm[l])
        sinkA = a_sink[l].reshape(A_KV_HEADS, A_HEADS // A_KV_HEADS, 1, 1)
        oA_l = window_attention(qA_l, kA_l, vA_l, kA_c, vA_c, sinkA)

        oB_c, oB_l = rwkv_mixer(pB_c, pB_l, shift_mu[l], decay_w0[l], decay_up[l], iclr_a0[l], iclr_up[l], gate_up[l], k_k[l], k_a[l], r_k[l], gn_g[l], gn_b[l], not last)

        qC_l, kC_l, vC_l = attn_heads(pC_l, C_HEADS, C_KV_HEADS, c_q_norm[l], c_k_norm[l])
        qC_l, kC_l = apply_rope(qC_l, cos, sin), apply_rope(kC_l, cos, sin)
        qC_c, kC_c, vC_c = attn_heads(pC_c, C_HEADS, C_KV_HEADS, c_q_norm[l], c_k_norm[l])
        oC_l = global_attention(qC_l, kC_l, vC_l, kC_c, vC_c)

        h_lat = h_lat + g1_l * (jnp.concatenate([oA_l, oB_l, oC_l], -1) @ w_out[l])
        u2_l = rms_norm(h_lat, norm2_g[l]) * (1.0 + sc2_l) + sh2_l
        h_lat = h_lat + g2_l * sqrelu_mlp(u2_l, mlp_w1[l], mlp_w2[l])

        if not last:
            oA_c = context_attention(qA_c, kA_c, vA_c, sinkA)
            oC_c = context_attention(qC_c, kC_c, vC_c, None)
            h_ctx = h_ctx + g1_c * (jnp.concatenate([oA_c, oB_c, oC_c], -1) @ w_out[l])
            u2_c = rms_norm(h_ctx, norm2_g[l]) * (1.0 + sc2_c) + sh2_c
            h_ctx = h_ctx + g2_c * sqrelu_mlp(u2_c, mlp_w1[l], mlp_w2[l])
    return h_lat
```

```python
import numpy as np
import ml_dtypes
from contextlib import ExitStack
import concourse.bass as bass
import concourse.mybir as mybir
from concourse.bass_utils import run_bass_kernel_spmd

F32 = mybir.dt.float32
BF16 = mybir.dt.bfloat16
AF = mybir.ActivationFunctionType
ALU = mybir.AluOpType
AX = mybir.AxisListType

D = 2048
NCH = 16
SEQ = 4096
CTX = 256
DEPTH = 4
NT = 1088
BLK = [(0, 512, 0), (512, 512, 0), (1024, 64, 1)]
TALL = SEQ + CTX
NORM_EPS = 1e-6
GN_EPS = 64e-5
SEM_LIMIT = 30000
GROUPS = [[0, 1, 2, 3], [4, 5, 6, 7]]
UPIECES = [(0, 3), (3, 3), (6, 3), (9, 3), (12, 3), (15, 1)]


class Buf:
    __slots__ = ("w", "r", "excl")

    def __init__(self, excl=False):
        self.w = None
        self.r = {}
        self.excl = excl


class Tile:
    def __init__(self, t):
        self.t = t
        self.b = Buf()
        self._sub = {}

    def sb(self, key):
        b = self._sub.get(key)
        if b is None:
            b = self._sub[key] = Buf()
        return b

    def __getitem__(self, idx):
        return self.t[idx]


class BankView:
    def __init__(self, t, i):
        self.t = t
        self.i = i
        self.b = Buf(excl=True)

    def __getitem__(self, idx):
        p, f = idx
        return self.t[p, self.i, f]


class K:
    def __init__(self):
        self.nc = bass.Bass("TRN2", target_bir_lowering=False)
        nc = self.nc
        self.ctx = ExitStack()
        self.engs = {"pe": nc.tensor, "dve": nc.vector, "act": nc.scalar, "pool": nc.gpsimd, "sp": nc.sync}
        self.cur = {}
        self.waited = {e: {} for e in self.engs}
        self.nsem = 0
        for e in self.engs:
            self._new_sem(e)
        self.dpool = {}
        self.dcnt = {}
        for q in ("sp", "pool", "act"):
            self.dpool[q] = []
            for i in range(12):
                nm = f"d_{q}_{i}"
                self.dpool[q].append([self.ctx.enter_context(nc.semaphore(nm)), nm, 0])
            self.dcnt[q] = 0
        self.cpool = [[self.ctx.enter_context(nc.semaphore(f"cc_{i}")), f"cc_{i}", 0] for i in range(6)]
        self.ccnt = 0
        self.out_toks = []
        self.nuniq = 0
        self.phase = None

    def _new_sem(self, e):
        nm = f"c_{e}_{self.nsem}"
        self.nsem += 1
        self.cur[e] = [self.ctx.enter_context(self.nc.semaphore(nm)), nm, 0]

    def sbuf(self, shape, dt, name=None):
        self.nuniq += 1
        ctx = self.phase if self.phase is not None else self.ctx
        return Tile(ctx.enter_context(self.nc.sbuf_tensor(f"s_{name or 'sb'}_{self.nuniq}", list(shape), dt)))

    def barrier(self):
        toks = [(c[1], c[0], c[2]) for c in self.cur.values() if c[2] > 0]
        for q in self.dpool:
            toks += [(sl[1], sl[0], sl[2]) for sl in self.dpool[q] if sl[2] > 0]
        toks += [(sl[1], sl[0], sl[2]) for sl in self.cpool if sl[2] > 0]
        for eng, e in self.engs.items():
            for nm, sem, val in toks:
                if nm == self.cur[eng][1]:
                    continue
                if self.waited[eng].get(nm, 0) < val:
                    e.wait_ge(sem, val)
                    self.waited[eng][nm] = val

    def psum(self, shape, dt, name=None):
        self.nuniq += 1
        t = Tile(self.ctx.enter_context(self.nc.psum_tensor("p_" + (name or f"ps{self.nuniq}"), list(shape), dt)))
        t.b.excl = True
        return t

    def dram(self, name, shape, dt, kind="Internal"):
        return Tile(self.nc.dram_tensor(name, list(shape), dt, kind=kind))

    def _deps(self, eng, reads, writes):
        deps = {}

        def add(t):
            if t is None:
                return
            o = deps.get(t[0])
            if o is None or o[2] < t[2]:
                deps[t[0]] = t

        for b in reads:
            add(b.w)
            if b.excl:
                for kk, t in b.r.items():
                    if kk != eng:
                        add(t)
        for b in writes:
            add(b.w)
            for t in b.r.values():
                add(t)
        e = self.engs[eng]
        wd = self.waited[eng]
        for nm, (_, sem, val) in deps.items():
            if eng == "pe" and nm == self.cur["pe"][1]:
                continue
            if wd.get(nm, 0) >= val:
                continue
            e.wait_ge(sem, val)
            wd[nm] = val

    def _mark(self, key, tok, reads, writes):
        for b in writes:
            b.w = tok
            b.r = {}
        for b in reads:
            if b.w is not tok:
                b.r[key] = tok

    def op(self, eng, fn, reads=(), writes=()):
        self._deps(eng, reads, writes)
        inst = fn(self.engs[eng])
        c = self.cur[eng]
        if c[2] >= SEM_LIMIT:
            self._new_sem(eng)
            c = self.cur[eng]
        inst.then_inc(c[0], 1)
        c[2] += 1
        tok = (c[1], c[0], c[2])
        self._mark(eng, tok, reads, writes)
        return tok

    def dma(self, q, out, in_, reads=(), writes=(), is_out=False, **kw):
        self._deps(q, reads, writes)
        e = self.engs[q]
        slot = self.dpool[q][self.dcnt[q] % len(self.dpool[q])]
        self.dcnt[q] += 1
        if slot[2] > 0 and self.waited[q].get(slot[1], 0) < slot[2]:
            e.wait_ge(slot[0], slot[2])
            self.waited[q][slot[1]] = slot[2]
        inst = e.dma_start(out=out, in_=in_, **kw)
        inst.then_inc(slot[0], 16)
        slot[2] += 16
        tok = (slot[1], slot[0], slot[2])
        self._mark(slot[1], tok, reads, writes)
        if is_out:
            self.out_toks.append(tok)
        return tok

    def collective(self, kind, alu, in_t, out_t, reads, writes):
        self._deps("pool", reads, writes)
        e = self.engs["pool"]
        slot = self.cpool[self.ccnt % len(self.cpool)]
        self.ccnt += 1
        if slot[2] > 0 and self.waited["pool"].get(slot[1], 0) < slot[2]:
            e.wait_ge(slot[0], slot[2])
            self.waited["pool"][slot[1]] = slot[2]
        inst = e.collective_compute(kind, alu, replica_groups=GROUPS, ins=[in_t.t.ap().opt()], outs=[out_t.t.ap().opt()])
        inst.then_inc(slot[0], 1)
        slot[2] += 1
        tok = (slot[1], slot[0], slot[2])
        self._mark(slot[1], tok, reads, writes)
        return tok

    def finish(self):
        e = self.engs["sp"]
        for nm, sem, val in self.out_toks:
            if self.waited["sp"].get(nm, 0) < val:
                e.wait_ge(sem, val)
                self.waited["sp"][nm] = val
        self.ctx.close()
        return self.nc


def emit_consts(k):
    c = {}
    c["ones_bf"] = k.sbuf([128, 128], BF16, "ones_bf")
    k.op("pool", lambda e: e.memset(c["ones_bf"][:], 1.0), writes=[c["ones_bf"].b])
    return c


def emit_rsqrt(k, out, src, scale, eps, n, p0=0, p1=128):
    k.op("act", lambda e: e.activation(out=out[p0:p1, 0:n], in_=src[p0:p1, 0:n], func=AF.Sqrt, bias=float(eps), scale=float(scale)),
         reads=[src.b], writes=[out.b])
    k.op("dve", lambda e: e.reciprocal(out=out[p0:p1, 0:n], in_=out[p0:p1, 0:n]), reads=[out.b], writes=[out.b])


class Env:
    pass


class Mods:
    def __init__(self, t):
        self.t = t
        self.b = t.b

    def ap(self, l, lc, w, m):
        return self.t[:, m // 4, (l * 6 + w) * 4 + m % 4, lc:lc + 1]

    def vec(self, l, lc, w):
        return self.t[:, :, (l * 6 + w) * 4:(l * 6 + w) * 4 + 4, lc]


def emit_modprep(k, mods, l, gn_ap, gn_b, which_sc, name):
    gsc = k.sbuf([128, 2, NCH], F32, name + "_gsc")
    for lc in range(2):
        ov = gsc[:, lc, :].rearrange("p (a b) -> p a b", a=4)
        k.op("dve", lambda e: e.tensor_scalar(out=ov, in0=mods.vec(l, lc, which_sc), scalar1=1.0, scalar2=None, op0=ALU.add),
             reads=[mods.b], writes=[gsc.b])
        k.op("dve", lambda e: e.tensor_tensor(out=ov, in0=ov, in1=gn_ap.rearrange("p (a b) -> p a b", a=4), op=ALU.mult),
             reads=[gsc.b, gn_b], writes=[gsc.b])
    return gsc


def emit_norm_mod(k, cst, hT, gsc, mods, l, which_sh, out_bf, sq, stat_ps, rstd):
    for bi, (t0, nb, lc) in enumerate(BLK):
        hb = [hT.sb((m, bi)) for m in range(NCH)]
        for m in range(NCH):
            k.op("act", lambda e: e.activation(out=sq[:, m % 8, 0:nb], in_=hT[:, m, t0:t0 + nb], func=AF.Square),
                 reads=[hb[m]], writes=[sq.sb(m % 8)])
            k.op("pe", lambda e: e.matmul(out=stat_ps[:, 0:nb], lhsT=cst["ones_bf"][:, :], rhs=sq[:, m % 8, 0:nb],
                                          start=(m == 0), stop=(m == NCH - 1)),
                 reads=[sq.sb(m % 8), cst["ones_bf"].b], writes=[stat_ps.b])
        emit_rsqrt(k, rstd, stat_ps, 1.0 / D, NORM_EPS, nb)
        for m in range(NCH):
            tt = k.tmpn[m % 2]
            k.op("dve", lambda e: e.scalar_tensor_tensor(out=tt[:, 0:nb], in0=hT[:, m, t0:t0 + nb], scalar=gsc[:, lc, m:m + 1],
                                                          in1=rstd[:, 0:nb], op0=ALU.mult, op1=ALU.mult),
                 reads=[hb[m], gsc.b, rstd.b], writes=[tt.b])
            k.op("act", lambda e: e.activation(out=out_bf[:, m, t0:t0 + nb], in_=tt[:, 0:nb], func=AF.Identity,
                                               bias=mods.ap(l, lc, which_sh, m), scale=1.0),
                 reads=[tt.b, mods.b], writes=[out_bf.sb((m, bi))])


def end_phase(k):
    k.barrier()
    k.phase.close()
    k.phase = None


def emit_u_exchange(k, E, ubf):
    for pc, (c0, n) in enumerate(UPIECES):
        k.dma("sp", E.u_in[pc].t.ap().rearrange("(c p) t -> p c t", p=128), ubf[:, c0:c0 + n, :],
              reads=[ubf.sb((m, bi)) for m in range(c0, c0 + n) for bi in range(3)], writes=[E.u_in[pc].b])
        k.collective("AllGather", ALU.bypass, E.u_in[pc], E.u_all[pc], reads=[E.u_in[pc].b], writes=[E.u_all[pc].b])


def emit_P0(k, E):
    k.phase = ExitStack()
    sc = k.sbuf([128, NCH, 2], F32)
    scb = k.sbuf([128, NCH, 2], BF16)
    abq = k.sbuf([128, 96], F32)
    modP = k.sbuf([128, 96, 2], F32)
    wb = [k.sbuf([128, NCH, 512], BF16) for _ in range(2)]
    k.dma("sp", sc[:], E.ccTd.rearrange("(c p) r -> p c r", p=128), writes=[sc.b])
    k.dma("sp", abq[:], E.abqd[:, :], writes=[abq.b])
    k.dma("sp", E.ind[:], E.indd[:, :], writes=[E.ind.b])
    k.dma("sp", E.gn1[:], E.gn1d[:, :], writes=[E.gn1.b])
    k.dma("sp", E.gn2[:], E.gn2d[:, :], writes=[E.gn2.b])
    k.dma("sp", E.cmat[:], E.cmatd[:, :, :], writes=[E.cmat.b])
    k.dma("sp", E.wmask[:], E.wmaskd[:, :, :], writes=[E.wmask.b])
    k.dma("sp", E.rmask[:], E.rmaskd[:, :, :], writes=[E.rmask.b])
    k.dma("sp", E.tri[:], E.trid[:, :, :], writes=[E.tri.b])
    k.op("act", lambda e: e.activation(out=scb[:], in_=sc[:], func=AF.Silu), reads=[sc.b], writes=[scb.b])
    awv = E.awqd.rearrange("(c p) n -> p c n", p=128)
    for g in range(24):
        w = wb[g % 2]
        k.dma("pool", w[:], awv[:, :, g * 512:(g + 1) * 512], writes=[w.b])
        for q in range(4):
            cc = g * 4 + q
            p = E.banks[cc % 2]
            for c in range(NCH):
                k.op("pe", lambda e: e.matmul(out=p[:, 0:2], lhsT=w[:, c, q * 128:(q + 1) * 128], rhs=scb[:, c, :],
                                              start=(c == 0), stop=(c == NCH - 1)), reads=[scb.b, w.b], writes=[p.b])
            k.op("dve", lambda e: e.tensor_scalar(out=modP[:, cc, :], in0=p[:, 0:2], scalar1=abq[:, cc:cc + 1], scalar2=None, op0=ALU.add),
                 reads=[p.b, abq.b], writes=[modP.b])
    k.dma("sp", E.mod_in.t.ap(), modP[:, :, :].rearrange("p a b -> p (a b)"), reads=[modP.b], writes=[E.mod_in.b])
    k.collective("AllGather", ALU.bypass, E.mod_in, E.mod_all, reads=[E.mod_in.b], writes=[E.mod_all.b])
    k.dma("sp", E.modS[:, :, :, :].rearrange("p r a b -> p r (a b)"), E.mod_all.t.ap().rearrange("(r p) n -> p r n", p=128),
          reads=[E.mod_all.b], writes=[E.modS.b])
    end_phase(k)


def emit_P1(k, E):
    k.phase = ExitStack()
    hT = k.sbuf([128, NCH, NT], F32, "hT")
    ubf = k.sbuf([128, NCH, NT], BF16, "ubf")
    sq = k.sbuf([128, 8, 512], BF16, "sq")
    k.tmpn = [k.sbuf([128, 512], F32) for _ in range(2)]
    rstd = k.sbuf([128, 512], F32, "rstd")
    hv = E.hT0d.rearrange("(c p) t -> p c t", p=128)
    for bi, (t0, nb, lc) in enumerate(BLK):
        k.dma("sp", hT[:, :, t0:t0 + nb], hv[:, :, t0:t0 + nb], writes=[hT.sb((m, bi)) for m in range(NCH)])
    gsc = emit_modprep(k, E.mods, 0, E.gn1[:, 0:NCH], E.gn1.b, 1, "n1")
    emit_norm_mod(k, E.cst, hT, gsc, E.mods, 0, 0, ubf, sq, E.banks[6], rstd)
    emit_u_exchange(k, E, ubf)
    end_phase(k)


def emit_C(k, E, l):
    k.phase = ExitStack()
    last = l == DEPTH - 1
    hT = k.sbuf([128, NCH, NT], F32, "hT")
    abf = k.sbuf([128, NCH, NT], BF16, "abf")
    h1 = k.sbuf([128, NCH, NT], BF16, "h1")
    sq = k.sbuf([128, 8, 512], BF16, "sq")
    k.tmpn = [k.sbuf([128, 512], F32) for _ in range(2)]
    rtmp = [k.sbuf([128, 512], F32) for _ in range(2)]
    rstd = k.sbuf([128, 512], F32, "rstd")
    wbuf = [k.sbuf([128, NCH, 256], BF16) for _ in range(3)]
    banks = [[E.banks[i * 3 + jj] for jj in range(3)] for i in range(2)]
    stat = E.banks[6]
    mods = E.mods
    hsrc = E.hT0d if l == 0 else E.h_spill.t.ap()
    hv = hsrc.rearrange("(c p) t -> p c t", p=128)
    ov = E.rs_out.t.ap().rearrange("(c p) t -> p c t", p=128)
    for bi, (t0, nb, lc) in enumerate(BLK):
        k.dma("sp", abf[:, :, t0:t0 + nb], ov[:, :, t0:t0 + nb], reads=[E.rs_out.b], writes=[abf.sb((m, bi)) for m in range(NCH)])
    for bi, (t0, nb, lc) in enumerate(BLK):
        k.dma("sp", hT[:, :, t0:t0 + nb], hv[:, :, t0:t0 + nb], reads=[E.h_spill.b], writes=[hT.sb((m, bi)) for m in range(NCH)])
    wcnt = [0]

    def load_w(src_ap):
        w = wbuf[wcnt[0] % 3]
        wcnt[0] += 1
        k.dma("pool", w[:], src_ap, writes=[w.b])
        return w

    def proj(w, mi, m, src, gate_idx):
        bk = banks[m % 2]
        for bi, (t0, nb, lc) in enumerate(BLK):
            for kc in range(NCH):
                k.op("pe", lambda e: e.matmul(out=bk[bi][:, 0:nb], lhsT=w[:, kc, mi * 128:(mi + 1) * 128], rhs=src[:, kc, t0:t0 + nb],
                                              start=(kc == 0), stop=(kc == NCH - 1)),
                     reads=[w.b, src.sb((kc, bi))], writes=[bk[bi].b])
            k.op("dve", lambda e: e.scalar_tensor_tensor(out=hT[:, m, t0:t0 + nb], in0=bk[bi][:, 0:nb],
                                                          scalar=mods.ap(l, lc, gate_idx, m), in1=hT[:, m, t0:t0 + nb],
                                                          op0=ALU.mult, op1=ALU.add),
                 reads=[bk[bi].b, mods.b, hT.sb((m, bi))], writes=[hT.sb((m, bi))])

    woutv = E.woutd[l].rearrange("(c p) n -> p c n", p=128)
    for mp in range(8):
        w = load_w(woutv[:, :, mp * 256:(mp + 1) * 256])
        for mi in range(2):
            proj(w, mi, mp * 2 + mi, abf, 2)
    gsc2 = emit_modprep(k, mods, l, E.gn2[:, l * NCH:(l + 1) * NCH], E.gn2.b, 4, f"n2_{l}")
    emit_norm_mod(k, E.cst, hT, gsc2, mods, l, 3, abf, sq, stat, rstd)
    w1v = E.w1d[l].rearrange("(c p) n -> p c n", p=128)
    w2v = E.w2d[l].rearrange("(c p) n -> p c n", p=128)
    ecnt = 0
    for q in range(4):
        for fp in range(8):
            f0 = (q * 16 + fp * 2) * 128
            w = load_w(w1v[:, :, f0:f0 + 256])
            for fi in range(2):
                fl = fp * 2 + fi
                bk = banks[fl % 2]
                for bi, (t0, nb, lc) in enumerate(BLK):
                    for kc in range(NCH):
                        k.op("pe", lambda e: e.matmul(out=bk[bi][:, 0:nb], lhsT=w[:, kc, fi * 128:(fi + 1) * 128],
                                                      rhs=abf[:, kc, t0:t0 + nb], start=(kc == 0), stop=(kc == NCH - 1)),
                             reads=[w.b, abf.sb((kc, bi))], writes=[bk[bi].b])
                    rt = rtmp[ecnt % 2]
                    ecnt += 1
                    k.op("act", lambda e: e.activation(out=rt[:, 0:nb], in_=bk[bi][:, 0:nb], func=AF.Relu),
                         reads=[bk[bi].b], writes=[rt.b])
                    k.op("pool", lambda e: e.tensor_tensor(out=h1[:, fl, t0:t0 + nb], in0=rt[:, 0:nb], in1=rt[:, 0:nb], op=ALU.mult),
                         reads=[rt.b], writes=[h1.sb((fl, bi))])
        for mp in range(8):
            w = load_w(w2v[:, q * 16:(q + 1) * 16, mp * 256:(mp + 1) * 256])
            for mi in range(2):
                proj(w, mi, mp * 2 + mi, h1, 5)
    if last:
        hov = E.hOd.rearrange("(c p) t -> p c t", p=128)
        for bi, (t0, nb, lc) in enumerate(BLK):
            k.dma("sp", hov[:, :, t0:t0 + nb], hT[:, :, t0:t0 + nb], reads=[hT.sb((m, bi)) for m in range(NCH)], is_out=True)
    else:
        hsv = E.h_spill.t.ap().rearrange("(c p) t -> p c t", p=128)
        for bi, (t0, nb, lc) in enumerate(BLK):
            k.dma("sp", hsv[:, :, t0:t0 + nb], hT[:, :, t0:t0 + nb], reads=[hT.sb((m, bi)) for m in range(NCH)], writes=[E.h_spill.b])
        gsc1 = emit_modprep(k, mods, l + 1, E.gn1[:, (l + 1) * NCH:(l + 2) * NCH], E.gn1.b, 1, f"n1_{l}")
        emit_norm_mod(k, E.cst, hT, gsc1, mods, l + 1, 0, h1, sq, stat, rstd)
        emit_u_exchange(k, E, h1)
    end_phase(k)


def out_write(k, E, src, p0, p1, ncols, segs, t0, n):
    m4 = E.m4[E.m4c[0] % 2]
    E.m4c[0] += 1
    for r in range(4):
        k.op("pool", lambda e: e.tensor_scalar(out=m4[p0:p1, r, 0:ncols], in0=src[p0:p1, 0:ncols], scalar1=E.ind[p0:p1, r:r + 1],
                                               scalar2=None, op0=ALU.mult), reads=[src.b, E.ind.b], writes=[m4.b])
    rs4 = E.rs_in.t.ap().rearrange("(j r q) t -> j q r t", j=4, r=4)
    if t0 < CTX:
        pieces = [(jj, 1024, 64, jj * 64) for jj in range(4)]
    else:
        lat = t0 - CTX
        pieces = [(lat // 1024, lat % 1024, n, 0)]
    P = p1 - p0
    for row0, coff in segs:
        for jj, lcol, cnt, so in pieces:
            k.dma("sp", rs4[jj][row0:row0 + P, :, lcol:lcol + cnt], m4[p0:p1, :, coff + so:coff + so + cnt],
                  reads=[m4.b], writes=[E.rs_in.b])


NW = 2048
CHUNKS = [(i * 128, 128) for i in range(11)] + [(1408 + i * 96, 96) for i in range(4)] + [(1792, 128), (1920, 128)]
PB_ROWS = 1408
PB_ROWOFF = [0, 128, 256, 384, 512, 640, 768, 864, 960, 1056, 1152, 1280]
PBW = TALL + 4
MBLK = [(0, 256)] + [(256 + i * 512, 512) for i in range(8)]
ATT_SCALE = 0.125
PP_QKG = 0
PP_MU = 4
PP_A0 = 28
PP_KK = 32
PP_KA = 34
PP_RK = 36
PP_GNG = 38
PP_GNB = 40
NPP = 42


def pb_col(t):
    return t + 1 if t < CTX else t + 3


def emit_B(k, E, l):
    nc = k.nc
    pB = E.pB
    pp, cmat, wmask, banks, PS = E.pp, E.cmat, E.wmask, E.banks, E.PS
    k.phase = ExitStack()
    E.m4 = [k.sbuf([128, 4, 512], BF16, f"m4_{i}") for i in range(2)]
    wres = k.sbuf([128, NCH, NW], BF16, "wres")
    bones = k.sbuf([128, 128], BF16, "bones")
    QK = [k.sbuf([128, TALL], BF16, f"qk{i}") for i in range(4)]
    V = [k.sbuf([128, 34, 65], BF16, f"v{i}") for i in range(2)]
    ub = [k.sbuf([128, NCH, 512], BF16, f"ub{i}") for i in range(2)]
    cs = [[k.sbuf([128, 512], F32, f"cs{i}{j}") for j in range(2)] for i in range(2)]
    x32 = [k.sbuf([128, 512], F32, "x32_0")] * 2
    sqb = [k.sbuf([128, 512], BF16, "sqb0")] * 2
    rs = [k.sbuf([128, 512], F32, "rs0")] * 2
    xn = [k.sbuf([128, 512], F32, "xn0")] * 2
    t1 = [k.sbuf([128, 512], F32, "t1_0")] * 2
    t2 = [k.sbuf([128, 512], F32, "t2_0")] * 2
    stg = [k.sbuf([128, 512], F32, f"stg{i}") for i in range(3)]
    zero = k.sbuf([128, 16], F32, "zero")

    k.dma("sp", pp[:], E.ppd[l], writes=[pp.b])
    k.op("pool", lambda e: e.memset(zero[:], 0.0), writes=[zero.b])
    k.op("dve", lambda e: e.tensor_copy(out=bones[:], in_=cmat[:, 1, :]), reads=[cmat.b], writes=[bones.b])
    for i in range(2):
        k.op("pool", lambda e: e.memset(V[i][:, :, 64:65], 1.0), writes=[V[i].b])
    pbb = E.pbb
    for col in ((0, 257, 258, PBW - 1) if l == 0 else ()):
        k.dma("sp", pB.t.ap()[:, col:col + 1].rearrange("(c p) o -> p c o", p=128), zero[:, 0:11].rearrange("p (c o) -> p c o", o=1),
              reads=[zero.b], writes=[pbb], allow_slow_non_contiguous=True)
    wv = E.wind[l].rearrange("(c p) n -> p c n", p=128)
    for i in range(4):
        k.dma("pool", wres[:, :, i * 512:(i + 1) * 512], wv[:, :, i * 512:(i + 1) * 512], writes=[wres.sb(i)])
    ident = cmat

    ecnt = [0]
    bk = [0]

    def nbank():
        b = banks[bk[0] % 4]
        bk[0] += 1
        return b

    for bi, (t0, nb) in enumerate(MBLK):
        u = ub[bi % 2]
        if t0 < CTX:
            srcs = [(rr, 1024, 64, rr * 64) for rr in range(4)]
        else:
            srcs = [((t0 - CTX) // 1024, (t0 - CTX) % 1024, nb, 0)]
        for pc, (c0_, n_) in enumerate(UPIECES):
            ua = E.u_all[pc].t.ap().rearrange("(r c p) t -> r p c t", r=4, p=128)
            for rr, scol, cnt, dcol in srcs:
                k.dma("sp", u[:, c0_:c0_ + n_, dcol:dcol + cnt], ua[rr][:, :, scol:scol + cnt], reads=[E.u_all[pc].b], writes=[u.b])
        cost, sint = cs[bi % 2]
        k.dma("sp", cost[:, 0:nb], E.cosd[:, t0:t0 + nb], writes=[cost.b])
        k.dma("sp", sint[:, 0:nb], E.sind[:, t0:t0 + nb], writes=[sint.b])
        for ci, (c0, M) in enumerate(CHUNKS):
            ps = nbank()
            for kc in range(NCH):
                k.op("pe", lambda e: e.matmul(out=ps[0:M, 0:nb], lhsT=wres[:, kc, c0:c0 + M], rhs=u[:, kc, 0:nb],
                                              start=(kc == 0), stop=(kc == NCH - 1)),
                     reads=[wres.sb(c0 // 512), wres.sb((c0 + M - 1) // 512), u.b], writes=[ps.b])
            if ci < 4:
                i2 = ecnt[0] % 2
                ecnt[0] += 1
                xx, sq, rr, xnn, a1, a2 = x32[i2], sqb[i2], rs[i2], xn[i2], t1[i2], t2[i2]
                k.op("act", lambda e: e.activation(out=sq[:, 0:nb], in_=ps[:, 0:nb], func=AF.Square), reads=[ps.b], writes=[sq.b])
                k.op("dve", lambda e: e.tensor_scalar(out=xx[:, 0:nb], in0=ps[:, 0:nb], scalar1=pp[:, PP_QKG + ci:PP_QKG + ci + 1],
                                                       scalar2=None, op0=ALU.mult), reads=[ps.b, pp.b], writes=[xx.b])
                ps2 = nbank()
                k.op("pe", lambda e: e.matmul(out=ps2[:, 0:nb], lhsT=bones[:, :], rhs=sq[:, 0:nb], start=True, stop=True),
                     reads=[bones.b, sq.b], writes=[ps2.b])
                emit_rsqrt(k, rr, ps2, 1.0 / 64, NORM_EPS, nb)
                k.op("dve", lambda e: e.tensor_tensor(out=xnn[:, 0:nb], in0=xx[:, 0:nb], in1=rr[:, 0:nb], op=ALU.mult),
                     reads=[xx.b, rr.b], writes=[xnn.b])
                ps3 = nbank()
                k.op("pe", lambda e: e.matmul(out=ps3[:, 0:nb], lhsT=cmat[:, 2, :], rhs=xnn[:, 0:nb], start=True, stop=True),
                     reads=[cmat.b, xnn.b], writes=[ps3.b])
                k.op("pool", lambda e: e.tensor_tensor(out=a1[:, 0:nb], in0=xnn[:, 0:nb], in1=cost[:, 0:nb], op=ALU.mult),
                     reads=[xnn.b, cost.b], writes=[a1.b])
                k.op("dve", lambda e: e.tensor_tensor(out=a2[:, 0:nb], in0=ps3[:, 0:nb], in1=sint[:, 0:nb], op=ALU.mult),
                     reads=[ps3.b, sint.b], writes=[a2.b])
                k.op("pool", lambda e: e.tensor_tensor(out=QK[ci][:, t0:t0 + nb], in0=a1[:, 0:nb], in1=a2[:, 0:nb], op=ALU.add),
                     reads=[a1.b, a2.b], writes=[QK[ci].sb(bi)])
            elif ci == 4:
                i2 = ecnt[0] % 2
                ecnt[0] += 1
                xx = x32[i2]
                k.op("act", lambda e: e.activation(out=xx[:, 0:nb], in_=ps[:, 0:nb], func=AF.Copy), reads=[ps.b], writes=[xx.b])
                for tt in range(nb // 128):
                    pt = nbank()
                    k.op("pe", lambda e: e.transpose(out=pt[:, 0:128], in_=xx[:, tt * 128:(tt + 1) * 128], identity=cmat[:, 0, :]),
                         reads=[xx.b, cmat.b], writes=[pt.b])
                    tile = t0 // 128 + tt
                    k.op("act", lambda e: e.activation(out=V[0][:, tile, 0:64], in_=pt[:, 0:64], func=AF.Copy),
                         reads=[pt.b], writes=[V[0].sb(tile)])
                    k.op("dve", lambda e: e.tensor_copy(out=V[1][:, tile, 0:64], in_=pt[:, 64:128]),
                         reads=[pt.b], writes=[V[1].sb(tile)])
            else:
                s = stg[ecnt[0] % 3]
                ecnt[0] += 1
                if ecnt[0] % 2:
                    k.op("act", lambda e: e.activation(out=s[0:M, 0:nb], in_=ps[0:M, 0:nb], func=AF.Copy), reads=[ps.b], writes=[s.b])
                else:
                    k.op("dve", lambda e: e.tensor_copy(out=s[0:M, 0:nb], in_=ps[0:M, 0:nb]), reads=[ps.b], writes=[s.b])
                r0 = PB_ROWOFF[ci - 5]
                k.dma("sp", pB.t.ap()[r0:r0 + M, pb_col(t0):pb_col(t0) + nb], s[0:M, 0:nb], reads=[s.b], writes=[pbb])

    pT = [k.sbuf([128, 512], BF16, f"pT{i}") for i in range(3)]
    oraw = [k.sbuf([65, 512], F32, f"oraw{i}") for i in range(2)]
    rrow = [k.sbuf([65, 512], F32, f"rrow{i}") for i in range(2)]
    ostg = [k.sbuf([64, 512], BF16, f"ostg{i}") for i in range(2)]
    onesf = k.sbuf([65, 64], F32, "onesf")
    sinkx = k.sbuf([1, 512], F32, "sinkx")
    sinkb = k.sbuf([1, 512], BF16, "sinkb")
    e64 = k.sbuf([1, 65], BF16, "e64")
    k.op("pool", lambda e: e.memset(onesf[:], 1.0), writes=[onesf.b])
    k.op("pool", lambda e: e.memset(e64[:], 0.0), writes=[e64.b])
    k.op("pool", lambda e: e.memset(e64[0:1, 64:65], 1.0), writes=[e64.b])
    k.dma("sp", sinkx[:], E.sinkd[l], writes=[sinkx.b])
    k.op("act", lambda e: e.activation(out=sinkb[:], in_=sinkx[:], func=AF.Exp), reads=[sinkx.b], writes=[sinkb.b])
    ps_s = [(0, 1), (2, 3)]
    ps_o = [banks[4], banks[5]]
    ps_b = [banks[6], banks[7]]
    pcnt = [0]
    qcnt = [0]
    allq = [QK[i].sb(bi) for i in range(4) for bi in range(len(MBLK))]
    allv = [V[i].sb(t) for i in range(2) for t in range(34)]

    def attend(Q, Kd, Vt, q0, ktiles, sink, orow0):
        io = qcnt[0] % 2
        qcnt[0] += 1
        po = ps_o[io]
        n = len(ktiles)
        for ii, (kt, mi) in enumerate(ktiles):
            b0, b1 = ps_s[pcnt[0] % 2]
            p = pT[pcnt[0] % 3]
            pcnt[0] += 1
            for h in range(2):
                bh = banks[(b0, b1)[h]]
                k.op("pe", lambda e: e.matmul(out=bh[:, 0:256], lhsT=Kd[h * 64:(h + 1) * 64, kt * 128:(kt + 1) * 128],
                                              rhs=Q[h * 64:(h + 1) * 64, q0:q0 + 256], start=True, stop=True),
                     reads=allq, writes=[bh.b])
            k.op("act", lambda e: e.activation(out=p[:, :].rearrange("p (h q) -> p h q", h=2), in_=PS.t[:, b0:b1 + 1, 0:256],
                                               func=AF.Exp, scale=ATT_SCALE),
                 reads=[banks[b0].b, banks[b1].b], writes=[p.b])
            if mi is not None:
                k.op("pool", lambda e: e.tensor_tensor(out=p[:], in0=p[:], in1=wmask[:, mi, :], op=ALU.mult),
                     reads=[p.b, wmask.b], writes=[p.b])
            k.op("pe", lambda e: e.matmul(out=po[0:65, :], lhsT=Vt[:, kt, 0:65], rhs=p[:], start=(ii == 0),
                                          stop=(ii == n - 1 and not sink)), reads=allv + [p.b], writes=[po.b])
        if sink:
            k.op("pe", lambda e: e.matmul(out=po[0:65, :], lhsT=e64[0:1, :], rhs=sinkb[0:1, :], start=False, stop=True),
                 reads=[e64.b, sinkb.b], writes=[po.b])
        orw, rr, os_ = oraw[io], rrow[io], ostg[io]
        k.op("act", lambda e: e.activation(out=orw[0:65, :], in_=po[0:65, :], func=AF.Copy), reads=[po.b], writes=[orw.b])
        k.op("dve", lambda e: e.reciprocal(out=rr[64:65, :], in_=orw[64:65, :]), reads=[orw.b], writes=[rr.b])
        pb_ = ps_b[io]
        k.op("pe", lambda e: e.matmul(out=pb_[0:64, :], lhsT=onesf[64:65, 0:64], rhs=rr[64:65, :], start=True, stop=True),
             reads=[onesf.b, rr.b], writes=[pb_.b])
        k.op("dve", lambda e: e.tensor_tensor(out=os_[0:64, :], in0=orw[0:64, :], in1=pb_[0:64, :], op=ALU.mult),
             reads=[orw.b, pb_.b], writes=[os_.b])
        out_write(k, E, os_, 0, 64, 512, [(orow0, 0), (orow0 + 64, 256)], q0, 256)

    attend(QK[0], QK[1], V[0], 0, [(0, None), (1, None)], True, 0)
    attend(QK[2], QK[3], V[1], 0, [(0, None), (1, None)], False, 384)
    for qb in range(16):
        kts = []
        for r in range(4):
            lt = 2 * qb - 1 + r
            if 0 <= lt < 32:
                kts.append((2 + lt, r))
        kts += [(0, None), (1, None)]
        attend(QK[0], QK[1], V[0], 256 + qb * 256, kts, True, 0)
    for qb in range(16):
        attend(QK[2], QK[3], V[1], 256 + qb * 256, [(t, None) for t in range(34)], False, 384)
    end_phase(k)
    k.phase = ExitStack()
    E.m4 = [k.sbuf([128, 4, 512], BF16, f"m4_{i}") for i in range(2)]
    emit_rwkv(k, E, l, pB, pbb, pp, cmat, banks)
    end_phase(k)


RW_SCALE = -float(np.exp(np.float32(-0.5)))


def emit_rwkv(k, E, l, pB, pbb, pp, cmat, banks):
    sb = k.sbuf
    w0row = sb([1, 512], F32, "w0row")
    wup = sb([96, 512], F32, "wup")
    aup = sb([96, 512], F32, "aup")
    gup = sb([128, 2, 256], F32, "gup")
    rmask, tri = E.rmask, E.tri
    ones_row = sb([1, 128], F32, "ones_row")
    for t_, d_ in ((w0row, E.w0d[l]), (wup, E.wupd[l]), (aup, E.aupd[l]), (gup, E.gupd[l])):
        k.dma("sp", t_[:], d_, writes=[t_.b])
    k.op("pool", lambda e: e.memset(ones_row[:], 1.0), writes=[ones_row.b])
    c0 = sb([128, 12], F32, "c0")
    c1 = sb([128, 2], F32, "c1")
    c2 = sb([128, 2], F32, "c2")
    muv = pp[:, PP_MU:PP_MU + 24].rearrange("p (c two) -> p c two", two=2)
    k.op("dve", lambda e: e.tensor_tensor(out=c0[:], in0=muv[:, :, 0], in1=muv[:, :, 1], op=ALU.add), reads=[pp.b], writes=[c0.b])
    k.op("dve", lambda e: e.tensor_scalar(out=c0[:], in0=c0[:], scalar1=-1.0, scalar2=1.0, op0=ALU.mult, op1=ALU.add),
         reads=[c0.b], writes=[c0.b])
    k.op("dve", lambda e: e.tensor_scalar(out=c1[:], in0=pp[:, PP_KA:PP_KA + 2], scalar1=-1.0, scalar2=1.0, op0=ALU.mult, op1=ALU.add),
         reads=[pp.b], writes=[c1.b])
    k.op("dve", lambda e: e.tensor_scalar(out=c2[:], in0=c1[:], scalar1=2.0, scalar2=None, op0=ALU.mult), reads=[c1.b], writes=[c2.b])
    MSK = {0: dict(Ms=0, MsT=1, MiT=2, nMiT=4), 1: dict(Ms=1, MsT=0, MiT=3, nMiT=5)}
    NB = 512
    raw = [sb([128, NB + 2], F32, f"raw{i}") for i in range(4)]
    rawc = [0]
    S = {n: sb([128, NB], F32, "S_" + n) for n in ("r", "k", "v", "wd", "ad", "ad2", "g0", "g1", "sqk", "rinv", "kap", "a", "a2", "tw",
                                                     "tmp", "kd", "bb", "eL", "eLn", "eLx", "ysum", "u1", "u2", "u3")}
    lw = sb([64, 8, 128], F32, "lw")
    bd = {n: [sb([128, 8, 2, 64], F32, f"bd_{n}{i}") for i in range(2)] for n in ("kap", "bt", "kt", "rt", "v")}
    for n in bd:
        for i in range(2):
            k.op("pool", lambda e: e.memset(bd[n][i][:], 0.0), writes=[bd[n][i].b])
    yf = sb([128, TALL], F32, "yf")
    Mst = [sb([128, 128], F32, f"Mst{i}") for i in range(2)]
    U = {}

    def ut(name):
        if name not in U:
            U[name] = [sb([128, 256 if name.startswith("X") else 128], F32, f"U_{name}{i}") for i in range(2)]
        return U[name]

    ostage = [sb([128, NB], BF16, f"ostage{i}") for i in range(2)]
    bkc = [0]

    def rb():
        b = banks[bkc[0] % 8]
        bkc[0] += 1
        return b

    ecount = [0]

    def ew():
        ecount[0] += 1
        return "dve" if ecount[0] % 2 else "pool"

    def load_shift(bc, M, dst, t0, nb):
        rw = raw[rawc[0] % 4]
        rawc[0] += 1
        r0 = PB_ROWOFF[bc]
        k.dma("sp", rw[0:M, 0:nb + 2], pB.t.ap()[r0:r0 + M, pb_col(t0) - 1:pb_col(t0) + nb + 1], reads=[pbb], writes=[rw.b])
        k.op("pool", lambda e: e.tensor_scalar(out=dst[0:M, 0:nb], in0=rw[0:M, 1:nb + 1], scalar1=c0[0:M, bc:bc + 1], scalar2=None,
                                               op0=ALU.mult), reads=[rw.b, c0.b], writes=[dst.b])
        eng = "dve"
        k.op(eng, lambda e: e.scalar_tensor_tensor(out=dst[0:M, 0:nb], in0=rw[0:M, 0:nb], scalar=pp[0:M, PP_MU + 2 * bc:PP_MU + 2 * bc + 1],
                                                   in1=dst[0:M, 0:nb], op0=ALU.mult, op1=ALU.add),
             reads=[rw.b, pp.b, dst.b], writes=[dst.b])
        k.op(eng, lambda e: e.scalar_tensor_tensor(out=dst[0:M, 0:nb], in0=rw[0:M, 2:nb + 2],
                                                   scalar=pp[0:M, PP_MU + 2 * bc + 1:PP_MU + 2 * bc + 2],
                                                   in1=dst[0:M, 0:nb], op0=ALU.mult, op1=ALU.add),
             reads=[rw.b, pp.b, dst.b], writes=[dst.b])

    def lora_a(dst, src, pr, d, nb):
        ps = rb()
        k.op("pe", lambda e: e.matmul(out=ps[:, 0:nb], lhsT=aup[0:96, d * 256 + pr * 128:d * 256 + (pr + 1) * 128], rhs=src[0:96, 0:nb],
                                      start=True, stop=True), reads=[aup.b, src.b], writes=[ps.b])
        k.op("act", lambda e: e.activation(out=dst[:, 0:nb], in_=ps[:, 0:nb], func=AF.Sigmoid,
                                           bias=pp[:, PP_A0 + pr * 2 + d:PP_A0 + pr * 2 + d + 1], scale=1.0),
             reads=[ps.b, pp.b], writes=[dst.b])

    def prep(pr, t0, nb, d, final, bdi):
        nch = nb // 64
        load_shift(0 + pr, 128, S["r"], t0, nb)
        load_shift(2 + pr, 128, S["k"], t0, nb)
        load_shift(4 + pr, 128, S["v"], t0, nb)
        load_shift(6 + d, 96, S["wd"], t0, nb)
        load_shift(8 + d, 96, S["ad"], t0, nb)
        if final:
            load_shift(8 + (1 - d), 96, S["ad2"], t0, nb)
            load_shift(10, 128, S["g0"], t0, nb)
            load_shift(11, 128, S["g1"], t0, nb)
        kS, rS, vS = S["k"], S["r"], S["v"]
        k.op("act", lambda e: e.activation(out=S["sqk"][:, 0:nb], in_=kS[:, 0:nb], func=AF.Square, scale=pp[:, PP_KK + pr:PP_KK + pr + 1]),
             reads=[kS.b, pp.b], writes=[S["sqk"].b])
        ps = rb()
        k.op("pe", lambda e: e.matmul(out=ps[:, 0:nb], lhsT=cmat[:, 1, :], rhs=S["sqk"][:, 0:nb], start=True, stop=True),
             reads=[cmat.b, S["sqk"].b], writes=[ps.b])
        k.op("act", lambda e: e.activation(out=S["rinv"][:, 0:nb], in_=ps[:, 0:nb], func=AF.Sqrt), reads=[ps.b], writes=[S["rinv"].b])
        k.op("dve", lambda e: e.tensor_scalar(out=S["rinv"][:, 0:nb], in0=S["rinv"][:, 0:nb], scalar1=1e-12, scalar2=None, op0=ALU.max),
             reads=[S["rinv"].b], writes=[S["rinv"].b])
        k.op("dve", lambda e: e.reciprocal(out=S["rinv"][:, 0:nb], in_=S["rinv"][:, 0:nb]), reads=[S["rinv"].b], writes=[S["rinv"].b])
        k.op("dve", lambda e: e.scalar_tensor_tensor(out=S["kap"][:, 0:nb], in0=kS[:, 0:nb], scalar=pp[:, PP_KK + pr:PP_KK + pr + 1],
                                                      in1=S["rinv"][:, 0:nb], op0=ALU.mult, op1=ALU.mult),
             reads=[kS.b, pp.b, S["rinv"].b], writes=[S["kap"].b])
        lora_a(S["a"], S["ad"], pr, d, nb)
        if final:
            lora_a(S["a2"], S["ad2"], pr, 1 - d, nb)
        k.op("act", lambda e: e.activation(out=S["tw"][0:96, 0:nb], in_=S["wd"][0:96, 0:nb], func=AF.Tanh), reads=[S["wd"].b], writes=[S["tw"].b])
        pl = [rb(), rb()]
        wsl = slice(d * 256 + pr * 128, d * 256 + (pr + 1) * 128)
        for c in range(nch):
            pb_ = pl[c // 4]
            k.op("pe", lambda e: e.matmul(out=pb_[0:64, (c % 4) * 128:(c % 4 + 1) * 128], lhsT=S["tw"][0:96, c * 64:(c + 1) * 64],
                                          rhs=wup[0:96, wsl], start=True, stop=False), reads=[S["tw"].b, wup.b], writes=[pb_.b])
            k.op("pe", lambda e: e.matmul(out=pb_[0:64, (c % 4) * 128:(c % 4 + 1) * 128], lhsT=ones_row[0:1, 0:64],
                                          rhs=w0row[0:1, wsl], start=False, stop=True), reads=[ones_row.b, w0row.b], writes=[pb_.b])
        for hb in range((nch + 3) // 4):
            n4 = min(4, nch - hb * 4)
            k.op("act", lambda e: e.activation(out=lw[0:64, hb * 4:hb * 4 + n4, :], in_=pl[hb][0:64, 0:n4 * 128].rearrange("p (c n) -> p c n", n=128),
                                               func=AF.Sigmoid), reads=[pl[hb].b], writes=[lw.b])
        pL, pLx = rb(), rb()
        for c in range(nch):
            k.op("pe", lambda e: e.matmul(out=pL[:, c * 64:(c + 1) * 64], lhsT=lw[0:64, c, :], rhs=tri[0:64, 2 * d, :], start=True, stop=True),
                 reads=[lw.b, tri.b], writes=[pL.b])
        for c in range(nch):
            k.op("pe", lambda e: e.matmul(out=pLx[:, c * 64:(c + 1) * 64], lhsT=lw[0:64, c, :], rhs=tri[0:64, 2 * d + 1, :], start=True, stop=True),
                 reads=[lw.b, tri.b], writes=[pLx.b])
        k.op("act", lambda e: e.activation(out=S["eL"][:, 0:nb], in_=pL[:, 0:nb], func=AF.Exp), reads=[pL.b], writes=[S["eL"].b])
        k.op("act", lambda e: e.activation(out=S["eLn"][:, 0:nb], in_=pL[:, 0:nb], func=AF.Exp, scale=-1.0), reads=[pL.b], writes=[S["eLn"].b])
        k.op("act", lambda e: e.activation(out=S["eLx"][:, 0:nb], in_=pLx[:, 0:nb], func=AF.Exp), reads=[pLx.b], writes=[S["eLx"].b])
        k.op("dve", lambda e: e.tensor_scalar(out=S["tmp"][:, 0:nb], in0=S["a"][:, 0:nb], scalar1=pp[:, PP_KA + pr:PP_KA + pr + 1],
                                               scalar2=c1[:, pr:pr + 1], op0=ALU.mult, op1=ALU.add),
             reads=[S["a"].b, pp.b, c1.b], writes=[S["tmp"].b])
        k.op("pool", lambda e: e.tensor_tensor(out=S["kd"][:, 0:nb], in0=kS[:, 0:nb], in1=S["tmp"][:, 0:nb], op=ALU.mult),
             reads=[kS.b, S["tmp"].b], writes=[S["kd"].b])
        k.op("pool", lambda e: e.tensor_tensor(out=S["bb"][:, 0:nb], in0=S["kap"][:, 0:nb], in1=S["a"][:, 0:nb], op=ALU.mult),
             reads=[S["kap"].b, S["a"].b], writes=[S["bb"].b])
        for name, x, y in (("kap", S["kap"], S["eLx"]), ("bt", S["bb"], S["eLn"]), ("kt", S["kd"], S["eLn"]), ("rt", rS, S["eL"]),
                           ("v", vS, None)):
            dst = bd[name][bdi]
            for h in range(2):
                hs = slice(h * 64, (h + 1) * 64)
                eng = ew()
                xv = x[hs, 0:nb].rearrange("p (c s) -> p c s", s=64)
                if y is None:
                    k.op(eng, lambda e: e.tensor_copy(out=dst[hs, 0:nch, h, :], in_=xv), reads=[x.b], writes=[dst.b])
                else:
                    yv = y[hs, 0:nb].rearrange("p (c s) -> p c s", s=64)
                    k.op(eng, lambda e: e.tensor_tensor(out=dst[hs, 0:nch, h, :], in0=xv, in1=yv, op=ALU.mult),
                         reads=[x.b, y.b], writes=[dst.b])

    ucnt = [0]

    def unit(pr, tok0, c, d, bdi, mi, final):
        ui = ucnt[0] % 2
        ucnt[0] += 1
        mk = MSK[d]
        Kap = bd["kap"][bdi][:, c, :, :].rearrange("p a b -> p (a b)")
        Bt = bd["bt"][bdi][:, c, :, :].rearrange("p a b -> p (a b)")
        Kt = bd["kt"][bdi][:, c, :, :].rearrange("p a b -> p (a b)")
        Rt = bd["rt"][bdi][:, c, :, :].rearrange("p a b -> p (a b)")
        Vf = bd["v"][bdi][:, c, :, :].rearrange("p a b -> p (a b)")
        bdb = [bd[n][bdi].b for n in bd]
        gcol = c * 64 + (63 if d == 0 else 0)
        gam = S["eL"][:, gcol:gcol + 1]

        def T(n):
            return ut(n)[ui]

        def mm(lhsT, rhs, rd, N=128, acc=None):
            ps = rb() if acc is None else acc[0]
            st, sp_ = (True, True) if acc is None else (acc[1], acc[2])
            k.op("pe", lambda e: e.matmul(out=ps[:, 0:N], lhsT=lhsT, rhs=rhs, start=st, stop=sp_), reads=rd, writes=[ps.b])
            return ps

        def ev_copy(dst_t, dst_ap, ps, N=128, scale=None):
            if scale is None:
                k.op("act", lambda e: e.activation(out=dst_ap, in_=ps[:, 0:N], func=AF.Copy), reads=[ps.b], writes=[dst_t.b])
            else:
                k.op("act", lambda e: e.activation(out=dst_ap, in_=ps[:, 0:N], func=AF.Copy, scale=scale), reads=[ps.b, S["eL"].b],
                     writes=[dst_t.b])

        def ev_tt(dst_t, dst_ap, in0_ap, in0_b, ps, op, N=128, psfirst=False):
            if psfirst:
                k.op("dve", lambda e: e.tensor_tensor(out=dst_ap, in0=ps[:, 0:N], in1=in0_ap, op=op), reads=[ps.b] + in0_b, writes=[dst_t.b])
            else:
                k.op("dve", lambda e: e.tensor_tensor(out=dst_ap, in0=in0_ap, in1=ps[:, 0:N], op=op), reads=[ps.b] + in0_b, writes=[dst_t.b])

        X = [T("Xa"), T("Xb")]
        ps = rb()
        k.op("pe", lambda e: e.transpose(out=ps[:, 0:128], in_=Kap, identity=cmat[:, 0, :]), reads=bdb + [cmat.b], writes=[ps.b])
        ev_copy(X[0], X[0][:, 0:128], ps)
        ps = rb()
        k.op("pe", lambda e: e.transpose(out=ps[:, 0:128], in_=Bt, identity=cmat[:, 0, :]), reads=bdb + [cmat.b], writes=[ps.b])
        ev_copy(T("nBtT"), T("nBtT")[:, :], ps, scale=-1.0)
        ps = rb()
        k.op("pe", lambda e: e.transpose(out=ps[:, 0:128], in_=Kt, identity=cmat[:, 0, :]), reads=bdb + [cmat.b], writes=[ps.b])
        ev_copy(T("KtT"), T("KtT")[:, :], ps)
        ps = rb()
        k.op("pe", lambda e: e.transpose(out=ps[:, 0:128], in_=Vf, identity=cmat[:, 0, :]), reads=bdb + [cmat.b], writes=[ps.b])
        ev_copy(T("VT"), T("VT")[:, :], ps)
        for nm, l_, r_, m_ in (("N", Kap, Bt, "Ms"), ("Z", Bt, Kap, "MsT"), ("AkkT", Kt, Kap, "MsT"), ("ArkT", Kt, Rt, "MiT"),
                               ("nArbT", Bt, Rt, "nMiT")):
            ps = mm(l_, r_, bdb)
            ev_tt(T(nm), T(nm)[:, :], rmask[:, mk[m_], :], [rmask.b], ps, ALU.mult, psfirst=True)
        ps = mm(T("AkkT")[:, :], T("VT")[:, :], [T("AkkT").b, T("VT").b])
        ev_copy(X[0], X[0][:, 128:256], ps)
        Zc, Nc = T("Z"), T("N")
        ps = mm(Zc[:, :], X[0][:, :], [Zc.b, X[0].b], N=256)
        ev_tt(X[1], X[1][:, :], X[0][:, :], [X[0].b], ps, ALU.subtract, N=256)
        xi = 1
        for lev in range(5):
            Zn = T(f"Z{lev}")
            ps = mm(Nc[:, :], Zc[:, :], [Nc.b, Zc.b])
            ev_copy(Zn, Zn[:, :], ps)
            if lev < 4:
                Nn = T(f"N{lev}")
                ps = mm(Zc[:, :], Nc[:, :], [Nc.b, Zc.b])
                ev_copy(Nn, Nn[:, :], ps)
                Nc = Nn
            Zc = Zn
            ps = mm(Zc[:, :], X[xi][:, :], [Zc.b, X[xi].b], N=256)
            ev_tt(X[1 - xi], X[1 - xi][:, :], X[xi][:, :], [X[xi].b], ps, ALU.add, N=256)
            xi = 1 - xi
        Xf = X[xi]
        P = Xf[:, 0:128]
        Q = Xf[:, 128:256]
        ps = mm(P, T("nBtT")[:, :], [Xf.b, T("nBtT").b])
        ev_tt(T("ET"), T("ET")[:, :], cmat[:, 0, :], [cmat.b], ps, ALU.add)
        ph = rb()
        mm(T("KtT")[:, :], T("VT")[:, :], [T("KtT").b, T("VT").b], acc=(ph, True, False))
        mm(T("nBtT")[:, :], Q, [T("nBtT").b, Xf.b], acc=(ph, False, True))
        ev_copy(T("Hg"), T("Hg")[:, :], ph, scale=gam)
        ps = mm(P, T("nArbT")[:, :], [Xf.b, T("nArbT").b])
        ev_tt(T("YmT"), T("YmT")[:, :], Rt, bdb, ps, ALU.add)
        M0 = Mst[mi]
        M1 = Mst[1 - mi]
        py = rb()
        mm(M0[:, :], T("YmT")[:, :], [M0.b, T("YmT").b], acc=(py, True, False))
        mm(T("VT")[:, :], T("ArkT")[:, :], [T("VT").b, T("ArkT").b], acc=(py, False, False))
        mm(Q, T("nArbT")[:, :], [Xf.b, T("nArbT").b], acc=(py, False, True))
        for h in range(2):
            hs = slice(h * 64, (h + 1) * 64)
            if not final:
                k.op("act", lambda e: e.activation(out=yf[hs, tok0:tok0 + 64], in_=py[hs, h * 64:(h + 1) * 64], func=AF.Copy),
                     reads=[py.b], writes=[yf.sb(tok0 // 512)])
            else:
                lc = (tok0 - (0 if tok0 < CTX else CTX)) % 512
                k.op("dve", lambda e: e.tensor_tensor(out=S["ysum"][hs, lc:lc + 64], in0=py[hs, h * 64:(h + 1) * 64],
                                                       in1=yf[hs, tok0:tok0 + 64], op=ALU.add),
                     reads=[py.b, yf.sb(tok0 // 512)], writes=[S["ysum"].b])
        pm = mm(T("ET")[:, :], M0[:, :], [T("ET").b, M0.b])
        k.op("dve", lambda e: e.scalar_tensor_tensor(out=M1[:, :], in0=pm[:, 0:128], scalar=gam, in1=T("Hg")[:, :], op0=ALU.mult, op1=ALU.add),
             reads=[pm.b, S["eL"].b, T("Hg").b], writes=[M1.b])
        return 1 - mi

    def finalize(pr, t0, nb):
        ys = S["ysum"]
        ps = rb()
        k.op("pe", lambda e: e.matmul(out=ps[:, 0:nb], lhsT=cmat[:, 1, :], rhs=ys[:, 0:nb], start=True, stop=True),
             reads=[cmat.b, ys.b], writes=[ps.b])
        k.op("dve", lambda e: e.scalar_tensor_tensor(out=S["u1"][:, 0:nb], in0=ps[:, 0:nb], scalar=-1.0 / 64, in1=ys[:, 0:nb],
                                                      op0=ALU.mult, op1=ALU.add), reads=[ps.b, ys.b], writes=[S["u1"].b])
        k.op("act", lambda e: e.activation(out=S["u2"][:, 0:nb], in_=S["u1"][:, 0:nb], func=AF.Square), reads=[S["u1"].b], writes=[S["u2"].b])
        ps = rb()
        k.op("pe", lambda e: e.matmul(out=ps[:, 0:nb], lhsT=cmat[:, 1, :], rhs=S["u2"][:, 0:nb], start=True, stop=True),
             reads=[cmat.b, S["u2"].b], writes=[ps.b])
        emit_rsqrt(k, S["u3"], ps, 1.0 / 64, GN_EPS, nb)
        k.op("dve", lambda e: e.tensor_tensor(out=S["u1"][:, 0:nb], in0=S["u1"][:, 0:nb], in1=S["u3"][:, 0:nb], op=ALU.mult),
             reads=[S["u1"].b, S["u3"].b], writes=[S["u1"].b])
        k.op("act", lambda e: e.activation(out=S["u1"][:, 0:nb], in_=S["u1"][:, 0:nb], func=AF.Identity,
                                           bias=pp[:, PP_GNB + pr:PP_GNB + pr + 1], scale=pp[:, PP_GNG + pr:PP_GNG + pr + 1]),
             reads=[S["u1"].b, pp.b], writes=[S["u1"].b])
        k.op("pool", lambda e: e.tensor_tensor(out=S["u2"][:, 0:nb], in0=S["a"][:, 0:nb], in1=S["a2"][:, 0:nb], op=ALU.add),
             reads=[S["a"].b, S["a2"].b], writes=[S["u2"].b])
        k.op("dve", lambda e: e.tensor_scalar(out=S["u2"][:, 0:nb], in0=S["u2"][:, 0:nb], scalar1=pp[:, PP_KA + pr:PP_KA + pr + 1],
                                               scalar2=c2[:, pr:pr + 1], op0=ALU.mult, op1=ALU.add),
             reads=[S["u2"].b, pp.b, c2.b], writes=[S["u2"].b])
        k.op("pool", lambda e: e.tensor_tensor(out=S["u2"][:, 0:nb], in0=S["u2"][:, 0:nb], in1=S["k"][:, 0:nb], op=ALU.mult),
             reads=[S["u2"].b, S["k"].b], writes=[S["u2"].b])
        k.op("dve", lambda e: e.scalar_tensor_tensor(out=S["u2"][:, 0:nb], in0=S["r"][:, 0:nb], scalar=pp[:, PP_RK + pr:PP_RK + pr + 1],
                                                      in1=S["u2"][:, 0:nb], op0=ALU.mult, op1=ALU.mult),
             reads=[S["r"].b, pp.b, S["u2"].b], writes=[S["u2"].b])
        ps = rb()
        k.op("pe", lambda e: e.matmul(out=ps[:, 0:nb], lhsT=cmat[:, 1, :], rhs=S["u2"][:, 0:nb], start=True, stop=True),
             reads=[cmat.b, S["u2"].b], writes=[ps.b])
        k.op("dve", lambda e: e.tensor_tensor(out=S["u3"][:, 0:nb], in0=ps[:, 0:nb], in1=S["v"][:, 0:nb], op=ALU.mult),
             reads=[ps.b, S["v"].b], writes=[S["u3"].b])
        k.op("pool", lambda e: e.tensor_tensor(out=S["u1"][:, 0:nb], in0=S["u1"][:, 0:nb], in1=S["u3"][:, 0:nb], op=ALU.add),
             reads=[S["u1"].b, S["u3"].b], writes=[S["u1"].b])
        for gi, gn_ in enumerate(("g0", "g1")):
            k.op("act", lambda e: e.activation(out=S[gn_][:, 0:nb], in_=S[gn_][:, 0:nb], func=AF.Sigmoid), reads=[S[gn_].b], writes=[S[gn_].b])
        ps = rb()
        for gi, gn_ in enumerate(("g0", "g1")):
            k.op("pe", lambda e: e.matmul(out=ps[:, 0:nb], lhsT=gup[:, gi, pr * 128:(pr + 1) * 128], rhs=S[gn_][:, 0:nb],
                                          start=(gi == 0), stop=(gi == 1)), reads=[gup.b, S[gn_].b], writes=[ps.b])
        os_ = ostage[ecount[0] % 2]
        ecount[0] += 1
        k.op("dve", lambda e: e.tensor_tensor(out=os_[:, 0:nb], in0=S["u1"][:, 0:nb], in1=ps[:, 0:nb], op=ALU.mult),
             reads=[S["u1"].b, ps.b], writes=[os_.b])
        out_write(k, E, os_, 0, 128, nb, [(128 + pr * 128, 0)], t0, nb)

    blocks_f = MBLK
    blocks_b = [MBLK[0]] + MBLK[:0:-1]
    bdc = 0
    for pr in range(2):
        for d in range(2):
            mi = 0
            k.op("pool", lambda e: e.memset(Mst[0][:], 0.0), writes=[Mst[0].b])
            for (t0, nb) in (blocks_f if d == 0 else blocks_b):
                prep(pr, t0, nb, d, d == 1, bdc % 2)
                nch = nb // 64
                for c in (range(nch) if d == 0 else range(nch - 1, -1, -1)):
                    mi = unit(pr, t0 + c * 64, c, d, bdc % 2, mi, d == 1)
                if d == 1:
                    finalize(pr, t0, nb)
                bdc += 1


def build_F():
    k = K()
    nc = k.nc
    E = Env()

    def din(name, shape, dt=F32):
        return nc.dram_tensor(name, list(shape), dt, kind="ExternalInput").ap()

    E.hT0d = din("hT0", [D, NT])
    E.ccTd = din("ccT", [D, 2])
    E.awqd = din("awq", [D, 96 * 128])
    E.abqd = din("abq", [128, 96])
    E.indd = din("ind", [128, 4])
    E.gn1d = din("gn1T", [128, DEPTH * NCH])
    E.gn2d = din("gn2T", [128, DEPTH * NCH])
    E.wind = din("win", [DEPTH, D, NW])
    E.ppd = din("pp", [DEPTH, 128, NPP])
    E.sinkd = din("sinkrow", [DEPTH, 1, 512])
    E.w0d = din("w0row", [DEPTH, 1, 512])
    E.wupd = din("wup", [DEPTH, 96, 512])
    E.aupd = din("aup", [DEPTH, 96, 512])
    E.gupd = din("gup", [DEPTH, 128, 2, 256])
    E.cosd = din("cosT", [128, TALL])
    E.sind = din("sinT", [128, TALL])
    E.cmatd = din("cmat", [128, 3, 128])
    E.wmaskd = din("wmask", [128, 4, 512], BF16)
    E.rmaskd = din("rmask", [128, 6, 128])
    E.trid = din("tri", [64, 4, 64])
    E.woutd = din("wout", [DEPTH, D, D])
    E.w1d = din("w1", [DEPTH, D, 4 * D])
    E.w2d = din("w2", [DEPTH, 4 * D, D])
    E.hOd = nc.dram_tensor("hTo", [D, NT], F32, kind="ExternalOutput").ap()
    E.mod_in = k.dram("mod_in", [128, 192], F32)
    E.mod_all = k.dram("mod_all", [512, 192], F32)
    E.u_in = [k.dram(f"u_in{i}", [n * 128, NT], BF16) for i, (c0, n) in enumerate(UPIECES)]
    E.u_all = [k.dram(f"u_all{i}", [4 * n * 128, NT], BF16) for i, (c0, n) in enumerate(UPIECES)]
    E.rs_in = k.dram("rs_in", [4 * D, NT], BF16)
    E.rs_out = k.dram("rs_out", [D, NT], BF16)
    E.h_spill = k.dram("h_spill", [D, NT], F32)
    E.pB = k.dram("pB", [PB_ROWS, PBW], F32)
    E.pbb = E.pB.b
    E.cst = emit_consts(k)
    E.modS = k.sbuf([128, 4, 96, 2], F32, "modS")
    E.mods = Mods(E.modS)
    E.ind = k.sbuf([128, 4], F32, "ind")
    E.gn1 = k.sbuf([128, DEPTH * NCH], F32, "gn1")
    E.gn2 = k.sbuf([128, DEPTH * NCH], F32, "gn2")
    E.pp = k.sbuf([128, NPP], F32, "pp")
    E.cmat = k.sbuf([128, 3, 128], F32, "cmat")
    E.wmask = k.sbuf([128, 4, 512], BF16, "wmask")
    E.rmask = k.sbuf([128, 6, 128], F32, "rmask")
    E.tri = k.sbuf([64, 4, 64], F32, "tri")
    E.m4c = [0]
    E.PS = k.psum([128, 8, 512], F32, "PSall")
    E.banks = [BankView(E.PS.t, i) for i in range(8)]
    emit_P0(k, E)
    emit_P1(k, E)
    for l in range(DEPTH):
        emit_B(k, E, l)
        k.collective("ReduceScatter", ALU.add, E.rs_in, E.rs_out, reads=[E.rs_in.b], writes=[E.rs_out.b])
        emit_C(k, E, l)
    return k.finish()


_PROG = {}


def _c(a):
    return np.ascontiguousarray(a)


def host_B_consts():
    ident = np.eye(128, dtype=np.float32)
    bo = np.zeros((128, 128), np.float32)
    bo[:64, :64] = 1
    bo[64:, 64:] = 1
    R = np.zeros((128, 128), np.float32)
    for m in range(128):
        if m % 64 < 32:
            R[m + 32, m] = -1.0
        else:
            R[m - 32, m] = 1.0
    cmat = _c(np.stack([ident, bo, R], 1))
    t = np.arange(SEQ)
    row = (t // 64).astype(np.float32)
    col = (t % 64).astype(np.float32)
    inv = (10000.0 ** (-np.arange(16, dtype=np.float32) / 16)).astype(np.float32)
    ang = np.concatenate([row[:, None] * inv, col[:, None] * inv], -1).astype(np.float32)
    cosT = np.ones((128, TALL), np.float32)
    sinT = np.zeros((128, TALL), np.float32)
    c = np.cos(ang).T.astype(np.float32)
    s = np.sin(ang).T.astype(np.float32)
    for p in range(128):
        cosT[p, CTX:] = c[p % 32]
        sinT[p, CTX:] = s[p % 32]
    wm = np.zeros((128, 4, 2, 256), np.float32)
    kl = np.arange(128)[:, None]
    ql = np.arange(256)[None, :]
    for r in range(4):
        d = ql - kl - (r - 1) * 128
        wm[:, r, :, :] = (np.abs(d) <= 128)[:, None, :]
    wmask = _c(wm.reshape(128, 4, 512).astype(ml_dtypes.bfloat16))
    i64 = np.arange(64)
    SL = (i64[:, None] > i64[None, :]).astype(np.float32)
    SU = SL.T.copy()
    UI = (i64[:, None] <= i64[None, :]).astype(np.float32)
    LI = UI.T.copy()
    rm = np.zeros((128, 6, 128), np.float32)
    for mi, mm_ in enumerate((SL, SU, UI, LI, -UI, -LI)):
        rm[:64, mi, :64] = mm_
        rm[64:, mi, 64:] = mm_
    tri = np.stack([UI, SU, LI, SL], 1).astype(np.float32) * np.float32(RW_SCALE)
    return {"cmat": cmat, "cosT": cosT, "sinT": sinT, "wmask": wmask, "rmask": _c(rm), "tri": _c(tri)}


A_IN_ = 768
B_IN_ = 3712


def host_core_inputs(I, core, consts):
    b, j = divmod(core, 4)
    kv = j // 2
    m = dict(consts)
    x, ctx = I["x"], I["ctx"]
    m["hT0"] = _c(np.concatenate([x[b, j * 1024:(j + 1) * 1024], ctx[b, j * 64:(j + 1) * 64]], 0).T)
    m["ccT"] = _c(np.stack([I["c"][b], I["c_ctx"]], 0).T)
    awq = np.empty((D, 96 * 128), np.float32)
    abq = np.empty((128, 96), np.float32)
    for l in range(DEPTH):
        for w in range(6):
            for cl in range(4):
                cc = (l * 6 + w) * 4 + cl
                g0 = w * D + (j * 4 + cl) * 128
                awq[:, cc * 128:(cc + 1) * 128] = I["ada_w"][l][:, g0:g0 + 128]
                abq[:, cc] = I["ada_b"][l][g0:g0 + 128]
    m["awq"] = awq
    m["abq"] = abq
    ind = np.zeros((128, 4), np.float32)
    ind[:, j] = 1.0
    m["ind"] = ind
    m["gn1T"] = _c(np.concatenate([I["norm1_g"][l].reshape(-1, 128).T for l in range(DEPTH)], 1))
    m["gn2T"] = _c(np.concatenate([I["norm2_g"][l].reshape(-1, 128).T for l in range(DEPTH)], 1))
    cols = []
    cols += list(range(2 * j * 64, (2 * j + 2) * 64))
    cols += list(range(512 + kv * 64, 512 + (kv + 1) * 64)) * 2
    cb = A_IN_ + B_IN_
    cols += list(range(cb + 2 * j * 64, cb + (2 * j + 2) * 64))
    cols += list(range(cb + 512 + kv * 64, cb + 512 + (kv + 1) * 64)) * 2
    cols += list(range(640 + kv * 64, 640 + (kv + 1) * 64))
    cols += list(range(cb + 640 + kv * 64, cb + 640 + (kv + 1) * 64))
    bb = A_IN_
    for part in range(3):
        cols += list(range(bb + part * 1024 + 4 * j * 64, bb + part * 1024 + (4 * j + 4) * 64))
    cols += list(range(bb + 3072, bb + 3712))
    assert len(cols) == NW
    m["win"] = _c(I["w_in"][:, :, cols])
    bcols = [c_ - bb for c_ in cols[640:]]
    hc = slice(4 * j * 64, (4 * j + 4) * 64)
    pp = np.zeros((DEPTH, 128, NPP), np.float32)
    for l in range(DEPTH):
        pp[l, :, PP_QKG + 0] = np.tile(I["a_q_norm"][l], 2)
        pp[l, :, PP_QKG + 1] = np.tile(I["a_k_norm"][l], 2)
        pp[l, :, PP_QKG + 2] = np.tile(I["c_q_norm"][l], 2)
        pp[l, :, PP_QKG + 3] = np.tile(I["c_k_norm"][l], 2)
        mu = I["shift_mu"][l]
        pos = 0
        for ci in range(5, 17):
            M = CHUNKS[ci][1]
            idx = bcols[pos:pos + M]
            pos += M
            pp[l, :M, PP_MU + (ci - 5) * 2 + 0] = mu[0][idx]
            pp[l, :M, PP_MU + (ci - 5) * 2 + 1] = mu[1][idx]
        for pr in range(2):
            sl = slice(4 * j * 64 + pr * 128, 4 * j * 64 + (pr + 1) * 128)
            for d in range(2):
                pp[l, :, PP_A0 + pr * 2 + d] = I["iclr_a0"][l][d][sl]
            pp[l, :, PP_KK + pr] = I["k_k"][l][sl]
            pp[l, :, PP_KA + pr] = I["k_a"][l][sl]
            pp[l, :, PP_RK + pr] = I["r_k"][l][sl]
            pp[l, :, PP_GNG + pr] = I["gn_g"][l][sl]
            pp[l, :, PP_GNB + pr] = I["gn_b"][l][sl]
    m["pp"] = pp
    m["sinkrow"] = _c(np.stack([np.repeat(I["a_sink"][l][2 * j:2 * j + 2], 256)[None, :] for l in range(DEPTH)]).astype(np.float32))
    m["w0row"] = _c(np.stack([np.concatenate([I["decay_w0"][l][d][hc] for d in range(2)])[None, :] for l in range(DEPTH)]))
    m["wup"] = _c(np.stack([np.concatenate([I["decay_up"][l][d][:, hc] for d in range(2)], 1) for l in range(DEPTH)]))
    m["aup"] = _c(np.stack([np.concatenate([I["iclr_up"][l][d][:, hc] for d in range(2)], 1) for l in range(DEPTH)]))
    m["gup"] = _c(np.stack([I["gate_up"][l][:, hc].reshape(2, 128, 256).transpose(1, 0, 2) for l in range(DEPTH)]))
    return m


def kernel(x, c, ctx, c_ctx, ada_w, ada_b, norm1_g, norm2_g, w_in, a_q_norm, a_k_norm, a_sink, c_q_norm, c_k_norm, shift_mu,
           decay_w0, decay_up, iclr_a0, iclr_up, gate_up, k_k, k_a, r_k, gn_g, gn_b, w_out, mlp_w1, mlp_w2):
    I = dict(x=x, c=c, ctx=ctx, c_ctx=c_ctx, ada_w=ada_w, ada_b=ada_b, norm1_g=norm1_g, norm2_g=norm2_g, w_in=w_in,
             a_q_norm=a_q_norm, a_k_norm=a_k_norm, a_sink=a_sink, c_q_norm=c_q_norm, c_k_norm=c_k_norm, shift_mu=shift_mu,
             decay_w0=decay_w0, decay_up=decay_up, iclr_a0=iclr_a0, iclr_up=iclr_up, gate_up=gate_up, k_k=k_k, k_a=k_a, r_k=r_k,
             gn_g=gn_g, gn_b=gn_b, w_out=w_out, mlp_w1=mlp_w1, mlp_w2=mlp_w2)
    I = {k_: np.asarray(v, dtype=np.float32) for k_, v in I.items()}
    consts = host_B_consts()
    perm = []
    for r in range(4):
        perm += list(range(2 * r * 64, 2 * r * 64 + 128)) + list(range(512 + 4 * r * 64, 512 + 4 * r * 64 + 256)) \
            + list(range(1536 + 2 * r * 64, 1536 + 2 * r * 64 + 128))
    shared = {"wout": _c(I["w_out"][:, perm, :]), "w1": I["mlp_w1"], "w2": I["mlp_w2"]}
    maps = []
    for core in range(8):
        m = host_core_inputs(I, core, consts)
        m.update(shared)
        maps.append(m)
    if "F" not in _PROG:
        _PROG["F"] = build_F()
    res = run_bass_kernel_spmd(_PROG["F"], maps, core_ids=list(range(8))).results
    out = np.empty((2, SEQ, D), np.float32)
    for core in range(8):
        b, j = divmod(core, 4)
        out[b, j * 1024:(j + 1) * 1024] = res[core]["hTo"][:, 0:1024].T
    return out
```

```python
import numpy as np
import ml_dtypes
from contextlib import ExitStack
import concourse.bass as bass
import concourse.mybir as mybir
from concourse.bass_utils import run_bass_kernel_spmd

F32 = mybir.dt.float32
BF16 = mybir.dt.bfloat16
AF = mybir.ActivationFunctionType
ALU = mybir.AluOpType
AX = mybir.AxisListType

D = 2048
NCH = 16
SEQ = 4096
CTX = 256
DEPTH = 4
NT = 1088
BLK = [(0, 512, 0), (512, 512, 0), (1024, 64, 1)]
TALL = SEQ + CTX
NORM_EPS = 1e-6
GN_EPS = 64e-5
SEM_LIMIT = 30000
GROUPS = [[0, 1, 2, 3], [4, 5, 6, 7]]
UPIECES = [(0, 3), (3, 3), (6, 3), (9, 3), (12, 3), (15, 1)]


class Buf:
    __slots__ = ("w", "r", "excl")

    def __init__(self, excl=False):
        self.w = None
        self.r = {}
        self.excl = excl


class Tile:
    def __init__(self, t):
        self.t = t
        self.b = Buf()
        self._sub = {}

    def sb(self, key):
        b = self._sub.get(key)
        if b is None:
            b = self._sub[key] = Buf()
        return b

    def __getitem__(self, idx):
        return self.t[idx]


class BankView:
    def __init__(self, t, i):
        self.t = t
        self.i = i
        self.b = Buf(excl=True)

    def __getitem__(self, idx):
        p, f = idx
        return self.t[p, self.i, f]


class K:
    def __init__(self):
        self.nc = bass.Bass("TRN2", target_bir_lowering=False)
        nc = self.nc
        self.ctx = ExitStack()
        self.engs = {"pe": nc.tensor, "dve": nc.vector, "act": nc.scalar, "pool": nc.gpsimd, "sp": nc.sync}
        self.cur = {}
        self.waited = {e: {} for e in self.engs}
        self.nsem = 0
        for e in self.engs:
            self._new_sem(e)
        self.dpool = {}
        self.dcnt = {}
        for q in ("sp", "pool", "act"):
            self.dpool[q] = []
            for i in range(12):
                nm = f"d_{q}_{i}"
                self.dpool[q].append([self.ctx.enter_context(nc.semaphore(nm)), nm, 0])
            self.dcnt[q] = 0
        self.cpool = [[self.ctx.enter_context(nc.semaphore(f"cc_{i}")), f"cc_{i}", 0] for i in range(6)]
        self.ccnt = 0
        self.out_toks = []
        self.nuniq = 0
        self.phase = None

    def _new_sem(self, e):
        nm = f"c_{e}_{self.nsem}"
        self.nsem += 1
        self.cur[e] = [self.ctx.enter_context(self.nc.semaphore(nm)), nm, 0]

    def sbuf(self, shape, dt, name=None):
        self.nuniq += 1
        ctx = self.phase if self.phase is not None else self.ctx
        return Tile(ctx.enter_context(self.nc.sbuf_tensor(f"s_{name or 'sb'}_{self.nuniq}", list(shape), dt)))

    def barrier(self):
        toks = [(c[1], c[0], c[2]) for c in self.cur.values() if c[2] > 0]
        for q in self.dpool:
            toks += [(sl[1], sl[0], sl[2]) for sl in self.dpool[q] if sl[2] > 0]
        toks += [(sl[1], sl[0], sl[2]) for sl in self.cpool if sl[2] > 0]
        for eng, e in self.engs.items():
            for nm, sem, val in toks:
                if nm == self.cur[eng][1]:
                    continue
                if self.waited[eng].get(nm, 0) < val:
                    e.wait_ge(sem, val)
                    self.waited[eng][nm] = val

    def psum(self, shape, dt, name=None):
        self.nuniq += 1
        t = Tile(self.ctx.enter_context(self.nc.psum_tensor("p_" + (name or f"ps{self.nuniq}"), list(shape), dt)))
        t.b.excl = True
        return t

    def dram(self, name, shape, dt, kind="Internal"):
        return Tile(self.nc.dram_tensor(name, list(shape), dt, kind=kind))

    def _deps(self, eng, reads, writes):
        deps = {}

        def add(t):
            if t is None:
                return
            o = deps.get(t[0])
            if o is None or o[2] < t[2]:
                deps[t[0]] = t

        for b in reads:
            add(b.w)
            if b.excl:
                for kk, t in b.r.items():
                    if kk != eng:
                        add(t)
        for b in writes:
            add(b.w)
            for t in b.r.values():
                add(t)
        e = self.engs[eng]
        wd = self.waited[eng]
        for nm, (_, sem, val) in deps.items():
            if eng == "pe" and nm == self.cur["pe"][1]:
                continue
            if wd.get(nm, 0) >= val:
                continue
            e.wait_ge(sem, val)
            wd[nm] = val

    def _mark(self, key, tok, reads, writes):
        for b in writes:
            b.w = tok
            b.r = {}
        for b in reads:
            if b.w is not tok:
                b.r[key] = tok

    def op(self, eng, fn, reads=(), writes=()):
        self._deps(eng, reads, writes)
        inst = fn(self.engs[eng])
        c = self.cur[eng]
        if c[2] >= SEM_LIMIT:
            self._new_sem(eng)
            c = self.cur[eng]
        inst.then_inc(c[0], 1)
        c[2] += 1
        tok = (c[1], c[0], c[2])
        self._mark(eng, tok, reads, writes)
        return tok

    def dma(self, q, out, in_, reads=(), writes=(), is_out=False, **kw):
        self._deps(q, reads, writes)
        e = self.engs[q]
        slot = self.dpool[q][self.dcnt[q] % len(self.dpool[q])]
        self.dcnt[q] += 1
        if slot[2] > 0 and self.waited[q].get(slot[1], 0) < slot[2]:
            e.wait_ge(slot[0], slot[2])
            self.waited[q][slot[1]] = slot[2]
        inst = e.dma_start(out=out, in_=in_, **kw)
        inst.then_inc(slot[0], 16)
        slot[2] += 16
        tok = (slot[1], slot[0], slot[2])
        self._mark(slot[1], tok, reads, writes)
        if is_out:
            self.out_toks.append(tok)
        return tok

    def collective(self, kind, alu, in_t, out_t, reads, writes):
        self._deps("pool", reads, writes)
        e = self.engs["pool"]
        slot = self.cpool[self.ccnt % len(self.cpool)]
        self.ccnt += 1
        if slot[2] > 0 and self.waited["pool"].get(slot[1], 0) < slot[2]:
            e.wait_ge(slot[0], slot[2])
            self.waited["pool"][slot[1]] = slot[2]
        inst = e.collective_compute(kind, alu, replica_groups=GROUPS, ins=[in_t.t.ap().opt()], outs=[out_t.t.ap().opt()])
        inst.then_inc(slot[0], 1)
        slot[2] += 1
        tok = (slot[1], slot[0], slot[2])
        self._mark(slot[1], tok, reads, writes)
        return tok

    def finish(self):
        e = self.engs["sp"]
        for nm, sem, val in self.out_toks:
            if self.waited["sp"].get(nm, 0) < val:
                e.wait_ge(sem, val)
                self.waited["sp"][nm] = val
        self.ctx.close()
        return self.nc


def emit_consts(k):
    c = {}
    c["ones_bf"] = k.sbuf([128, 128], BF16, "ones_bf")
    k.op("pool", lambda e: e.memset(c["ones_bf"][:], 1.0), writes=[c["ones_bf"].b])
    return c


def emit_rsqrt(k, out, src, scale, eps, n, p0=0, p1=128):
    k.op("act", lambda e: e.activation(out=out[p0:p1, 0:n], in_=src[p0:p1, 0:n], func=AF.Sqrt, bias=float(eps), scale=float(scale)),
         reads=[src.b], writes=[out.b])
    k.op("dve", lambda e: e.reciprocal(out=out[p0:p1, 0:n], in_=out[p0:p1, 0:n]), reads=[out.b], writes=[out.b])


class Env:
    pass


class Mods:
    def __init__(self, t):
        self.t = t
        self.b = t.b

    def ap(self, l, lc, w, m):
        return self.t[:, m // 4, (l * 6 + w) * 4 + m % 4, lc:lc + 1]

    def vec(self, l, lc, w):
        return self.t[:, :, (l * 6 + w) * 4:(l * 6 + w) * 4 + 4, lc]


def emit_modprep(k, mods, l, gn_ap, gn_b, which_sc, name):
    gsc = k.sbuf([128, 2, NCH], F32, name + "_gsc")
    for lc in range(2):
        ov = gsc[:, lc, :].rearrange("p (a b) -> p a b", a=4)
        k.op("dve", lambda e: e.tensor_scalar(out=ov, in0=mods.vec(l, lc, which_sc), scalar1=1.0, scalar2=None, op0=ALU.add),
             reads=[mods.b], writes=[gsc.b])
        k.op("dve", lambda e: e.tensor_tensor(out=ov, in0=ov, in1=gn_ap.rearrange("p (a b) -> p a b", a=4), op=ALU.mult),
             reads=[gsc.b, gn_b], writes=[gsc.b])
    return gsc


def emit_norm_mod(k, cst, hT, gsc, mods, l, which_sh, out_bf, sq, stat_ps, rstd):
    for bi, (t0, nb, lc) in enumerate(BLK):
        hb = [hT.sb((m, bi)) for m in range(NCH)]
        for m in range(NCH):
            k.op("act", lambda e: e.activation(out=sq[:, m % 8, 0:nb], in_=hT[:, m, t0:t0 + nb], func=AF.Square),
                 reads=[hb[m]], writes=[sq.sb(m % 8)])
            k.op("pe", lambda e: e.matmul(out=stat_ps[:, 0:nb], lhsT=cst["ones_bf"][:, :], rhs=sq[:, m % 8, 0:nb],
                                          start=(m == 0), stop=(m == NCH - 1)),
                 reads=[sq.sb(m % 8), cst["ones_bf"].b], writes=[stat_ps.b])
        emit_rsqrt(k, rstd, stat_ps, 1.0 / D, NORM_EPS, nb)
        for m in range(NCH):
            tt = k.tmpn[m % 2]
            k.op("dve", lambda e: e.scalar_tensor_tensor(out=tt[:, 0:nb], in0=hT[:, m, t0:t0 + nb], scalar=gsc[:, lc, m:m + 1],
                                                          in1=rstd[:, 0:nb], op0=ALU.mult, op1=ALU.mult),
                 reads=[hb[m], gsc.b, rstd.b], writes=[tt.b])
            k.op("act", lambda e: e.activation(out=out_bf[:, m, t0:t0 + nb], in_=tt[:, 0:nb], func=AF.Identity,
                                               bias=mods.ap(l, lc, which_sh, m), scale=1.0),
                 reads=[tt.b, mods.b], writes=[out_bf.sb((m, bi))])


def end_phase(k):
    k.barrier()
    k.phase.close()
    k.phase = None


def emit_u_exchange(k, E, ubf):
    for pc, (c0, n) in enumerate(UPIECES):
        k.dma("sp", E.u_in[pc].t.ap().rearrange("(c p) t -> p c t", p=128), ubf[:, c0:c0 + n, :],
              reads=[ubf.sb((m, bi)) for m in range(c0, c0 + n) for bi in range(3)], writes=[E.u_in[pc].b])
        k.collective("AllGather", ALU.bypass, E.u_in[pc], E.u_all[pc], reads=[E.u_in[pc].b], writes=[E.u_all[pc].b])


def emit_P0(k, E):
    k.phase = ExitStack()
    sc = k.sbuf([128, NCH, 2], F32)
    scb = k.sbuf([128, NCH, 2], BF16)
    abq = k.sbuf([128, 96], F32)
    modP = k.sbuf([128, 96, 2], F32)
    wb = [k.sbuf([128, NCH, 512], BF16) for _ in range(2)]
    k.dma("sp", sc[:], E.ccTd.rearrange("(c p) r -> p c r", p=128), writes=[sc.b])
    k.dma("sp", abq[:], E.abqd[:, :], writes=[abq.b])
    k.dma("sp", E.ind[:], E.indd[:, :], writes=[E.ind.b])
    k.dma("sp", E.gn1[:], E.gn1d[:, :], writes=[E.gn1.b])
    k.dma("sp", E.gn2[:], E.gn2d[:, :], writes=[E.gn2.b])
    k.dma("sp", E.cmat[:], E.cmatd[:, :, :], writes=[E.cmat.b])
    k.dma("sp", E.wmask[:], E.wmaskd[:, :, :], writes=[E.wmask.b])
    k.dma("sp", E.rmask[:], E.rmaskd[:, :, :], writes=[E.rmask.b])
    k.dma("sp", E.tri[:], E.trid[:, :, :], writes=[E.tri.b])
    k.op("act", lambda e: e.activation(out=scb[:], in_=sc[:], func=AF.Silu), reads=[sc.b], writes=[scb.b])
    awv = E.awqd.rearrange("(c p) n -> p c n", p=128)
    for g in range(24):
        w = wb[g % 2]
        k.dma("pool", w[:], awv[:, :, g * 512:(g + 1) * 512], writes=[w.b])
        for q in range(4):
            cc = g * 4 + q
            p = E.banks[cc % 2]
            for c in range(NCH):
                k.op("pe", lambda e: e.matmul(out=p[:, 0:2], lhsT=w[:, c, q * 128:(q + 1) * 128], rhs=scb[:, c, :],
                                              start=(c == 0), stop=(c == NCH - 1)), reads=[scb.b, w.b], writes=[p.b])
            k.op("dve", lambda e: e.tensor_scalar(out=modP[:, cc, :], in0=p[:, 0:2], scalar1=abq[:, cc:cc + 1], scalar2=None, op0=ALU.add),
                 reads=[p.b, abq.b], writes=[modP.b])
    k.dma("sp", E.mod_in.t.ap(), modP[:, :, :].rearrange("p a b -> p (a b)"), reads=[modP.b], writes=[E.mod_in.b])
    k.collective("AllGather", ALU.bypass, E.mod_in, E.mod_all, reads=[E.mod_in.b], writes=[E.mod_all.b])
    k.dma("sp", E.modS[:, :, :, :].rearrange("p r a b -> p r (a b)"), E.mod_all.t.ap().rearrange("(r p) n -> p r n", p=128),
          reads=[E.mod_all.b], writes=[E.modS.b])
    end_phase(k)


def emit_P1(k, E):
    k.phase = ExitStack()
    hT = k.sbuf([128, NCH, NT], F32, "hT")
    ubf = k.sbuf([128, NCH, NT], BF16, "ubf")
    sq = k.sbuf([128, 8, 512], BF16, "sq")
    k.tmpn = [k.sbuf([128, 512], F32) for _ in range(2)]
    rstd = k.sbuf([128, 512], F32, "rstd")
    hv = E.hT0d.rearrange("(c p) t -> p c t", p=128)
    for bi, (t0, nb, lc) in enumerate(BLK):
        k.dma("sp", hT[:, :, t0:t0 + nb], hv[:, :, t0:t0 + nb], writes=[hT.sb((m, bi)) for m in range(NCH)])
    gsc = emit_modprep(k, E.mods, 0, E.gn1[:, 0:NCH], E.gn1.b, 1, "n1")
    emit_norm_mod(k, E.cst, hT, gsc, E.mods, 0, 0, ubf, sq, E.banks[6], rstd)
    emit_u_exchange(k, E, ubf)
    end_phase(k)


def emit_C(k, E, l):
    k.phase = ExitStack()
    last = l == DEPTH - 1
    hT = k.sbuf([128, NCH, NT], F32, "hT")
    abf = k.sbuf([128, NCH, NT], BF16, "abf")
    h1 = k.sbuf([128, NCH, NT], BF16, "h1")
    sq = k.sbuf([128, 8, 512], BF16, "sq")
    k.tmpn = [k.sbuf([128, 512], F32) for _ in range(2)]
    rtmp = [k.sbuf([128, 512], F32) for _ in range(2)]
    rstd = k.sbuf([128, 512], F32, "rstd")
    wbuf = [k.sbuf([128, NCH, 256], BF16) for _ in range(3)]
    banks = [[E.banks[i * 3 + jj] for jj in range(3)] for i in range(2)]
    stat = E.banks[6]
    mods = E.mods
    hsrc = E.hT0d if l == 0 else E.h_spill.t.ap()
    hv = hsrc.rearrange("(c p) t -> p c t", p=128)
    ov = E.rs_out.t.ap().rearrange("(c p) t -> p c t", p=128)
    for bi, (t0, nb, lc) in enumerate(BLK):
        k.dma("sp", abf[:, :, t0:t0 + nb], ov[:, :, t0:t0 + nb], reads=[E.rs_out.b], writes=[abf.sb((m, bi)) for m in range(NCH)])
    for bi, (t0, nb, lc) in enumerate(BLK):
        k.dma("sp", hT[:, :, t0:t0 + nb], hv[:, :, t0:t0 + nb], reads=[E.h_spill.b], writes=[hT.sb((m, bi)) for m in range(NCH)])
    wcnt = [0]

    def load_w(src_ap):
        w = wbuf[wcnt[0] % 3]
        wcnt[0] += 1
        k.dma("pool", w[:], src_ap, writes=[w.b])
        return w

    def proj(w, mi, m, src, gate_idx):
        bk = banks[m % 2]
        for bi, (t0, nb, lc) in enumerate(BLK):
            for kc in range(NCH):
                k.op("pe", lambda e: e.matmul(out=bk[bi][:, 0:nb], lhsT=w[:, kc, mi * 128:(mi + 1) * 128], rhs=src[:, kc, t0:t0 + nb],
                                              start=(kc == 0), stop=(kc == NCH - 1)),
                     reads=[w.b, src.sb((kc, bi))], writes=[bk[bi].b])
            k.op("dve", lambda e: e.scalar_tensor_tensor(out=hT[:, m, t0:t0 + nb], in0=bk[bi][:, 0:nb],
                                                          scalar=mods.ap(l, lc, gate_idx, m), in1=hT[:, m, t0:t0 + nb],
                                                          op0=ALU.mult, op1=ALU.add),
                 reads=[bk[bi].b, mods.b, hT.sb((m, bi))], writes=[hT.sb((m, bi))])

    woutv = E.woutd[l].rearrange("(c p) n -> p c n", p=128)
    for mp in range(8):
        w = load_w(woutv[:, :, mp * 256:(mp + 1) * 256])
        for mi in range(2):
            proj(w, mi, mp * 2 + mi, abf, 2)
    gsc2 = emit_modprep(k, mods, l, E.gn2[:, l * NCH:(l + 1) * NCH], E.gn2.b, 4, f"n2_{l}")
    emit_norm_mod(k, E.cst, hT, gsc2, mods, l, 3, abf, sq, stat, rstd)
    w1v = E.w1d[l].rearrange("(c p) n -> p c n", p=128)
    w2v = E.w2d[l].rearrange("(c p) n -> p c n", p=128)
    ecnt = 0
    for q in range(4):
        for fp in range(8):
            f0 = (q * 16 + fp * 2) * 128
            w = load_w(w1v[:, :, f0:f0 + 256])
            for fi in range(2):
                fl = fp * 2 + fi
                bk = banks[fl % 2]
                for bi, (t0, nb, lc) in enumerate(BLK):
                    for kc in range(NCH):
                        k.op("pe", lambda e: e.matmul(out=bk[bi][:, 0:nb], lhsT=w[:, kc, fi * 128:(fi + 1) * 128],
                                                      rhs=abf[:, kc, t0:t0 + nb], start=(kc == 0), stop=(kc == NCH - 1)),
                             reads=[w.b, abf.sb((kc, bi))], writes=[bk[bi].b])
                    rt = rtmp[ecnt % 2]
                    ecnt += 1
                    k.op("act", lambda e: e.activation(out=rt[:, 0:nb], in_=bk[bi][:, 0:nb], func=AF.Relu),
                         reads=[bk[bi].b], writes=[rt.b])
                    k.op("pool", lambda e: e.tensor_tensor(out=h1[:, fl, t0:t0 + nb], in0=rt[:, 0:nb], in1=rt[:, 0:nb], op=ALU.mult),
                         reads=[rt.b], writes=[h1.sb((fl, bi))])
        for mp in range(8):
            w = load_w(w2v[:, q * 16:(q + 1) * 16, mp * 256:(mp + 1) * 256])
            for mi in range(2):
                proj(w, mi, mp * 2 + mi, h1, 5)
    if last:
        hov = E.hOd.rearrange("(c p) t -> p c t", p=128)
        for bi, (t0, nb, lc) in enumerate(BLK):
            k.dma("sp", hov[:, :, t0:t0 + nb], hT[:, :, t0:t0 + nb], reads=[hT.sb((m, bi)) for m in range(NCH)], is_out=True)
    else:
        hsv = E.h_spill.t.ap().rearrange("(c p) t -> p c t", p=128)
        for bi, (t0, nb, lc) in enumerate(BLK):
            k.dma("sp", hsv[:, :, t0:t0 + nb], hT[:, :, t0:t0 + nb], reads=[hT.sb((m, bi)) for m in range(NCH)], writes=[E.h_spill.b])
        gsc1 = emit_modprep(k, mods, l + 1, E.gn1[:, (l + 1) * NCH:(l + 2) * NCH], E.gn1.b, 1, f"n1_{l}")
        emit_norm_mod(k, E.cst, hT, gsc1, mods, l + 1, 0, h1, sq, stat, rstd)
        emit_u_exchange(k, E, h1)
    end_phase(k)


def out_write(k, E, src, p0, p1, ncols, segs, t0, n):
    m4 = E.m4[E.m4c[0] % 2]
    E.m4c[0] += 1
    for r in range(4):
        k.op("pool", lambda e: e.tensor_scalar(out=m4[p0:p1, r, 0:ncols], in0=src[p0:p1, 0:ncols], scalar1=E.ind[p0:p1, r:r + 1],
                                               scalar2=None, op0=ALU.mult), reads=[src.b, E.ind.b], writes=[m4.b])
    rs4 = E.rs_in.t.ap().rearrange("(j r q) t -> j q r t", j=4, r=4)
    if t0 < CTX:
        pieces = [(jj, 1024, 64, jj * 64) for jj in range(4)]
    else:
        lat = t0 - CTX
        pieces = [(lat // 1024, lat % 1024, n, 0)]
    P = p1 - p0
    for row0, coff in segs:
        for jj, lcol, cnt, so in pieces:
            k.dma("sp", rs4[jj][row0:row0 + P, :, lcol:lcol + cnt], m4[p0:p1, :, coff + so:coff + so + cnt],
                  reads=[m4.b], writes=[E.rs_in.b])


NW = 2048
CHUNKS = [(i * 128, 128) for i in range(11)] + [(1408 + i * 96, 96) for i in range(4)] + [(1792, 128), (1920, 128)]
PB_ROWS = 1408
PB_ROWOFF = [0, 128, 256, 384, 512, 640, 768, 864, 960, 1056, 1152, 1280]
PBW = TALL + 4
MBLK = [(0, 256)] + [(256 + i * 512, 512) for i in range(8)]
ATT_SCALE = 0.125
PP_QKG = 0
PP_MU = 4
PP_A0 = 28
PP_KK = 32
PP_KA = 34
PP_RK = 36
PP_GNG = 38
PP_GNB = 40
NPP = 42


def pb_col(t):
    return t + 1 if t < CTX else t + 3


def emit_B(k, E, l):
    nc = k.nc
    pB = E.pB
    pp, cmat, wmask, banks, PS = E.pp, E.cmat, E.wmask, E.banks, E.PS
    k.phase = ExitStack()
    E.m4 = [k.sbuf([128, 4, 512], BF16, f"m4_{i}") for i in range(2)]
    wres = k.sbuf([128, NCH, NW], BF16, "wres")
    bones = k.sbuf([128, 128], BF16, "bones")
    QK = [k.sbuf([128, TALL], BF16, f"qk{i}") for i in range(4)]
    V = [k.sbuf([128, 34, 65], BF16, f"v{i}") for i in range(2)]
    ub = [k.sbuf([128, NCH, 512], BF16, f"ub{i}") for i in range(2)]
    cs = [[k.sbuf([128, 512], F32, f"cs{i}{j}") for j in range(2)] for i in range(2)]
    x32 = [k.sbuf([128, 512], F32, "x32_0")] * 2
    sqb = [k.sbuf([128, 512], BF16, "sqb0")] * 2
    rs = [k.sbuf([128, 512], F32, "rs0")] * 2
    xn = [k.sbuf([128, 512], F32, "xn0")] * 2
    t1 = [k.sbuf([128, 512], F32, "t1_0")] * 2
    t2 = [k.sbuf([128, 512], F32, "t2_0")] * 2
    stg = [k.sbuf([128, 512], F32, f"stg{i}") for i in range(3)]
    zero = k.sbuf([128, 16], F32, "zero")

    k.dma("sp", pp[:], E.ppd[l], writes=[pp.b])
    k.op("pool", lambda e: e.memset(zero[:], 0.0), writes=[zero.b])
    k.op("dve", lambda e: e.tensor_copy(out=bones[:], in_=cmat[:, 1, :]), reads=[cmat.b], writes=[bones.b])
    for i in range(2):
        k.op("pool", lambda e: e.memset(V[i][:, :, 64:65], 1.0), writes=[V[i].b])
    pbb = E.pbb
    for col in ((0, 257, 258, PBW - 1) if l == 0 else ()):
        k.dma("sp", pB.t.ap()[:, col:col + 1].rearrange("(c p) o -> p c o", p=128), zero[:, 0:11].rearrange("p (c o) -> p c o", o=1),
              reads=[zero.b], writes=[pbb], allow_slow_non_contiguous=True)
    wv = E.wind[l].rearrange("(c p) n -> p c n", p=128)
    for i in range(4):
        k.dma("pool", wres[:, :, i * 512:(i + 1) * 512], wv[:, :, i * 512:(i + 1) * 512], writes=[wres.sb(i)])
    ident = cmat

    ecnt = [0]
    bk = [0]

    def nbank():
        b = banks[bk[0] % 4]
        bk[0] += 1
        return b

    for bi, (t0, nb) in enumerate(MBLK):
        u = ub[bi % 2]
        if t0 < CTX:
            srcs = [(rr, 1024, 64, rr * 64) for rr in range(4)]
        else:
            srcs = [((t0 - CTX) // 1024, (t0 - CTX) % 1024, nb, 0)]
        for pc, (c0_, n_) in enumerate(UPIECES):
            ua = E.u_all[pc].t.ap().rearrange("(r c p) t -> r p c t", r=4, p=128)
            for rr, scol, cnt, dcol in srcs:
                k.dma("sp", u[:, c0_:c0_ + n_, dcol:dcol + cnt], ua[rr][:, :, scol:scol + cnt], reads=[E.u_all[pc].b], writes=[u.b])
        cost, sint = cs[bi % 2]
        k.dma("sp", cost[:, 0:nb], E.cosd[:, t0:t0 + nb], writes=[cost.b])
        k.dma("sp", sint[:, 0:nb], E.sind[:, t0:t0 + nb], writes=[sint.b])
        for ci, (c0, M) in enumerate(CHUNKS):
            ps = nbank()
            for kc in range(NCH):
                k.op("pe", lambda e: e.matmul(out=ps[0:M, 0:nb], lhsT=wres[:, kc, c0:c0 + M], rhs=u[:, kc, 0:nb],
                                              start=(kc == 0), stop=(kc == NCH - 1)),
                     reads=[wres.sb(c0 // 512), wres.sb((c0 + M - 1) // 512), u.b], writes=[ps.b])
            if ci < 4:
                i2 = ecnt[0] % 2
                ecnt[0] += 1
                xx, sq, rr, xnn, a1, a2 = x32[i2], sqb[i2], rs[i2], xn[i2], t1[i2], t2[i2]
                k.op("act", lambda e: e.activation(out=sq[:, 0:nb], in_=ps[:, 0:nb], func=AF.Square), reads=[ps.b], writes=[sq.b])
                k.op("dve", lambda e: e.tensor_scalar(out=xx[:, 0:nb], in0=ps[:, 0:nb], scalar1=pp[:, PP_QKG + ci:PP_QKG + ci + 1],
                                                       scalar2=None, op0=ALU.mult), reads=[ps.b, pp.b], writes=[xx.b])
                ps2 = nbank()
                k.op("pe", lambda e: e.matmul(out=ps2[:, 0:nb], lhsT=bones[:, :], rhs=sq[:, 0:nb], start=True, stop=True),
                     reads=[bones.b, sq.b], writes=[ps2.b])
                emit_rsqrt(k, rr, ps2, 1.0 / 64, NORM_EPS, nb)
                k.op("dve", lambda e: e.tensor_tensor(out=xnn[:, 0:nb], in0=xx[:, 0:nb], in1=rr[:, 0:nb], op=ALU.mult),
                     reads=[xx.b, rr.b], writes=[xnn.b])
                ps3 = nbank()
                k.op("pe", lambda e: e.matmul(out=ps3[:, 0:nb], lhsT=cmat[:, 2, :], rhs=xnn[:, 0:nb], start=True, stop=True),
                     reads=[cmat.b, xnn.b], writes=[ps3.b])
                k.op("pool", lambda e: e.tensor_tensor(out=a1[:, 0:nb], in0=xnn[:, 0:nb], in1=cost[:, 0:nb], op=ALU.mult),
                     reads=[xnn.b, cost.b], writes=[a1.b])
                k.op("dve", lambda e: e.tensor_tensor(out=a2[:, 0:nb], in0=ps3[:, 0:nb], in1=sint[:, 0:nb], op=ALU.mult),
                     reads=[ps3.b, sint.b], writes=[a2.b])
                k.op("pool", lambda e: e.tensor_tensor(out=QK[ci][:, t0:t0 + nb], in0=a1[:, 0:nb], in1=a2[:, 0:nb], op=ALU.add),
                     reads=[a1.b, a2.b], writes=[QK[ci].sb(bi)])
            elif ci == 4:
                i2 = ecnt[0] % 2
                ecnt[0] += 1
                xx = x32[i2]
                k.op("act", lambda e: e.activation(out=xx[:, 0:nb], in_=ps[:, 0:nb], func=AF.Copy), reads=[ps.b], writes=[xx.b])
                for tt in range(nb // 128):
                    pt = nbank()
                    k.op("pe", lambda e: e.transpose(out=pt[:, 0:128], in_=xx[:, tt * 128:(tt + 1) * 128], identity=cmat[:, 0, :]),
                         reads=[xx.b, cmat.b], writes=[pt.b])
                    tile = t0 // 128 + tt
                    k.op("act", lambda e: e.activation(out=V[0][:, tile, 0:64], in_=pt[:, 0:64], func=AF.Copy),
                         reads=[pt.b], writes=[V[0].sb(tile)])
                    k.op("dve", lambda e: e.tensor_copy(out=V[1][:, tile, 0:64], in_=pt[:, 64:128]),
                         reads=[pt.b], writes=[V[1].sb(tile)])
            else:
                s = stg[ecnt[0] % 3]
                ecnt[0] += 1
                if ecnt[0] % 2:
                    k.op("act", lambda e: e.activation(out=s[0:M, 0:nb], in_=ps[0:M, 0:nb], func=AF.Copy), reads=[ps.b], writes=[s.b])
                else:
                    k.op("dve", lambda e: e.tensor_copy(out=s[0:M, 0:nb], in_=ps[0:M, 0:nb]), reads=[ps.b], writes=[s.b])
                r0 = PB_ROWOFF[ci - 5]
                k.dma("sp", pB.t.ap()[r0:r0 + M, pb_col(t0):pb_col(t0) + nb], s[0:M, 0:nb], reads=[s.b], writes=[pbb])

    pT = [k.sbuf([128, 512], BF16, f"pT{i}") for i in range(3)]
    oraw = [k.sbuf([65, 512], F32, f"oraw{i}") for i in range(2)]
    rrow = [k.sbuf([65, 512], F32, f"rrow{i}") for i in range(2)]
    ostg = [k.sbuf([64, 512], BF16, f"ostg{i}") for i in range(2)]
    onesf = k.sbuf([65, 64], F32, "onesf")
    sinkx = k.sbuf([1, 512], F32, "sinkx")
    sinkb = k.sbuf([1, 512], BF16, "sinkb")
    e64 = k.sbuf([1, 65], BF16, "e64")
    k.op("pool", lambda e: e.memset(onesf[:], 1.0), writes=[onesf.b])
    k.op("pool", lambda e: e.memset(e64[:], 0.0), writes=[e64.b])
    k.op("pool", lambda e: e.memset(e64[0:1, 64:65], 1.0), writes=[e64.b])
    k.dma("sp", sinkx[:], E.sinkd[l], writes=[sinkx.b])
    k.op("act", lambda e: e.activation(out=sinkb[:], in_=sinkx[:], func=AF.Exp), reads=[sinkx.b], writes=[sinkb.b])
    ps_s = [(0, 1), (2, 3), (6, 7)]
    ps_o = [banks[4], banks[5]]
    ps_b = [banks[6], banks[7]]
    pcnt = [0]
    qcnt = [0]
    allq = [QK[i].sb(bi) for i in range(4) for bi in range(len(MBLK))]
    allv = [V[i].sb(t) for i in range(2) for t in range(34)]

    def attend(Q, Kd, Vt, q0, ktiles, sink, orow0):
        io = qcnt[0] % 2
        qcnt[0] += 1
        po = ps_o[io]
        n = len(ktiles)
        slots = []

        def qk(ii):
            kt, mi = ktiles[ii]
            b0, b1 = ps_s[pcnt[0] % 3]
            p = pT[pcnt[0] % 3]
            pcnt[0] += 1
            slots.append((b0, b1, p))
            for h in range(2):
                bh = banks[(b0, b1)[h]]
                k.op("pe", lambda e: e.matmul(out=bh[:, 0:256], lhsT=Kd[h * 64:(h + 1) * 64, kt * 128:(kt + 1) * 128],
                                              rhs=Q[h * 64:(h + 1) * 64, q0:q0 + 256], start=True, stop=True),
                     reads=allq, writes=[bh.b])

        def rest(ii):
            kt, mi = ktiles[ii]
            b0, b1, p = slots[ii]
            k.op("act", lambda e: e.activation(out=p[:, :].rearrange("p (h q) -> p h q", h=2), in_=PS.t[:, b0:b1 + 1, 0:256],
                                               func=AF.Exp, scale=ATT_SCALE),
                 reads=[banks[b0].b, banks[b1].b], writes=[p.b])
            if mi is not None:
                k.op("pool", lambda e: e.tensor_tensor(out=p[:], in0=p[:], in1=wmask[:, mi, :], op=ALU.mult),
                     reads=[p.b, wmask.b], writes=[p.b])
            k.op("pe", lambda e: e.matmul(out=po[0:65, :], lhsT=Vt[:, kt, 0:65], rhs=p[:], start=(ii == 0),
                                          stop=(ii == n - 1 and not sink)), reads=allv + [p.b], writes=[po.b])

        LA = 2
        for ii in range(min(LA, n)):
            qk(ii)
        for ii in range(n):
            if ii + LA < n:
                qk(ii + LA)
            rest(ii)
        if sink:
            k.op("pe", lambda e: e.matmul(out=po[0:65, :], lhsT=e64[0:1, :], rhs=sinkb[0:1, :], start=False, stop=True),
                 reads=[e64.b, sinkb.b], writes=[po.b])
        orw, rr, os_ = oraw[io], rrow[io], ostg[io]
        k.op("act", lambda e: e.activation(out=orw[0:65, :], in_=po[0:65, :], func=AF.Copy), reads=[po.b], writes=[orw.b])
        k.op("dve", lambda e: e.reciprocal(out=rr[64:65, :], in_=orw[64:65, :]), reads=[orw.b], writes=[rr.b])
        pb_ = ps_b[io]
        k.op("pe", lambda e: e.matmul(out=pb_[0:64, :], lhsT=onesf[64:65, 0:64], rhs=rr[64:65, :], start=True, stop=True),
             reads=[onesf.b, rr.b], writes=[pb_.b])
        k.op("dve", lambda e: e.tensor_tensor(out=os_[0:64, :], in0=orw[0:64, :], in1=pb_[0:64, :], op=ALU.mult),
             reads=[orw.b, pb_.b], writes=[os_.b])
        out_write(k, E, os_, 0, 64, 512, [(orow0, 0), (orow0 + 64, 256)], q0, 256)

    attend(QK[0], QK[1], V[0], 0, [(0, None), (1, None)], True, 0)
    attend(QK[2], QK[3], V[1], 0, [(0, None), (1, None)], False, 384)
    for qb in range(16):
        kts = []
        for r in range(4):
            lt = 2 * qb - 1 + r
            if 0 <= lt < 32:
                kts.append((2 + lt, r))
        kts += [(0, None), (1, None)]
        attend(QK[0], QK[1], V[0], 256 + qb * 256, kts, True, 0)
    for qb in range(16):
        attend(QK[2], QK[3], V[1], 256 + qb * 256, [(t, None) for t in range(34)], False, 384)
    end_phase(k)
    k.phase = ExitStack()
    E.m4 = [k.sbuf([128, 4, 512], BF16, f"m4_{i}") for i in range(2)]
    emit_rwkv(k, E, l, pB, pbb, pp, cmat, banks)
    end_phase(k)


RW_SCALE = -float(np.exp(np.float32(-0.5)))


def emit_rwkv(k, E, l, pB, pbb, pp, cmat, banks):
    sb = k.sbuf
    w0row = sb([1, 512], F32, "w0row")
    wup = sb([96, 512], F32, "wup")
    aup = sb([96, 512], F32, "aup")
    gup = sb([128, 2, 256], F32, "gup")
    rmask, tri = E.rmask, E.tri
    ones_row = sb([1, 128], F32, "ones_row")
    for t_, d_ in ((w0row, E.w0d[l]), (wup, E.wupd[l]), (aup, E.aupd[l]), (gup, E.gupd[l])):
        k.dma("sp", t_[:], d_, writes=[t_.b])
    k.op("pool", lambda e: e.memset(ones_row[:], 1.0), writes=[ones_row.b])
    c0 = sb([128, 12], F32, "c0")
    c1 = sb([128, 2], F32, "c1")
    c2 = sb([128, 2], F32, "c2")
    muv = pp[:, PP_MU:PP_MU + 24].rearrange("p (c two) -> p c two", two=2)
    k.op("dve", lambda e: e.tensor_tensor(out=c0[:], in0=muv[:, :, 0], in1=muv[:, :, 1], op=ALU.add), reads=[pp.b], writes=[c0.b])
    k.op("dve", lambda e: e.tensor_scalar(out=c0[:], in0=c0[:], scalar1=-1.0, scalar2=1.0, op0=ALU.mult, op1=ALU.add),
         reads=[c0.b], writes=[c0.b])
    k.op("dve", lambda e: e.tensor_scalar(out=c1[:], in0=pp[:, PP_KA:PP_KA + 2], scalar1=-1.0, scalar2=1.0, op0=ALU.mult, op1=ALU.add),
         reads=[pp.b], writes=[c1.b])
    k.op("dve", lambda e: e.tensor_scalar(out=c2[:], in0=c1[:], scalar1=2.0, scalar2=None, op0=ALU.mult), reads=[c1.b], writes=[c2.b])
    MSK = {0: dict(Ms=0, MsT=1, MiT=2, nMiT=4), 1: dict(Ms=1, MsT=0, MiT=3, nMiT=5)}
    NB = 512
    RD = F32
    F32R = mybir.dt.float32r

    def rv(ap):
        return ap.bitcast(F32R)

    WIDTH = 4
    STAGGER = 3
    identb = sb([128, 128], RD, "identb")
    k.op("dve", lambda e: e.tensor_copy(out=rv(identb[:]), in_=cmat[:, 0, :]), reads=[cmat.b], writes=[identb.b])
    raw = [sb([128, NB + 2], F32, f"raw{i}") for i in range(4)]
    rawc = [0]
    S = {n: sb([128, NB], F32, "S_" + n) for n in ("r", "k", "v", "wd", "ad", "ad2", "g0", "g1", "sqk", "rinv", "kap", "a", "a2", "tw",
                                                     "tmp", "kd", "bb", "eL", "eLn", "eLx", "ysum", "u1", "u2", "u3")}
    lw = sb([64, 8, 128], F32, "lw")
    bd = {n: [sb([128, 8, 2, 64], RD, f"bd_{n}{i}") for i in range(2)] for n in ("kap", "bt", "kt", "rt", "v")}
    zbig = S["u1"]
    k.op("pool", lambda e: e.memset(zbig[:], 0.0), writes=[zbig.b])
    for n in bd:
        for i in range(2):
            for hh in range(2):
                k.op("dve", lambda e: e.tensor_copy(out=rv(bd[n][i][:, :, hh, :]), in_=zbig[:, :].rearrange("p (c s) -> p c s", s=64)),
                     reads=[zbig.b], writes=[bd[n][i].b])
    yf = sb([128, TALL], F32, "yf")
    Mst = [sb([128, 128], F32, f"Mst{i}") for i in range(2)]
    U = {}

    def ut(name):
        if name not in U:
            dt_ = F32 if name in ("ET", "Hg", "YmT") else RD
            U[name] = [sb([128, 256 if name.startswith("X") else 128], dt_, f"U_{name}{i}") for i in range(WIDTH)]
        return U[name]

    ostage = [sb([128, NB], BF16, f"ostage{i}") for i in range(2)]
    bkc = [0]

    def rb():
        b = banks[bkc[0] % 8]
        bkc[0] += 1
        return b

    ecount = [0]

    def ew():
        ecount[0] += 1
        return "dve" if ecount[0] % 2 else "pool"

    dg = sb([128, 12, 3, 128], F32, "shift_diag")
    for bc_ in range(12):
        for j_, coef in enumerate((c0[:, bc_:bc_ + 1], pp[:, PP_MU + 2 * bc_:PP_MU + 2 * bc_ + 1], pp[:, PP_MU + 2 * bc_ + 1:PP_MU + 2 * bc_ + 2])):
            k.op("pool", lambda e: e.tensor_scalar(out=dg[:, bc_, j_, :], in0=cmat[:, 0, :], scalar1=coef, scalar2=None, op0=ALU.mult),
                 reads=[cmat.b, c0.b, pp.b], writes=[dg.b])

    def load_shift(bc, M, dst, t0, nb):
        rw = raw[rawc[0] % 4]
        rawc[0] += 1
        r0 = PB_ROWOFF[bc]
        k.dma("sp", rw[0:M, 0:nb + 2], pB.t.ap()[r0:r0 + M, pb_col(t0) - 1:pb_col(t0) + nb + 1], reads=[pbb], writes=[rw.b])
        ps = rb()
        for j_, off in enumerate((1, 0, 2)):
            k.op("pe", lambda e: e.matmul(out=ps[0:M, 0:nb], lhsT=dg[0:M, bc, j_, 0:M], rhs=rw[0:M, off:off + nb],
                                          start=(j_ == 0), stop=(j_ == 2)), reads=[dg.b, rw.b], writes=[ps.b])
        k.op("act", lambda e: e.activation(out=dst[0:M, 0:nb], in_=ps[0:M, 0:nb], func=AF.Copy), reads=[ps.b], writes=[dst.b])

    def lora_a(dst, src, pr, d, nb):
        ps = rb()
        k.op("pe", lambda e: e.matmul(out=ps[:, 0:nb], lhsT=aup[0:96, d * 256 + pr * 128:d * 256 + (pr + 1) * 128], rhs=src[0:96, 0:nb],
                                      start=True, stop=True), reads=[aup.b, src.b], writes=[ps.b])
        k.op("act", lambda e: e.activation(out=dst[:, 0:nb], in_=ps[:, 0:nb], func=AF.Sigmoid,
                                           bias=pp[:, PP_A0 + pr * 2 + d:PP_A0 + pr * 2 + d + 1], scale=1.0),
             reads=[ps.b, pp.b], writes=[dst.b])

    def prep(pr, t0, nb, d, final, bdi):
        nch = nb // 64
        load_shift(0 + pr, 128, S["r"], t0, nb)
        load_shift(2 + pr, 128, S["k"], t0, nb)
        load_shift(4 + pr, 128, S["v"], t0, nb)
        load_shift(6 + d, 96, S["wd"], t0, nb)
        load_shift(8 + d, 96, S["ad"], t0, nb)
        if final:
            load_shift(8 + (1 - d), 96, S["ad2"], t0, nb)
            load_shift(10, 128, S["g0"], t0, nb)
            load_shift(11, 128, S["g1"], t0, nb)
        kS, rS, vS = S["k"], S["r"], S["v"]
        k.op("act", lambda e: e.activation(out=S["sqk"][:, 0:nb], in_=kS[:, 0:nb], func=AF.Square, scale=pp[:, PP_KK + pr:PP_KK + pr + 1]),
             reads=[kS.b, pp.b], writes=[S["sqk"].b])
        ps = rb()
        k.op("pe", lambda e: e.matmul(out=ps[:, 0:nb], lhsT=cmat[:, 1, :], rhs=S["sqk"][:, 0:nb], start=True, stop=True),
             reads=[cmat.b, S["sqk"].b], writes=[ps.b])
        k.op("act", lambda e: e.activation(out=S["rinv"][:, 0:nb], in_=ps[:, 0:nb], func=AF.Sqrt), reads=[ps.b], writes=[S["rinv"].b])
        k.op("dve", lambda e: e.tensor_scalar(out=S["rinv"][:, 0:nb], in0=S["rinv"][:, 0:nb], scalar1=1e-12, scalar2=None, op0=ALU.max),
             reads=[S["rinv"].b], writes=[S["rinv"].b])
        k.op("dve", lambda e: e.reciprocal(out=S["rinv"][:, 0:nb], in_=S["rinv"][:, 0:nb]), reads=[S["rinv"].b], writes=[S["rinv"].b])
        k.op("dve", lambda e: e.scalar_tensor_tensor(out=S["kap"][:, 0:nb], in0=kS[:, 0:nb], scalar=pp[:, PP_KK + pr:PP_KK + pr + 1],
                                                      in1=S["rinv"][:, 0:nb], op0=ALU.mult, op1=ALU.mult),
             reads=[kS.b, pp.b, S["rinv"].b], writes=[S["kap"].b])
        lora_a(S["a"], S["ad"], pr, d, nb)
        if final:
            lora_a(S["a2"], S["ad2"], pr, 1 - d, nb)
        k.op("act", lambda e: e.activation(out=S["tw"][0:96, 0:nb], in_=S["wd"][0:96, 0:nb], func=AF.Tanh), reads=[S["wd"].b], writes=[S["tw"].b])
        pl = [rb(), rb()]
        wsl = slice(d * 256 + pr * 128, d * 256 + (pr + 1) * 128)
        for c in range(nch):
            pb_ = pl[c // 4]
            k.op("pe", lambda e: e.matmul(out=pb_[0:64, (c % 4) * 128:(c % 4 + 1) * 128], lhsT=S["tw"][0:96, c * 64:(c + 1) * 64],
                                          rhs=wup[0:96, wsl], start=True, stop=False), reads=[S["tw"].b, wup.b], writes=[pb_.b])
            k.op("pe", lambda e: e.matmul(out=pb_[0:64, (c % 4) * 128:(c % 4 + 1) * 128], lhsT=ones_row[0:1, 0:64],
                                          rhs=w0row[0:1, wsl], start=False, stop=True), reads=[ones_row.b, w0row.b], writes=[pb_.b])
        for hb in range((nch + 3) // 4):
            n4 = min(4, nch - hb * 4)
            k.op("act", lambda e: e.activation(out=lw[0:64, hb * 4:hb * 4 + n4, :], in_=pl[hb][0:64, 0:n4 * 128].rearrange("p (c n) -> p c n", n=128),
                                               func=AF.Sigmoid), reads=[pl[hb].b], writes=[lw.b])
        pL, pLx = rb(), rb()
        for c in range(nch):
            k.op("pe", lambda e: e.matmul(out=pL[:, c * 64:(c + 1) * 64], lhsT=lw[0:64, c, :], rhs=tri[0:64, 2 * d, :], start=True, stop=True),
                 reads=[lw.b, tri.b], writes=[pL.b])
        for c in range(nch):
            k.op("pe", lambda e: e.matmul(out=pLx[:, c * 64:(c + 1) * 64], lhsT=lw[0:64, c, :], rhs=tri[0:64, 2 * d + 1, :], start=True, stop=True),
                 reads=[lw.b, tri.b], writes=[pLx.b])
        k.op("act", lambda e: e.activation(out=S["eL"][:, 0:nb], in_=pL[:, 0:nb], func=AF.Exp), reads=[pL.b], writes=[S["eL"].b])
        k.op("act", lambda e: e.activation(out=S["eLn"][:, 0:nb], in_=pL[:, 0:nb], func=AF.Exp, scale=-1.0), reads=[pL.b], writes=[S["eLn"].b])
        k.op("act", lambda e: e.activation(out=S["eLx"][:, 0:nb], in_=pLx[:, 0:nb], func=AF.Exp), reads=[pLx.b], writes=[S["eLx"].b])
        k.op("dve", lambda e: e.tensor_scalar(out=S["tmp"][:, 0:nb], in0=S["a"][:, 0:nb], scalar1=pp[:, PP_KA + pr:PP_KA + pr + 1],
                                               scalar2=c1[:, pr:pr + 1], op0=ALU.mult, op1=ALU.add),
             reads=[S["a"].b, pp.b, c1.b], writes=[S["tmp"].b])
        k.op("pool", lambda e: e.tensor_tensor(out=S["kd"][:, 0:nb], in0=kS[:, 0:nb], in1=S["tmp"][:, 0:nb], op=ALU.mult),
             reads=[kS.b, S["tmp"].b], writes=[S["kd"].b])
        k.op("pool", lambda e: e.tensor_tensor(out=S["bb"][:, 0:nb], in0=S["kap"][:, 0:nb], in1=S["a"][:, 0:nb], op=ALU.mult),
             reads=[S["kap"].b, S["a"].b], writes=[S["bb"].b])
        for name, x, y in (("kap", S["kap"], S["eLx"]), ("bt", S["bb"], S["eLn"]), ("kt", S["kd"], S["eLn"]), ("rt", rS, S["eL"]),
                           ("v", vS, None)):
            dst = bd[name][bdi]
            for h in range(2):
                hs = slice(h * 64, (h + 1) * 64)
                eng = ew()
                xv = x[hs, 0:nb].rearrange("p (c s) -> p c s", s=64)
                if y is None:
                    k.op(eng, lambda e: e.tensor_copy(out=rv(dst[hs, 0:nch, h, :]), in_=xv), reads=[x.b], writes=[dst.b])
                else:
                    yv = y[hs, 0:nb].rearrange("p (c s) -> p c s", s=64)
                    k.op(eng, lambda e: e.tensor_tensor(out=rv(dst[hs, 0:nch, h, :]), in0=xv, in1=yv, op=ALU.mult),
                         reads=[x.b, y.b], writes=[dst.b])

    def unit(pr, tok0, c, d, bdi, useq, final):
        ui = useq % WIDTH
        mk = MSK[d]
        Kap = bd["kap"][bdi][:, c, :, :].rearrange("p a b -> p (a b)")
        Bt = bd["bt"][bdi][:, c, :, :].rearrange("p a b -> p (a b)")
        Kt = bd["kt"][bdi][:, c, :, :].rearrange("p a b -> p (a b)")
        Rt = bd["rt"][bdi][:, c, :, :].rearrange("p a b -> p (a b)")
        Vf = bd["v"][bdi][:, c, :, :].rearrange("p a b -> p (a b)")
        bdb = [bd[n][bdi].b for n in bd]
        gcol = c * 64 + (63 if d == 0 else 0)
        gam = S["eL"][:, gcol:gcol + 1]

        def T(n):
            return ut(n)[ui]

        def mm(lhsT, rhs, rd, N=128, acc=None, full=False):
            ps = rb() if acc is None else acc[0]
            st, sp_ = (True, True) if acc is None else (acc[1], acc[2])
            if not full:
                lhsT, rhs = rv(lhsT), rv(rhs)
            k.op("pe", lambda e: e.matmul(out=ps[:, 0:N], lhsT=lhsT, rhs=rhs, start=st, stop=sp_), reads=rd, writes=[ps.b])
            return ps

        def ev_copy(dst_t, dst_ap, ps, N=128, scale=None, full=False):
            if not full:
                dst_ap = rv(dst_ap)
            if scale is None:
                k.op("act", lambda e: e.activation(out=dst_ap, in_=ps[:, 0:N], func=AF.Copy), reads=[ps.b], writes=[dst_t.b])
            else:
                k.op("act", lambda e: e.activation(out=dst_ap, in_=ps[:, 0:N], func=AF.Copy, scale=scale), reads=[ps.b, S["eL"].b],
                     writes=[dst_t.b])

        def ev_tt(dst_t, dst_ap, in0_ap, in0_b, ps, op, N=128, psfirst=False, full=False):
            if not full:
                dst_ap = rv(dst_ap)
            if psfirst:
                k.op("dve", lambda e: e.tensor_tensor(out=dst_ap, in0=ps[:, 0:N], in1=in0_ap, op=op), reads=[ps.b] + in0_b, writes=[dst_t.b])
            else:
                k.op("dve", lambda e: e.tensor_tensor(out=dst_ap, in0=in0_ap, in1=ps[:, 0:N], op=op), reads=[ps.b] + in0_b, writes=[dst_t.b])

        X = [T("Xa"), T("Xb")]
        idb = [identb.b]
        ps = mm(Kap, identb[:, :], bdb + idb)
        ev_copy(X[0], X[0][:, 0:128], ps)
        yield
        ps = mm(Bt, identb[:, :], bdb + idb)
        ev_copy(T("nBtT"), T("nBtT")[:, :], ps, scale=-1.0)
        yield
        ps = mm(Kt, identb[:, :], bdb + idb)
        ev_copy(T("KtT"), T("KtT")[:, :], ps)
        yield
        ps = mm(Vf, identb[:, :], bdb + idb)
        ev_copy(T("VT"), T("VT")[:, :], ps)
        yield
        for nm, l_, r_, m_ in (("N", Kap, Bt, "Ms"), ("Z", Bt, Kap, "MsT"), ("AkkT", Kt, Kap, "MsT"), ("ArkT", Kt, Rt, "MiT"),
                               ("nArbT", Bt, Rt, "nMiT")):
            ps = mm(l_, r_, bdb)
            ev_tt(T(nm), T(nm)[:, :], rmask[:, mk[m_], :], [rmask.b], ps, ALU.mult, psfirst=True)
            yield
        ps = mm(T("AkkT")[:, :], T("VT")[:, :], [T("AkkT").b, T("VT").b])
        ev_copy(X[0], X[0][:, 128:256], ps)
        yield
        Zc, Nc = T("Z"), T("N")
        Zalt, Nalt = T("Zalt"), T("Nalt")
        ps = mm(Zc[:, :], X[0][:, :], [Zc.b, X[0].b], N=256)
        ev_tt(X[1], X[1][:, :], X[0][:, :], [X[0].b], ps, ALU.subtract, N=256)
        yield
        xi = 1
        for lev in range(5):
            Zn = Zalt
            ps = mm(Nc[:, :], Zc[:, :], [Nc.b, Zc.b])
            ev_copy(Zn, Zn[:, :], ps)
            yield
            if lev < 4:
                Nn = Nalt
                ps = mm(Zc[:, :], Nc[:, :], [Nc.b, Zc.b])
                ev_copy(Nn, Nn[:, :], ps)
                Nc, Nalt = Nn, Nc
                yield
            Zc, Zalt = Zn, Zc
            ps = mm(Zc[:, :], X[xi][:, :], [Zc.b, X[xi].b], N=256)
            ev_tt(X[1 - xi], X[1 - xi][:, :], X[xi][:, :], [X[xi].b], ps, ALU.add, N=256)
            xi = 1 - xi
            yield
        Xf = X[xi]
        P = Xf[:, 0:128]
        Q = Xf[:, 128:256]
        ps = mm(P, T("nBtT")[:, :], [Xf.b, T("nBtT").b])
        ev_tt(T("ET"), T("ET")[:, :], cmat[:, 0, :], [cmat.b], ps, ALU.add, full=True)
        yield
        ph = rb()
        mm(T("KtT")[:, :], T("VT")[:, :], [T("KtT").b, T("VT").b], acc=(ph, True, False))
        mm(T("nBtT")[:, :], Q, [T("nBtT").b, Xf.b], acc=(ph, False, True))
        ev_copy(T("Hg"), T("Hg")[:, :], ph, scale=gam, full=True)
        yield
        ps = mm(P, T("nArbT")[:, :], [Xf.b, T("nArbT").b])
        ev_tt(T("YmT"), T("YmT")[:, :], Rt, bdb, ps, ALU.add, full=True)
        yield
        M0 = Mst[useq % 2]
        M1 = Mst[(useq + 1) % 2]
        py = rb()
        mm(M0[:, :], T("YmT")[:, :], [M0.b, T("YmT").b], acc=(py, True, False), full=True)
        mm(T("VT")[:, :], T("ArkT")[:, :], [T("VT").b, T("ArkT").b], acc=(py, False, False))
        mm(Q, T("nArbT")[:, :], [Xf.b, T("nArbT").b], acc=(py, False, True))
        for h in range(2):
            hs = slice(h * 64, (h + 1) * 64)
            if not final:
                k.op("act", lambda e: e.activation(out=yf[hs, tok0:tok0 + 64], in_=py[hs, h * 64:(h + 1) * 64], func=AF.Copy),
                     reads=[py.b], writes=[yf.sb(tok0 // 512)])
            else:
                lc = (tok0 - (0 if tok0 < CTX else CTX)) % 512
                k.op("dve", lambda e: e.tensor_tensor(out=S["ysum"][hs, lc:lc + 64], in0=py[hs, h * 64:(h + 1) * 64],
                                                       in1=yf[hs, tok0:tok0 + 64], op=ALU.add),
                     reads=[py.b, yf.sb(tok0 // 512)], writes=[S["ysum"].b])
        pm = mm(T("ET")[:, :], M0[:, :], [T("ET").b, M0.b], full=True)
        k.op("dve", lambda e: e.scalar_tensor_tensor(out=M1[:, :], in0=pm[:, 0:128], scalar=gam, in1=T("Hg")[:, :], op0=ALU.mult, op1=ALU.add),
             reads=[pm.b, S["eL"].b, T("Hg").b], writes=[M1.b])
        yield

    def run_lockstep(gens):
        pending = list(gens)
        active = []
        rounds = 0
        while pending or active:
            if pending and len(active) < WIDTH and (not active or rounds % STAGGER == 0):
                active.append(pending.pop(0))
            for g in list(active):
                try:
                    next(g)
                except StopIteration:
                    active.remove(g)
            rounds += 1

    def finalize(pr, t0, nb):
        ys = S["ysum"]
        ps = rb()
        k.op("pe", lambda e: e.matmul(out=ps[:, 0:nb], lhsT=cmat[:, 1, :], rhs=ys[:, 0:nb], start=True, stop=True),
             reads=[cmat.b, ys.b], writes=[ps.b])
        k.op("dve", lambda e: e.scalar_tensor_tensor(out=S["u1"][:, 0:nb], in0=ps[:, 0:nb], scalar=-1.0 / 64, in1=ys[:, 0:nb],
                                                      op0=ALU.mult, op1=ALU.add), reads=[ps.b, ys.b], writes=[S["u1"].b])
        k.op("act", lambda e: e.activation(out=S["u2"][:, 0:nb], in_=S["u1"][:, 0:nb], func=AF.Square), reads=[S["u1"].b], writes=[S["u2"].b])
        ps = rb()
        k.op("pe", lambda e: e.matmul(out=ps[:, 0:nb], lhsT=cmat[:, 1, :], rhs=S["u2"][:, 0:nb], start=True, stop=True),
             reads=[cmat.b, S["u2"].b], writes=[ps.b])
        emit_rsqrt(k, S["u3"], ps, 1.0 / 64, GN_EPS, nb)
        k.op("dve", lambda e: e.tensor_tensor(out=S["u1"][:, 0:nb], in0=S["u1"][:, 0:nb], in1=S["u3"][:, 0:nb], op=ALU.mult),
             reads=[S["u1"].b, S["u3"].b], writes=[S["u1"].b])
        k.op("act", lambda e: e.activation(out=S["u1"][:, 0:nb], in_=S["u1"][:, 0:nb], func=AF.Identity,
                                           bias=pp[:, PP_GNB + pr:PP_GNB + pr + 1], scale=pp[:, PP_GNG + pr:PP_GNG + pr + 1]),
             reads=[S["u1"].b, pp.b], writes=[S["u1"].b])
        k.op("pool", lambda e: e.tensor_tensor(out=S["u2"][:, 0:nb], in0=S["a"][:, 0:nb], in1=S["a2"][:, 0:nb], op=ALU.add),
             reads=[S["a"].b, S["a2"].b], writes=[S["u2"].b])
        k.op("dve", lambda e: e.tensor_scalar(out=S["u2"][:, 0:nb], in0=S["u2"][:, 0:nb], scalar1=pp[:, PP_KA + pr:PP_KA + pr + 1],
                                               scalar2=c2[:, pr:pr + 1], op0=ALU.mult, op1=ALU.add),
             reads=[S["u2"].b, pp.b, c2.b], writes=[S["u2"].b])
        k.op("pool", lambda e: e.tensor_tensor(out=S["u2"][:, 0:nb], in0=S["u2"][:, 0:nb], in1=S["k"][:, 0:nb], op=ALU.mult),
             reads=[S["u2"].b, S["k"].b], writes=[S["u2"].b])
        k.op("dve", lambda e: e.scalar_tensor_tensor(out=S["u2"][:, 0:nb], in0=S["r"][:, 0:nb], scalar=pp[:, PP_RK + pr:PP_RK + pr + 1],
                                                      in1=S["u2"][:, 0:nb], op0=ALU.mult, op1=ALU.mult),
             reads=[S["r"].b, pp.b, S["u2"].b], writes=[S["u2"].b])
        ps = rb()
        k.op("pe", lambda e: e.matmul(out=ps[:, 0:nb], lhsT=cmat[:, 1, :], rhs=S["u2"][:, 0:nb], start=True, stop=True),
             reads=[cmat.b, S["u2"].b], writes=[ps.b])
        k.op("dve", lambda e: e.tensor_tensor(out=S["u3"][:, 0:nb], in0=ps[:, 0:nb], in1=S["v"][:, 0:nb], op=ALU.mult),
             reads=[ps.b, S["v"].b], writes=[S["u3"].b])
        k.op("pool", lambda e: e.tensor_tensor(out=S["u1"][:, 0:nb], in0=S["u1"][:, 0:nb], in1=S["u3"][:, 0:nb], op=ALU.add),
             reads=[S["u1"].b, S["u3"].b], writes=[S["u1"].b])
        for gi, gn_ in enumerate(("g0", "g1")):
            k.op("act", lambda e: e.activation(out=S[gn_][:, 0:nb], in_=S[gn_][:, 0:nb], func=AF.Sigmoid), reads=[S[gn_].b], writes=[S[gn_].b])
        ps = rb()
        for gi, gn_ in enumerate(("g0", "g1")):
            k.op("pe", lambda e: e.matmul(out=ps[:, 0:nb], lhsT=gup[:, gi, pr * 128:(pr + 1) * 128], rhs=S[gn_][:, 0:nb],
                                          start=(gi == 0), stop=(gi == 1)), reads=[gup.b, S[gn_].b], writes=[ps.b])
        os_ = ostage[ecount[0] % 2]
        ecount[0] += 1
        k.op("dve", lambda e: e.tensor_tensor(out=os_[:, 0:nb], in0=S["u1"][:, 0:nb], in1=ps[:, 0:nb], op=ALU.mult),
             reads=[S["u1"].b, ps.b], writes=[os_.b])
        out_write(k, E, os_, 0, 128, nb, [(128 + pr * 128, 0)], t0, nb)

    blocks_f = MBLK
    blocks_b = [MBLK[0]] + MBLK[:0:-1]
    bdc = 0
    for pr in range(2):
        for d in range(2):
            useq = 0
            k.op("pool", lambda e: e.memset(Mst[0][:], 0.0), writes=[Mst[0].b])
            for (t0, nb) in (blocks_f if d == 0 else blocks_b):
                prep(pr, t0, nb, d, d == 1, bdc % 2)
                nch = nb // 64
                order = list(range(nch)) if d == 0 else list(range(nch - 1, -1, -1))
                run_lockstep([unit(pr, t0 + c * 64, c, d, bdc % 2, useq + i, d == 1) for i, c in enumerate(order)])
                useq += nch
                if d == 1:
                    finalize(pr, t0, nb)
                bdc += 1


def build_F():
    k = K()
    nc = k.nc
    E = Env()

    def din(name, shape, dt=F32):
        return nc.dram_tensor(name, list(shape), dt, kind="ExternalInput").ap()

    E.hT0d = din("hT0", [D, NT])
    E.ccTd = din("ccT", [D, 2])
    E.awqd = din("awq", [D, 96 * 128])
    E.abqd = din("abq", [128, 96])
    E.indd = din("ind", [128, 4])
    E.gn1d = din("gn1T", [128, DEPTH * NCH])
    E.gn2d = din("gn2T", [128, DEPTH * NCH])
    E.wind = din("win", [DEPTH, D, NW])
    E.ppd = din("pp", [DEPTH, 128, NPP])
    E.sinkd = din("sinkrow", [DEPTH, 1, 512])
    E.w0d = din("w0row", [DEPTH, 1, 512])
    E.wupd = din("wup", [DEPTH, 96, 512])
    E.aupd = din("aup", [DEPTH, 96, 512])
    E.gupd = din("gup", [DEPTH, 128, 2, 256])
    E.cosd = din("cosT", [128, TALL])
    E.sind = din("sinT", [128, TALL])
    E.cmatd = din("cmat", [128, 3, 128])
    E.wmaskd = din("wmask", [128, 4, 512], BF16)
    E.rmaskd = din("rmask", [128, 6, 128])
    E.trid = din("tri", [64, 4, 64])
    E.woutd = din("wout", [DEPTH, D, D])
    E.w1d = din("w1", [DEPTH, D, 4 * D])
    E.w2d = din("w2", [DEPTH, 4 * D, D])
    E.hOd = nc.dram_tensor("hTo", [D, NT], F32, kind="ExternalOutput").ap()
    E.mod_in = k.dram("mod_in", [128, 192], F32)
    E.mod_all = k.dram("mod_all", [512, 192], F32)
    E.u_in = [k.dram(f"u_in{i}", [n * 128, NT], BF16) for i, (c0, n) in enumerate(UPIECES)]
    E.u_all = [k.dram(f"u_all{i}", [4 * n * 128, NT], BF16) for i, (c0, n) in enumerate(UPIECES)]
    E.rs_in = k.dram("rs_in", [4 * D, NT], BF16)
    E.rs_out = k.dram("rs_out", [D, NT], BF16)
    E.h_spill = k.dram("h_spill", [D, NT], F32)
    E.pB = k.dram("pB", [PB_ROWS, PBW], F32)
    E.pbb = E.pB.b
    E.cst = emit_consts(k)
    E.modS = k.sbuf([128, 4, 96, 2], F32, "modS")
    E.mods = Mods(E.modS)
    E.ind = k.sbuf([128, 4], F32, "ind")
    E.gn1 = k.sbuf([128, DEPTH * NCH], F32, "gn1")
    E.gn2 = k.sbuf([128, DEPTH * NCH], F32, "gn2")
    E.pp = k.sbuf([128, NPP], F32, "pp")
    E.cmat = k.sbuf([128, 3, 128], F32, "cmat")
    E.wmask = k.sbuf([128, 4, 512], BF16, "wmask")
    E.rmask = k.sbuf([128, 6, 128], F32, "rmask")
    E.tri = k.sbuf([64, 4, 64], F32, "tri")
    E.m4c = [0]
    E.PS = k.psum([128, 8, 512], F32, "PSall")
    E.banks = [BankView(E.PS.t, i) for i in range(8)]
    emit_P0(k, E)
    emit_P1(k, E)
    for l in range(DEPTH):
        emit_B(k, E, l)
        k.collective("ReduceScatter", ALU.add, E.rs_in, E.rs_out, reads=[E.rs_in.b], writes=[E.rs_out.b])
        emit_C(k, E, l)
    return k.finish()


_PROG = {}


def _c(a):
    return np.ascontiguousarray(a)


def host_B_consts():
    ident = np.eye(128, dtype=np.float32)
    bo = np.zeros((128, 128), np.float32)
    bo[:64, :64] = 1
    bo[64:, 64:] = 1
    R = np.zeros((128, 128), np.float32)
    for m in range(128):
        if m % 64 < 32:
            R[m + 32, m] = -1.0
        else:
            R[m - 32, m] = 1.0
    cmat = _c(np.stack([ident, bo, R], 1))
    t = np.arange(SEQ)
    row = (t // 64).astype(np.float32)
    col = (t % 64).astype(np.float32)
    inv = (10000.0 ** (-np.arange(16, dtype=np.float32) / 16)).astype(np.float32)
    ang = np.concatenate([row[:, None] * inv, col[:, None] * inv], -1).astype(np.float32)
    cosT = np.ones((128, TALL), np.float32)
    sinT = np.zeros((128, TALL), np.float32)
    c = np.cos(ang).T.astype(np.float32)
    s = np.sin(ang).T.astype(np.float32)
    for p in range(128):
        cosT[p, CTX:] = c[p % 32]
        sinT[p, CTX:] = s[p % 32]
    wm = np.zeros((128, 4, 2, 256), np.float32)
    kl = np.arange(128)[:, None]
    ql = np.arange(256)[None, :]
    for r in range(4):
        d = ql - kl - (r - 1) * 128
        wm[:, r, :, :] = (np.abs(d) <= 128)[:, None, :]
    wmask = _c(wm.reshape(128, 4, 512).astype(ml_dtypes.bfloat16))
    i64 = np.arange(64)
    SL = (i64[:, None] > i64[None, :]).astype(np.float32)
    SU = SL.T.copy()
    UI = (i64[:, None] <= i64[None, :]).astype(np.float32)
    LI = UI.T.copy()
    rm = np.zeros((128, 6, 128), np.float32)
    for mi, mm_ in enumerate((SL, SU, UI, LI, -UI, -LI)):
        rm[:64, mi, :64] = mm_
        rm[64:, mi, 64:] = mm_
    tri = np.stack([UI, SU, LI, SL], 1).astype(np.float32) * np.float32(RW_SCALE)
    return {"cmat": cmat, "cosT": cosT, "sinT": sinT, "wmask": wmask, "rmask": _c(rm), "tri": _c(tri)}


A_IN_ = 768
B_IN_ = 3712


def host_core_inputs(I, core, consts):
    b, j = divmod(core, 4)
    kv = j // 2
    m = dict(consts)
    x, ctx = I["x"], I["ctx"]
    m["hT0"] = _c(np.concatenate([x[b, j * 1024:(j + 1) * 1024], ctx[b, j * 64:(j + 1) * 64]], 0).T)
    m["ccT"] = _c(np.stack([I["c"][b], I["c_ctx"]], 0).T)
    awq = np.empty((D, 96 * 128), np.float32)
    abq = np.empty((128, 96), np.float32)
    for l in range(DEPTH):
        for w in range(6):
            for cl in range(4):
                cc = (l * 6 + w) * 4 + cl
                g0 = w * D + (j * 4 + cl) * 128
                awq[:, cc * 128:(cc + 1) * 128] = I["ada_w"][l][:, g0:g0 + 128]
                abq[:, cc] = I["ada_b"][l][g0:g0 + 128]
    m["awq"] = awq
    m["abq"] = abq
    ind = np.zeros((128, 4), np.float32)
    ind[:, j] = 1.0
    m["ind"] = ind
    m["gn1T"] = _c(np.concatenate([I["norm1_g"][l].reshape(-1, 128).T for l in range(DEPTH)], 1))
    m["gn2T"] = _c(np.concatenate([I["norm2_g"][l].reshape(-1, 128).T for l in range(DEPTH)], 1))
    cols = []
    cols += list(range(2 * j * 64, (2 * j + 2) * 64))
    cols += list(range(512 + kv * 64, 512 + (kv + 1) * 64)) * 2
    cb = A_IN_ + B_IN_
    cols += list(range(cb + 2 * j * 64, cb + (2 * j + 2) * 64))
    cols += list(range(cb + 512 + kv * 64, cb + 512 + (kv + 1) * 64)) * 2
    cols += list(range(640 + kv * 64, 640 + (kv + 1) * 64))
    cols += list(range(cb + 640 + kv * 64, cb + 640 + (kv + 1) * 64))
    bb = A_IN_
    for part in range(3):
        cols += list(range(bb + part * 1024 + 4 * j * 64, bb + part * 1024 + (4 * j + 4) * 64))
    cols += list(range(bb + 3072, bb + 3712))
    assert len(cols) == NW
    m["win"] = _c(I["w_in"][:, :, cols])
    bcols = [c_ - bb for c_ in cols[640:]]
    hc = slice(4 * j * 64, (4 * j + 4) * 64)
    pp = np.zeros((DEPTH, 128, NPP), np.float32)
    for l in range(DEPTH):
        pp[l, :, PP_QKG + 0] = np.tile(I["a_q_norm"][l], 2)
        pp[l, :, PP_QKG + 1] = np.tile(I["a_k_norm"][l], 2)
        pp[l, :, PP_QKG + 2] = np.tile(I["c_q_norm"][l], 2)
        pp[l, :, PP_QKG + 3] = np.tile(I["c_k_norm"][l], 2)
        mu = I["shift_mu"][l]
        pos = 0
        for ci in range(5, 17):
            M = CHUNKS[ci][1]
            idx = bcols[pos:pos + M]
            pos += M
            pp[l, :M, PP_MU + (ci - 5) * 2 + 0] = mu[0][idx]
            pp[l, :M, PP_MU + (ci - 5) * 2 + 1] = mu[1][idx]
        for pr in range(2):
            sl = slice(4 * j * 64 + pr * 128, 4 * j * 64 + (pr + 1) * 128)
            for d in range(2):
                pp[l, :, PP_A0 + pr * 2 + d] = I["iclr_a0"][l][d][sl]
            pp[l, :, PP_KK + pr] = I["k_k"][l][sl]
            pp[l, :, PP_KA + pr] = I["k_a"][l][sl]
            pp[l, :, PP_RK + pr] = I["r_k"][l][sl]
            pp[l, :, PP_GNG + pr] = I["gn_g"][l][sl]
            pp[l, :, PP_GNB + pr] = I["gn_b"][l][sl]
    m["pp"] = pp
    m["sinkrow"] = _c(np.stack([np.repeat(I["a_sink"][l][2 * j:2 * j + 2], 256)[None, :] for l in range(DEPTH)]).astype(np.float32))
    m["w0row"] = _c(np.stack([np.concatenate([I["decay_w0"][l][d][hc] for d in range(2)])[None, :] for l in range(DEPTH)]))
    m["wup"] = _c(np.stack([np.concatenate([I["decay_up"][l][d][:, hc] for d in range(2)], 1) for l in range(DEPTH)]))
    m["aup"] = _c(np.stack([np.concatenate([I["iclr_up"][l][d][:, hc] for d in range(2)], 1) for l in range(DEPTH)]))
    m["gup"] = _c(np.stack([I["gate_up"][l][:, hc].reshape(2, 128, 256).transpose(1, 0, 2) for l in range(DEPTH)]))
    return m


def kernel(x, c, ctx, c_ctx, ada_w, ada_b, norm1_g, norm2_g, w_in, a_q_norm, a_k_norm, a_sink, c_q_norm, c_k_norm, shift_mu,
           decay_w0, decay_up, iclr_a0, iclr_up, gate_up, k_k, k_a, r_k, gn_g, gn_b, w_out, mlp_w1, mlp_w2):
    I = dict(x=x, c=c, ctx=ctx, c_ctx=c_ctx, ada_w=ada_w, ada_b=ada_b, norm1_g=norm1_g, norm2_g=norm2_g, w_in=w_in,
             a_q_norm=a_q_norm, a_k_norm=a_k_norm, a_sink=a_sink, c_q_norm=c_q_norm, c_k_norm=c_k_norm, shift_mu=shift_mu,
             decay_w0=decay_w0, decay_up=decay_up, iclr_a0=iclr_a0, iclr_up=iclr_up, gate_up=gate_up, k_k=k_k, k_a=k_a, r_k=r_k,
             gn_g=gn_g, gn_b=gn_b, w_out=w_out, mlp_w1=mlp_w1, mlp_w2=mlp_w2)
    I = {k_: np.asarray(v, dtype=np.float32) for k_, v in I.items()}
    consts = host_B_consts()
    perm = []
    for r in range(4):
        perm += list(range(2 * r * 64, 2 * r * 64 + 128)) + list(range(512 + 4 * r * 64, 512 + 4 * r * 64 + 256)) \
            + list(range(1536 + 2 * r * 64, 1536 + 2 * r * 64 + 128))
    shared = {"wout": _c(I["w_out"][:, perm, :]), "w1": I["mlp_w1"], "w2": I["mlp_w2"]}
    maps = []
    for core in range(8):
        m = host_core_inputs(I, core, consts)
        m.update(shared)
        maps.append(m)
    if "F" not in _PROG:
        _PROG["F"] = build_F()
    res = run_bass_kernel_spmd(_PROG["F"], maps, core_ids=list(range(8))).results
    out = np.empty((2, SEQ, D), np.float32)
    for core in range(8):
        b, j = divmod(core, 4)
        out[b, j * 1024:(j + 1) * 1024] = res[core]["hTo"][:, 0:1024].T
    return out
```

```python
import numpy as np
import ml_dtypes
from contextlib import ExitStack
import concourse.bass as bass
import concourse.mybir as mybir
from concourse.bass_utils import run_bass_kernel_spmd

F32 = mybir.dt.float32
BF16 = mybir.dt.bfloat16
AF = mybir.ActivationFunctionType
ALU = mybir.AluOpType
AX = mybir.AxisListType

D = 2048
NCH = 16
SEQ = 4096
CTX = 256
DEPTH = 4
NT = 1088
BLK = [(0, 512, 0), (512, 512, 0), (1024, 64, 1)]
TALL = SEQ + CTX
NORM_EPS = 1e-6
GN_EPS = 64e-5
SEM_LIMIT = 30000
GROUPS = [[0, 1, 2, 3], [4, 5, 6, 7]]
UPIECES = [(0, 3), (3, 3), (6, 3), (9, 3), (12, 3), (15, 1)]


class Buf:
    __slots__ = ("w", "r", "excl")

    def __init__(self, excl=False):
        self.w = None
        self.r = {}
        self.excl = excl


class Tile:
    def __init__(self, t):
        self.t = t
        self.b = Buf()
        self._sub = {}

    def sb(self, key):
        b = self._sub.get(key)
        if b is None:
            b = self._sub[key] = Buf()
        return b

    def __getitem__(self, idx):
        return self.t[idx]


class BankView:
    def __init__(self, t, i):
        self.t = t
        self.i = i
        self.b = Buf(excl=True)

    def __getitem__(self, idx):
        p, f = idx
        return self.t[p, self.i, f]


class K:
    def __init__(self):
        self.nc = bass.Bass("TRN2", target_bir_lowering=False)
        nc = self.nc
        self.ctx = ExitStack()
        self.engs = {"pe": nc.tensor, "dve": nc.vector, "act": nc.scalar, "pool": nc.gpsimd, "sp": nc.sync}
        self.cur = {}
        self.waited = {e: {} for e in self.engs}
        self.nsem = 0
        for e in self.engs:
            self._new_sem(e)
        self.dpool = {}
        self.dcnt = {}
        for q in ("sp", "pool", "act"):
            self.dpool[q] = []
            for i in range(12):
                nm = f"d_{q}_{i}"
                self.dpool[q].append([self.ctx.enter_context(nc.semaphore(nm)), nm, 0])
            self.dcnt[q] = 0
        self.cpool = [[self.ctx.enter_context(nc.semaphore(f"cc_{i}")), f"cc_{i}", 0] for i in range(6)]
        self.ccnt = 0
        self.out_toks = []
        self.nuniq = 0
        self.phase = None

    def _new_sem(self, e):
        nm = f"c_{e}_{self.nsem}"
        self.nsem += 1
        self.cur[e] = [self.ctx.enter_context(self.nc.semaphore(nm)), nm, 0]

    def sbuf(self, shape, dt, name=None):
        self.nuniq += 1
        ctx = self.phase if self.phase is not None else self.ctx
        return Tile(ctx.enter_context(self.nc.sbuf_tensor(f"s_{name or 'sb'}_{self.nuniq}", list(shape), dt)))

    def barrier(self):
        toks = [(c[1], c[0], c[2]) for c in self.cur.values() if c[2] > 0]
        for q in self.dpool:
            toks += [(sl[1], sl[0], sl[2]) for sl in self.dpool[q] if sl[2] > 0]
        toks += [(sl[1], sl[0], sl[2]) for sl in self.cpool if sl[2] > 0]
        for eng, e in self.engs.items():
            for nm, sem, val in toks:
                if nm == self.cur[eng][1]:
                    continue
                if self.waited[eng].get(nm, 0) < val:
                    e.wait_ge(sem, val)
                    self.waited[eng][nm] = val

    def psum(self, shape, dt, name=None):
        self.nuniq += 1
        t = Tile(self.ctx.enter_context(self.nc.psum_tensor("p_" + (name or f"ps{self.nuniq}"), list(shape), dt)))
        t.b.excl = True
        return t

    def dram(self, name, shape, dt, kind="Internal"):
        return Tile(self.nc.dram_tensor(name, list(shape), dt, kind=kind))

    def _deps(self, eng, reads, writes):
        deps = {}

        def add(t):
            if t is None:
                return
            o = deps.get(t[0])
            if o is None or o[2] < t[2]:
                deps[t[0]] = t

        for b in reads:
            add(b.w)
            if b.excl:
                for kk, t in b.r.items():
                    if kk != eng:
                        add(t)
        for b in writes:
            add(b.w)
            for t in b.r.values():
                add(t)
        e = self.engs[eng]
        wd = self.waited[eng]
        for nm, (_, sem, val) in deps.items():
            if eng == "pe" and nm == self.cur["pe"][1]:
                continue
            if wd.get(nm, 0) >= val:
                continue
            e.wait_ge(sem, val)
            wd[nm] = val

    def _mark(self, key, tok, reads, writes):
        for b in writes:
            b.w = tok
            b.r = {}
        for b in reads:
            if b.w is not tok:
                b.r[key] = tok

    def op(self, eng, fn, reads=(), writes=()):
        self._deps(eng, reads, writes)
        inst = fn(self.engs[eng])
        c = self.cur[eng]
        if c[2] >= SEM_LIMIT:
            self._new_sem(eng)
            c = self.cur[eng]
        inst.then_inc(c[0], 1)
        c[2] += 1
        tok = (c[1], c[0], c[2])
        self._mark(eng, tok, reads, writes)
        return tok

    def dma(self, q, out, in_, reads=(), writes=(), is_out=False, **kw):
        self._deps(q, reads, writes)
        e = self.engs[q]
        slot = self.dpool[q][self.dcnt[q] % len(self.dpool[q])]
        self.dcnt[q] += 1
        if slot[2] > 0 and self.waited[q].get(slot[1], 0) < slot[2]:
            e.wait_ge(slot[0], slot[2])
            self.waited[q][slot[1]] = slot[2]
        inst = e.dma_start(out=out, in_=in_, **kw)
        inst.then_inc(slot[0], 16)
        slot[2] += 16
        tok = (slot[1], slot[0], slot[2])
        self._mark(slot[1], tok, reads, writes)
        if is_out:
            self.out_toks.append(tok)
        return tok

    def collective(self, kind, alu, in_t, out_t, reads, writes):
        self._deps("pool", reads, writes)
        e = self.engs["pool"]
        slot = self.cpool[self.ccnt % len(self.cpool)]
        self.ccnt += 1
        if slot[2] > 0 and self.waited["pool"].get(slot[1], 0) < slot[2]:
            e.wait_ge(slot[0], slot[2])
            self.waited["pool"][slot[1]] = slot[2]
        inst = e.collective_compute(kind, alu, replica_groups=GROUPS, ins=[in_t.t.ap().opt()], outs=[out_t.t.ap().opt()])
        inst.then_inc(slot[0], 1)
        slot[2] += 1
        tok = (slot[1], slot[0], slot[2])
        self._mark(slot[1], tok, reads, writes)
        return tok

    def finish(self):
        e = self.engs["sp"]
        for nm, sem, val in self.out_toks:
            if self.waited["sp"].get(nm, 0) < val:
                e.wait_ge(sem, val)
                self.waited["sp"][nm] = val
        self.ctx.close()
        return self.nc


def emit_consts(k):
    c = {}
    c["ones_bf"] = k.sbuf([128, 128], BF16, "ones_bf")
    k.op("pool", lambda e: e.memset(c["ones_bf"][:], 1.0), writes=[c["ones_bf"].b])
    return c


def emit_rsqrt(k, out, src, scale, eps, n, p0=0, p1=128):
    k.op("act", lambda e: e.activation(out=out[p0:p1, 0:n], in_=src[p0:p1, 0:n], func=AF.Sqrt, bias=float(eps), scale=float(scale)),
         reads=[src.b], writes=[out.b])
    k.op("dve", lambda e: e.reciprocal(out=out[p0:p1, 0:n], in_=out[p0:p1, 0:n]), reads=[out.b], writes=[out.b])


class Env:
    pass


class Mods:
    def __init__(self, t):
        self.t = t
        self.b = t.b

    def ap(self, l, lc, w, m):
        return self.t[:, m // 4, (l * 6 + w) * 4 + m % 4, lc:lc + 1]

    def vec(self, l, lc, w):
        return self.t[:, :, (l * 6 + w) * 4:(l * 6 + w) * 4 + 4, lc]


def emit_modprep(k, mods, l, gn_ap, gn_b, which_sc, name):
    gsc = k.sbuf([128, 2, NCH], F32, name + "_gsc")
    for lc in range(2):
        ov = gsc[:, lc, :].rearrange("p (a b) -> p a b", a=4)
        k.op("dve", lambda e: e.tensor_scalar(out=ov, in0=mods.vec(l, lc, which_sc), scalar1=1.0, scalar2=None, op0=ALU.add),
             reads=[mods.b], writes=[gsc.b])
        k.op("dve", lambda e: e.tensor_tensor(out=ov, in0=ov, in1=gn_ap.rearrange("p (a b) -> p a b", a=4), op=ALU.mult),
             reads=[gsc.b, gn_b], writes=[gsc.b])
    return gsc


def emit_norm_mod(k, cst, hT, gsc, mods, l, which_sh, out_bf, sq, stat_ps, rstd):
    for bi, (t0, nb, lc) in enumerate(BLK):
        hb = [hT.sb((m, bi)) for m in range(NCH)]
        for m in range(NCH):
            k.op("act", lambda e: e.activation(out=sq[:, m % 8, 0:nb], in_=hT[:, m, t0:t0 + nb], func=AF.Square),
                 reads=[hb[m]], writes=[sq.sb(m % 8)])
            k.op("pe", lambda e: e.matmul(out=stat_ps[:, 0:nb], lhsT=cst["ones_bf"][:, :], rhs=sq[:, m % 8, 0:nb],
                                          start=(m == 0), stop=(m == NCH - 1)),
                 reads=[sq.sb(m % 8), cst["ones_bf"].b], writes=[stat_ps.b])
        emit_rsqrt(k, rstd, stat_ps, 1.0 / D, NORM_EPS, nb)
        for m in range(NCH):
            tt = k.tmpn[m % 2]
            k.op("dve", lambda e: e.scalar_tensor_tensor(out=tt[:, 0:nb], in0=hT[:, m, t0:t0 + nb], scalar=gsc[:, lc, m:m + 1],
                                                          in1=rstd[:, 0:nb], op0=ALU.mult, op1=ALU.mult),
                 reads=[hb[m], gsc.b, rstd.b], writes=[tt.b])
            k.op("act", lambda e: e.activation(out=out_bf[:, m, t0:t0 + nb], in_=tt[:, 0:nb], func=AF.Identity,
                                               bias=mods.ap(l, lc, which_sh, m), scale=1.0),
                 reads=[tt.b, mods.b], writes=[out_bf.sb((m, bi))])


def end_phase(k):
    k.barrier()
    k.phase.close()
    k.phase = None


def emit_u_exchange(k, E, ubf):
    for pc, (c0, n) in enumerate(UPIECES):
        k.dma("sp", E.u_in[pc].t.ap().rearrange("(c p) t -> p c t", p=128), ubf[:, c0:c0 + n, :],
              reads=[ubf.sb((m, bi)) for m in range(c0, c0 + n) for bi in range(3)], writes=[E.u_in[pc].b])
        k.collective("AllGather", ALU.bypass, E.u_in[pc], E.u_all[pc], reads=[E.u_in[pc].b], writes=[E.u_all[pc].b])


def emit_P0(k, E):
    k.phase = ExitStack()
    sc = k.sbuf([128, NCH, 2], F32)
    scb = k.sbuf([128, NCH, 2], BF16)
    abq = k.sbuf([128, 96], F32)
    modP = k.sbuf([128, 96, 2], F32)
    wb = [k.sbuf([128, NCH, 512], BF16) for _ in range(2)]
    k.dma("sp", sc[:], E.ccTd.rearrange("(c p) r -> p c r", p=128), writes=[sc.b])
    k.dma("sp", abq[:], E.abqd[:, :], writes=[abq.b])
    k.dma("sp", E.ind[:], E.indd[:, :], writes=[E.ind.b])
    k.dma("sp", E.gn1[:], E.gn1d[:, :], writes=[E.gn1.b])
    k.dma("sp", E.gn2[:], E.gn2d[:, :], writes=[E.gn2.b])
    k.dma("sp", E.cmat[:], E.cmatd[:, :, :], writes=[E.cmat.b])
    k.dma("sp", E.wmask[:], E.wmaskd[:, :, :], writes=[E.wmask.b])
    k.dma("sp", E.rmask[:], E.rmaskd[:, :, :], writes=[E.rmask.b])
    k.dma("sp", E.tri[:], E.trid[:, :, :], writes=[E.tri.b])
    k.op("act", lambda e: e.activation(out=scb[:], in_=sc[:], func=AF.Silu), reads=[sc.b], writes=[scb.b])
    awv = E.awqd.rearrange("(c p) n -> p c n", p=128)
    for g in range(24):
        w = wb[g % 2]
        k.dma("pool", w[:], awv[:, :, g * 512:(g + 1) * 512], writes=[w.b])
        for q in range(4):
            cc = g * 4 + q
            p = E.banks[cc % 2]
            for c in range(NCH):
                k.op("pe", lambda e: e.matmul(out=p[:, 0:2], lhsT=w[:, c, q * 128:(q + 1) * 128], rhs=scb[:, c, :],
                                              start=(c == 0), stop=(c == NCH - 1)), reads=[scb.b, w.b], writes=[p.b])
            k.op("dve", lambda e: e.tensor_scalar(out=modP[:, cc, :], in0=p[:, 0:2], scalar1=abq[:, cc:cc + 1], scalar2=None, op0=ALU.add),
                 reads=[p.b, abq.b], writes=[modP.b])
    k.dma("sp", E.mod_in.t.ap(), modP[:, :, :].rearrange("p a b -> p (a b)"), reads=[modP.b], writes=[E.mod_in.b])
    k.collective("AllGather", ALU.bypass, E.mod_in, E.mod_all, reads=[E.mod_in.b], writes=[E.mod_all.b])
    k.dma("sp", E.modS[:, :, :, :].rearrange("p r a b -> p r (a b)"), E.mod_all.t.ap().rearrange("(r p) n -> p r n", p=128),
          reads=[E.mod_all.b], writes=[E.modS.b])
    end_phase(k)


def emit_P1(k, E):
    k.phase = ExitStack()
    hT = k.sbuf([128, NCH, NT], F32, "hT")
    ubf = k.sbuf([128, NCH, NT], BF16, "ubf")
    sq = k.sbuf([128, 8, 512], BF16, "sq")
    k.tmpn = [k.sbuf([128, 512], F32) for _ in range(2)]
    rstd = k.sbuf([128, 512], F32, "rstd")
    hv = E.hT0d.rearrange("(c p) t -> p c t", p=128)
    for bi, (t0, nb, lc) in enumerate(BLK):
        k.dma("sp", hT[:, :, t0:t0 + nb], hv[:, :, t0:t0 + nb], writes=[hT.sb((m, bi)) for m in range(NCH)])
    gsc = emit_modprep(k, E.mods, 0, E.gn1[:, 0:NCH], E.gn1.b, 1, "n1")
    emit_norm_mod(k, E.cst, hT, gsc, E.mods, 0, 0, ubf, sq, E.banks[6], rstd)
    emit_u_exchange(k, E, ubf)
    end_phase(k)


def emit_C(k, E, l):
    k.phase = ExitStack()
    last = l == DEPTH - 1
    hT = k.sbuf([128, NCH, NT], F32, "hT")
    abf = k.sbuf([128, NCH, NT], BF16, "abf")
    h1 = k.sbuf([128, NCH, NT], BF16, "h1")
    sq = k.sbuf([128, 8, 512], BF16, "sq")
    k.tmpn = [k.sbuf([128, 512], F32) for _ in range(2)]
    rtmp = [k.sbuf([128, 512], F32) for _ in range(2)]
    rstd = k.sbuf([128, 512], F32, "rstd")
    wbuf = [k.sbuf([128, NCH, 256], BF16) for _ in range(3)]
    banks = [[E.banks[i * 3 + jj] for jj in range(3)] for i in range(2)]
    stat = E.banks[6]
    mods = E.mods
    hsrc = E.hT0d if l == 0 else E.h_spill.t.ap()
    hv = hsrc.rearrange("(c p) t -> p c t", p=128)
    ov = E.rs_out.t.ap().rearrange("(c p) t -> p c t", p=128)
    for bi, (t0, nb, lc) in enumerate(BLK):
        k.dma("sp", abf[:, :, t0:t0 + nb], ov[:, :, t0:t0 + nb], reads=[E.rs_out.b], writes=[abf.sb((m, bi)) for m in range(NCH)])
    for bi, (t0, nb, lc) in enumerate(BLK):
        k.dma("sp", hT[:, :, t0:t0 + nb], hv[:, :, t0:t0 + nb], reads=[E.h_spill.b], writes=[hT.sb((m, bi)) for m in range(NCH)])
    wcnt = [0]

    def load_w(src_ap):
        w = wbuf[wcnt[0] % 3]
        wcnt[0] += 1
        k.dma("pool", w[:], src_ap, writes=[w.b])
        return w

    def proj(w, mi, m, src, gate_idx):
        bk = banks[m % 2]
        for bi, (t0, nb, lc) in enumerate(BLK):
            for kc in range(NCH):
                k.op("pe", lambda e: e.matmul(out=bk[bi][:, 0:nb], lhsT=w[:, kc, mi * 128:(mi + 1) * 128], rhs=src[:, kc, t0:t0 + nb],
                                              start=(kc == 0), stop=(kc == NCH - 1)),
                     reads=[w.b, src.sb((kc, bi))], writes=[bk[bi].b])
            k.op("dve", lambda e: e.scalar_tensor_tensor(out=hT[:, m, t0:t0 + nb], in0=bk[bi][:, 0:nb],
                                                          scalar=mods.ap(l, lc, gate_idx, m), in1=hT[:, m, t0:t0 + nb],
                                                          op0=ALU.mult, op1=ALU.add),
                 reads=[bk[bi].b, mods.b, hT.sb((m, bi))], writes=[hT.sb((m, bi))])

    woutv = E.woutd[l].rearrange("(c p) n -> p c n", p=128)
    for mp in range(8):
        w = load_w(woutv[:, :, mp * 256:(mp + 1) * 256])
        for mi in range(2):
            proj(w, mi, mp * 2 + mi, abf, 2)
    gsc2 = emit_modprep(k, mods, l, E.gn2[:, l * NCH:(l + 1) * NCH], E.gn2.b, 4, f"n2_{l}")
    emit_norm_mod(k, E.cst, hT, gsc2, mods, l, 3, abf, sq, stat, rstd)
    w1v = E.w1d[l].rearrange("(c p) n -> p c n", p=128)
    w2v = E.w2d[l].rearrange("(c p) n -> p c n", p=128)
    ecnt = 0
    for q in range(4):
        for fp in range(8):
            f0 = (q * 16 + fp * 2) * 128
            w = load_w(w1v[:, :, f0:f0 + 256])
            for fi in range(2):
                fl = fp * 2 + fi
                bk = banks[fl % 2]
                for bi, (t0, nb, lc) in enumerate(BLK):
                    for kc in range(NCH):
                        k.op("pe", lambda e: e.matmul(out=bk[bi][:, 0:nb], lhsT=w[:, kc, fi * 128:(fi + 1) * 128],
                                                      rhs=abf[:, kc, t0:t0 + nb], start=(kc == 0), stop=(kc == NCH - 1)),
                             reads=[w.b, abf.sb((kc, bi))], writes=[bk[bi].b])
                    rt = rtmp[ecnt % 2]
                    ecnt += 1
                    k.op("act", lambda e: e.activation(out=rt[:, 0:nb], in_=bk[bi][:, 0:nb], func=AF.Relu),
                         reads=[bk[bi].b], writes=[rt.b])
                    k.op("dve", lambda e: e.tensor_tensor(out=h1[:, fl, t0:t0 + nb], in0=rt[:, 0:nb], in1=rt[:, 0:nb], op=ALU.mult),
                         reads=[rt.b], writes=[h1.sb((fl, bi))])
        for mp in range(8):
            w = load_w(w2v[:, q * 16:(q + 1) * 16, mp * 256:(mp + 1) * 256])
            for mi in range(2):
                proj(w, mi, mp * 2 + mi, h1, 5)
    if last:
        hov = E.hOd.rearrange("(c p) t -> p c t", p=128)
        for bi, (t0, nb, lc) in enumerate(BLK):
            k.dma("sp", hov[:, :, t0:t0 + nb], hT[:, :, t0:t0 + nb], reads=[hT.sb((m, bi)) for m in range(NCH)], is_out=True)
    else:
        hsv = E.h_spill.t.ap().rearrange("(c p) t -> p c t", p=128)
        for bi, (t0, nb, lc) in enumerate(BLK):
            k.dma("sp", hsv[:, :, t0:t0 + nb], hT[:, :, t0:t0 + nb], reads=[hT.sb((m, bi)) for m in range(NCH)], writes=[E.h_spill.b])
        gsc1 = emit_modprep(k, mods, l + 1, E.gn1[:, (l + 1) * NCH:(l + 2) * NCH], E.gn1.b, 1, f"n1_{l}")
        emit_norm_mod(k, E.cst, hT, gsc1, mods, l + 1, 0, h1, sq, stat, rstd)
        emit_u_exchange(k, E, h1)
    end_phase(k)


def out_write(k, E, src, p0, p1, ncols, segs, t0, n):
    m4 = E.m4[E.m4c[0] % 2]
    E.m4c[0] += 1
    for r in range(4):
        if r % 2 == 0:
            k.op("dve", lambda e: e.tensor_scalar(out=m4[p0:p1, r, 0:ncols], in0=src[p0:p1, 0:ncols], scalar1=E.ind[p0:p1, r:r + 1],
                                                   scalar2=None, op0=ALU.mult), reads=[src.b, E.ind.b], writes=[m4.b])
        else:
            k.op("act", lambda e: e.activation(out=m4[p0:p1, r, 0:ncols], in_=src[p0:p1, 0:ncols], func=AF.Copy,
                                               scale=E.ind[p0:p1, r:r + 1]), reads=[src.b, E.ind.b], writes=[m4.b])
    rs4 = E.rs_in.t.ap().rearrange("(j r q) t -> j q r t", j=4, r=4)
    if t0 < CTX:
        pieces = [(jj, 1024, 64, jj * 64) for jj in range(4)]
    else:
        lat = t0 - CTX
        pieces = [(lat // 1024, lat % 1024, n, 0)]
    P = p1 - p0
    for row0, coff in segs:
        for jj, lcol, cnt, so in pieces:
            k.dma("sp", rs4[jj][row0:row0 + P, :, lcol:lcol + cnt], m4[p0:p1, :, coff + so:coff + so + cnt],
                  reads=[m4.b], writes=[E.rs_in.b])


NW = 2048
CHUNKS = [(i * 128, 128) for i in range(11)] + [(1408 + i * 96, 96) for i in range(4)] + [(1792, 128), (1920, 128)]
PB_ROWS = 1408
PB_ROWOFF = [0, 128, 256, 384, 512, 640, 768, 864, 960, 1056, 1152, 1280]
PBW = TALL + 4
MBLK = [(0, 256)] + [(256 + i * 512, 512) for i in range(8)]
ATT_SCALE = 0.125
PP_QKG = 0
PP_MU = 4
PP_A0 = 28
PP_KK = 32
PP_KA = 34
PP_RK = 36
PP_GNG = 38
PP_GNB = 40
NPP = 42


def pb_col(t):
    return t + 1 if t < CTX else t + 3


def emit_B(k, E, l):
    nc = k.nc
    pB = E.pB
    pp, cmat, wmask, banks, PS = E.pp, E.cmat, E.wmask, E.banks, E.PS
    k.phase = ExitStack()
    E.m4 = [k.sbuf([128, 4, 512], BF16, f"m4_{i}") for i in range(2)]
    wres = k.sbuf([128, NCH, NW], BF16, "wres")
    bones = k.sbuf([128, 128], BF16, "bones")
    QK = [k.sbuf([128, TALL], BF16, f"qk{i}") for i in range(4)]
    V = [k.sbuf([128, 34, 65], BF16, f"v{i}") for i in range(2)]
    ub = [k.sbuf([128, NCH, 512], BF16, f"ub{i}") for i in range(2)]
    cs = [[k.sbuf([128, 512], F32, f"cs{i}{j}") for j in range(2)] for i in range(2)]
    x32 = [k.sbuf([128, 512], F32, "x32_0")] * 2
    sqb = [k.sbuf([128, 512], BF16, "sqb0")] * 2
    rs = [k.sbuf([128, 512], F32, "rs0")] * 2
    xn = [k.sbuf([128, 512], F32, "xn0")] * 2
    t1 = [k.sbuf([128, 512], F32, "t1_0")] * 2
    t2 = [k.sbuf([128, 512], F32, "t2_0")] * 2
    stg = [k.sbuf([128, 512], F32, f"stg{i}") for i in range(3)]
    zero = k.sbuf([128, 16], F32, "zero")

    k.dma("sp", pp[:], E.ppd[l], writes=[pp.b])
    k.op("pool", lambda e: e.memset(zero[:], 0.0), writes=[zero.b])
    k.op("dve", lambda e: e.tensor_copy(out=bones[:], in_=cmat[:, 1, :]), reads=[cmat.b], writes=[bones.b])
    for i in range(2):
        k.op("pool", lambda e: e.memset(V[i][:, :, 64:65], 1.0), writes=[V[i].b])
    pbb = E.pbb
    for col in ((0, 257, 258, PBW - 1) if l == 0 else ()):
        k.dma("sp", pB.t.ap()[:, col:col + 1].rearrange("(c p) o -> p c o", p=128), zero[:, 0:11].rearrange("p (c o) -> p c o", o=1),
              reads=[zero.b], writes=[pbb], allow_slow_non_contiguous=True)
    wv = E.wind[l].rearrange("(c p) n -> p c n", p=128)
    for i in range(4):
        k.dma("pool", wres[:, :, i * 512:(i + 1) * 512], wv[:, :, i * 512:(i + 1) * 512], writes=[wres.sb(i)])
    ident = cmat

    ecnt = [0]
    bk = [0]

    def nbank():
        b = banks[bk[0] % 4]
        bk[0] += 1
        return b

    for bi, (t0, nb) in enumerate(MBLK):
        u = ub[bi % 2]
        if t0 < CTX:
            srcs = [(rr, 1024, 64, rr * 64) for rr in range(4)]
        else:
            srcs = [((t0 - CTX) // 1024, (t0 - CTX) % 1024, nb, 0)]
        for pc, (c0_, n_) in enumerate(UPIECES):
            ua = E.u_all[pc].t.ap().rearrange("(r c p) t -> r p c t", r=4, p=128)
            for rr, scol, cnt, dcol in srcs:
                k.dma("sp", u[:, c0_:c0_ + n_, dcol:dcol + cnt], ua[rr][:, :, scol:scol + cnt], reads=[E.u_all[pc].b], writes=[u.b])
        cost, sint = cs[bi % 2]
        k.dma("sp", cost[:, 0:nb], E.cosd[:, t0:t0 + nb], writes=[cost.b])
        k.dma("sp", sint[:, 0:nb], E.sind[:, t0:t0 + nb], writes=[sint.b])
        for ci, (c0, M) in enumerate(CHUNKS):
            ps = nbank()
            for kc in range(NCH):
                k.op("pe", lambda e: e.matmul(out=ps[0:M, 0:nb], lhsT=wres[:, kc, c0:c0 + M], rhs=u[:, kc, 0:nb],
                                              start=(kc == 0), stop=(kc == NCH - 1)),
                     reads=[wres.sb(c0 // 512), wres.sb((c0 + M - 1) // 512), u.b], writes=[ps.b])
            if ci < 4:
                i2 = ecnt[0] % 2
                ecnt[0] += 1
                xx, sq, rr, xnn, a1, a2 = x32[i2], sqb[i2], rs[i2], xn[i2], t1[i2], t2[i2]
                k.op("act", lambda e: e.activation(out=sq[:, 0:nb], in_=ps[:, 0:nb], func=AF.Square), reads=[ps.b], writes=[sq.b])
                k.op("dve", lambda e: e.tensor_scalar(out=xx[:, 0:nb], in0=ps[:, 0:nb], scalar1=pp[:, PP_QKG + ci:PP_QKG + ci + 1],
                                                       scalar2=None, op0=ALU.mult), reads=[ps.b, pp.b], writes=[xx.b])
                ps2 = nbank()
                k.op("pe", lambda e: e.matmul(out=ps2[:, 0:nb], lhsT=bones[:, :], rhs=sq[:, 0:nb], start=True, stop=True),
                     reads=[bones.b, sq.b], writes=[ps2.b])
                emit_rsqrt(k, rr, ps2, 1.0 / 64, NORM_EPS, nb)
                k.op("dve", lambda e: e.tensor_tensor(out=xnn[:, 0:nb], in0=xx[:, 0:nb], in1=rr[:, 0:nb], op=ALU.mult),
                     reads=[xx.b, rr.b], writes=[xnn.b])
                ps3 = nbank()
                k.op("pe", lambda e: e.matmul(out=ps3[:, 0:nb], lhsT=cmat[:, 2, :], rhs=xnn[:, 0:nb], start=True, stop=True),
                     reads=[cmat.b, xnn.b], writes=[ps3.b])
                k.op("pool", lambda e: e.tensor_tensor(out=a1[:, 0:nb], in0=xnn[:, 0:nb], in1=cost[:, 0:nb], op=ALU.mult),
                     reads=[xnn.b, cost.b], writes=[a1.b])
                k.op("dve", lambda e: e.tensor_tensor(out=a2[:, 0:nb], in0=ps3[:, 0:nb], in1=sint[:, 0:nb], op=ALU.mult),
                     reads=[ps3.b, sint.b], writes=[a2.b])
                k.op("pool", lambda e: e.tensor_tensor(out=QK[ci][:, t0:t0 + nb], in0=a1[:, 0:nb], in1=a2[:, 0:nb], op=ALU.add),
                     reads=[a1.b, a2.b], writes=[QK[ci].sb(bi)])
            elif ci == 4:
                i2 = ecnt[0] % 2
                ecnt[0] += 1
                xx = x32[i2]
                k.op("act", lambda e: e.activation(out=xx[:, 0:nb], in_=ps[:, 0:nb], func=AF.Copy), reads=[ps.b], writes=[xx.b])
                for tt in range(nb // 128):
                    pt = nbank()
                    k.op("pe", lambda e: e.transpose(out=pt[:, 0:128], in_=xx[:, tt * 128:(tt + 1) * 128], identity=cmat[:, 0, :]),
                         reads=[xx.b, cmat.b], writes=[pt.b])
                    tile = t0 // 128 + tt
                    k.op("act", lambda e: e.activation(out=V[0][:, tile, 0:64], in_=pt[:, 0:64], func=AF.Copy),
                         reads=[pt.b], writes=[V[0].sb(tile)])
                    k.op("dve", lambda e: e.tensor_copy(out=V[1][:, tile, 0:64], in_=pt[:, 64:128]),
                         reads=[pt.b], writes=[V[1].sb(tile)])
            else:
                s = stg[ecnt[0] % 3]
                ecnt[0] += 1
                if ecnt[0] % 2:
                    k.op("act", lambda e: e.activation(out=s[0:M, 0:nb], in_=ps[0:M, 0:nb], func=AF.Copy), reads=[ps.b], writes=[s.b])
                else:
                    k.op("dve", lambda e: e.tensor_copy(out=s[0:M, 0:nb], in_=ps[0:M, 0:nb]), reads=[ps.b], writes=[s.b])
                r0 = PB_ROWOFF[ci - 5]
                k.dma("sp", pB.t.ap()[r0:r0 + M, pb_col(t0):pb_col(t0) + nb], s[0:M, 0:nb], reads=[s.b], writes=[pbb])

    pT = [k.sbuf([128, 512], BF16, f"pT{i}") for i in range(3)]
    oraw = [k.sbuf([65, 512], F32, f"oraw{i}") for i in range(2)]
    rrow = [k.sbuf([65, 512], F32, f"rrow{i}") for i in range(2)]
    ostg = [k.sbuf([64, 512], BF16, f"ostg{i}") for i in range(2)]
    onesf = k.sbuf([65, 64], F32, "onesf")
    sinkx = k.sbuf([1, 512], F32, "sinkx")
    sinkb = k.sbuf([1, 512], BF16, "sinkb")
    e64 = k.sbuf([1, 65], BF16, "e64")
    k.op("pool", lambda e: e.memset(onesf[:], 1.0), writes=[onesf.b])
    k.op("pool", lambda e: e.memset(e64[:], 0.0), writes=[e64.b])
    k.op("pool", lambda e: e.memset(e64[0:1, 64:65], 1.0), writes=[e64.b])
    k.dma("sp", sinkx[:], E.sinkd[l], writes=[sinkx.b])
    k.op("act", lambda e: e.activation(out=sinkb[:], in_=sinkx[:], func=AF.Exp), reads=[sinkx.b], writes=[sinkb.b])
    ps_s = [(0, 1), (2, 3), (6, 7)]
    ps_o = [banks[4], banks[5]]
    ps_b = [banks[6], banks[7]]
    pcnt = [0]
    qcnt = [0]
    allq = [QK[i].sb(bi) for i in range(4) for bi in range(len(MBLK))]
    allv = [V[i].sb(t) for i in range(2) for t in range(34)]

    def attend(Q, Kd, Vt, q0, ktiles, sink, orow0):
        io = qcnt[0] % 2
        qcnt[0] += 1
        po = ps_o[io]
        n = len(ktiles)
        slots = []

        def qk(ii):
            kt, mi = ktiles[ii]
            b0, b1 = ps_s[pcnt[0] % 3]
            p = pT[pcnt[0] % 3]
            pcnt[0] += 1
            slots.append((b0, b1, p))
            for h in range(2):
                bh = banks[(b0, b1)[h]]
                k.op("pe", lambda e: e.matmul(out=bh[:, 0:256], lhsT=Kd[h * 64:(h + 1) * 64, kt * 128:(kt + 1) * 128],
                                              rhs=Q[h * 64:(h + 1) * 64, q0:q0 + 256], start=True, stop=True),
                     reads=allq, writes=[bh.b])

        def rest(ii):
            kt, mi = ktiles[ii]
            b0, b1, p = slots[ii]
            k.op("act", lambda e: e.activation(out=p[:, :].rearrange("p (h q) -> p h q", h=2), in_=PS.t[:, b0:b1 + 1, 0:256],
                                               func=AF.Exp, scale=ATT_SCALE),
                 reads=[banks[b0].b, banks[b1].b], writes=[p.b])
            if mi is not None:
                k.op("dve", lambda e: e.tensor_tensor(out=p[:], in0=p[:], in1=wmask[:, mi, :], op=ALU.mult),
                     reads=[p.b, wmask.b], writes=[p.b])
            k.op("pe", lambda e: e.matmul(out=po[0:65, :], lhsT=Vt[:, kt, 0:65], rhs=p[:], start=(ii == 0),
                                          stop=(ii == n - 1 and not sink)), reads=allv + [p.b], writes=[po.b])

        LA = 2
        for ii in range(min(LA, n)):
            qk(ii)
        for ii in range(n):
            if ii + LA < n:
                qk(ii + LA)
            rest(ii)
        if sink:
            k.op("pe", lambda e: e.matmul(out=po[0:65, :], lhsT=e64[0:1, :], rhs=sinkb[0:1, :], start=False, stop=True),
                 reads=[e64.b, sinkb.b], writes=[po.b])
        orw, rr, os_ = oraw[io], rrow[io], ostg[io]
        k.op("act", lambda e: e.activation(out=orw[0:65, :], in_=po[0:65, :], func=AF.Copy), reads=[po.b], writes=[orw.b])
        k.op("dve", lambda e: e.reciprocal(out=rr[64:65, :], in_=orw[64:65, :]), reads=[orw.b], writes=[rr.b])
        pb_ = ps_b[io]
        k.op("pe", lambda e: e.matmul(out=pb_[0:64, :], lhsT=onesf[64:65, 0:64], rhs=rr[64:65, :], start=True, stop=True),
             reads=[onesf.b, rr.b], writes=[pb_.b])
        k.op("dve", lambda e: e.tensor_tensor(out=os_[0:64, :], in0=orw[0:64, :], in1=pb_[0:64, :], op=ALU.mult),
             reads=[orw.b, pb_.b], writes=[os_.b])
        out_write(k, E, os_, 0, 64, 512, [(orow0, 0), (orow0 + 64, 256)], q0, 256)

    attend(QK[0], QK[1], V[0], 0, [(0, None), (1, None)], True, 0)
    attend(QK[2], QK[3], V[1], 0, [(0, None), (1, None)], False, 384)
    for qb in range(16):
        kts = []
        for r in range(4):
            lt = 2 * qb - 1 + r
            if 0 <= lt < 32:
                kts.append((2 + lt, r))
        kts += [(0, None), (1, None)]
        attend(QK[0], QK[1], V[0], 256 + qb * 256, kts, True, 0)
    for qb in range(16):
        attend(QK[2], QK[3], V[1], 256 + qb * 256, [(t, None) for t in range(34)], False, 384)
    end_phase(k)
    k.phase = ExitStack()
    E.m4 = [k.sbuf([128, 4, 512], BF16, f"m4_{i}") for i in range(2)]
    emit_rwkv(k, E, l, pB, pbb, pp, cmat, banks)
    end_phase(k)


RW_SCALE = -float(np.exp(np.float32(-0.5)))


def emit_rwkv(k, E, l, pB, pbb, pp, cmat, banks):
    sb = k.sbuf
    w0row = sb([1, 512], F32, "w0row")
    wup = sb([96, 512], F32, "wup")
    aup = sb([96, 512], F32, "aup")
    gup = sb([128, 2, 256], F32, "gup")
    rmask, tri = E.rmask, E.tri
    ones_row = sb([1, 128], F32, "ones_row")
    for t_, d_ in ((w0row, E.w0d[l]), (wup, E.wupd[l]), (aup, E.aupd[l]), (gup, E.gupd[l])):
        k.dma("sp", t_[:], d_, writes=[t_.b])
    k.op("pool", lambda e: e.memset(ones_row[:], 1.0), writes=[ones_row.b])
    c0 = sb([128, 12], F32, "c0")
    c1 = sb([128, 2], F32, "c1")
    c2 = sb([128, 2], F32, "c2")
    muv = pp[:, PP_MU:PP_MU + 24].rearrange("p (c two) -> p c two", two=2)
    k.op("dve", lambda e: e.tensor_tensor(out=c0[:], in0=muv[:, :, 0], in1=muv[:, :, 1], op=ALU.add), reads=[pp.b], writes=[c0.b])
    k.op("dve", lambda e: e.tensor_scalar(out=c0[:], in0=c0[:], scalar1=-1.0, scalar2=1.0, op0=ALU.mult, op1=ALU.add),
         reads=[c0.b], writes=[c0.b])
    k.op("dve", lambda e: e.tensor_scalar(out=c1[:], in0=pp[:, PP_KA:PP_KA + 2], scalar1=-1.0, scalar2=1.0, op0=ALU.mult, op1=ALU.add),
         reads=[pp.b], writes=[c1.b])
    k.op("dve", lambda e: e.tensor_scalar(out=c2[:], in0=c1[:], scalar1=2.0, scalar2=None, op0=ALU.mult), reads=[c1.b], writes=[c2.b])
    MSK = {0: dict(Ms=0, MsT=1, MiT=2, nMiT=4), 1: dict(Ms=1, MsT=0, MiT=3, nMiT=5)}
    NB = 512
    RD = F32
    F32R = mybir.dt.float32r

    def rv(ap):
        return ap.bitcast(F32R)

    WIDTH = 4
    STAGGER = 3
    identb = sb([128, 128], RD, "identb")
    k.op("dve", lambda e: e.tensor_copy(out=rv(identb[:]), in_=cmat[:, 0, :]), reads=[cmat.b], writes=[identb.b])
    raw = [sb([128, NB + 2], F32, f"raw{i}") for i in range(4)]
    rawc = [0]
    S = {n: sb([128, NB], F32, "S_" + n) for n in ("r", "k", "v", "wd", "ad", "ad2", "g0", "g1", "sqk", "rinv", "kap", "a", "a2", "tw",
                                                     "tmp", "kd", "bb", "eL", "eLn", "eLx", "ysum", "u1", "u2", "u3")}
    lw = sb([64, 8, 128], F32, "lw")
    bd = {n: [sb([128, 8, 2, 64], RD, f"bd_{n}{i}") for i in range(2)] for n in ("kap", "bt", "kt", "rt", "v")}
    zbig = S["u1"]
    k.op("pool", lambda e: e.memset(zbig[:], 0.0), writes=[zbig.b])
    for n in bd:
        for i in range(2):
            for hh in range(2):
                k.op("dve", lambda e: e.tensor_copy(out=rv(bd[n][i][:, :, hh, :]), in_=zbig[:, :].rearrange("p (c s) -> p c s", s=64)),
                     reads=[zbig.b], writes=[bd[n][i].b])
    yf = sb([128, TALL], F32, "yf")
    Mst = [sb([128, 128], F32, f"Mst{i}") for i in range(2)]
    U = {}

    def ut(name):
        if name not in U:
            dt_ = F32 if name in ("ET", "Hg", "YmT") else RD
            U[name] = [sb([128, 256 if name.startswith("X") else 128], dt_, f"U_{name}{i}") for i in range(WIDTH)]
        return U[name]

    ostage = [sb([128, NB], BF16, f"ostage{i}") for i in range(2)]
    bkc = [0]

    def rb():
        b = banks[bkc[0] % 8]
        bkc[0] += 1
        return b

    ecount = [0]

    def ew():
        ecount[0] += 1
        return "dve" if ecount[0] % 2 else "pool"

    dg = sb([128, 12, 3, 128], F32, "shift_diag")
    for bc_ in range(12):
        for j_, coef in enumerate((c0[:, bc_:bc_ + 1], pp[:, PP_MU + 2 * bc_:PP_MU + 2 * bc_ + 1], pp[:, PP_MU + 2 * bc_ + 1:PP_MU + 2 * bc_ + 2])):
            k.op("pool", lambda e: e.tensor_scalar(out=dg[:, bc_, j_, :], in0=cmat[:, 0, :], scalar1=coef, scalar2=None, op0=ALU.mult),
                 reads=[cmat.b, c0.b, pp.b], writes=[dg.b])

    def load_shift(bc, M, dst, t0, nb):
        rw = raw[rawc[0] % 4]
        rawc[0] += 1
        r0 = PB_ROWOFF[bc]
        k.dma("sp", rw[0:M, 0:nb + 2], pB.t.ap()[r0:r0 + M, pb_col(t0) - 1:pb_col(t0) + nb + 1], reads=[pbb], writes=[rw.b])
        ps = rb()
        for j_, off in enumerate((1, 0, 2)):
            k.op("pe", lambda e: e.matmul(out=ps[0:M, 0:nb], lhsT=dg[0:M, bc, j_, 0:M], rhs=rw[0:M, off:off + nb],
                                          start=(j_ == 0), stop=(j_ == 2)), reads=[dg.b, rw.b], writes=[ps.b])
        k.op("act", lambda e: e.activation(out=dst[0:M, 0:nb], in_=ps[0:M, 0:nb], func=AF.Copy), reads=[ps.b], writes=[dst.b])

    def lora_a(dst, src, pr, d, nb):
        ps = rb()
        k.op("pe", lambda e: e.matmul(out=ps[:, 0:nb], lhsT=aup[0:96, d * 256 + pr * 128:d * 256 + (pr + 1) * 128], rhs=src[0:96, 0:nb],
                                      start=True, stop=True), reads=[aup.b, src.b], writes=[ps.b])
        k.op("act", lambda e: e.activation(out=dst[:, 0:nb], in_=ps[:, 0:nb], func=AF.Sigmoid,
                                           bias=pp[:, PP_A0 + pr * 2 + d:PP_A0 + pr * 2 + d + 1], scale=1.0),
             reads=[ps.b, pp.b], writes=[dst.b])

    def prep(pr, t0, nb, d, final, bdi):
        nch = nb // 64
        load_shift(0 + pr, 128, S["r"], t0, nb)
        load_shift(2 + pr, 128, S["k"], t0, nb)
        load_shift(4 + pr, 128, S["v"], t0, nb)
        load_shift(6 + d, 96, S["wd"], t0, nb)
        load_shift(8 + d, 96, S["ad"], t0, nb)
        if final:
            load_shift(8 + (1 - d), 96, S["ad2"], t0, nb)
            load_shift(10, 128, S["g0"], t0, nb)
            load_shift(11, 128, S["g1"], t0, nb)
        kS, rS, vS = S["k"], S["r"], S["v"]
        k.op("act", lambda e: e.activation(out=S["sqk"][:, 0:nb], in_=kS[:, 0:nb], func=AF.Square, scale=pp[:, PP_KK + pr:PP_KK + pr + 1]),
             reads=[kS.b, pp.b], writes=[S["sqk"].b])
        ps = rb()
        k.op("pe", lambda e: e.matmul(out=ps[:, 0:nb], lhsT=cmat[:, 1, :], rhs=S["sqk"][:, 0:nb], start=True, stop=True),
             reads=[cmat.b, S["sqk"].b], writes=[ps.b])
        k.op("act", lambda e: e.activation(out=S["rinv"][:, 0:nb], in_=ps[:, 0:nb], func=AF.Sqrt), reads=[ps.b], writes=[S["rinv"].b])
        k.op("dve", lambda e: e.tensor_scalar(out=S["rinv"][:, 0:nb], in0=S["rinv"][:, 0:nb], scalar1=1e-12, scalar2=None, op0=ALU.max),
             reads=[S["rinv"].b], writes=[S["rinv"].b])
        k.op("dve", lambda e: e.reciprocal(out=S["rinv"][:, 0:nb], in_=S["rinv"][:, 0:nb]), reads=[S["rinv"].b], writes=[S["rinv"].b])
        k.op("dve", lambda e: e.scalar_tensor_tensor(out=S["kap"][:, 0:nb], in0=kS[:, 0:nb], scalar=pp[:, PP_KK + pr:PP_KK + pr + 1],
                                                      in1=S["rinv"][:, 0:nb], op0=ALU.mult, op1=ALU.mult),
             reads=[kS.b, pp.b, S["rinv"].b], writes=[S["kap"].b])
        lora_a(S["a"], S["ad"], pr, d, nb)
        if final:
            lora_a(S["a2"], S["ad2"], pr, 1 - d, nb)
        k.op("act", lambda e: e.activation(out=S["tw"][0:96, 0:nb], in_=S["wd"][0:96, 0:nb], func=AF.Tanh), reads=[S["wd"].b], writes=[S["tw"].b])
        pl = [rb(), rb()]
        wsl = slice(d * 256 + pr * 128, d * 256 + (pr + 1) * 128)
        for c in range(nch):
            pb_ = pl[c // 4]
            k.op("pe", lambda e: e.matmul(out=pb_[0:64, (c % 4) * 128:(c % 4 + 1) * 128], lhsT=S["tw"][0:96, c * 64:(c + 1) * 64],
                                          rhs=wup[0:96, wsl], start=True, stop=False), reads=[S["tw"].b, wup.b], writes=[pb_.b])
            k.op("pe", lambda e: e.matmul(out=pb_[0:64, (c % 4) * 128:(c % 4 + 1) * 128], lhsT=ones_row[0:1, 0:64],
                                          rhs=w0row[0:1, wsl], start=False, stop=True), reads=[ones_row.b, w0row.b], writes=[pb_.b])
        for hb in range((nch + 3) // 4):
            n4 = min(4, nch - hb * 4)
            k.op("act", lambda e: e.activation(out=lw[0:64, hb * 4:hb * 4 + n4, :], in_=pl[hb][0:64, 0:n4 * 128].rearrange("p (c n) -> p c n", n=128),
                                               func=AF.Sigmoid), reads=[pl[hb].b], writes=[lw.b])
        pL, pLx = rb(), rb()
        for c in range(nch):
            k.op("pe", lambda e: e.matmul(out=pL[:, c * 64:(c + 1) * 64], lhsT=lw[0:64, c, :], rhs=tri[0:64, 2 * d, :], start=True, stop=True),
                 reads=[lw.b, tri.b], writes=[pL.b])
        for c in range(nch):
            k.op("pe", lambda e: e.matmul(out=pLx[:, c * 64:(c + 1) * 64], lhsT=lw[0:64, c, :], rhs=tri[0:64, 2 * d + 1, :], start=True, stop=True),
                 reads=[lw.b, tri.b], writes=[pLx.b])
        k.op("act", lambda e: e.activation(out=S["eL"][:, 0:nb], in_=pL[:, 0:nb], func=AF.Exp), reads=[pL.b], writes=[S["eL"].b])
        k.op("act", lambda e: e.activation(out=S["eLn"][:, 0:nb], in_=pL[:, 0:nb], func=AF.Exp, scale=-1.0), reads=[pL.b], writes=[S["eLn"].b])
        k.op("act", lambda e: e.activation(out=S["eLx"][:, 0:nb], in_=pLx[:, 0:nb], func=AF.Exp), reads=[pLx.b], writes=[S["eLx"].b])
        k.op("dve", lambda e: e.tensor_scalar(out=S["tmp"][:, 0:nb], in0=S["a"][:, 0:nb], scalar1=pp[:, PP_KA + pr:PP_KA + pr + 1],
                                               scalar2=c1[:, pr:pr + 1], op0=ALU.mult, op1=ALU.add),
             reads=[S["a"].b, pp.b, c1.b], writes=[S["tmp"].b])
        k.op("pool", lambda e: e.tensor_tensor(out=S["kd"][:, 0:nb], in0=kS[:, 0:nb], in1=S["tmp"][:, 0:nb], op=ALU.mult),
             reads=[kS.b, S["tmp"].b], writes=[S["kd"].b])
        k.op("pool", lambda e: e.tensor_tensor(out=S["bb"][:, 0:nb], in0=S["kap"][:, 0:nb], in1=S["a"][:, 0:nb], op=ALU.mult),
             reads=[S["kap"].b, S["a"].b], writes=[S["bb"].b])
        for name, x, y in (("kap", S["kap"], S["eLx"]), ("bt", S["bb"], S["eLn"]), ("kt", S["kd"], S["eLn"]), ("rt", rS, S["eL"]),
                           ("v", vS, None)):
            dst = bd[name][bdi]
            for h in range(2):
                hs = slice(h * 64, (h + 1) * 64)
                eng = ew()
                xv = x[hs, 0:nb].rearrange("p (c s) -> p c s", s=64)
                if y is None:
                    k.op(eng, lambda e: e.tensor_copy(out=rv(dst[hs, 0:nch, h, :]), in_=xv), reads=[x.b], writes=[dst.b])
                else:
                    yv = y[hs, 0:nb].rearrange("p (c s) -> p c s", s=64)
                    k.op(eng, lambda e: e.tensor_tensor(out=rv(dst[hs, 0:nch, h, :]), in0=xv, in1=yv, op=ALU.mult),
                         reads=[x.b, y.b], writes=[dst.b])

    def unit(pr, tok0, c, d, bdi, useq, final):
        ui = useq % WIDTH
        mk = MSK[d]
        Kap = bd["kap"][bdi][:, c, :, :].rearrange("p a b -> p (a b)")
        Bt = bd["bt"][bdi][:, c, :, :].rearrange("p a b -> p (a b)")
        Kt = bd["kt"][bdi][:, c, :, :].rearrange("p a b -> p (a b)")
        Rt = bd["rt"][bdi][:, c, :, :].rearrange("p a b -> p (a b)")
        Vf = bd["v"][bdi][:, c, :, :].rearrange("p a b -> p (a b)")
        bdb = [bd[n][bdi].b for n in bd]
        gcol = c * 64 + (63 if d == 0 else 0)
        gam = S["eL"][:, gcol:gcol + 1]

        def T(n):
            return ut(n)[ui]

        def mm(lhsT, rhs, rd, N=128, acc=None, full=False):
            ps = rb() if acc is None else acc[0]
            st, sp_ = (True, True) if acc is None else (acc[1], acc[2])
            if not full:
                lhsT, rhs = rv(lhsT), rv(rhs)
            k.op("pe", lambda e: e.matmul(out=ps[:, 0:N], lhsT=lhsT, rhs=rhs, start=st, stop=sp_), reads=rd, writes=[ps.b])
            return ps

        def ev_copy(dst_t, dst_ap, ps, N=128, scale=None, full=False):
            if not full:
                dst_ap = rv(dst_ap)
            if scale is None:
                k.op("act", lambda e: e.activation(out=dst_ap, in_=ps[:, 0:N], func=AF.Copy), reads=[ps.b], writes=[dst_t.b])
            else:
                k.op("act", lambda e: e.activation(out=dst_ap, in_=ps[:, 0:N], func=AF.Copy, scale=scale), reads=[ps.b, S["eL"].b],
                     writes=[dst_t.b])

        def ev_tt(dst_t, dst_ap, in0_ap, in0_b, ps, op, N=128, psfirst=False, full=False):
            if not full:
                dst_ap = rv(dst_ap)
            if psfirst:
                k.op("dve", lambda e: e.tensor_tensor(out=dst_ap, in0=ps[:, 0:N], in1=in0_ap, op=op), reads=[ps.b] + in0_b, writes=[dst_t.b])
            else:
                k.op("dve", lambda e: e.tensor_tensor(out=dst_ap, in0=in0_ap, in1=ps[:, 0:N], op=op), reads=[ps.b] + in0_b, writes=[dst_t.b])

        X = [T("Xa"), T("Xb")]
        idb = [identb.b]
        ps = mm(Kap, identb[:, :], bdb + idb)
        ev_copy(X[0], X[0][:, 0:128], ps)
        yield
        ps = mm(Bt, identb[:, :], bdb + idb)
        ev_copy(T("nBtT"), T("nBtT")[:, :], ps, scale=-1.0)
        yield
        ps = mm(Kt, identb[:, :], bdb + idb)
        ev_copy(T("KtT"), T("KtT")[:, :], ps)
        yield
        ps = mm(Vf, identb[:, :], bdb + idb)
        ev_copy(T("VT"), T("VT")[:, :], ps)
        yield
        for nm, l_, r_, m_ in (("N", Kap, Bt, "Ms"), ("Z", Bt, Kap, "MsT"), ("AkkT", Kt, Kap, "MsT"), ("ArkT", Kt, Rt, "MiT"),
                               ("nArbT", Bt, Rt, "nMiT")):
            ps = mm(l_, r_, bdb)
            ev_tt(T(nm), T(nm)[:, :], rmask[:, mk[m_], :], [rmask.b], ps, ALU.mult, psfirst=True)
            yield
        ps = mm(T("AkkT")[:, :], T("VT")[:, :], [T("AkkT").b, T("VT").b])
        ev_copy(X[0], X[0][:, 128:256], ps)
        yield
        Zc, Nc = T("Z"), T("N")
        Zalt, Nalt = T("Zalt"), T("Nalt")
        ps = mm(Zc[:, :], X[0][:, :], [Zc.b, X[0].b], N=256)
        ev_tt(X[1], X[1][:, :], X[0][:, :], [X[0].b], ps, ALU.subtract, N=256)
        yield
        xi = 1
        for lev in range(5):
            Zn = Zalt
            ps = mm(Nc[:, :], Zc[:, :], [Nc.b, Zc.b])
            ev_copy(Zn, Zn[:, :], ps)
            yield
            if lev < 4:
                Nn = Nalt
                ps = mm(Zc[:, :], Nc[:, :], [Nc.b, Zc.b])
                ev_copy(Nn, Nn[:, :], ps)
                Nc, Nalt = Nn, Nc
                yield
            Zc, Zalt = Zn, Zc
            ps = mm(Zc[:, :], X[xi][:, :], [Zc.b, X[xi].b], N=256)
            ev_tt(X[1 - xi], X[1 - xi][:, :], X[xi][:, :], [X[xi].b], ps, ALU.add, N=256)
            xi = 1 - xi
            yield
        Xf = X[xi]
        P = Xf[:, 0:128]
        Q = Xf[:, 128:256]
        ps = mm(P, T("nBtT")[:, :], [Xf.b, T("nBtT").b])
        ev_tt(T("ET"), T("ET")[:, :], cmat[:, 0, :], [cmat.b], ps, ALU.add, full=True)
        yield
        ph = rb()
        mm(T("KtT")[:, :], T("VT")[:, :], [T("KtT").b, T("VT").b], acc=(ph, True, False))
        mm(T("nBtT")[:, :], Q, [T("nBtT").b, Xf.b], acc=(ph, False, True))
        ev_copy(T("Hg"), T("Hg")[:, :], ph, scale=gam, full=True)
        yield
        ps = mm(P, T("nArbT")[:, :], [Xf.b, T("nArbT").b])
        ev_tt(T("YmT"), T("YmT")[:, :], Rt, bdb, ps, ALU.add, full=True)
        yield
        M0 = Mst[useq % 2]
        M1 = Mst[(useq + 1) % 2]
        py = rb()
        mm(M0[:, :], T("YmT")[:, :], [M0.b, T("YmT").b], acc=(py, True, False), full=True)
        mm(T("VT")[:, :], T("ArkT")[:, :], [T("VT").b, T("ArkT").b], acc=(py, False, False))
        mm(Q, T("nArbT")[:, :], [Xf.b, T("nArbT").b], acc=(py, False, True))
        for h in range(2):
            hs = slice(h * 64, (h + 1) * 64)
            if not final:
                k.op("act", lambda e: e.activation(out=yf[hs, tok0:tok0 + 64], in_=py[hs, h * 64:(h + 1) * 64], func=AF.Copy),
                     reads=[py.b], writes=[yf.sb(tok0 // 512)])
            else:
                lc = (tok0 - (0 if tok0 < CTX else CTX)) % 512
                k.op("dve", lambda e: e.tensor_tensor(out=S["ysum"][hs, lc:lc + 64], in0=py[hs, h * 64:(h + 1) * 64],
                                                       in1=yf[hs, tok0:tok0 + 64], op=ALU.add),
                     reads=[py.b, yf.sb(tok0 // 512)], writes=[S["ysum"].b])
        pm = mm(T("ET")[:, :], M0[:, :], [T("ET").b, M0.b], full=True)
        k.op("dve", lambda e: e.scalar_tensor_tensor(out=M1[:, :], in0=pm[:, 0:128], scalar=gam, in1=T("Hg")[:, :], op0=ALU.mult, op1=ALU.add),
             reads=[pm.b, S["eL"].b, T("Hg").b], writes=[M1.b])
        yield

    def run_lockstep(gens):
        pending = list(gens)
        active = []
        rounds = 0
        while pending or active:
            if pending and len(active) < WIDTH and (not active or rounds % STAGGER == 0):
                active.append(pending.pop(0))
            for g in list(active):
                try:
                    next(g)
                except StopIteration:
                    active.remove(g)
            rounds += 1

    def finalize(pr, t0, nb):
        ys = S["ysum"]
        ps = rb()
        k.op("pe", lambda e: e.matmul(out=ps[:, 0:nb], lhsT=cmat[:, 1, :], rhs=ys[:, 0:nb], start=True, stop=True),
             reads=[cmat.b, ys.b], writes=[ps.b])
        k.op("dve", lambda e: e.scalar_tensor_tensor(out=S["u1"][:, 0:nb], in0=ps[:, 0:nb], scalar=-1.0 / 64, in1=ys[:, 0:nb],
                                                      op0=ALU.mult, op1=ALU.add), reads=[ps.b, ys.b], writes=[S["u1"].b])
        k.op("act", lambda e: e.activation(out=S["u2"][:, 0:nb], in_=S["u1"][:, 0:nb], func=AF.Square), reads=[S["u1"].b], writes=[S["u2"].b])
        ps = rb()
        k.op("pe", lambda e: e.matmul(out=ps[:, 0:nb], lhsT=cmat[:, 1, :], rhs=S["u2"][:, 0:nb], start=True, stop=True),
             reads=[cmat.b, S["u2"].b], writes=[ps.b])
        emit_rsqrt(k, S["u3"], ps, 1.0 / 64, GN_EPS, nb)
        k.op("dve", lambda e: e.tensor_tensor(out=S["u1"][:, 0:nb], in0=S["u1"][:, 0:nb], in1=S["u3"][:, 0:nb], op=ALU.mult),
             reads=[S["u1"].b, S["u3"].b], writes=[S["u1"].b])
        k.op("act", lambda e: e.activation(out=S["u1"][:, 0:nb], in_=S["u1"][:, 0:nb], func=AF.Identity,
                                           bias=pp[:, PP_GNB + pr:PP_GNB + pr + 1], scale=pp[:, PP_GNG + pr:PP_GNG + pr + 1]),
             reads=[S["u1"].b, pp.b], writes=[S["u1"].b])
        k.op("pool", lambda e: e.tensor_tensor(out=S["u2"][:, 0:nb], in0=S["a"][:, 0:nb], in1=S["a2"][:, 0:nb], op=ALU.add),
             reads=[S["a"].b, S["a2"].b], writes=[S["u2"].b])
        k.op("dve", lambda e: e.tensor_scalar(out=S["u2"][:, 0:nb], in0=S["u2"][:, 0:nb], scalar1=pp[:, PP_KA + pr:PP_KA + pr + 1],
                                               scalar2=c2[:, pr:pr + 1], op0=ALU.mult, op1=ALU.add),
             reads=[S["u2"].b, pp.b, c2.b], writes=[S["u2"].b])
        k.op("pool", lambda e: e.tensor_tensor(out=S["u2"][:, 0:nb], in0=S["u2"][:, 0:nb], in1=S["k"][:, 0:nb], op=ALU.mult),
             reads=[S["u2"].b, S["k"].b], writes=[S["u2"].b])
        k.op("dve", lambda e: e.scalar_tensor_tensor(out=S["u2"][:, 0:nb], in0=S["r"][:, 0:nb], scalar=pp[:, PP_RK + pr:PP_RK + pr + 1],
                                                      in1=S["u2"][:, 0:nb], op0=ALU.mult, op1=ALU.mult),
             reads=[S["r"].b, pp.b, S["u2"].b], writes=[S["u2"].b])
        ps = rb()
        k.op("pe", lambda e: e.matmul(out=ps[:, 0:nb], lhsT=cmat[:, 1, :], rhs=S["u2"][:, 0:nb], start=True, stop=True),
             reads=[cmat.b, S["u2"].b], writes=[ps.b])
        k.op("dve", lambda e: e.tensor_tensor(out=S["u3"][:, 0:nb], in0=ps[:, 0:nb], in1=S["v"][:, 0:nb], op=ALU.mult),
             reads=[ps.b, S["v"].b], writes=[S["u3"].b])
        k.op("pool", lambda e: e.tensor_tensor(out=S["u1"][:, 0:nb], in0=S["u1"][:, 0:nb], in1=S["u3"][:, 0:nb], op=ALU.add),
             reads=[S["u1"].b, S["u3"].b], writes=[S["u1"].b])
        for gi, gn_ in enumerate(("g0", "g1")):
            k.op("act", lambda e: e.activation(out=S[gn_][:, 0:nb], in_=S[gn_][:, 0:nb], func=AF.Sigmoid), reads=[S[gn_].b], writes=[S[gn_].b])
        ps = rb()
        for gi, gn_ in enumerate(("g0", "g1")):
            k.op("pe", lambda e: e.matmul(out=ps[:, 0:nb], lhsT=gup[:, gi, pr * 128:(pr + 1) * 128], rhs=S[gn_][:, 0:nb],
                                          start=(gi == 0), stop=(gi == 1)), reads=[gup.b, S[gn_].b], writes=[ps.b])
        os_ = ostage[ecount[0] % 2]
        ecount[0] += 1
        k.op("dve", lambda e: e.tensor_tensor(out=os_[:, 0:nb], in0=S["u1"][:, 0:nb], in1=ps[:, 0:nb], op=ALU.mult),
             reads=[S["u1"].b, ps.b], writes=[os_.b])
        out_write(k, E, os_, 0, 128, nb, [(128 + pr * 128, 0)], t0, nb)

    blocks_f = MBLK
    blocks_b = [MBLK[0]] + MBLK[:0:-1]
    bdc = 0
    for pr in range(2):
        for d in range(2):
            useq = 0
            k.op("pool", lambda e: e.memset(Mst[0][:], 0.0), writes=[Mst[0].b])
            for (t0, nb) in (blocks_f if d == 0 else blocks_b):
                prep(pr, t0, nb, d, d == 1, bdc % 2)
                nch = nb // 64
                order = list(range(nch)) if d == 0 else list(range(nch - 1, -1, -1))
                run_lockstep([unit(pr, t0 + c * 64, c, d, bdc % 2, useq + i, d == 1) for i, c in enumerate(order)])
                useq += nch
                if d == 1:
                    finalize(pr, t0, nb)
                bdc += 1


def build_F():
    k = K()
    nc = k.nc
    E = Env()

    def din(name, shape, dt=F32):
        return nc.dram_tensor(name, list(shape), dt, kind="ExternalInput").ap()

    E.hT0d = din("hT0", [D, NT])
    E.ccTd = din("ccT", [D, 2])
    E.awqd = din("awq", [D, 96 * 128])
    E.abqd = din("abq", [128, 96])
    E.indd = din("ind", [128, 4])
    E.gn1d = din("gn1T", [128, DEPTH * NCH])
    E.gn2d = din("gn2T", [128, DEPTH * NCH])
    E.wind = din("win", [DEPTH, D, NW])
    E.ppd = din("pp", [DEPTH, 128, NPP])
    E.sinkd = din("sinkrow", [DEPTH, 1, 512])
    E.w0d = din("w0row", [DEPTH, 1, 512])
    E.wupd = din("wup", [DEPTH, 96, 512])
    E.aupd = din("aup", [DEPTH, 96, 512])
    E.gupd = din("gup", [DEPTH, 128, 2, 256])
    E.cosd = din("cosT", [128, TALL])
    E.sind = din("sinT", [128, TALL])
    E.cmatd = din("cmat", [128, 3, 128])
    E.wmaskd = din("wmask", [128, 4, 512], BF16)
    E.rmaskd = din("rmask", [128, 6, 128])
    E.trid = din("tri", [64, 4, 64])
    E.woutd = din("wout", [DEPTH, D, D])
    E.w1d = din("w1", [DEPTH, D, 4 * D])
    E.w2d = din("w2", [DEPTH, 4 * D, D])
    E.hOd = nc.dram_tensor("hTo", [D, NT], F32, kind="ExternalOutput").ap()
    E.mod_in = k.dram("mod_in", [128, 192], F32)
    E.mod_all = k.dram("mod_all", [512, 192], F32)
    E.u_in = [k.dram(f"u_in{i}", [n * 128, NT], BF16) for i, (c0, n) in enumerate(UPIECES)]
    E.u_all = [k.dram(f"u_all{i}", [4 * n * 128, NT], BF16) for i, (c0, n) in enumerate(UPIECES)]
    E.rs_in = k.dram("rs_in", [4 * D, NT], BF16)
    E.rs_out = k.dram("rs_out", [D, NT], BF16)
    E.h_spill = k.dram("h_spill", [D, NT], F32)
    E.pB = k.dram("pB", [PB_ROWS, PBW], F32)
    E.pbb = E.pB.b
    E.cst = emit_consts(k)
    E.modS = k.sbuf([128, 4, 96, 2], F32, "modS")
    E.mods = Mods(E.modS)
    E.ind = k.sbuf([128, 4], F32, "ind")
    E.gn1 = k.sbuf([128, DEPTH * NCH], F32, "gn1")
    E.gn2 = k.sbuf([128, DEPTH * NCH], F32, "gn2")
    E.pp = k.sbuf([128, NPP], F32, "pp")
    E.cmat = k.sbuf([128, 3, 128], F32, "cmat")
    E.wmask = k.sbuf([128, 4, 512], BF16, "wmask")
    E.rmask = k.sbuf([128, 6, 128], F32, "rmask")
    E.tri = k.sbuf([64, 4, 64], F32, "tri")
    E.m4c = [0]
    E.PS = k.psum([128, 8, 512], F32, "PSall")
    E.banks = [BankView(E.PS.t, i) for i in range(8)]
    emit_P0(k, E)
    emit_P1(k, E)
    for l in range(DEPTH):
        emit_B(k, E, l)
        k.collective("ReduceScatter", ALU.add, E.rs_in, E.rs_out, reads=[E.rs_in.b], writes=[E.rs_out.b])
        emit_C(k, E, l)
    return k.finish()


_PROG = {}


def _c(a):
    return np.ascontiguousarray(a)


def host_B_consts():
    ident = np.eye(128, dtype=np.float32)
    bo = np.zeros((128, 128), np.float32)
    bo[:64, :64] = 1
    bo[64:, 64:] = 1
    R = np.zeros((128, 128), np.float32)
    for m in range(128):
        if m % 64 < 32:
            R[m + 32, m] = -1.0
        else:
            R[m - 32, m] = 1.0
    cmat = _c(np.stack([ident, bo, R], 1))
    t = np.arange(SEQ)
    row = (t // 64).astype(np.float32)
    col = (t % 64).astype(np.float32)
    inv = (10000.0 ** (-np.arange(16, dtype=np.float32) / 16)).astype(np.float32)
    ang = np.concatenate([row[:, None] * inv, col[:, None] * inv], -1).astype(np.float32)
    cosT = np.ones((128, TALL), np.float32)
    sinT = np.zeros((128, TALL), np.float32)
    c = np.cos(ang).T.astype(np.float32)
    s = np.sin(ang).T.astype(np.float32)
    for p in range(128):
        cosT[p, CTX:] = c[p % 32]
        sinT[p, CTX:] = s[p % 32]
    wm = np.zeros((128, 4, 2, 256), np.float32)
    kl = np.arange(128)[:, None]
    ql = np.arange(256)[None, :]
    for r in range(4):
        d = ql - kl - (r - 1) * 128
        wm[:, r, :, :] = (np.abs(d) <= 128)[:, None, :]
    wmask = _c(wm.reshape(128, 4, 512).astype(ml_dtypes.bfloat16))
    i64 = np.arange(64)
    SL = (i64[:, None] > i64[None, :]).astype(np.float32)
    SU = SL.T.copy()
    UI = (i64[:, None] <= i64[None, :]).astype(np.float32)
    LI = UI.T.copy()
    rm = np.zeros((128, 6, 128), np.float32)
    for mi, mm_ in enumerate((SL, SU, UI, LI, -UI, -LI)):
        rm[:64, mi, :64] = mm_
        rm[64:, mi, 64:] = mm_
    tri = np.stack([UI, SU, LI, SL], 1).astype(np.float32) * np.float32(RW_SCALE)
    return {"cmat": cmat, "cosT": cosT, "sinT": sinT, "wmask": wmask, "rmask": _c(rm), "tri": _c(tri)}


A_IN_ = 768
B_IN_ = 3712


def host_core_inputs(I, core, consts):
    b, j = divmod(core, 4)
    kv = j // 2
    m = dict(consts)
    x, ctx = I["x"], I["ctx"]
    m["hT0"] = _c(np.concatenate([x[b, j * 1024:(j + 1) * 1024], ctx[b, j * 64:(j + 1) * 64]], 0).T)
    m["ccT"] = _c(np.stack([I["c"][b], I["c_ctx"]], 0).T)
    awq = np.empty((D, 96 * 128), np.float32)
    abq = np.empty((128, 96), np.float32)
    for l in range(DEPTH):
        for w in range(6):
            for cl in range(4):
                cc = (l * 6 + w) * 4 + cl
                g0 = w * D + (j * 4 + cl) * 128
                awq[:, cc * 128:(cc + 1) * 128] = I["ada_w"][l][:, g0:g0 + 128]
                abq[:, cc] = I["ada_b"][l][g0:g0 + 128]
    m["awq"] = awq
    m["abq"] = abq
    ind = np.zeros((128, 4), np.float32)
    ind[:, j] = 1.0
    m["ind"] = ind
    m["gn1T"] = _c(np.concatenate([I["norm1_g"][l].reshape(-1, 128).T for l in range(DEPTH)], 1))
    m["gn2T"] = _c(np.concatenate([I["norm2_g"][l].reshape(-1, 128).T for l in range(DEPTH)], 1))
    cols = []
    cols += list(range(2 * j * 64, (2 * j + 2) * 64))
    cols += list(range(512 + kv * 64, 512 + (kv + 1) * 64)) * 2
    cb = A_IN_ + B_IN_
    cols += list(range(cb + 2 * j * 64, cb + (2 * j + 2) * 64))
    cols += list(range(cb + 512 + kv * 64, cb + 512 + (kv + 1) * 64)) * 2
    cols += list(range(640 + kv * 64, 640 + (kv + 1) * 64))
    cols += list(range(cb + 640 + kv * 64, cb + 640 + (kv + 1) * 64))
    bb = A_IN_
    for part in range(3):
        cols += list(range(bb + part * 1024 + 4 * j * 64, bb + part * 1024 + (4 * j + 4) * 64))
    cols += list(range(bb + 3072, bb + 3712))
    assert len(cols) == NW
    m["win"] = _c(I["w_in"][:, :, cols])
    bcols = [c_ - bb for c_ in cols[640:]]
    hc = slice(4 * j * 64, (4 * j + 4) * 64)
    pp = np.zeros((DEPTH, 128, NPP), np.float32)
    for l in range(DEPTH):
        pp[l, :, PP_QKG + 0] = np.tile(I["a_q_norm"][l], 2)
        pp[l, :, PP_QKG + 1] = np.tile(I["a_k_norm"][l], 2)
        pp[l, :, PP_QKG + 2] = np.tile(I["c_q_norm"][l], 2)
        pp[l, :, PP_QKG + 3] = np.tile(I["c_k_norm"][l], 2)
        mu = I["shift_mu"][l]
        pos = 0
        for ci in range(5, 17):
            M = CHUNKS[ci][1]
            idx = bcols[pos:pos + M]
            pos += M
            pp[l, :M, PP_MU + (ci - 5) * 2 + 0] = mu[0][idx]
            pp[l, :M, PP_MU + (ci - 5) * 2 + 1] = mu[1][idx]
        for pr in range(2):
            sl = slice(4 * j * 64 + pr * 128, 4 * j * 64 + (pr + 1) * 128)
            for d in range(2):
                pp[l, :, PP_A0 + pr * 2 + d] = I["iclr_a0"][l][d][sl]
            pp[l, :, PP_KK + pr] = I["k_k"][l][sl]
            pp[l, :, PP_KA + pr] = I["k_a"][l][sl]
            pp[l, :, PP_RK + pr] = I["r_k"][l][sl]
            pp[l, :, PP_GNG + pr] = I["gn_g"][l][sl]
            pp[l, :, PP_GNB + pr] = I["gn_b"][l][sl]
    m["pp"] = pp
    m["sinkrow"] = _c(np.stack([np.repeat(I["a_sink"][l][2 * j:2 * j + 2], 256)[None, :] for l in range(DEPTH)]).astype(np.float32))
    m["w0row"] = _c(np.stack([np.concatenate([I["decay_w0"][l][d][hc] for d in range(2)])[None, :] for l in range(DEPTH)]))
    m["wup"] = _c(np.stack([np.concatenate([I["decay_up"][l][d][:, hc] for d in range(2)], 1) for l in range(DEPTH)]))
    m["aup"] = _c(np.stack([np.concatenate([I["iclr_up"][l][d][:, hc] for d in range(2)], 1) for l in range(DEPTH)]))
    m["gup"] = _c(np.stack([I["gate_up"][l][:, hc].reshape(2, 128, 256).transpose(1, 0, 2) for l in range(DEPTH)]))
    return m


def kernel(x, c, ctx, c_ctx, ada_w, ada_b, norm1_g, norm2_g, w_in, a_q_norm, a_k_norm, a_sink, c_q_norm, c_k_norm, shift_mu,
           decay_w0, decay_up, iclr_a0, iclr_up, gate_up, k_k, k_a, r_k, gn_g, gn_b, w_out, mlp_w1, mlp_w2):
    I = dict(x=x, c=c, ctx=ctx, c_ctx=c_ctx, ada_w=ada_w, ada_b=ada_b, norm1_g=norm1_g, norm2_g=norm2_g, w_in=w_in,
             a_q_norm=a_q_norm, a_k_norm=a_k_norm, a_sink=a_sink, c_q_norm=c_q_norm, c_k_norm=c_k_norm, shift_mu=shift_mu,
             decay_w0=decay_w0, decay_up=decay_up, iclr_a0=iclr_a0, iclr_up=iclr_up, gate_up=gate_up, k_k=k_k, k_a=k_a, r_k=r_k,
             gn_g=gn_g, gn_b=gn_b, w_out=w_out, mlp_w1=mlp_w1, mlp_w2=mlp_w2)
    I = {k_: np.asarray(v, dtype=np.float32) for k_, v in I.items()}
    consts = host_B_consts()
    perm = []
    for r in range(4):
        perm += list(range(2 * r * 64, 2 * r * 64 + 128)) + list(range(512 + 4 * r * 64, 512 + 4 * r * 64 + 256)) \
            + list(range(1536 + 2 * r * 64, 1536 + 2 * r * 64 + 128))
    shared = {"wout": _c(I["w_out"][:, perm, :]), "w1": I["mlp_w1"], "w2": I["mlp_w2"]}
    maps = []
    for core in range(8):
        m = host_core_inputs(I, core, consts)
        m.update(shared)
        maps.append(m)
    if "F" not in _PROG:
        _PROG["F"] = build_F()
    res = run_bass_kernel_spmd(_PROG["F"], maps, core_ids=list(range(8))).results
    out = np.empty((2, SEQ, D), np.float32)
    for core in range(8):
        b, j = divmod(core, 4)
        out[b, j * 1024:(j + 1) * 1024] = res[core]["hTo"][:, 0:1024].T
    return out
```

```python
import numpy as np
import ml_dtypes
from contextlib import ExitStack
import concourse.bass as bass
import concourse.mybir as mybir
from concourse.bass_utils import run_bass_kernel_spmd

F32 = mybir.dt.float32
BF16 = mybir.dt.bfloat16
AF = mybir.ActivationFunctionType
ALU = mybir.AluOpType
AX = mybir.AxisListType

D = 2048
NCH = 16
SEQ = 4096
CTX = 256
DEPTH = 4
NT = 1088
BLK = [(0, 512, 0), (512, 512, 0), (1024, 64, 1)]
TALL = SEQ + CTX
NORM_EPS = 1e-6
GN_EPS = 64e-5
SEM_LIMIT = 30000
GROUPS = [[0, 1, 2, 3], [4, 5, 6, 7]]
UPIECES = [(0, 3), (3, 3), (6, 3), (9, 3), (12, 3), (15, 1)]


class Buf:
    __slots__ = ("w", "r", "excl")

    def __init__(self, excl=False):
        self.w = None
        self.r = {}
        self.excl = excl


class Tile:
    def __init__(self, t):
        self.t = t
        self.b = Buf()
        self._sub = {}

    def sb(self, key):
        b = self._sub.get(key)
        if b is None:
            b = self._sub[key] = Buf()
        return b

    def __getitem__(self, idx):
        return self.t[idx]


class BankView:
    def __init__(self, t, i):
        self.t = t
        self.i = i
        self.b = Buf(excl=True)

    def __getitem__(self, idx):
        p, f = idx
        return self.t[p, self.i, f]


class K:
    def __init__(self):
        self.nc = bass.Bass("TRN2", target_bir_lowering=False)
        nc = self.nc
        self.ctx = ExitStack()
        self.engs = {"pe": nc.tensor, "dve": nc.vector, "act": nc.scalar, "pool": nc.gpsimd, "sp": nc.sync}
        self.cur = {}
        self.waited = {e: {} for e in self.engs}
        self.nsem = 0
        for e in self.engs:
            self._new_sem(e)
        self.dpool = {}
        self.dcnt = {}
        for q in ("sp", "pool", "act"):
            self.dpool[q] = []
            for i in range(12):
                nm = f"d_{q}_{i}"
                self.dpool[q].append([self.ctx.enter_context(nc.semaphore(nm)), nm, 0])
            self.dcnt[q] = 0
        self.cpool = [[self.ctx.enter_context(nc.semaphore(f"cc_{i}")), f"cc_{i}", 0] for i in range(6)]
        self.ccnt = 0
        self.out_toks = []
        self.nuniq = 0
        self.phase = None

    def _new_sem(self, e):
        nm = f"c_{e}_{self.nsem}"
        self.nsem += 1
        self.cur[e] = [self.ctx.enter_context(self.nc.semaphore(nm)), nm, 0]

    def sbuf(self, shape, dt, name=None):
        self.nuniq += 1
        ctx = self.phase if self.phase is not None else self.ctx
        return Tile(ctx.enter_context(self.nc.sbuf_tensor(f"s_{name or 'sb'}_{self.nuniq}", list(shape), dt)))

    def barrier(self):
        toks = [(c[1], c[0], c[2]) for c in self.cur.values() if c[2] > 0]
        for q in self.dpool:
            toks += [(sl[1], sl[0], sl[2]) for sl in self.dpool[q] if sl[2] > 0]
        toks += [(sl[1], sl[0], sl[2]) for sl in self.cpool if sl[2] > 0]
        for eng, e in self.engs.items():
            for nm, sem, val in toks:
                if nm == self.cur[eng][1]:
                    continue
                if self.waited[eng].get(nm, 0) < val:
                    e.wait_ge(sem, val)
                    self.waited[eng][nm] = val

    def psum(self, shape, dt, name=None):
        self.nuniq += 1
        t = Tile(self.ctx.enter_context(self.nc.psum_tensor("p_" + (name or f"ps{self.nuniq}"), list(shape), dt)))
        t.b.excl = True
        return t

    def dram(self, name, shape, dt, kind="Internal"):
        return Tile(self.nc.dram_tensor(name, list(shape), dt, kind=kind))

    def _deps(self, eng, reads, writes):
        deps = {}

        def add(t):
            if t is None:
                return
            o = deps.get(t[0])
            if o is None or o[2] < t[2]:
                deps[t[0]] = t

        for b in reads:
            add(b.w)
            if b.excl:
                for kk, t in b.r.items():
                    if kk != eng:
                        add(t)
        for b in writes:
            add(b.w)
            for t in b.r.values():
                add(t)
        e = self.engs[eng]
        wd = self.waited[eng]
        for nm, (_, sem, val) in deps.items():
            if eng == "pe" and nm == self.cur["pe"][1]:
                continue
            if wd.get(nm, 0) >= val:
                continue
            e.wait_ge(sem, val)
            wd[nm] = val

    def _mark(self, key, tok, reads, writes):
        for b in writes:
            b.w = tok
            b.r = {}
        for b in reads:
            if b.w is not tok:
                b.r[key] = tok

    def op(self, eng, fn, reads=(), writes=()):
        self._deps(eng, reads, writes)
        inst = fn(self.engs[eng])
        c = self.cur[eng]
        if c[2] >= SEM_LIMIT:
            self._new_sem(eng)
            c = self.cur[eng]
        inst.then_inc(c[0], 1)
        c[2] += 1
        tok = (c[1], c[0], c[2])
        self._mark(eng, tok, reads, writes)
        return tok

    def dma(self, q, out, in_, reads=(), writes=(), is_out=False, **kw):
        self._deps(q, reads, writes)
        e = self.engs[q]
        slot = self.dpool[q][self.dcnt[q] % len(self.dpool[q])]
        self.dcnt[q] += 1
        if slot[2] > 0 and self.waited[q].get(slot[1], 0) < slot[2]:
            e.wait_ge(slot[0], slot[2])
            self.waited[q][slot[1]] = slot[2]
        inst = e.dma_start(out=out, in_=in_, **kw)
        inst.then_inc(slot[0], 16)
        slot[2] += 16
        tok = (slot[1], slot[0], slot[2])
        self._mark(slot[1], tok, reads, writes)
        if is_out:
            self.out_toks.append(tok)
        return tok

    def collective(self, kind, alu, in_t, out_t, reads, writes):
        self._deps("pool", reads, writes)
        e = self.engs["pool"]
        slot = self.cpool[self.ccnt % len(self.cpool)]
        self.ccnt += 1
        if slot[2] > 0 and self.waited["pool"].get(slot[1], 0) < slot[2]:
            e.wait_ge(slot[0], slot[2])
            self.waited["pool"][slot[1]] = slot[2]
        inst = e.collective_compute(kind, alu, replica_groups=GROUPS, ins=[in_t.t.ap().opt()], outs=[out_t.t.ap().opt()])
        inst.then_inc(slot[0], 1)
        slot[2] += 1
        tok = (slot[1], slot[0], slot[2])
        self._mark(slot[1], tok, reads, writes)
        return tok

    def finish(self):
        e = self.engs["sp"]
        for nm, sem, val in self.out_toks:
            if self.waited["sp"].get(nm, 0) < val:
                e.wait_ge(sem, val)
                self.waited["sp"][nm] = val
        self.ctx.close()
        return self.nc


def emit_consts(k):
    c = {}
    c["ones_bf"] = k.sbuf([128, 128], BF16, "ones_bf")
    k.op("pool", lambda e: e.memset(c["ones_bf"][:], 1.0), writes=[c["ones_bf"].b])
    return c


def emit_rsqrt(k, out, src, scale, eps, n, p0=0, p1=128):
    k.op("dve", lambda e: e.tensor_scalar(out=out[p0:p1, 0:n], in0=src[p0:p1, 0:n], scalar1=float(scale), scalar2=float(eps),
                                           op0=ALU.mult, op1=ALU.add), reads=[src.b], writes=[out.b])
    k.op("act", lambda e: e.activation(out=out[p0:p1, 0:n], in_=out[p0:p1, 0:n], func=AF.Ln), reads=[out.b], writes=[out.b])
    k.op("act", lambda e: e.activation(out=out[p0:p1, 0:n], in_=out[p0:p1, 0:n], func=AF.Exp, scale=-0.5), reads=[out.b], writes=[out.b])


class Env:
    pass


class Mods:
    def __init__(self, t):
        self.t = t
        self.b = t.b

    def ap(self, l, lc, w, m):
        return self.t[:, m // 4, (l * 6 + w) * 4 + m % 4, lc:lc + 1]

    def vec(self, l, lc, w):
        return self.t[:, :, (l * 6 + w) * 4:(l * 6 + w) * 4 + 4, lc]


def emit_modprep(k, mods, l, gn_ap, gn_b, which_sc, name):
    gsc = k.sbuf([128, 2, NCH], F32, name + "_gsc")
    for lc in range(2):
        ov = gsc[:, lc, :].rearrange("p (a b) -> p a b", a=4)
        k.op("dve", lambda e: e.tensor_scalar(out=ov, in0=mods.vec(l, lc, which_sc), scalar1=1.0, scalar2=None, op0=ALU.add),
             reads=[mods.b], writes=[gsc.b])
        k.op("dve", lambda e: e.tensor_tensor(out=ov, in0=ov, in1=gn_ap.rearrange("p (a b) -> p a b", a=4), op=ALU.mult),
             reads=[gsc.b, gn_b], writes=[gsc.b])
    return gsc


def emit_norm_mod(k, cst, hT, gsc, mods, l, which_sh, out_bf, sq, stat_ps, rstd):
    for bi, (t0, nb, lc) in enumerate(BLK):
        hb = [hT.sb((m, bi)) for m in range(NCH)]
        for m in range(NCH):
            k.op("act", lambda e: e.activation(out=sq[:, m % 8, 0:nb], in_=hT[:, m, t0:t0 + nb], func=AF.Square),
                 reads=[hb[m]], writes=[sq.sb(m % 8)])
            k.op("pe", lambda e: e.matmul(out=stat_ps[:, 0:nb], lhsT=cst["ones_bf"][:, :], rhs=sq[:, m % 8, 0:nb],
                                          start=(m == 0), stop=(m == NCH - 1)),
                 reads=[sq.sb(m % 8), cst["ones_bf"].b], writes=[stat_ps.b])
        emit_rsqrt(k, rstd, stat_ps, 1.0 / D, NORM_EPS, nb)
        for m in range(NCH):
            tt = k.tmpn[m % 2]
            k.op("dve", lambda e: e.scalar_tensor_tensor(out=tt[:, 0:nb], in0=hT[:, m, t0:t0 + nb], scalar=gsc[:, lc, m:m + 1],
                                                          in1=rstd[:, 0:nb], op0=ALU.mult, op1=ALU.mult),
                 reads=[hb[m], gsc.b, rstd.b], writes=[tt.b])
            k.op("act", lambda e: e.activation(out=out_bf[:, m, t0:t0 + nb], in_=tt[:, 0:nb], func=AF.Identity,
                                               bias=mods.ap(l, lc, which_sh, m), scale=1.0),
                 reads=[tt.b, mods.b], writes=[out_bf.sb((m, bi))])


def end_phase(k):
    k.barrier()
    k.phase.close()
    k.phase = None


def emit_u_exchange(k, E, ubf):
    for pc, (c0, n) in enumerate(UPIECES):
        k.dma("sp", E.u_in[pc].t.ap().rearrange("(c p) t -> p c t", p=128), ubf[:, c0:c0 + n, :],
              reads=[ubf.sb((m, bi)) for m in range(c0, c0 + n) for bi in range(3)], writes=[E.u_in[pc].b])
        k.collective("AllGather", ALU.bypass, E.u_in[pc], E.u_all[pc], reads=[E.u_in[pc].b], writes=[E.u_all[pc].b])


def emit_P0(k, E):
    k.phase = ExitStack()
    sc = k.sbuf([128, NCH, 2], F32)
    scb = k.sbuf([128, NCH, 2], BF16)
    abq = k.sbuf([128, 96], F32)
    modP = k.sbuf([128, 96, 2], F32)
    wb = [k.sbuf([128, NCH, 512], BF16) for _ in range(2)]
    k.dma("sp", sc[:], E.ccTd.rearrange("(c p) r -> p c r", p=128), writes=[sc.b])
    k.dma("sp", abq[:], E.abqd[:, :], writes=[abq.b])
    k.dma("sp", E.ind[:], E.indd[:, :], writes=[E.ind.b])
    k.dma("sp", E.gn1[:], E.gn1d[:, :], writes=[E.gn1.b])
    k.dma("sp", E.gn2[:], E.gn2d[:, :], writes=[E.gn2.b])
    k.dma("sp", E.cmat[:], E.cmatd[:, :, :], writes=[E.cmat.b])
    k.dma("sp", E.wmask[:], E.wmaskd[:, :, :], writes=[E.wmask.b])
    k.dma("sp", E.rmask[:], E.rmaskd[:, :, :], writes=[E.rmask.b])
    k.dma("sp", E.tri[:], E.trid[:, :, :], writes=[E.tri.b])
    k.op("act", lambda e: e.activation(out=scb[:], in_=sc[:], func=AF.Silu), reads=[sc.b], writes=[scb.b])
    awv = E.awqd.rearrange("(c p) n -> p c n", p=128)
    for g in range(24):
        w = wb[g % 2]
        k.dma("pool", w[:], awv[:, :, g * 512:(g + 1) * 512], writes=[w.b])
        for q in range(4):
            cc = g * 4 + q
            p = E.banks[cc % 2]
            for c in range(NCH):
                k.op("pe", lambda e: e.matmul(out=p[:, 0:2], lhsT=w[:, c, q * 128:(q + 1) * 128], rhs=scb[:, c, :],
                                              start=(c == 0), stop=(c == NCH - 1)), reads=[scb.b, w.b], writes=[p.b])
            k.op("dve", lambda e: e.tensor_scalar(out=modP[:, cc, :], in0=p[:, 0:2], scalar1=abq[:, cc:cc + 1], scalar2=None, op0=ALU.add),
                 reads=[p.b, abq.b], writes=[modP.b])
    k.dma("sp", E.mod_in.t.ap(), modP[:, :, :].rearrange("p a b -> p (a b)"), reads=[modP.b], writes=[E.mod_in.b])
    k.collective("AllGather", ALU.bypass, E.mod_in, E.mod_all, reads=[E.mod_in.b], writes=[E.mod_all.b])
    k.dma("sp", E.modS[:, :, :, :].rearrange("p r a b -> p r (a b)"), E.mod_all.t.ap().rearrange("(r p) n -> p r n", p=128),
          reads=[E.mod_all.b], writes=[E.modS.b])
    end_phase(k)


def emit_P1(k, E):
    k.phase = ExitStack()
    hT = k.sbuf([128, NCH, NT], F32, "hT")
    ubf = k.sbuf([128, NCH, NT], BF16, "ubf")
    sq = k.sbuf([128, 8, 512], BF16, "sq")
    k.tmpn = [k.sbuf([128, 512], F32) for _ in range(2)]
    rstd = k.sbuf([128, 512], F32, "rstd")
    hv = E.hT0d.rearrange("(c p) t -> p c t", p=128)
    for bi, (t0, nb, lc) in enumerate(BLK):
        k.dma("sp", hT[:, :, t0:t0 + nb], hv[:, :, t0:t0 + nb], writes=[hT.sb((m, bi)) for m in range(NCH)])
    gsc = emit_modprep(k, E.mods, 0, E.gn1[:, 0:NCH], E.gn1.b, 1, "n1")
    emit_norm_mod(k, E.cst, hT, gsc, E.mods, 0, 0, ubf, sq, E.banks[6], rstd)
    emit_u_exchange(k, E, ubf)
    end_phase(k)


def emit_C(k, E, l):
    k.phase = ExitStack()
    last = l == DEPTH - 1
    hT = k.sbuf([128, NCH, NT], F32, "hT")
    abf = k.sbuf([128, NCH, NT], BF16, "abf")
    h1 = k.sbuf([128, NCH, NT], BF16, "h1")
    sq = k.sbuf([128, 8, 512], BF16, "sq")
    k.tmpn = [k.sbuf([128, 512], F32) for _ in range(2)]
    rtmp = [k.sbuf([128, 512], F32) for _ in range(2)]
    rstd = k.sbuf([128, 512], F32, "rstd")
    wbuf = [k.sbuf([128, NCH, 256], BF16) for _ in range(3)]
    banks = [[E.banks[i * 3 + jj] for jj in range(3)] for i in range(2)]
    stat = E.banks[6]
    mods = E.mods
    hsrc = E.hT0d if l == 0 else E.h_spill.t.ap()
    hv = hsrc.rearrange("(c p) t -> p c t", p=128)
    ov = E.rs_out.t.ap().rearrange("(c p) t -> p c t", p=128)
    for bi, (t0, nb, lc) in enumerate(BLK):
        k.dma("sp", abf[:, :, t0:t0 + nb], ov[:, :, t0:t0 + nb], reads=[E.rs_out.b], writes=[abf.sb((m, bi)) for m in range(NCH)])
    for bi, (t0, nb, lc) in enumerate(BLK):
        k.dma("sp", hT[:, :, t0:t0 + nb], hv[:, :, t0:t0 + nb], reads=[E.h_spill.b], writes=[hT.sb((m, bi)) for m in range(NCH)])
    wcnt = [0]

    def load_w(src_ap):
        w = wbuf[wcnt[0] % 3]
        wcnt[0] += 1
        k.dma("pool", w[:], src_ap, writes=[w.b])
        return w

    def proj(w, mi, m, src, gate_idx):
        bk = banks[m % 2]
        for bi, (t0, nb, lc) in enumerate(BLK):
            for kc in range(NCH):
                k.op("pe", lambda e: e.matmul(out=bk[bi][:, 0:nb], lhsT=w[:, kc, mi * 128:(mi + 1) * 128], rhs=src[:, kc, t0:t0 + nb],
                                              start=(kc == 0), stop=(kc == NCH - 1)),
                     reads=[w.b, src.sb((kc, bi))], writes=[bk[bi].b])
            k.op("dve", lambda e: e.scalar_tensor_tensor(out=hT[:, m, t0:t0 + nb], in0=bk[bi][:, 0:nb],
                                                          scalar=mods.ap(l, lc, gate_idx, m), in1=hT[:, m, t0:t0 + nb],
                                                          op0=ALU.mult, op1=ALU.add),
                 reads=[bk[bi].b, mods.b, hT.sb((m, bi))], writes=[hT.sb((m, bi))])

    woutv = E.woutd[l].rearrange("(c p) n -> p c n", p=128)
    for mp in range(8):
        w = load_w(woutv[:, :, mp * 256:(mp + 1) * 256])
        for mi in range(2):
            proj(w, mi, mp * 2 + mi, abf, 2)
    gsc2 = emit_modprep(k, mods, l, E.gn2[:, l * NCH:(l + 1) * NCH], E.gn2.b, 4, f"n2_{l}")
    emit_norm_mod(k, E.cst, hT, gsc2, mods, l, 3, abf, sq, stat, rstd)
    w1v = E.w1d[l].rearrange("(c p) n -> p c n", p=128)
    w2v = E.w2d[l].rearrange("(c p) n -> p c n", p=128)
    ecnt = 0
    for q in range(4):
        for fp in range(8):
            f0 = (q * 16 + fp * 2) * 128
            w = load_w(w1v[:, :, f0:f0 + 256])
            for fi in range(2):
                fl = fp * 2 + fi
                bk = banks[fl % 2]
                for bi, (t0, nb, lc) in enumerate(BLK):
                    for kc in range(NCH):
                        k.op("pe", lambda e: e.matmul(out=bk[bi][:, 0:nb], lhsT=w[:, kc, fi * 128:(fi + 1) * 128],
                                                      rhs=abf[:, kc, t0:t0 + nb], start=(kc == 0), stop=(kc == NCH - 1)),
                             reads=[w.b, abf.sb((kc, bi))], writes=[bk[bi].b])
                    rt = rtmp[ecnt % 2]
                    ecnt += 1
                    k.op("act", lambda e: e.activation(out=rt[:, 0:nb], in_=bk[bi][:, 0:nb], func=AF.Relu),
                         reads=[bk[bi].b], writes=[rt.b])
                    k.op("dve", lambda e: e.tensor_tensor(out=h1[:, fl, t0:t0 + nb], in0=rt[:, 0:nb], in1=rt[:, 0:nb], op=ALU.mult),
                         reads=[rt.b], writes=[h1.sb((fl, bi))])
        for mp in range(8):
            w = load_w(w2v[:, q * 16:(q + 1) * 16, mp * 256:(mp + 1) * 256])
            for mi in range(2):
                proj(w, mi, mp * 2 + mi, h1, 5)
    if last:
        hov = E.hOd.rearrange("(c p) t -> p c t", p=128)
        for bi, (t0, nb, lc) in enumerate(BLK):
            k.dma("sp", hov[:, :, t0:t0 + nb], hT[:, :, t0:t0 + nb], reads=[hT.sb((m, bi)) for m in range(NCH)], is_out=True)
    else:
        hsv = E.h_spill.t.ap().rearrange("(c p) t -> p c t", p=128)
        for bi, (t0, nb, lc) in enumerate(BLK):
            k.dma("sp", hsv[:, :, t0:t0 + nb], hT[:, :, t0:t0 + nb], reads=[hT.sb((m, bi)) for m in range(NCH)], writes=[E.h_spill.b])
        gsc1 = emit_modprep(k, mods, l + 1, E.gn1[:, (l + 1) * NCH:(l + 2) * NCH], E.gn1.b, 1, f"n1_{l}")
        emit_norm_mod(k, E.cst, hT, gsc1, mods, l + 1, 0, h1, sq, stat, rstd)
        emit_u_exchange(k, E, h1)
    end_phase(k)


def out_write(k, E, src, p0, p1, ncols, segs, t0, n):
    m4 = E.m4[E.m4c[0] % 2]
    E.m4c[0] += 1
    for r in range(4):
        if r % 2 == 0:
            k.op("dve", lambda e: e.tensor_scalar(out=m4[p0:p1, r, 0:ncols], in0=src[p0:p1, 0:ncols], scalar1=E.ind[p0:p1, r:r + 1],
                                                   scalar2=None, op0=ALU.mult), reads=[src.b, E.ind.b], writes=[m4.b])
        else:
            k.op("act", lambda e: e.activation(out=m4[p0:p1, r, 0:ncols], in_=src[p0:p1, 0:ncols], func=AF.Copy,
                                               scale=E.ind[p0:p1, r:r + 1]), reads=[src.b, E.ind.b], writes=[m4.b])
    rs4 = E.rs_in.t.ap().rearrange("(j r q) t -> j q r t", j=4, r=4)
    if t0 < CTX:
        pieces = [(jj, 1024, 64, jj * 64) for jj in range(4)]
    else:
        lat = t0 - CTX
        pieces = [(lat // 1024, lat % 1024, n, 0)]
    P = p1 - p0
    for row0, coff in segs:
        for jj, lcol, cnt, so in pieces:
            k.dma("sp", rs4[jj][row0:row0 + P, :, lcol:lcol + cnt], m4[p0:p1, :, coff + so:coff + so + cnt],
                  reads=[m4.b], writes=[E.rs_in.b])


NW = 2048
CHUNKS = [(i * 128, 128) for i in range(11)] + [(1408 + i * 96, 96) for i in range(4)] + [(1792, 128), (1920, 128)]
PB_ROWS = 1408
PB_ROWOFF = [0, 128, 256, 384, 512, 640, 768, 864, 960, 1056, 1152, 1280]
PBW = TALL + 4
MBLK = [(0, 256)] + [(256 + i * 512, 512) for i in range(8)]
ATT_SCALE = 0.125
PP_QKG = 0
PP_MU = 4
PP_A0 = 28
PP_KK = 32
PP_KA = 34
PP_RK = 36
PP_GNG = 38
PP_GNB = 40
NPP = 42


def pb_col(t):
    return t + 1 if t < CTX else t + 3


def emit_B(k, E, l):
    nc = k.nc
    pB = E.pB
    pp, cmat, wmask, banks, PS = E.pp, E.cmat, E.wmask, E.banks, E.PS
    k.phase = ExitStack()
    E.m4 = [k.sbuf([128, 4, 512], BF16, f"m4_{i}") for i in range(2)]
    wres = k.sbuf([128, NCH, NW], BF16, "wres")
    bones = k.sbuf([128, 128], BF16, "bones")
    QK = [k.sbuf([128, TALL], BF16, f"qk{i}") for i in range(4)]
    V = [k.sbuf([128, 34, 65], BF16, f"v{i}") for i in range(2)]
    ub = [k.sbuf([128, NCH, 512], BF16, f"ub{i}") for i in range(2)]
    cs = [[k.sbuf([128, 512], F32, f"cs{i}{j}") for j in range(2)] for i in range(2)]
    x32 = [k.sbuf([128, 512], F32, "x32_0")] * 2
    sqb = [k.sbuf([128, 512], BF16, "sqb0")] * 2
    rs = [k.sbuf([128, 512], F32, "rs0")] * 2
    xn = [k.sbuf([128, 512], F32, "xn0")] * 2
    t1 = [k.sbuf([128, 512], F32, "t1_0")] * 2
    t2 = [k.sbuf([128, 512], F32, "t2_0")] * 2
    stg = [k.sbuf([128, 512], F32, f"stg{i}") for i in range(3)]
    zero = k.sbuf([128, 16], F32, "zero")

    k.dma("sp", pp[:], E.ppd[l], writes=[pp.b])
    k.op("pool", lambda e: e.memset(zero[:], 0.0), writes=[zero.b])
    k.op("dve", lambda e: e.tensor_copy(out=bones[:], in_=cmat[:, 1, :]), reads=[cmat.b], writes=[bones.b])
    for i in range(2):
        k.op("pool", lambda e: e.memset(V[i][:, :, 64:65], 1.0), writes=[V[i].b])
    pbb = E.pbb
    for col in ((0, 257, 258, PBW - 1) if l == 0 else ()):
        k.dma("sp", pB.t.ap()[:, col:col + 1].rearrange("(c p) o -> p c o", p=128), zero[:, 0:11].rearrange("p (c o) -> p c o", o=1),
              reads=[zero.b], writes=[pbb], allow_slow_non_contiguous=True)
    wv = E.wind[l].rearrange("(c p) n -> p c n", p=128)
    for i in range(4):
        k.dma("pool", wres[:, :, i * 512:(i + 1) * 512], wv[:, :, i * 512:(i + 1) * 512], writes=[wres.sb(i)])
    ident = cmat

    ecnt = [0]
    bk = [0]

    def nbank():
        b = banks[bk[0] % 4]
        bk[0] += 1
        return b

    for bi, (t0, nb) in enumerate(MBLK):
        u = ub[bi % 2]
        if t0 < CTX:
            srcs = [(rr, 1024, 64, rr * 64) for rr in range(4)]
        else:
            srcs = [((t0 - CTX) // 1024, (t0 - CTX) % 1024, nb, 0)]
        for pc, (c0_, n_) in enumerate(UPIECES):
            ua = E.u_all[pc].t.ap().rearrange("(r c p) t -> r p c t", r=4, p=128)
            for rr, scol, cnt, dcol in srcs:
                k.dma("sp", u[:, c0_:c0_ + n_, dcol:dcol + cnt], ua[rr][:, :, scol:scol + cnt], reads=[E.u_all[pc].b], writes=[u.b])
        cost, sint = cs[bi % 2]
        k.dma("sp", cost[:, 0:nb], E.cosd[:, t0:t0 + nb], writes=[cost.b])
        k.dma("sp", sint[:, 0:nb], E.sind[:, t0:t0 + nb], writes=[sint.b])
        for ci, (c0, M) in enumerate(CHUNKS):
            ps = nbank()
            for kc in range(NCH):
                k.op("pe", lambda e: e.matmul(out=ps[0:M, 0:nb], lhsT=wres[:, kc, c0:c0 + M], rhs=u[:, kc, 0:nb],
                                              start=(kc == 0), stop=(kc == NCH - 1)),
                     reads=[wres.sb(c0 // 512), wres.sb((c0 + M - 1) // 512), u.b], writes=[ps.b])
            if ci < 4:
                i2 = ecnt[0] % 2
                ecnt[0] += 1
                xx, sq, rr, xnn, a1, a2 = x32[i2], sqb[i2], rs[i2], xn[i2], t1[i2], t2[i2]
                k.op("act", lambda e: e.activation(out=sq[:, 0:nb], in_=ps[:, 0:nb], func=AF.Square), reads=[ps.b], writes=[sq.b])
                k.op("dve", lambda e: e.tensor_scalar(out=xx[:, 0:nb], in0=ps[:, 0:nb], scalar1=pp[:, PP_QKG + ci:PP_QKG + ci + 1],
                                                       scalar2=None, op0=ALU.mult), reads=[ps.b, pp.b], writes=[xx.b])
                ps2 = nbank()
                k.op("pe", lambda e: e.matmul(out=ps2[:, 0:nb], lhsT=bones[:, :], rhs=sq[:, 0:nb], start=True, stop=True),
                     reads=[bones.b, sq.b], writes=[ps2.b])
                emit_rsqrt(k, rr, ps2, 1.0 / 64, NORM_EPS, nb)
                k.op("dve", lambda e: e.tensor_tensor(out=xnn[:, 0:nb], in0=xx[:, 0:nb], in1=rr[:, 0:nb], op=ALU.mult),
                     reads=[xx.b, rr.b], writes=[xnn.b])
                ps3 = nbank()
                k.op("pe", lambda e: e.matmul(out=ps3[:, 0:nb], lhsT=cmat[:, 2, :], rhs=xnn[:, 0:nb], start=True, stop=True),
                     reads=[cmat.b, xnn.b], writes=[ps3.b])
                k.op("pool", lambda e: e.tensor_tensor(out=a1[:, 0:nb], in0=xnn[:, 0:nb], in1=cost[:, 0:nb], op=ALU.mult),
                     reads=[xnn.b, cost.b], writes=[a1.b])
                k.op("dve", lambda e: e.tensor_tensor(out=a2[:, 0:nb], in0=ps3[:, 0:nb], in1=sint[:, 0:nb], op=ALU.mult),
                     reads=[ps3.b, sint.b], writes=[a2.b])
                k.op("pool", lambda e: e.tensor_tensor(out=QK[ci][:, t0:t0 + nb], in0=a1[:, 0:nb], in1=a2[:, 0:nb], op=ALU.add),
                     reads=[a1.b, a2.b], writes=[QK[ci].sb(bi)])
            elif ci == 4:
                i2 = ecnt[0] % 2
                ecnt[0] += 1
                xx = x32[i2]
                k.op("act", lambda e: e.activation(out=xx[:, 0:nb], in_=ps[:, 0:nb], func=AF.Copy), reads=[ps.b], writes=[xx.b])
                for tt in range(nb // 128):
                    pt = nbank()
                    k.op("pe", lambda e: e.transpose(out=pt[:, 0:128], in_=xx[:, tt * 128:(tt + 1) * 128], identity=cmat[:, 0, :]),
                         reads=[xx.b, cmat.b], writes=[pt.b])
                    tile = t0 // 128 + tt
                    k.op("act", lambda e: e.activation(out=V[0][:, tile, 0:64], in_=pt[:, 0:64], func=AF.Copy),
                         reads=[pt.b], writes=[V[0].sb(tile)])
                    k.op("dve", lambda e: e.tensor_copy(out=V[1][:, tile, 0:64], in_=pt[:, 64:128]),
                         reads=[pt.b], writes=[V[1].sb(tile)])
            else:
                s = stg[ecnt[0] % 3]
                ecnt[0] += 1
                if ecnt[0] % 2:
                    k.op("act", lambda e: e.activation(out=s[0:M, 0:nb], in_=ps[0:M, 0:nb], func=AF.Copy), reads=[ps.b], writes=[s.b])
                else:
                    k.op("dve", lambda e: e.tensor_copy(out=s[0:M, 0:nb], in_=ps[0:M, 0:nb]), reads=[ps.b], writes=[s.b])
                r0 = PB_ROWOFF[ci - 5]
                k.dma("sp", pB.t.ap()[r0:r0 + M, pb_col(t0):pb_col(t0) + nb], s[0:M, 0:nb], reads=[s.b], writes=[pbb])

    pT = [k.sbuf([128, 512], BF16, f"pT{i}") for i in range(3)]
    oraw = [k.sbuf([65, 512], F32, f"oraw{i}") for i in range(2)]
    rrow = [k.sbuf([65, 512], F32, f"rrow{i}") for i in range(2)]
    ostg = [k.sbuf([64, 512], BF16, f"ostg{i}") for i in range(2)]
    onesf = k.sbuf([65, 64], F32, "onesf")
    sinkx = k.sbuf([1, 512], F32, "sinkx")
    sinkb = k.sbuf([1, 512], BF16, "sinkb")
    e64 = k.sbuf([1, 65], BF16, "e64")
    k.op("pool", lambda e: e.memset(onesf[:], 1.0), writes=[onesf.b])
    k.op("pool", lambda e: e.memset(e64[:], 0.0), writes=[e64.b])
    k.op("pool", lambda e: e.memset(e64[0:1, 64:65], 1.0), writes=[e64.b])
    k.dma("sp", sinkx[:], E.sinkd[l], writes=[sinkx.b])
    k.op("act", lambda e: e.activation(out=sinkb[:], in_=sinkx[:], func=AF.Exp), reads=[sinkx.b], writes=[sinkb.b])
    ps_s = [(0, 1), (2, 3), (6, 7)]
    ps_o = [banks[4], banks[5]]
    ps_b = [banks[6], banks[7]]
    pcnt = [0]
    qcnt = [0]
    allq = [QK[i].sb(bi) for i in range(4) for bi in range(len(MBLK))]
    allv = [V[i].sb(t) for i in range(2) for t in range(34)]

    def attend(Q, Kd, Vt, q0, ktiles, sink, orow0):
        io = qcnt[0] % 2
        qcnt[0] += 1
        po = ps_o[io]
        n = len(ktiles)
        slots = []

        def qk(ii):
            kt, mi = ktiles[ii]
            b0, b1 = ps_s[pcnt[0] % 3]
            p = pT[pcnt[0] % 3]
            pcnt[0] += 1
            slots.append((b0, b1, p))
            for h in range(2):
                bh = banks[(b0, b1)[h]]
                k.op("pe", lambda e: e.matmul(out=bh[:, 0:256], lhsT=Kd[h * 64:(h + 1) * 64, kt * 128:(kt + 1) * 128],
                                              rhs=Q[h * 64:(h + 1) * 64, q0:q0 + 256], start=True, stop=True),
                     reads=allq, writes=[bh.b])

        def rest(ii):
            kt, mi = ktiles[ii]
            b0, b1, p = slots[ii]
            k.op("act", lambda e: e.activation(out=p[:, :].rearrange("p (h q) -> p h q", h=2), in_=PS.t[:, b0:b1 + 1, 0:256],
                                               func=AF.Exp, scale=ATT_SCALE),
                 reads=[banks[b0].b, banks[b1].b], writes=[p.b])
            if mi is not None:
                k.op("dve", lambda e: e.tensor_tensor(out=p[:], in0=p[:], in1=wmask[:, mi, :], op=ALU.mult),
                     reads=[p.b, wmask.b], writes=[p.b])
            k.op("pe", lambda e: e.matmul(out=po[0:65, :], lhsT=Vt[:, kt, 0:65], rhs=p[:], start=(ii == 0),
                                          stop=(ii == n - 1 and not sink)), reads=allv + [p.b], writes=[po.b])

        LA = 2
        for ii in range(min(LA, n)):
            qk(ii)
        for ii in range(n):
            if ii + LA < n:
                qk(ii + LA)
            rest(ii)
        if sink:
            k.op("pe", lambda e: e.matmul(out=po[0:65, :], lhsT=e64[0:1, :], rhs=sinkb[0:1, :], start=False, stop=True),
                 reads=[e64.b, sinkb.b], writes=[po.b])
        orw, rr, os_ = oraw[io], rrow[io], ostg[io]
        k.op("act", lambda e: e.activation(out=orw[0:65, :], in_=po[0:65, :], func=AF.Copy), reads=[po.b], writes=[orw.b])
        k.op("act", lambda e: e.activation(out=rr[64:65, :], in_=orw[64:65, :], func=AF.Ln), reads=[orw.b], writes=[rr.b])
        k.op("act", lambda e: e.activation(out=rr[64:65, :], in_=rr[64:65, :], func=AF.Exp, scale=-1.0), reads=[rr.b], writes=[rr.b])
        pb_ = ps_b[io]
        k.op("pe", lambda e: e.matmul(out=pb_[0:64, :], lhsT=onesf[64:65, 0:64], rhs=rr[64:65, :], start=True, stop=True),
             reads=[onesf.b, rr.b], writes=[pb_.b])
        k.op("dve", lambda e: e.tensor_tensor(out=os_[0:64, :], in0=orw[0:64, :], in1=pb_[0:64, :], op=ALU.mult),
             reads=[orw.b, pb_.b], writes=[os_.b])
        out_write(k, E, os_, 0, 64, 512, [(orow0, 0), (orow0 + 64, 256)], q0, 256)

    attend(QK[0], QK[1], V[0], 0, [(0, None), (1, None)], True, 0)
    attend(QK[2], QK[3], V[1], 0, [(0, None), (1, None)], False, 384)
    for qb in range(16):
        kts = []
        for r in range(4):
            lt = 2 * qb - 1 + r
            if 0 <= lt < 32:
                kts.append((2 + lt, r))
        kts += [(0, None), (1, None)]
        attend(QK[0], QK[1], V[0], 256 + qb * 256, kts, True, 0)
    for qb in range(16):
        attend(QK[2], QK[3], V[1], 256 + qb * 256, [(t, None) for t in range(34)], False, 384)
    end_phase(k)
    k.phase = ExitStack()
    E.m4 = [k.sbuf([128, 4, 512], BF16, f"m4_{i}") for i in range(2)]
    emit_rwkv(k, E, l, pB, pbb, pp, cmat, banks)
    end_phase(k)


RW_SCALE = -float(np.exp(np.float32(-0.5)))


def emit_rwkv(k, E, l, pB, pbb, pp, cmat, banks):
    sb = k.sbuf
    w0row = sb([1, 512], F32, "w0row")
    wup = sb([96, 512], F32, "wup")
    aup = sb([96, 512], F32, "aup")
    gup = sb([128, 2, 256], F32, "gup")
    rmask, tri = E.rmask, E.tri
    ones_row = sb([1, 128], F32, "ones_row")
    for t_, d_ in ((w0row, E.w0d[l]), (wup, E.wupd[l]), (aup, E.aupd[l]), (gup, E.gupd[l])):
        k.dma("sp", t_[:], d_, writes=[t_.b])
    k.op("pool", lambda e: e.memset(ones_row[:], 1.0), writes=[ones_row.b])
    c0 = sb([128, 12], F32, "c0")
    c1 = sb([128, 2], F32, "c1")
    c2 = sb([128, 2], F32, "c2")
    muv = pp[:, PP_MU:PP_MU + 24].rearrange("p (c two) -> p c two", two=2)
    k.op("dve", lambda e: e.tensor_tensor(out=c0[:], in0=muv[:, :, 0], in1=muv[:, :, 1], op=ALU.add), reads=[pp.b], writes=[c0.b])
    k.op("dve", lambda e: e.tensor_scalar(out=c0[:], in0=c0[:], scalar1=-1.0, scalar2=1.0, op0=ALU.mult, op1=ALU.add),
         reads=[c0.b], writes=[c0.b])
    k.op("dve", lambda e: e.tensor_scalar(out=c1[:], in0=pp[:, PP_KA:PP_KA + 2], scalar1=-1.0, scalar2=1.0, op0=ALU.mult, op1=ALU.add),
         reads=[pp.b], writes=[c1.b])
    k.op("dve", lambda e: e.tensor_scalar(out=c2[:], in0=c1[:], scalar1=2.0, scalar2=None, op0=ALU.mult), reads=[c1.b], writes=[c2.b])
    MSK = {0: dict(Ms=0, MsT=1, MiT=2, nMiT=4), 1: dict(Ms=1, MsT=0, MiT=3, nMiT=5)}
    NB = 512
    RD = F32
    F32R = mybir.dt.float32r

    def rv(ap):
        return ap.bitcast(F32R)

    WIDTH = 4
    STAGGER = 3
    identb = sb([128, 128], RD, "identb")
    k.op("dve", lambda e: e.tensor_copy(out=rv(identb[:]), in_=cmat[:, 0, :]), reads=[cmat.b], writes=[identb.b])
    raw = [sb([128, NB + 2], F32, f"raw{i}") for i in range(3)]
    rawc = [0]
    S = {n: sb([128, NB], F32, "S_" + n) for n in ("r", "k", "v", "wd", "ad", "ad2", "g0", "g1", "sqk", "rinv", "kap", "a", "a2", "tw",
                                                     "tmp", "kd", "bb", "eLn", "eLx", "ysum", "u1", "u2", "u3")}
    eLb = [sb([128, NB], F32, f"eL{i}") for i in range(2)]
    lw = sb([64, 8, 128], F32, "lw")
    bd = {n: [sb([128, 8, 2, 64], RD, f"bd_{n}{i}") for i in range(2)] for n in ("kap", "bt", "kt", "rt", "v")}
    zbig = S["u1"]
    k.op("pool", lambda e: e.memset(zbig[:], 0.0), writes=[zbig.b])
    for n in bd:
        for i in range(2):
            for hh in range(2):
                k.op("dve", lambda e: e.tensor_copy(out=rv(bd[n][i][:, :, hh, :]), in_=zbig[:, :].rearrange("p (c s) -> p c s", s=64)),
                     reads=[zbig.b], writes=[bd[n][i].b])
    yf = sb([128, TALL], F32, "yf")
    Mst = [sb([128, 128], F32, f"Mst{i}") for i in range(2)]
    U = {}

    def ut(name):
        if name not in U:
            dt_ = F32 if name in ("ET", "Hg", "YmT") else RD
            U[name] = [sb([128, 256 if name.startswith("X") else 128], dt_, f"U_{name}{i}") for i in range(WIDTH)]
        return U[name]

    ostage = [sb([128, NB], BF16, f"ostage{i}") for i in range(2)]
    bkc = [0]

    def rb():
        b = banks[bkc[0] % 8]
        bkc[0] += 1
        return b

    ecount = [0]

    def ew():
        ecount[0] += 1
        return "dve" if ecount[0] % 2 else "pool"

    dg = sb([128, 12, 3, 128], F32, "shift_diag")
    for bc_ in range(12):
        for j_, coef in enumerate((c0[:, bc_:bc_ + 1], pp[:, PP_MU + 2 * bc_:PP_MU + 2 * bc_ + 1], pp[:, PP_MU + 2 * bc_ + 1:PP_MU + 2 * bc_ + 2])):
            k.op("pool", lambda e: e.tensor_scalar(out=dg[:, bc_, j_, :], in0=cmat[:, 0, :], scalar1=coef, scalar2=None, op0=ALU.mult),
                 reads=[cmat.b, c0.b, pp.b], writes=[dg.b])

    def load_shift(bc, M, dst, t0, nb):
        rw = raw[rawc[0] % 3]
        rawc[0] += 1
        r0 = PB_ROWOFF[bc]
        k.dma("sp", rw[0:M, 0:nb + 2], pB.t.ap()[r0:r0 + M, pb_col(t0) - 1:pb_col(t0) + nb + 1], reads=[pbb], writes=[rw.b])
        ps = rb()
        for j_, off in enumerate((1, 0, 2)):
            k.op("pe", lambda e: e.matmul(out=ps[0:M, 0:nb], lhsT=dg[0:M, bc, j_, 0:M], rhs=rw[0:M, off:off + nb],
                                          start=(j_ == 0), stop=(j_ == 2)), reads=[dg.b, rw.b], writes=[ps.b])
        k.op("act", lambda e: e.activation(out=dst[0:M, 0:nb], in_=ps[0:M, 0:nb], func=AF.Copy), reads=[ps.b], writes=[dst.b])

    def lora_a(dst, src, pr, d, nb):
        ps = rb()
        k.op("pe", lambda e: e.matmul(out=ps[:, 0:nb], lhsT=aup[0:96, d * 256 + pr * 128:d * 256 + (pr + 1) * 128], rhs=src[0:96, 0:nb],
                                      start=True, stop=True), reads=[aup.b, src.b], writes=[ps.b])
        k.op("act", lambda e: e.activation(out=dst[:, 0:nb], in_=ps[:, 0:nb], func=AF.Sigmoid,
                                           bias=pp[:, PP_A0 + pr * 2 + d:PP_A0 + pr * 2 + d + 1], scale=1.0),
             reads=[ps.b, pp.b], writes=[dst.b])

    def prep(pr, t0, nb, d, final, bdi):
        nch = nb // 64
        load_shift(0 + pr, 128, S["r"], t0, nb)
        yield
        load_shift(2 + pr, 128, S["k"], t0, nb)
        yield
        load_shift(4 + pr, 128, S["v"], t0, nb)
        yield
        load_shift(6 + d, 96, S["wd"], t0, nb)
        yield
        load_shift(8 + d, 96, S["ad"], t0, nb)
        yield
        if final:
            load_shift(8 + (1 - d), 96, S["ad2"], t0, nb)
            yield
            load_shift(10, 128, S["g0"], t0, nb)
            yield
            load_shift(11, 128, S["g1"], t0, nb)
            yield
        kS, rS, vS = S["k"], S["r"], S["v"]
        k.op("act", lambda e: e.activation(out=S["sqk"][:, 0:nb], in_=kS[:, 0:nb], func=AF.Square, scale=pp[:, PP_KK + pr:PP_KK + pr + 1]),
             reads=[kS.b, pp.b], writes=[S["sqk"].b])
        ps = rb()
        k.op("pe", lambda e: e.matmul(out=ps[:, 0:nb], lhsT=cmat[:, 1, :], rhs=S["sqk"][:, 0:nb], start=True, stop=True),
             reads=[cmat.b, S["sqk"].b], writes=[ps.b])
        k.op("act", lambda e: e.activation(out=S["rinv"][:, 0:nb], in_=ps[:, 0:nb], func=AF.Sqrt), reads=[ps.b], writes=[S["rinv"].b])
        k.op("dve", lambda e: e.tensor_scalar(out=S["rinv"][:, 0:nb], in0=S["rinv"][:, 0:nb], scalar1=1e-12, scalar2=None, op0=ALU.max),
             reads=[S["rinv"].b], writes=[S["rinv"].b])
        k.op("dve", lambda e: e.reciprocal(out=S["rinv"][:, 0:nb], in_=S["rinv"][:, 0:nb]), reads=[S["rinv"].b], writes=[S["rinv"].b])
        k.op("dve", lambda e: e.scalar_tensor_tensor(out=S["kap"][:, 0:nb], in0=kS[:, 0:nb], scalar=pp[:, PP_KK + pr:PP_KK + pr + 1],
                                                      in1=S["rinv"][:, 0:nb], op0=ALU.mult, op1=ALU.mult),
             reads=[kS.b, pp.b, S["rinv"].b], writes=[S["kap"].b])
        yield
        lora_a(S["a"], S["ad"], pr, d, nb)
        yield
        if final:
            lora_a(S["a2"], S["ad2"], pr, 1 - d, nb)
            yield
        yield
        k.op("act", lambda e: e.activation(out=S["tw"][0:96, 0:nb], in_=S["wd"][0:96, 0:nb], func=AF.Tanh), reads=[S["wd"].b], writes=[S["tw"].b])
        pl = [rb(), rb()]
        wsl = slice(d * 256 + pr * 128, d * 256 + (pr + 1) * 128)
        for c in range(nch):
            pb_ = pl[c // 4]
            k.op("pe", lambda e: e.matmul(out=pb_[0:64, (c % 4) * 128:(c % 4 + 1) * 128], lhsT=S["tw"][0:96, c * 64:(c + 1) * 64],
                                          rhs=wup[0:96, wsl], start=True, stop=False), reads=[S["tw"].b, wup.b], writes=[pb_.b])
            k.op("pe", lambda e: e.matmul(out=pb_[0:64, (c % 4) * 128:(c % 4 + 1) * 128], lhsT=ones_row[0:1, 0:64],
                                          rhs=w0row[0:1, wsl], start=False, stop=True), reads=[ones_row.b, w0row.b], writes=[pb_.b])
        for hb in range((nch + 3) // 4):
            n4 = min(4, nch - hb * 4)
            k.op("act", lambda e: e.activation(out=lw[0:64, hb * 4:hb * 4 + n4, :], in_=pl[hb][0:64, 0:n4 * 128].rearrange("p (c n) -> p c n", n=128),
                                               func=AF.Sigmoid), reads=[pl[hb].b], writes=[lw.b])
        yield
        pL, pLx = rb(), rb()
        for c in range(nch):
            k.op("pe", lambda e: e.matmul(out=pL[:, c * 64:(c + 1) * 64], lhsT=lw[0:64, c, :], rhs=tri[0:64, 2 * d, :], start=True, stop=True),
                 reads=[lw.b, tri.b], writes=[pL.b])
        for c in range(nch):
            k.op("pe", lambda e: e.matmul(out=pLx[:, c * 64:(c + 1) * 64], lhsT=lw[0:64, c, :], rhs=tri[0:64, 2 * d + 1, :], start=True, stop=True),
                 reads=[lw.b, tri.b], writes=[pLx.b])
        k.op("act", lambda e: e.activation(out=eLb[bdi][:, 0:nb], in_=pL[:, 0:nb], func=AF.Exp), reads=[pL.b], writes=[eLb[bdi].b])
        k.op("act", lambda e: e.activation(out=S["eLn"][:, 0:nb], in_=pL[:, 0:nb], func=AF.Exp, scale=-1.0), reads=[pL.b], writes=[S["eLn"].b])
        k.op("act", lambda e: e.activation(out=S["eLx"][:, 0:nb], in_=pLx[:, 0:nb], func=AF.Exp), reads=[pLx.b], writes=[S["eLx"].b])
        yield
        k.op("dve", lambda e: e.tensor_scalar(out=S["tmp"][:, 0:nb], in0=S["a"][:, 0:nb], scalar1=pp[:, PP_KA + pr:PP_KA + pr + 1],
                                               scalar2=c1[:, pr:pr + 1], op0=ALU.mult, op1=ALU.add),
             reads=[S["a"].b, pp.b, c1.b], writes=[S["tmp"].b])
        k.op("pool", lambda e: e.tensor_tensor(out=S["kd"][:, 0:nb], in0=kS[:, 0:nb], in1=S["tmp"][:, 0:nb], op=ALU.mult),
             reads=[kS.b, S["tmp"].b], writes=[S["kd"].b])
        k.op("pool", lambda e: e.tensor_tensor(out=S["bb"][:, 0:nb], in0=S["kap"][:, 0:nb], in1=S["a"][:, 0:nb], op=ALU.mult),
             reads=[S["kap"].b, S["a"].b], writes=[S["bb"].b])
        yield
        for name, x, y in (("kap", S["kap"], S["eLx"]), ("bt", S["bb"], S["eLn"]), ("kt", S["kd"], S["eLn"]), ("rt", rS, eLb[bdi]),
                           ("v", vS, None)):
            dst = bd[name][bdi]
            for h in range(2):
                hs = slice(h * 64, (h + 1) * 64)
                eng = ew()
                xv = x[hs, 0:nb].rearrange("p (c s) -> p c s", s=64)
                if y is None:
                    k.op(eng, lambda e: e.tensor_copy(out=rv(dst[hs, 0:nch, h, :]), in_=xv), reads=[x.b], writes=[dst.b])
                else:
                    yv = y[hs, 0:nb].rearrange("p (c s) -> p c s", s=64)
                    k.op(eng, lambda e: e.tensor_tensor(out=rv(dst[hs, 0:nch, h, :]), in0=xv, in1=yv, op=ALU.mult),
                         reads=[x.b, y.b], writes=[dst.b])
            yield

    def unit(pr, tok0, c, d, bdi, useq, final):
        ui = useq % WIDTH
        mk = MSK[d]
        Kap = bd["kap"][bdi][:, c, :, :].rearrange("p a b -> p (a b)")
        Bt = bd["bt"][bdi][:, c, :, :].rearrange("p a b -> p (a b)")
        Kt = bd["kt"][bdi][:, c, :, :].rearrange("p a b -> p (a b)")
        Rt = bd["rt"][bdi][:, c, :, :].rearrange("p a b -> p (a b)")
        Vf = bd["v"][bdi][:, c, :, :].rearrange("p a b -> p (a b)")
        bdb = [bd[n][bdi].b for n in bd]
        gcol = c * 64 + (63 if d == 0 else 0)
        gam = eLb[bdi][:, gcol:gcol + 1]

        def T(n):
            return ut(n)[ui]

        def mm(lhsT, rhs, rd, N=128, acc=None, full=False):
            ps = rb() if acc is None else acc[0]
            st, sp_ = (True, True) if acc is None else (acc[1], acc[2])
            if not full:
                lhsT, rhs = rv(lhsT), rv(rhs)
            k.op("pe", lambda e: e.matmul(out=ps[:, 0:N], lhsT=lhsT, rhs=rhs, start=st, stop=sp_), reads=rd, writes=[ps.b])
            return ps

        def ev_copy(dst_t, dst_ap, ps, N=128, scale=None, full=False):
            if not full:
                dst_ap = rv(dst_ap)
            if scale is None:
                k.op("act", lambda e: e.activation(out=dst_ap, in_=ps[:, 0:N], func=AF.Copy), reads=[ps.b], writes=[dst_t.b])
            else:
                k.op("act", lambda e: e.activation(out=dst_ap, in_=ps[:, 0:N], func=AF.Copy, scale=scale), reads=[ps.b, eLb[bdi].b],
                     writes=[dst_t.b])

        def ev_tt(dst_t, dst_ap, in0_ap, in0_b, ps, op, N=128, psfirst=False, full=False):
            if not full:
                dst_ap = rv(dst_ap)
            if psfirst:
                k.op("dve", lambda e: e.tensor_tensor(out=dst_ap, in0=ps[:, 0:N], in1=in0_ap, op=op), reads=[ps.b] + in0_b, writes=[dst_t.b])
            else:
                k.op("dve", lambda e: e.tensor_tensor(out=dst_ap, in0=in0_ap, in1=ps[:, 0:N], op=op), reads=[ps.b] + in0_b, writes=[dst_t.b])

        X = [T("Xa"), T("Xb")]
        idb = [identb.b]
        ps = mm(Kap, identb[:, :], bdb + idb)
        ev_copy(X[0], X[0][:, 0:128], ps)
        yield
        ps = mm(Bt, identb[:, :], bdb + idb)
        ev_copy(T("nBtT"), T("nBtT")[:, :], ps, scale=-1.0)
        yield
        ps = mm(Kt, identb[:, :], bdb + idb)
        ev_copy(T("KtT"), T("KtT")[:, :], ps)
        yield
        ps = mm(Vf, identb[:, :], bdb + idb)
        ev_copy(T("VT"), T("VT")[:, :], ps)
        yield
        for nm, l_, r_, m_ in (("N", Kap, Bt, "Ms"), ("Z", Bt, Kap, "MsT"), ("AkkT", Kt, Kap, "MsT"), ("ArkT", Kt, Rt, "MiT"),
                               ("nArbT", Bt, Rt, "nMiT")):
            ps = mm(l_, r_, bdb)
            ev_tt(T(nm), T(nm)[:, :], rmask[:, mk[m_], :], [rmask.b], ps, ALU.mult, psfirst=True)
            yield
        ps = mm(T("AkkT")[:, :], T("VT")[:, :], [T("AkkT").b, T("VT").b])
        ev_copy(X[0], X[0][:, 128:256], ps)
        yield
        Zc, Nc = T("Z"), T("N")
        Zalt, Nalt = T("Zalt"), T("Nalt")
        ps = mm(Zc[:, :], X[0][:, :], [Zc.b, X[0].b], N=256)
        ev_tt(X[1], X[1][:, :], X[0][:, :], [X[0].b], ps, ALU.subtract, N=256)
        yield
        xi = 1
        for lev in range(5):
            Zn = Zalt
            ps = mm(Nc[:, :], Zc[:, :], [Nc.b, Zc.b])
            ev_copy(Zn, Zn[:, :], ps)
            yield
            if lev < 4:
                Nn = Nalt
                ps = mm(Zc[:, :], Nc[:, :], [Nc.b, Zc.b])
                ev_copy(Nn, Nn[:, :], ps)
                Nc, Nalt = Nn, Nc
                yield
            Zc, Zalt = Zn, Zc
            ps = mm(Zc[:, :], X[xi][:, :], [Zc.b, X[xi].b], N=256)
            ev_tt(X[1 - xi], X[1 - xi][:, :], X[xi][:, :], [X[xi].b], ps, ALU.add, N=256)
            xi = 1 - xi
            yield
        Xf = X[xi]
        P = Xf[:, 0:128]
        Q = Xf[:, 128:256]
        ps = mm(P, T("nBtT")[:, :], [Xf.b, T("nBtT").b])
        ev_tt(T("ET"), T("ET")[:, :], cmat[:, 0, :], [cmat.b], ps, ALU.add, full=True)
        yield
        ph = rb()
        mm(T("KtT")[:, :], T("VT")[:, :], [T("KtT").b, T("VT").b], acc=(ph, True, False))
        mm(T("nBtT")[:, :], Q, [T("nBtT").b, Xf.b], acc=(ph, False, True))
        ev_copy(T("Hg"), T("Hg")[:, :], ph, scale=gam, full=True)
        yield
        ps = mm(P, T("nArbT")[:, :], [Xf.b, T("nArbT").b])
        ev_tt(T("YmT"), T("YmT")[:, :], Rt, bdb, ps, ALU.add, full=True)
        yield
        M0 = Mst[useq % 2]
        M1 = Mst[(useq + 1) % 2]
        py = rb()
        mm(M0[:, :], T("YmT")[:, :], [M0.b, T("YmT").b], acc=(py, True, False), full=True)
        mm(T("VT")[:, :], T("ArkT")[:, :], [T("VT").b, T("ArkT").b], acc=(py, False, False))
        mm(Q, T("nArbT")[:, :], [Xf.b, T("nArbT").b], acc=(py, False, True))
        for h in range(2):
            hs = slice(h * 64, (h + 1) * 64)
            if not final:
                k.op("act", lambda e: e.activation(out=yf[hs, tok0:tok0 + 64], in_=py[hs, h * 64:(h + 1) * 64], func=AF.Copy),
                     reads=[py.b], writes=[yf.sb(tok0 // 512)])
            else:
                lc = (tok0 - (0 if tok0 < CTX else CTX)) % 512
                k.op("dve", lambda e: e.tensor_tensor(out=S["ysum"][hs, lc:lc + 64], in0=py[hs, h * 64:(h + 1) * 64],
                                                       in1=yf[hs, tok0:tok0 + 64], op=ALU.add),
                     reads=[py.b, yf.sb(tok0 // 512)], writes=[S["ysum"].b])
        pm = mm(T("ET")[:, :], M0[:, :], [T("ET").b, M0.b], full=True)
        k.op("dve", lambda e: e.scalar_tensor_tensor(out=M1[:, :], in0=pm[:, 0:128], scalar=gam, in1=T("Hg")[:, :], op0=ALU.mult, op1=ALU.add),
             reads=[pm.b, eLb[bdi].b, T("Hg").b], writes=[M1.b])
        yield

    def run_lockstep(gens, extra=None):
        pending = list(gens)
        active = []
        rounds = 0
        while pending or active:
            if pending and len(active) < WIDTH and (not active or rounds % STAGGER == 0):
                active.append(pending.pop(0))
            for g in list(active):
                try:
                    next(g)
                except StopIteration:
                    active.remove(g)
            if extra is not None and rounds >= 4:
                try:
                    next(extra)
                except StopIteration:
                    extra = None
            rounds += 1
        if extra is not None:
            for _ in extra:
                pass

    def finalize(pr, t0, nb):
        ys = S["ysum"]
        ps = rb()
        k.op("pe", lambda e: e.matmul(out=ps[:, 0:nb], lhsT=cmat[:, 1, :], rhs=ys[:, 0:nb], start=True, stop=True),
             reads=[cmat.b, ys.b], writes=[ps.b])
        k.op("dve", lambda e: e.scalar_tensor_tensor(out=S["u1"][:, 0:nb], in0=ps[:, 0:nb], scalar=-1.0 / 64, in1=ys[:, 0:nb],
                                                      op0=ALU.mult, op1=ALU.add), reads=[ps.b, ys.b], writes=[S["u1"].b])
        k.op("act", lambda e: e.activation(out=S["u2"][:, 0:nb], in_=S["u1"][:, 0:nb], func=AF.Square), reads=[S["u1"].b], writes=[S["u2"].b])
        ps = rb()
        k.op("pe", lambda e: e.matmul(out=ps[:, 0:nb], lhsT=cmat[:, 1, :], rhs=S["u2"][:, 0:nb], start=True, stop=True),
             reads=[cmat.b, S["u2"].b], writes=[ps.b])
        emit_rsqrt(k, S["u3"], ps, 1.0 / 64, GN_EPS, nb)
        k.op("dve", lambda e: e.tensor_tensor(out=S["u1"][:, 0:nb], in0=S["u1"][:, 0:nb], in1=S["u3"][:, 0:nb], op=ALU.mult),
             reads=[S["u1"].b, S["u3"].b], writes=[S["u1"].b])
        k.op("act", lambda e: e.activation(out=S["u1"][:, 0:nb], in_=S["u1"][:, 0:nb], func=AF.Identity,
                                           bias=pp[:, PP_GNB + pr:PP_GNB + pr + 1], scale=pp[:, PP_GNG + pr:PP_GNG + pr + 1]),
             reads=[S["u1"].b, pp.b], writes=[S["u1"].b])
        k.op("pool", lambda e: e.tensor_tensor(out=S["u2"][:, 0:nb], in0=S["a"][:, 0:nb], in1=S["a2"][:, 0:nb], op=ALU.add),
             reads=[S["a"].b, S["a2"].b], writes=[S["u2"].b])
        k.op("dve", lambda e: e.tensor_scalar(out=S["u2"][:, 0:nb], in0=S["u2"][:, 0:nb], scalar1=pp[:, PP_KA + pr:PP_KA + pr + 1],
                                               scalar2=c2[:, pr:pr + 1], op0=ALU.mult, op1=ALU.add),
             reads=[S["u2"].b, pp.b, c2.b], writes=[S["u2"].b])
        k.op("pool", lambda e: e.tensor_tensor(out=S["u2"][:, 0:nb], in0=S["u2"][:, 0:nb], in1=S["k"][:, 0:nb], op=ALU.mult),
             reads=[S["u2"].b, S["k"].b], writes=[S["u2"].b])
        k.op("dve", lambda e: e.scalar_tensor_tensor(out=S["u2"][:, 0:nb], in0=S["r"][:, 0:nb], scalar=pp[:, PP_RK + pr:PP_RK + pr + 1],
                                                      in1=S["u2"][:, 0:nb], op0=ALU.mult, op1=ALU.mult),
             reads=[S["r"].b, pp.b, S["u2"].b], writes=[S["u2"].b])
        ps = rb()
        k.op("pe", lambda e: e.matmul(out=ps[:, 0:nb], lhsT=cmat[:, 1, :], rhs=S["u2"][:, 0:nb], start=True, stop=True),
             reads=[cmat.b, S["u2"].b], writes=[ps.b])
        k.op("dve", lambda e: e.tensor_tensor(out=S["u3"][:, 0:nb], in0=ps[:, 0:nb], in1=S["v"][:, 0:nb], op=ALU.mult),
             reads=[ps.b, S["v"].b], writes=[S["u3"].b])
        k.op("pool", lambda e: e.tensor_tensor(out=S["u1"][:, 0:nb], in0=S["u1"][:, 0:nb], in1=S["u3"][:, 0:nb], op=ALU.add),
             reads=[S["u1"].b, S["u3"].b], writes=[S["u1"].b])
        for gi, gn_ in enumerate(("g0", "g1")):
            k.op("act", lambda e: e.activation(out=S[gn_][:, 0:nb], in_=S[gn_][:, 0:nb], func=AF.Sigmoid), reads=[S[gn_].b], writes=[S[gn_].b])
        ps = rb()
        for gi, gn_ in enumerate(("g0", "g1")):
            k.op("pe", lambda e: e.matmul(out=ps[:, 0:nb], lhsT=gup[:, gi, pr * 128:(pr + 1) * 128], rhs=S[gn_][:, 0:nb],
                                          start=(gi == 0), stop=(gi == 1)), reads=[gup.b, S[gn_].b], writes=[ps.b])
        os_ = ostage[ecount[0] % 2]
        ecount[0] += 1
        k.op("dve", lambda e: e.tensor_tensor(out=os_[:, 0:nb], in0=S["u1"][:, 0:nb], in1=ps[:, 0:nb], op=ALU.mult),
             reads=[S["u1"].b, ps.b], writes=[os_.b])
        out_write(k, E, os_, 0, 128, nb, [(128 + pr * 128, 0)], t0, nb)

    blocks_f = MBLK
    blocks_b = [MBLK[0]] + MBLK[:0:-1]
    bdc = 0
    for pr in range(2):
        for d in range(2):
            useq = 0
            k.op("pool", lambda e: e.memset(Mst[0][:], 0.0), writes=[Mst[0].b])
            blks = blocks_f if d == 0 else blocks_b
            if d == 0:
                for _ in prep(pr, blks[0][0], blks[0][1], d, False, bdc % 2):
                    pass
            for bi_, (t0, nb) in enumerate(blks):
                extra = None
                if d == 1:
                    for _ in prep(pr, t0, nb, d, True, bdc % 2):
                        pass
                elif bi_ + 1 < len(blks):
                    extra = prep(pr, blks[bi_ + 1][0], blks[bi_ + 1][1], d, False, (bdc + 1) % 2)
                nch = nb // 64
                order = list(range(nch)) if d == 0 else list(range(nch - 1, -1, -1))
                run_lockstep([unit(pr, t0 + c * 64, c, d, bdc % 2, useq + i, d == 1) for i, c in enumerate(order)], extra)
                useq += nch
                if d == 1:
                    finalize(pr, t0, nb)
                bdc += 1


def build_F():
    k = K()
    nc = k.nc
    E = Env()

    def din(name, shape, dt=F32):
        return nc.dram_tensor(name, list(shape), dt, kind="ExternalInput").ap()

    E.hT0d = din("hT0", [D, NT])
    E.ccTd = din("ccT", [D, 2])
    E.awqd = din("awq", [D, 96 * 128])
    E.abqd = din("abq", [128, 96])
    E.indd = din("ind", [128, 4])
    E.gn1d = din("gn1T", [128, DEPTH * NCH])
    E.gn2d = din("gn2T", [128, DEPTH * NCH])
    E.wind = din("win", [DEPTH, D, NW])
    E.ppd = din("pp", [DEPTH, 128, NPP])
    E.sinkd = din("sinkrow", [DEPTH, 1, 512])
    E.w0d = din("w0row", [DEPTH, 1, 512])
    E.wupd = din("wup", [DEPTH, 96, 512])
    E.aupd = din("aup", [DEPTH, 96, 512])
    E.gupd = din("gup", [DEPTH, 128, 2, 256])
    E.cosd = din("cosT", [128, TALL])
    E.sind = din("sinT", [128, TALL])
    E.cmatd = din("cmat", [128, 3, 128])
    E.wmaskd = din("wmask", [128, 4, 512], BF16)
    E.rmaskd = din("rmask", [128, 6, 128])
    E.trid = din("tri", [64, 4, 64])
    E.woutd = din("wout", [DEPTH, D, D])
    E.w1d = din("w1", [DEPTH, D, 4 * D])
    E.w2d = din("w2", [DEPTH, 4 * D, D])
    E.hOd = nc.dram_tensor("hTo", [D, NT], F32, kind="ExternalOutput").ap()
    E.mod_in = k.dram("mod_in", [128, 192], F32)
    E.mod_all = k.dram("mod_all", [512, 192], F32)
    E.u_in = [k.dram(f"u_in{i}", [n * 128, NT], BF16) for i, (c0, n) in enumerate(UPIECES)]
    E.u_all = [k.dram(f"u_all{i}", [4 * n * 128, NT], BF16) for i, (c0, n) in enumerate(UPIECES)]
    E.rs_in = k.dram("rs_in", [4 * D, NT], BF16)
    E.rs_out = k.dram("rs_out", [D, NT], BF16)
    E.h_spill = k.dram("h_spill", [D, NT], F32)
    E.pB = k.dram("pB", [PB_ROWS, PBW], F32)
    E.pbb = E.pB.b
    E.cst = emit_consts(k)
    E.modS = k.sbuf([128, 4, 96, 2], F32, "modS")
    E.mods = Mods(E.modS)
    E.ind = k.sbuf([128, 4], F32, "ind")
    E.gn1 = k.sbuf([128, DEPTH * NCH], F32, "gn1")
    E.gn2 = k.sbuf([128, DEPTH * NCH], F32, "gn2")
    E.pp = k.sbuf([128, NPP], F32, "pp")
    E.cmat = k.sbuf([128, 3, 128], F32, "cmat")
    E.wmask = k.sbuf([128, 4, 512], BF16, "wmask")
    E.rmask = k.sbuf([128, 6, 128], F32, "rmask")
    E.tri = k.sbuf([64, 4, 64], F32, "tri")
    E.m4c = [0]
    E.PS = k.psum([128, 8, 512], F32, "PSall")
    E.banks = [BankView(E.PS.t, i) for i in range(8)]
    emit_P0(k, E)
    emit_P1(k, E)
    for l in range(DEPTH):
        emit_B(k, E, l)
        k.collective("ReduceScatter", ALU.add, E.rs_in, E.rs_out, reads=[E.rs_in.b], writes=[E.rs_out.b])
        emit_C(k, E, l)
    return k.finish()


_PROG = {}


def _c(a):
    return np.ascontiguousarray(a)


def host_B_consts():
    ident = np.eye(128, dtype=np.float32)
    bo = np.zeros((128, 128), np.float32)
    bo[:64, :64] = 1
    bo[64:, 64:] = 1
    R = np.zeros((128, 128), np.float32)
    for m in range(128):
        if m % 64 < 32:
            R[m + 32, m] = -1.0
        else:
            R[m - 32, m] = 1.0
    cmat = _c(np.stack([ident, bo, R], 1))
    t = np.arange(SEQ)
    row = (t // 64).astype(np.float32)
    col = (t % 64).astype(np.float32)
    inv = (10000.0 ** (-np.arange(16, dtype=np.float32) / 16)).astype(np.float32)
    ang = np.concatenate([row[:, None] * inv, col[:, None] * inv], -1).astype(np.float32)
    cosT = np.ones((128, TALL), np.float32)
    sinT = np.zeros((128, TALL), np.float32)
    c = np.cos(ang).T.astype(np.float32)
    s = np.sin(ang).T.astype(np.float32)
    for p in range(128):
        cosT[p, CTX:] = c[p % 32]
        sinT[p, CTX:] = s[p % 32]
    wm = np.zeros((128, 4, 2, 256), np.float32)
    kl = np.arange(128)[:, None]
    ql = np.arange(256)[None, :]
    for r in range(4):
        d = ql - kl - (r - 1) * 128
        wm[:, r, :, :] = (np.abs(d) <= 128)[:, None, :]
    wmask = _c(wm.reshape(128, 4, 512).astype(ml_dtypes.bfloat16))
    i64 = np.arange(64)
    SL = (i64[:, None] > i64[None, :]).astype(np.float32)
    SU = SL.T.copy()
    UI = (i64[:, None] <= i64[None, :]).astype(np.float32)
    LI = UI.T.copy()
    rm = np.zeros((128, 6, 128), np.float32)
    for mi, mm_ in enumerate((SL, SU, UI, LI, -UI, -LI)):
        rm[:64, mi, :64] = mm_
        rm[64:, mi, 64:] = mm_
    tri = np.stack([UI, SU, LI, SL], 1).astype(np.float32) * np.float32(RW_SCALE)
    return {"cmat": cmat, "cosT": cosT, "sinT": sinT, "wmask": wmask, "rmask": _c(rm), "tri": _c(tri)}


A_IN_ = 768
B_IN_ = 3712


def host_core_inputs(I, core, consts):
    b, j = divmod(core, 4)
    kv = j // 2
    m = dict(consts)
    x, ctx = I["x"], I["ctx"]
    m["hT0"] = _c(np.concatenate([x[b, j * 1024:(j + 1) * 1024], ctx[b, j * 64:(j + 1) * 64]], 0).T)
    m["ccT"] = _c(np.stack([I["c"][b], I["c_ctx"]], 0).T)
    awq = np.empty((D, 96 * 128), np.float32)
    abq = np.empty((128, 96), np.float32)
    for l in range(DEPTH):
        for w in range(6):
            for cl in range(4):
                cc = (l * 6 + w) * 4 + cl
                g0 = w * D + (j * 4 + cl) * 128
                awq[:, cc * 128:(cc + 1) * 128] = I["ada_w"][l][:, g0:g0 + 128]
                abq[:, cc] = I["ada_b"][l][g0:g0 + 128]
    m["awq"] = awq
    m["abq"] = abq
    ind = np.zeros((128, 4), np.float32)
    ind[:, j] = 1.0
    m["ind"] = ind
    m["gn1T"] = _c(np.concatenate([I["norm1_g"][l].reshape(-1, 128).T for l in range(DEPTH)], 1))
    m["gn2T"] = _c(np.concatenate([I["norm2_g"][l].reshape(-1, 128).T for l in range(DEPTH)], 1))
    cols = []
    cols += list(range(2 * j * 64, (2 * j + 2) * 64))
    cols += list(range(512 + kv * 64, 512 + (kv + 1) * 64)) * 2
    cb = A_IN_ + B_IN_
    cols += list(range(cb + 2 * j * 64, cb + (2 * j + 2) * 64))
    cols += list(range(cb + 512 + kv * 64, cb + 512 + (kv + 1) * 64)) * 2
    cols += list(range(640 + kv * 64, 640 + (kv + 1) * 64))
    cols += list(range(cb + 640 + kv * 64, cb + 640 + (kv + 1) * 64))
    bb = A_IN_
    for part in range(3):
        cols += list(range(bb + part * 1024 + 4 * j * 64, bb + part * 1024 + (4 * j + 4) * 64))
    cols += list(range(bb + 3072, bb + 3712))
    assert len(cols) == NW
    m["win"] = _c(I["w_in"][:, :, cols])
    bcols = [c_ - bb for c_ in cols[640:]]
    hc = slice(4 * j * 64, (4 * j + 4) * 64)
    pp = np.zeros((DEPTH, 128, NPP), np.float32)
    for l in range(DEPTH):
        pp[l, :, PP_QKG + 0] = np.tile(I["a_q_norm"][l], 2)
        pp[l, :, PP_QKG + 1] = np.tile(I["a_k_norm"][l], 2)
        pp[l, :, PP_QKG + 2] = np.tile(I["c_q_norm"][l], 2)
        pp[l, :, PP_QKG + 3] = np.tile(I["c_k_norm"][l], 2)
        mu = I["shift_mu"][l]
        pos = 0
        for ci in range(5, 17):
            M = CHUNKS[ci][1]
            idx = bcols[pos:pos + M]
            pos += M
            pp[l, :M, PP_MU + (ci - 5) * 2 + 0] = mu[0][idx]
            pp[l, :M, PP_MU + (ci - 5) * 2 + 1] = mu[1][idx]
        for pr in range(2):
            sl = slice(4 * j * 64 + pr * 128, 4 * j * 64 + (pr + 1) * 128)
            for d in range(2):
                pp[l, :, PP_A0 + pr * 2 + d] = I["iclr_a0"][l][d][sl]
            pp[l, :, PP_KK + pr] = I["k_k"][l][sl]
            pp[l, :, PP_KA + pr] = I["k_a"][l][sl]
            pp[l, :, PP_RK + pr] = I["r_k"][l][sl]
            pp[l, :, PP_GNG + pr] = I["gn_g"][l][sl]
            pp[l, :, PP_GNB + pr] = I["gn_b"][l][sl]
    m["pp"] = pp
    m["sinkrow"] = _c(np.stack([np.repeat(I["a_sink"][l][2 * j:2 * j + 2], 256)[None, :] for l in range(DEPTH)]).astype(np.float32))
    m["w0row"] = _c(np.stack([np.concatenate([I["decay_w0"][l][d][hc] for d in range(2)])[None, :] for l in range(DEPTH)]))
    m["wup"] = _c(np.stack([np.concatenate([I["decay_up"][l][d][:, hc] for d in range(2)], 1) for l in range(DEPTH)]))
    m["aup"] = _c(np.stack([np.concatenate([I["iclr_up"][l][d][:, hc] for d in range(2)], 1) for l in range(DEPTH)]))
    m["gup"] = _c(np.stack([I["gate_up"][l][:, hc].reshape(2, 128, 256).transpose(1, 0, 2) for l in range(DEPTH)]))
    return m


def kernel(x, c, ctx, c_ctx, ada_w, ada_b, norm1_g, norm2_g, w_in, a_q_norm, a_k_norm, a_sink, c_q_norm, c_k_norm, shift_mu,
           decay_w0, decay_up, iclr_a0, iclr_up, gate_up, k_k, k_a, r_k, gn_g, gn_b, w_out, mlp_w1, mlp_w2):
    I = dict(x=x, c=c, ctx=ctx, c_ctx=c_ctx, ada_w=ada_w, ada_b=ada_b, norm1_g=norm1_g, norm2_g=norm2_g, w_in=w_in,
             a_q_norm=a_q_norm, a_k_norm=a_k_norm, a_sink=a_sink, c_q_norm=c_q_norm, c_k_norm=c_k_norm, shift_mu=shift_mu,
             decay_w0=decay_w0, decay_up=decay_up, iclr_a0=iclr_a0, iclr_up=iclr_up, gate_up=gate_up, k_k=k_k, k_a=k_a, r_k=r_k,
             gn_g=gn_g, gn_b=gn_b, w_out=w_out, mlp_w1=mlp_w1, mlp_w2=mlp_w2)
    I = {k_: np.asarray(v, dtype=np.float32) for k_, v in I.items()}
    consts = host_B_consts()
    perm = []
    for r in range(4):
        perm += list(range(2 * r * 64, 2 * r * 64 + 128)) + list(range(512 + 4 * r * 64, 512 + 4 * r * 64 + 256)) \
            + list(range(1536 + 2 * r * 64, 1536 + 2 * r * 64 + 128))
    shared = {"wout": _c(I["w_out"][:, perm, :]), "w1": I["mlp_w1"], "w2": I["mlp_w2"]}
    maps = []
    for core in range(8):
        m = host_core_inputs(I, core, consts)
        m.update(shared)
        maps.append(m)
    if "F" not in _PROG:
        _PROG["F"] = build_F()
    res = run_bass_kernel_spmd(_PROG["F"], maps, core_ids=list(range(8))).results
    out = np.empty((2, SEQ, D), np.float32)
    for core in range(8):
        b, j = divmod(core, 4)
        out[b, j * 1024:(j + 1) * 1024] = res[core]["hTo"][:, 0:1024].T
    return out
```

```python
import numpy as np
import ml_dtypes
from contextlib import ExitStack
import concourse.bass as bass
import concourse.mybir as mybir
from concourse.bass_utils import run_bass_kernel_spmd

F32 = mybir.dt.float32
BF16 = mybir.dt.bfloat16
AF = mybir.ActivationFunctionType
ALU = mybir.AluOpType
AX = mybir.AxisListType

D = 2048
NCH = 16
SEQ = 4096
CTX = 256
DEPTH = 4
NT = 1088
BLK = [(0, 512, 0), (512, 512, 0), (1024, 64, 1)]
TALL = SEQ + CTX
NORM_EPS = 1e-6
GN_EPS = 64e-5
SEM_LIMIT = 30000
GROUPS = [[0, 1, 2, 3], [4, 5, 6, 7]]
UPIECES = [(0, 3), (3, 3), (6, 3), (9, 3), (12, 3), (15, 1)]


class Buf:
    __slots__ = ("w", "r", "excl")

    def __init__(self, excl=False):
        self.w = None
        self.r = {}
        self.excl = excl


class Tile:
    def __init__(self, t):
        self.t = t
        self.b = Buf()
        self._sub = {}

    def sb(self, key):
        b = self._sub.get(key)
        if b is None:
            b = self._sub[key] = Buf()
        return b

    def __getitem__(self, idx):
        return self.t[idx]


class BankView:
    def __init__(self, t, i):
        self.t = t
        self.i = i
        self.b = Buf(excl=True)

    def __getitem__(self, idx):
        p, f = idx
        return self.t[p, self.i, f]


class K:
    def __init__(self):
        self.nc = bass.Bass("TRN2", target_bir_lowering=False)
        nc = self.nc
        self.ctx = ExitStack()
        self.engs = {"pe": nc.tensor, "dve": nc.vector, "act": nc.scalar, "pool": nc.gpsimd, "sp": nc.sync}
        self.cur = {}
        self.waited = {e: {} for e in self.engs}
        self.nsem = 0
        for e in self.engs:
            self._new_sem(e)
        self.dpool = {}
        self.dcnt = {}
        for q in ("sp", "pool", "act"):
            self.dpool[q] = []
            for i in range(12):
                nm = f"d_{q}_{i}"
                self.dpool[q].append([self.ctx.enter_context(nc.semaphore(nm)), nm, 0])
            self.dcnt[q] = 0
        self.cpool = [[self.ctx.enter_context(nc.semaphore(f"cc_{i}")), f"cc_{i}", 0] for i in range(6)]
        self.ccnt = 0
        self.out_toks = []
        self.nuniq = 0
        self.phase = None

    def _new_sem(self, e):
        nm = f"c_{e}_{self.nsem}"
        self.nsem += 1
        self.cur[e] = [self.ctx.enter_context(self.nc.semaphore(nm)), nm, 0]

    def sbuf(self, shape, dt, name=None):
        self.nuniq += 1
        ctx = self.phase if self.phase is not None else self.ctx
        return Tile(ctx.enter_context(self.nc.sbuf_tensor(f"s_{name or 'sb'}_{self.nuniq}", list(shape), dt)))

    def barrier(self):
        toks = [(c[1], c[0], c[2]) for c in self.cur.values() if c[2] > 0]
        for q in self.dpool:
            toks += [(sl[1], sl[0], sl[2]) for sl in self.dpool[q] if sl[2] > 0]
        toks += [(sl[1], sl[0], sl[2]) for sl in self.cpool if sl[2] > 0]
        for eng, e in self.engs.items():
            for nm, sem, val in toks:
                if nm == self.cur[eng][1]:
                    continue
                if self.waited[eng].get(nm, 0) < val:
                    e.wait_ge(sem, val)
                    self.waited[eng][nm] = val

    def psum(self, shape, dt, name=None):
        self.nuniq += 1
        t = Tile(self.ctx.enter_context(self.nc.psum_tensor("p_" + (name or f"ps{self.nuniq}"), list(shape), dt)))
        t.b.excl = True
        return t

    def dram(self, name, shape, dt, kind="Internal"):
        return Tile(self.nc.dram_tensor(name, list(shape), dt, kind=kind))

    def _deps(self, eng, reads, writes):
        deps = {}

        def add(t):
            if t is None:
                return
            o = deps.get(t[0])
            if o is None or o[2] < t[2]:
                deps[t[0]] = t

        for b in reads:
            add(b.w)
            if b.excl:
                for kk, t in b.r.items():
                    if kk != eng:
                        add(t)
        for b in writes:
            add(b.w)
            for t in b.r.values():
                add(t)
        e = self.engs[eng]
        wd = self.waited[eng]
        for nm, (_, sem, val) in deps.items():
            if eng == "pe" and nm == self.cur["pe"][1]:
                continue
            if wd.get(nm, 0) >= val:
                continue
            e.wait_ge(sem, val)
            wd[nm] = val

    def _mark(self, key, tok, reads, writes):
        for b in writes:
            b.w = tok
            b.r = {}
        for b in reads:
            if b.w is not tok:
                b.r[key] = tok

    def op(self, eng, fn, reads=(), writes=()):
        self._deps(eng, reads, writes)
        inst = fn(self.engs[eng])
        c = self.cur[eng]
        if c[2] >= SEM_LIMIT:
            self._new_sem(eng)
            c = self.cur[eng]
        inst.then_inc(c[0], 1)
        c[2] += 1
        tok = (c[1], c[0], c[2])
        self._mark(eng, tok, reads, writes)
        return tok

    def dma(self, q, out, in_, reads=(), writes=(), is_out=False, **kw):
        self._deps(q, reads, writes)
        e = self.engs[q]
        slot = self.dpool[q][self.dcnt[q] % len(self.dpool[q])]
        self.dcnt[q] += 1
        if slot[2] > 0 and self.waited[q].get(slot[1], 0) < slot[2]:
            e.wait_ge(slot[0], slot[2])
            self.waited[q][slot[1]] = slot[2]
        inst = e.dma_start(out=out, in_=in_, **kw)
        inst.then_inc(slot[0], 16)
        slot[2] += 16
        tok = (slot[1], slot[0], slot[2])
        self._mark(slot[1], tok, reads, writes)
        if is_out:
            self.out_toks.append(tok)
        return tok

    def collective(self, kind, alu, in_t, out_t, reads, writes):
        self._deps("pool", reads, writes)
        e = self.engs["pool"]
        slot = self.cpool[self.ccnt % len(self.cpool)]
        self.ccnt += 1
        if slot[2] > 0 and self.waited["pool"].get(slot[1], 0) < slot[2]:
            e.wait_ge(slot[0], slot[2])
            self.waited["pool"][slot[1]] = slot[2]
        inst = e.collective_compute(kind, alu, replica_groups=GROUPS, ins=[in_t.t.ap().opt()], outs=[out_t.t.ap().opt()])
        inst.then_inc(slot[0], 1)
        slot[2] += 1
        tok = (slot[1], slot[0], slot[2])
        self._mark(slot[1], tok, reads, writes)
        return tok

    def finish(self):
        e = self.engs["sp"]
        for nm, sem, val in self.out_toks:
            if self.waited["sp"].get(nm, 0) < val:
                e.wait_ge(sem, val)
                self.waited["sp"][nm] = val
        self.ctx.close()
        return self.nc


def emit_consts(k):
    c = {}
    c["ones_bf"] = k.sbuf([128, 128], BF16, "ones_bf")
    k.op("pool", lambda e: e.memset(c["ones_bf"][:], 1.0), writes=[c["ones_bf"].b])
    return c


def emit_rsqrt(k, out, src, scale, eps, n, p0=0, p1=128):
    k.op("dve", lambda e: e.tensor_scalar(out=out[p0:p1, 0:n], in0=src[p0:p1, 0:n], scalar1=float(scale), scalar2=float(eps),
                                           op0=ALU.mult, op1=ALU.add), reads=[src.b], writes=[out.b])
    k.op("act", lambda e: e.activation(out=out[p0:p1, 0:n], in_=out[p0:p1, 0:n], func=AF.Ln), reads=[out.b], writes=[out.b])
    k.op("act", lambda e: e.activation(out=out[p0:p1, 0:n], in_=out[p0:p1, 0:n], func=AF.Exp, scale=-0.5), reads=[out.b], writes=[out.b])


class Env:
    pass


class Mods:
    def __init__(self, t):
        self.t = t
        self.b = t.b

    def ap(self, l, lc, w, m):
        return self.t[:, m // 4, (l * 6 + w) * 4 + m % 4, lc:lc + 1]

    def vec(self, l, lc, w):
        return self.t[:, :, (l * 6 + w) * 4:(l * 6 + w) * 4 + 4, lc]


def emit_modprep(k, mods, l, gn_ap, gn_b, which_sc, name):
    gsc = k.sbuf([128, 2, NCH], F32, name + "_gsc")
    for lc in range(2):
        ov = gsc[:, lc, :].rearrange("p (a b) -> p a b", a=4)
        k.op("dve", lambda e: e.tensor_scalar(out=ov, in0=mods.vec(l, lc, which_sc), scalar1=1.0, scalar2=None, op0=ALU.add),
             reads=[mods.b], writes=[gsc.b])
        k.op("dve", lambda e: e.tensor_tensor(out=ov, in0=ov, in1=gn_ap.rearrange("p (a b) -> p a b", a=4), op=ALU.mult),
             reads=[gsc.b, gn_b], writes=[gsc.b])
    return gsc


def emit_norm_mod(k, cst, hT, gsc, mods, l, which_sh, out_bf, sq, stat_ps, rstd):
    for bi, (t0, nb, lc) in enumerate(BLK):
        hb = [hT.sb((m, bi)) for m in range(NCH)]
        for m in range(NCH):
            k.op("act", lambda e: e.activation(out=sq[:, m % 8, 0:nb], in_=hT[:, m, t0:t0 + nb], func=AF.Square),
                 reads=[hb[m]], writes=[sq.sb(m % 8)])
            k.op("pe", lambda e: e.matmul(out=stat_ps[:, 0:nb], lhsT=cst["ones_bf"][:, :], rhs=sq[:, m % 8, 0:nb],
                                          start=(m == 0), stop=(m == NCH - 1)),
                 reads=[sq.sb(m % 8), cst["ones_bf"].b], writes=[stat_ps.b])
        emit_rsqrt(k, rstd, stat_ps, 1.0 / D, NORM_EPS, nb)
        for m in range(NCH):
            tt = k.tmpn[m % 2]
            k.op("dve", lambda e: e.scalar_tensor_tensor(out=tt[:, 0:nb], in0=hT[:, m, t0:t0 + nb], scalar=gsc[:, lc, m:m + 1],
                                                          in1=rstd[:, 0:nb], op0=ALU.mult, op1=ALU.mult),
                 reads=[hb[m], gsc.b, rstd.b], writes=[tt.b])
            k.op("act", lambda e: e.activation(out=out_bf[:, m, t0:t0 + nb], in_=tt[:, 0:nb], func=AF.Identity,
                                               bias=mods.ap(l, lc, which_sh, m), scale=1.0),
                 reads=[tt.b, mods.b], writes=[out_bf.sb((m, bi))])


def end_phase(k):
    k.barrier()
    k.phase.close()
    k.phase = None


def emit_u_exchange(k, E, ubf):
    for pc, (c0, n) in enumerate(UPIECES):
        k.dma("sp", E.u_in[pc].t.ap().rearrange("(c p) t -> p c t", p=128), ubf[:, c0:c0 + n, :],
              reads=[ubf.sb((m, bi)) for m in range(c0, c0 + n) for bi in range(3)], writes=[E.u_in[pc].b])
        k.collective("AllGather", ALU.bypass, E.u_in[pc], E.u_all[pc], reads=[E.u_in[pc].b], writes=[E.u_all[pc].b])


def emit_P0(k, E):
    k.phase = ExitStack()
    sc = k.sbuf([128, NCH, 2], F32)
    scb = k.sbuf([128, NCH, 2], BF16)
    abq = k.sbuf([128, 96], F32)
    modP = k.sbuf([128, 96, 2], F32)
    wb = [k.sbuf([128, NCH, 512], BF16) for _ in range(2)]
    k.dma("sp", sc[:], E.ccTd.rearrange("(c p) r -> p c r", p=128), writes=[sc.b])
    k.dma("sp", abq[:], E.abqd[:, :], writes=[abq.b])
    k.dma("sp", E.ind[:], E.indd[:, :], writes=[E.ind.b])
    k.dma("sp", E.gn1[:], E.gn1d[:, :], writes=[E.gn1.b])
    k.dma("sp", E.gn2[:], E.gn2d[:, :], writes=[E.gn2.b])
    k.dma("sp", E.cmat[:], E.cmatd[:, :, :], writes=[E.cmat.b])
    k.dma("sp", E.wmask[:], E.wmaskd[:, :, :], writes=[E.wmask.b])
    k.dma("sp", E.rmask[:], E.rmaskd[:, :, :], writes=[E.rmask.b])
    k.dma("sp", E.tri[:], E.trid[:, :, :], writes=[E.tri.b])
    k.op("act", lambda e: e.activation(out=scb[:], in_=sc[:], func=AF.Silu), reads=[sc.b], writes=[scb.b])
    awv = E.awqd.rearrange("(c p) n -> p c n", p=128)
    for g in range(24):
        w = wb[g % 2]
        k.dma("pool", w[:], awv[:, :, g * 512:(g + 1) * 512], writes=[w.b])
        for q in range(4):
            cc = g * 4 + q
            p = E.banks[cc % 2]
            for c in range(NCH):
                k.op("pe", lambda e: e.matmul(out=p[:, 0:2], lhsT=w[:, c, q * 128:(q + 1) * 128], rhs=scb[:, c, :],
                                              start=(c == 0), stop=(c == NCH - 1)), reads=[scb.b, w.b], writes=[p.b])
            k.op("dve", lambda e: e.tensor_scalar(out=modP[:, cc, :], in0=p[:, 0:2], scalar1=abq[:, cc:cc + 1], scalar2=None, op0=ALU.add),
                 reads=[p.b, abq.b], writes=[modP.b])
    k.dma("sp", E.mod_in.t.ap(), modP[:, :, :].rearrange("p a b -> p (a b)"), reads=[modP.b], writes=[E.mod_in.b])
    k.collective("AllGather", ALU.bypass, E.mod_in, E.mod_all, reads=[E.mod_in.b], writes=[E.mod_all.b])
    k.dma("sp", E.modS[:, :, :, :].rearrange("p r a b -> p r (a b)"), E.mod_all.t.ap().rearrange("(r p) n -> p r n", p=128),
          reads=[E.mod_all.b], writes=[E.modS.b])
    end_phase(k)


def emit_P1(k, E):
    k.phase = ExitStack()
    hT = k.sbuf([128, NCH, NT], F32, "hT")
    ubf = k.sbuf([128, NCH, NT], BF16, "ubf")
    sq = k.sbuf([128, 8, 512], BF16, "sq")
    k.tmpn = [k.sbuf([128, 512], F32) for _ in range(2)]
    rstd = k.sbuf([128, 512], F32, "rstd")
    hv = E.hT0d.rearrange("(c p) t -> p c t", p=128)
    for bi, (t0, nb, lc) in enumerate(BLK):
        k.dma("sp", hT[:, :, t0:t0 + nb], hv[:, :, t0:t0 + nb], writes=[hT.sb((m, bi)) for m in range(NCH)])
    gsc = emit_modprep(k, E.mods, 0, E.gn1[:, 0:NCH], E.gn1.b, 1, "n1")
    emit_norm_mod(k, E.cst, hT, gsc, E.mods, 0, 0, ubf, sq, E.banks[6], rstd)
    emit_u_exchange(k, E, ubf)
    end_phase(k)


def emit_C(k, E, l):
    k.phase = ExitStack()
    last = l == DEPTH - 1
    hT = k.sbuf([128, NCH, NT], F32, "hT")
    abf = k.sbuf([128, NCH, NT], BF16, "abf")
    h1 = k.sbuf([128, NCH, NT], BF16, "h1")
    sq = k.sbuf([128, 8, 512], BF16, "sq")
    k.tmpn = [k.sbuf([128, 512], F32) for _ in range(2)]
    rtmp = [k.sbuf([128, 512], F32) for _ in range(2)]
    rstd = k.sbuf([128, 512], F32, "rstd")
    wbuf = [k.sbuf([128, NCH, 256], BF16) for _ in range(3)]
    banks = [[E.banks[i * 3 + jj] for jj in range(3)] for i in range(2)]
    stat = E.banks[6]
    mods = E.mods
    hsrc = E.hT0d if l == 0 else E.h_spill.t.ap()
    hv = hsrc.rearrange("(c p) t -> p c t", p=128)
    for hf, rs_t in enumerate((E.rs_outAC, E.rs_outB)):
        ov = rs_t.t.ap().rearrange("(c p) t -> p c t", p=128)
        for bi, (t0, nb, lc) in enumerate(BLK):
            k.dma("sp", abf[:, hf * 8:(hf + 1) * 8, t0:t0 + nb], ov[:, :, t0:t0 + nb], reads=[rs_t.b],
                  writes=[abf.sb((m, bi)) for m in range(hf * 8, (hf + 1) * 8)])
    for bi, (t0, nb, lc) in enumerate(BLK):
        k.dma("sp", hT[:, :, t0:t0 + nb], hv[:, :, t0:t0 + nb], reads=[E.h_spill.b], writes=[hT.sb((m, bi)) for m in range(NCH)])
    wcnt = [0]

    def load_w(src_ap):
        w = wbuf[wcnt[0] % 3]
        wcnt[0] += 1
        k.dma("pool", w[:], src_ap, writes=[w.b])
        return w

    def proj(w, mi, m, src, gate_idx):
        bk = banks[m % 2]
        for bi, (t0, nb, lc) in enumerate(BLK):
            for kc in range(NCH):
                k.op("pe", lambda e: e.matmul(out=bk[bi][:, 0:nb], lhsT=w[:, kc, mi * 128:(mi + 1) * 128], rhs=src[:, kc, t0:t0 + nb],
                                              start=(kc == 0), stop=(kc == NCH - 1)),
                     reads=[w.b, src.sb((kc, bi))], writes=[bk[bi].b])
            k.op("dve", lambda e: e.scalar_tensor_tensor(out=hT[:, m, t0:t0 + nb], in0=bk[bi][:, 0:nb],
                                                          scalar=mods.ap(l, lc, gate_idx, m), in1=hT[:, m, t0:t0 + nb],
                                                          op0=ALU.mult, op1=ALU.add),
                 reads=[bk[bi].b, mods.b, hT.sb((m, bi))], writes=[hT.sb((m, bi))])

    woutv = E.woutd[l].rearrange("(c p) n -> p c n", p=128)
    for mp in range(8):
        w = load_w(woutv[:, :, mp * 256:(mp + 1) * 256])
        for mi in range(2):
            proj(w, mi, mp * 2 + mi, abf, 2)
    gsc2 = emit_modprep(k, mods, l, E.gn2[:, l * NCH:(l + 1) * NCH], E.gn2.b, 4, f"n2_{l}")
    emit_norm_mod(k, E.cst, hT, gsc2, mods, l, 3, abf, sq, stat, rstd)
    w1v = E.w1d[l].rearrange("(c p) n -> p c n", p=128)
    w2v = E.w2d[l].rearrange("(c p) n -> p c n", p=128)
    ecnt = 0
    for q in range(4):
        for fp in range(8):
            f0 = (q * 16 + fp * 2) * 128
            w = load_w(w1v[:, :, f0:f0 + 256])
            for fi in range(2):
                fl = fp * 2 + fi
                bk = banks[fl % 2]
                for bi, (t0, nb, lc) in enumerate(BLK):
                    for kc in range(NCH):
                        k.op("pe", lambda e: e.matmul(out=bk[bi][:, 0:nb], lhsT=w[:, kc, fi * 128:(fi + 1) * 128],
                                                      rhs=abf[:, kc, t0:t0 + nb], start=(kc == 0), stop=(kc == NCH - 1)),
                             reads=[w.b, abf.sb((kc, bi))], writes=[bk[bi].b])
                    rt = rtmp[ecnt % 2]
                    ecnt += 1
                    k.op("act", lambda e: e.activation(out=rt[:, 0:nb], in_=bk[bi][:, 0:nb], func=AF.Relu),
                         reads=[bk[bi].b], writes=[rt.b])
                    k.op("dve", lambda e: e.tensor_tensor(out=h1[:, fl, t0:t0 + nb], in0=rt[:, 0:nb], in1=rt[:, 0:nb], op=ALU.mult),
                         reads=[rt.b], writes=[h1.sb((fl, bi))])
        for mp in range(8):
            w = load_w(w2v[:, q * 16:(q + 1) * 16, mp * 256:(mp + 1) * 256])
            for mi in range(2):
                proj(w, mi, mp * 2 + mi, h1, 5)
    if last:
        hov = E.hOd.rearrange("(c p) t -> p c t", p=128)
        for bi, (t0, nb, lc) in enumerate(BLK):
            k.dma("sp", hov[:, :, t0:t0 + nb], hT[:, :, t0:t0 + nb], reads=[hT.sb((m, bi)) for m in range(NCH)], is_out=True)
    else:
        hsv = E.h_spill.t.ap().rearrange("(c p) t -> p c t", p=128)
        for bi, (t0, nb, lc) in enumerate(BLK):
            k.dma("sp", hsv[:, :, t0:t0 + nb], hT[:, :, t0:t0 + nb], reads=[hT.sb((m, bi)) for m in range(NCH)], writes=[E.h_spill.b])
        gsc1 = emit_modprep(k, mods, l + 1, E.gn1[:, (l + 1) * NCH:(l + 2) * NCH], E.gn1.b, 1, f"n1_{l}")
        emit_norm_mod(k, E.cst, hT, gsc1, mods, l + 1, 0, h1, sq, stat, rstd)
        emit_u_exchange(k, E, h1)
    end_phase(k)


def out_write(k, E, src, p0, p1, ncols, segs, t0, n):
    m4 = E.m4[E.m4c[0] % 2]
    E.m4c[0] += 1
    for r in range(4):
        if r % 2 == 0:
            k.op("dve", lambda e: e.tensor_scalar(out=m4[p0:p1, r, 0:ncols], in0=src[p0:p1, 0:ncols], scalar1=E.ind[p0:p1, r:r + 1],
                                                   scalar2=None, op0=ALU.mult), reads=[src.b, E.ind.b], writes=[m4.b])
        else:
            k.op("act", lambda e: e.activation(out=m4[p0:p1, r, 0:ncols], in_=src[p0:p1, 0:ncols], func=AF.Copy,
                                               scale=E.ind[p0:p1, r:r + 1]), reads=[src.b, E.ind.b], writes=[m4.b])
    rs4s = {w_: t_.t.ap().rearrange("(j r q) t -> j q r t", j=4, r=4) for w_, t_ in (("AC", E.rs_inAC), ("B", E.rs_inB))}
    if t0 < CTX:
        pieces = [(jj, 1024, 64, jj * 64) for jj in range(4)]
    else:
        lat = t0 - CTX
        pieces = [(lat // 1024, lat % 1024, n, 0)]
    P = p1 - p0
    for row0, coff in segs:
        if row0 < 128:
            which, rr0 = "AC", row0
        elif row0 >= 384:
            which, rr0 = "AC", row0 - 256
        else:
            which, rr0 = "B", row0 - 128
        dst_t = E.rs_inAC if which == "AC" else E.rs_inB
        for jj, lcol, cnt, so in pieces:
            k.dma("sp", rs4s[which][jj][rr0:rr0 + P, :, lcol:lcol + cnt], m4[p0:p1, :, coff + so:coff + so + cnt],
                  reads=[m4.b], writes=[dst_t.b])


NW = 2048
CHUNKS = [(i * 128, 128) for i in range(11)] + [(1408 + i * 96, 96) for i in range(4)] + [(1792, 128), (1920, 128)]
PB_ROWS = 1408
PB_ROWOFF = [0, 128, 256, 384, 512, 640, 768, 864, 960, 1056, 1152, 1280]
PBW = TALL + 4
MBLK = [(0, 256)] + [(256 + i * 512, 512) for i in range(8)]
ATT_SCALE = 0.125
PP_QKG = 0
PP_MU = 4
PP_A0 = 28
PP_KK = 32
PP_KA = 34
PP_RK = 36
PP_GNG = 38
PP_GNB = 40
NPP = 42


def pb_col(t):
    return t + 1 if t < CTX else t + 3


def emit_B(k, E, l):
    nc = k.nc
    pB = E.pB
    pp, cmat, wmask, banks, PS = E.pp, E.cmat, E.wmask, E.banks, E.PS
    k.phase = ExitStack()
    E.m4 = [k.sbuf([128, 4, 512], BF16, f"m4_{i}") for i in range(2)]
    wres = k.sbuf([128, NCH, NW], BF16, "wres")
    bones = k.sbuf([128, 128], BF16, "bones")
    QK = [k.sbuf([128, TALL], BF16, f"qk{i}") for i in range(4)]
    V = [k.sbuf([128, 34, 65], BF16, f"v{i}") for i in range(2)]
    ub = [k.sbuf([128, NCH, 512], BF16, f"ub{i}") for i in range(2)]
    cs = [[k.sbuf([128, 512], F32, f"cs{i}{j}") for j in range(2)] for i in range(2)]
    x32 = [k.sbuf([128, 512], F32, "x32_0")] * 2
    sqb = [k.sbuf([128, 512], BF16, "sqb0")] * 2
    rs = [k.sbuf([128, 512], F32, "rs0")] * 2
    xn = [k.sbuf([128, 512], F32, "xn0")] * 2
    t1 = [k.sbuf([128, 512], F32, "t1_0")] * 2
    t2 = [k.sbuf([128, 512], F32, "t2_0")] * 2
    stg = [k.sbuf([128, 512], F32, f"stg{i}") for i in range(3)]
    zero = k.sbuf([128, 16], F32, "zero")

    k.dma("sp", pp[:], E.ppd[l], writes=[pp.b])
    k.op("pool", lambda e: e.memset(zero[:], 0.0), writes=[zero.b])
    k.op("dve", lambda e: e.tensor_copy(out=bones[:], in_=cmat[:, 1, :]), reads=[cmat.b], writes=[bones.b])
    for i in range(2):
        k.op("pool", lambda e: e.memset(V[i][:, :, 64:65], 1.0), writes=[V[i].b])
    pbb = E.pbb
    for col in ((0, 257, 258, PBW - 1) if l == 0 else ()):
        k.dma("sp", pB.t.ap()[:, col:col + 1].rearrange("(c p) o -> p c o", p=128), zero[:, 0:11].rearrange("p (c o) -> p c o", o=1),
              reads=[zero.b], writes=[pbb], allow_slow_non_contiguous=True)
    wv = E.wind[l].rearrange("(c p) n -> p c n", p=128)
    for i in range(4):
        k.dma("pool", wres[:, :, i * 512:(i + 1) * 512], wv[:, :, i * 512:(i + 1) * 512], writes=[wres.sb(i)])
    ident = cmat

    ecnt = [0]
    bk = [0]

    def nbank():
        b = banks[bk[0] % 4]
        bk[0] += 1
        return b

    for bi, (t0, nb) in enumerate(MBLK):
        u = ub[bi % 2]
        if t0 < CTX:
            srcs = [(rr, 1024, 64, rr * 64) for rr in range(4)]
        else:
            srcs = [((t0 - CTX) // 1024, (t0 - CTX) % 1024, nb, 0)]
        for pc, (c0_, n_) in enumerate(UPIECES):
            ua = E.u_all[pc].t.ap().rearrange("(r c p) t -> r p c t", r=4, p=128)
            for rr, scol, cnt, dcol in srcs:
                k.dma("sp", u[:, c0_:c0_ + n_, dcol:dcol + cnt], ua[rr][:, :, scol:scol + cnt], reads=[E.u_all[pc].b], writes=[u.b])
        cost, sint = cs[bi % 2]
        k.dma("sp", cost[:, 0:nb], E.cosd[:, t0:t0 + nb], writes=[cost.b])
        k.dma("sp", sint[:, 0:nb], E.sind[:, t0:t0 + nb], writes=[sint.b])
        for ci, (c0, M) in enumerate(CHUNKS):
            ps = nbank()
            for kc in range(NCH):
                k.op("pe", lambda e: e.matmul(out=ps[0:M, 0:nb], lhsT=wres[:, kc, c0:c0 + M], rhs=u[:, kc, 0:nb],
                                              start=(kc == 0), stop=(kc == NCH - 1)),
                     reads=[wres.sb(c0 // 512), wres.sb((c0 + M - 1) // 512), u.b], writes=[ps.b])
            if ci < 4:
                i2 = ecnt[0] % 2
                ecnt[0] += 1
                xx, sq, rr, xnn, a1, a2 = x32[i2], sqb[i2], rs[i2], xn[i2], t1[i2], t2[i2]
                k.op("act", lambda e: e.activation(out=sq[:, 0:nb], in_=ps[:, 0:nb], func=AF.Square), reads=[ps.b], writes=[sq.b])
                k.op("dve", lambda e: e.tensor_scalar(out=xx[:, 0:nb], in0=ps[:, 0:nb], scalar1=pp[:, PP_QKG + ci:PP_QKG + ci + 1],
                                                       scalar2=None, op0=ALU.mult), reads=[ps.b, pp.b], writes=[xx.b])
                ps2 = nbank()
                k.op("pe", lambda e: e.matmul(out=ps2[:, 0:nb], lhsT=bones[:, :], rhs=sq[:, 0:nb], start=True, stop=True),
                     reads=[bones.b, sq.b], writes=[ps2.b])
                emit_rsqrt(k, rr, ps2, 1.0 / 64, NORM_EPS, nb)
                k.op("dve", lambda e: e.tensor_tensor(out=xnn[:, 0:nb], in0=xx[:, 0:nb], in1=rr[:, 0:nb], op=ALU.mult),
                     reads=[xx.b, rr.b], writes=[xnn.b])
                ps3 = nbank()
                k.op("pe", lambda e: e.matmul(out=ps3[:, 0:nb], lhsT=cmat[:, 2, :], rhs=xnn[:, 0:nb], start=True, stop=True),
                     reads=[cmat.b, xnn.b], writes=[ps3.b])
                k.op("pool", lambda e: e.tensor_tensor(out=a1[:, 0:nb], in0=xnn[:, 0:nb], in1=cost[:, 0:nb], op=ALU.mult),
                     reads=[xnn.b, cost.b], writes=[a1.b])
                k.op("dve", lambda e: e.tensor_tensor(out=a2[:, 0:nb], in0=ps3[:, 0:nb], in1=sint[:, 0:nb], op=ALU.mult),
                     reads=[ps3.b, sint.b], writes=[a2.b])
                k.op("pool", lambda e: e.tensor_tensor(out=QK[ci][:, t0:t0 + nb], in0=a1[:, 0:nb], in1=a2[:, 0:nb], op=ALU.add),
                     reads=[a1.b, a2.b], writes=[QK[ci].sb(bi)])
            elif ci == 4:
                i2 = ecnt[0] % 2
                ecnt[0] += 1
                xx = x32[i2]
                k.op("act", lambda e: e.activation(out=xx[:, 0:nb], in_=ps[:, 0:nb], func=AF.Copy), reads=[ps.b], writes=[xx.b])
                for tt in range(nb // 128):
                    pt = nbank()
                    k.op("pe", lambda e: e.transpose(out=pt[:, 0:128], in_=xx[:, tt * 128:(tt + 1) * 128], identity=cmat[:, 0, :]),
                         reads=[xx.b, cmat.b], writes=[pt.b])
                    tile = t0 // 128 + tt
                    k.op("act", lambda e: e.activation(out=V[0][:, tile, 0:64], in_=pt[:, 0:64], func=AF.Copy),
                         reads=[pt.b], writes=[V[0].sb(tile)])
                    k.op("dve", lambda e: e.tensor_copy(out=V[1][:, tile, 0:64], in_=pt[:, 64:128]),
                         reads=[pt.b], writes=[V[1].sb(tile)])
            else:
                s = stg[ecnt[0] % 3]
                ecnt[0] += 1
                if ecnt[0] % 2:
                    k.op("act", lambda e: e.activation(out=s[0:M, 0:nb], in_=ps[0:M, 0:nb], func=AF.Copy), reads=[ps.b], writes=[s.b])
                else:
                    k.op("dve", lambda e: e.tensor_copy(out=s[0:M, 0:nb], in_=ps[0:M, 0:nb]), reads=[ps.b], writes=[s.b])
                r0 = PB_ROWOFF[ci - 5]
                k.dma("sp", pB.t.ap()[r0:r0 + M, pb_col(t0):pb_col(t0) + nb], s[0:M, 0:nb], reads=[s.b], writes=[pbb])

    pT = [k.sbuf([128, 512], BF16, f"pT{i}") for i in range(3)]
    oraw = [k.sbuf([65, 512], F32, f"oraw{i}") for i in range(2)]
    rrow = [k.sbuf([65, 512], F32, f"rrow{i}") for i in range(2)]
    ostg = [k.sbuf([64, 512], BF16, f"ostg{i}") for i in range(2)]
    onesf = k.sbuf([65, 64], F32, "onesf")
    sinkx = k.sbuf([1, 512], F32, "sinkx")
    sinkb = k.sbuf([1, 512], BF16, "sinkb")
    e64 = k.sbuf([1, 65], BF16, "e64")
    k.op("pool", lambda e: e.memset(onesf[:], 1.0), writes=[onesf.b])
    k.op("pool", lambda e: e.memset(e64[:], 0.0), writes=[e64.b])
    k.op("pool", lambda e: e.memset(e64[0:1, 64:65], 1.0), writes=[e64.b])
    k.dma("sp", sinkx[:], E.sinkd[l], writes=[sinkx.b])
    k.op("act", lambda e: e.activation(out=sinkb[:], in_=sinkx[:], func=AF.Exp), reads=[sinkx.b], writes=[sinkb.b])
    ps_s = [(0, 1), (2, 3), (6, 7)]
    ps_o = [banks[4], banks[5]]
    ps_b = [banks[6], banks[7]]
    pcnt = [0]
    qcnt = [0]
    allq = [QK[i].sb(bi) for i in range(4) for bi in range(len(MBLK))]
    allv = [V[i].sb(t) for i in range(2) for t in range(34)]

    def attend(Q, Kd, Vt, q0, ktiles, sink, orow0):
        io = qcnt[0] % 2
        qcnt[0] += 1
        po = ps_o[io]
        n = len(ktiles)
        slots = []

        def qk(ii):
            kt, mi = ktiles[ii]
            b0, b1 = ps_s[pcnt[0] % 3]
            p = pT[pcnt[0] % 3]
            pcnt[0] += 1
            slots.append((b0, b1, p))
            for h in range(2):
                bh = banks[(b0, b1)[h]]
                k.op("pe", lambda e: e.matmul(out=bh[:, 0:256], lhsT=Kd[h * 64:(h + 1) * 64, kt * 128:(kt + 1) * 128],
                                              rhs=Q[h * 64:(h + 1) * 64, q0:q0 + 256], start=True, stop=True),
                     reads=allq, writes=[bh.b])

        def rest(ii):
            kt, mi = ktiles[ii]
            b0, b1, p = slots[ii]
            k.op("act", lambda e: e.activation(out=p[:, :].rearrange("p (h q) -> p h q", h=2), in_=PS.t[:, b0:b1 + 1, 0:256],
                                               func=AF.Exp, scale=ATT_SCALE),
                 reads=[banks[b0].b, banks[b1].b], writes=[p.b])
            if mi is not None:
                k.op("dve", lambda e: e.tensor_tensor(out=p[:], in0=p[:], in1=wmask[:, mi, :], op=ALU.mult),
                     reads=[p.b, wmask.b], writes=[p.b])
            k.op("pe", lambda e: e.matmul(out=po[0:65, :], lhsT=Vt[:, kt, 0:65], rhs=p[:], start=(ii == 0),
                                          stop=(ii == n - 1 and not sink)), reads=allv + [p.b], writes=[po.b])

        LA = 2
        for ii in range(min(LA, n)):
            qk(ii)
        for ii in range(n):
            if ii + LA < n:
                qk(ii + LA)
            rest(ii)
        if sink:
            k.op("pe", lambda e: e.matmul(out=po[0:65, :], lhsT=e64[0:1, :], rhs=sinkb[0:1, :], start=False, stop=True),
                 reads=[e64.b, sinkb.b], writes=[po.b])
        orw, rr, os_ = oraw[io], rrow[io], ostg[io]
        k.op("act", lambda e: e.activation(out=orw[0:65, :], in_=po[0:65, :], func=AF.Copy), reads=[po.b], writes=[orw.b])
        k.op("act", lambda e: e.activation(out=rr[64:65, :], in_=orw[64:65, :], func=AF.Ln), reads=[orw.b], writes=[rr.b])
        k.op("act", lambda e: e.activation(out=rr[64:65, :], in_=rr[64:65, :], func=AF.Exp, scale=-1.0), reads=[rr.b], writes=[rr.b])
        pb_ = ps_b[io]
        k.op("pe", lambda e: e.matmul(out=pb_[0:64, :], lhsT=onesf[64:65, 0:64], rhs=rr[64:65, :], start=True, stop=True),
             reads=[onesf.b, rr.b], writes=[pb_.b])
        k.op("dve", lambda e: e.tensor_tensor(out=os_[0:64, :], in0=orw[0:64, :], in1=pb_[0:64, :], op=ALU.mult),
             reads=[orw.b, pb_.b], writes=[os_.b])
        out_write(k, E, os_, 0, 64, 512, [(orow0, 0), (orow0 + 64, 256)], q0, 256)

    attend(QK[0], QK[1], V[0], 0, [(0, None), (1, None)], True, 0)
    attend(QK[2], QK[3], V[1], 0, [(0, None), (1, None)], False, 384)
    for qb in range(16):
        kts = []
        for r in range(4):
            lt = 2 * qb - 1 + r
            if 0 <= lt < 32:
                kts.append((2 + lt, r))
        kts += [(0, None), (1, None)]
        attend(QK[0], QK[1], V[0], 256 + qb * 256, kts, True, 0)
    for qb in range(16):
        attend(QK[2], QK[3], V[1], 256 + qb * 256, [(t, None) for t in range(34)], False, 384)
    end_phase(k)
    k.collective("ReduceScatter", ALU.add, E.rs_inAC, E.rs_outAC, reads=[E.rs_inAC.b], writes=[E.rs_outAC.b])
    k.phase = ExitStack()
    E.m4 = [k.sbuf([128, 4, 512], BF16, "m4_r")] * 2
    emit_rwkv(k, E, l, pB, pbb, pp, cmat, banks)
    end_phase(k)


RW_SCALE = -float(np.exp(np.float32(-0.5)))


def emit_rwkv(k, E, l, pB, pbb, pp, cmat, banks):
    sb = k.sbuf
    w0row = sb([1, 512], F32, "w0row")
    wup = sb([96, 512], F32, "wup")
    aup = sb([96, 512], F32, "aup")
    gup = sb([128, 2, 256], F32, "gup")
    rmask, tri = E.rmask, E.tri
    ones_row = sb([1, 128], F32, "ones_row")
    for t_, d_ in ((w0row, E.w0d[l]), (wup, E.wupd[l]), (aup, E.aupd[l]), (gup, E.gupd[l])):
        k.dma("sp", t_[:], d_, writes=[t_.b])
    k.op("pool", lambda e: e.memset(ones_row[:], 1.0), writes=[ones_row.b])
    c0 = sb([128, 12], F32, "c0")
    c1 = sb([128, 2], F32, "c1")
    c2 = sb([128, 2], F32, "c2")
    muv = pp[:, PP_MU:PP_MU + 24].rearrange("p (c two) -> p c two", two=2)
    k.op("dve", lambda e: e.tensor_tensor(out=c0[:], in0=muv[:, :, 0], in1=muv[:, :, 1], op=ALU.add), reads=[pp.b], writes=[c0.b])
    k.op("dve", lambda e: e.tensor_scalar(out=c0[:], in0=c0[:], scalar1=-1.0, scalar2=1.0, op0=ALU.mult, op1=ALU.add),
         reads=[c0.b], writes=[c0.b])
    k.op("dve", lambda e: e.tensor_scalar(out=c1[:], in0=pp[:, PP_KA:PP_KA + 2], scalar1=-1.0, scalar2=1.0, op0=ALU.mult, op1=ALU.add),
         reads=[pp.b], writes=[c1.b])
    k.op("dve", lambda e: e.tensor_scalar(out=c2[:], in0=c1[:], scalar1=2.0, scalar2=None, op0=ALU.mult), reads=[c1.b], writes=[c2.b])
    MSK = {0: dict(Ms=0, MsT=1, MiT=2, nMiT=4), 1: dict(Ms=1, MsT=0, MiT=3, nMiT=5)}
    NB = 512
    RD = F32
    F32R = mybir.dt.float32r

    def rv(ap):
        return ap.bitcast(F32R)

    WIDTH = 5
    STAGGER = 3
    identb = sb([128, 128], RD, "identb")
    k.op("dve", lambda e: e.tensor_copy(out=rv(identb[:]), in_=cmat[:, 0, :]), reads=[cmat.b], writes=[identb.b])
    raw = [sb([128, NB + 2], F32, f"raw{i}") for i in range(3)]
    rawc = [0]
    S = {n: sb([128, NB], F32, "S_" + n) for n in ("r", "k", "v", "wd", "ad", "ad2", "g0", "g1", "sqk", "rinv", "kap", "a", "a2", "tw",
                                                     "tmp", "kd", "bb", "eLn", "eLx", "ysum", "u1", "u2", "u3")}
    eLb = [sb([128, NB], F32, f"eL{i}") for i in range(2)]
    lw = sb([64, 8, 128], F32, "lw")
    bd = {n: [sb([128, 8, 2, 64], RD, f"bd_{n}{i}") for i in range(2)] for n in ("kap", "bt", "kt", "rt", "v")}
    zbig = S["u1"]
    k.op("pool", lambda e: e.memset(zbig[:], 0.0), writes=[zbig.b])
    for n in bd:
        for i in range(2):
            for hh in range(2):
                k.op("dve", lambda e: e.tensor_copy(out=rv(bd[n][i][:, :, hh, :]), in_=zbig[:, :].rearrange("p (c s) -> p c s", s=64)),
                     reads=[zbig.b], writes=[bd[n][i].b])
    yf = sb([128, TALL], F32, "yf")
    Mst = [sb([128, 128], F32, f"Mst{i}") for i in range(2)]
    U = {}

    def ut(name):
        if name not in U:
            dt_ = F32 if name in ("ET", "Hg", "YmT") else RD
            U[name] = [sb([128, 256 if name.startswith("X") else 128], dt_, f"U_{name}{i}") for i in range(WIDTH)]
        return U[name]

    ostage = [sb([128, NB], BF16, f"ostage{i}") for i in range(2)]
    bkc = [0]

    def rb():
        b = banks[bkc[0] % 8]
        bkc[0] += 1
        return b

    ecount = [0]

    def ew():
        ecount[0] += 1
        return "dve" if ecount[0] % 2 else "pool"

    dg = sb([128, 12, 3, 128], F32, "shift_diag")
    for bc_ in range(12):
        for j_, coef in enumerate((c0[:, bc_:bc_ + 1], pp[:, PP_MU + 2 * bc_:PP_MU + 2 * bc_ + 1], pp[:, PP_MU + 2 * bc_ + 1:PP_MU + 2 * bc_ + 2])):
            k.op("pool", lambda e: e.tensor_scalar(out=dg[:, bc_, j_, :], in0=cmat[:, 0, :], scalar1=coef, scalar2=None, op0=ALU.mult),
                 reads=[cmat.b, c0.b, pp.b], writes=[dg.b])

    def load_shift(bc, M, dst, t0, nb):
        rw = raw[rawc[0] % 3]
        rawc[0] += 1
        r0 = PB_ROWOFF[bc]
        k.dma("sp", rw[0:M, 0:nb + 2], pB.t.ap()[r0:r0 + M, pb_col(t0) - 1:pb_col(t0) + nb + 1], reads=[pbb], writes=[rw.b])
        ps = rb()
        for j_, off in enumerate((1, 0, 2)):
            k.op("pe", lambda e: e.matmul(out=ps[0:M, 0:nb], lhsT=dg[0:M, bc, j_, 0:M], rhs=rw[0:M, off:off + nb],
                                          start=(j_ == 0), stop=(j_ == 2)), reads=[dg.b, rw.b], writes=[ps.b])
        k.op("act", lambda e: e.activation(out=dst[0:M, 0:nb], in_=ps[0:M, 0:nb], func=AF.Copy), reads=[ps.b], writes=[dst.b])

    def lora_a(dst, src, pr, d, nb):
        ps = rb()
        k.op("pe", lambda e: e.matmul(out=ps[:, 0:nb], lhsT=aup[0:96, d * 256 + pr * 128:d * 256 + (pr + 1) * 128], rhs=src[0:96, 0:nb],
                                      start=True, stop=True), reads=[aup.b, src.b], writes=[ps.b])
        k.op("act", lambda e: e.activation(out=dst[:, 0:nb], in_=ps[:, 0:nb], func=AF.Sigmoid,
                                           bias=pp[:, PP_A0 + pr * 2 + d:PP_A0 + pr * 2 + d + 1], scale=1.0),
             reads=[ps.b, pp.b], writes=[dst.b])

    def prep(pr, t0, nb, d, final, bdi):
        nch = nb // 64
        load_shift(0 + pr, 128, S["r"], t0, nb)
        yield
        load_shift(2 + pr, 128, S["k"], t0, nb)
        yield
        load_shift(4 + pr, 128, S["v"], t0, nb)
        yield
        load_shift(6 + d, 96, S["wd"], t0, nb)
        yield
        load_shift(8 + d, 96, S["ad"], t0, nb)
        yield
        if final:
            load_shift(8 + (1 - d), 96, S["ad2"], t0, nb)
            yield
            load_shift(10, 128, S["g0"], t0, nb)
            yield
            load_shift(11, 128, S["g1"], t0, nb)
            yield
        kS, rS, vS = S["k"], S["r"], S["v"]
        k.op("act", lambda e: e.activation(out=S["sqk"][:, 0:nb], in_=kS[:, 0:nb], func=AF.Square, scale=pp[:, PP_KK + pr:PP_KK + pr + 1]),
             reads=[kS.b, pp.b], writes=[S["sqk"].b])
        ps = rb()
        k.op("pe", lambda e: e.matmul(out=ps[:, 0:nb], lhsT=cmat[:, 1, :], rhs=S["sqk"][:, 0:nb], start=True, stop=True),
             reads=[cmat.b, S["sqk"].b], writes=[ps.b])
        k.op("act", lambda e: e.activation(out=S["rinv"][:, 0:nb], in_=ps[:, 0:nb], func=AF.Sqrt), reads=[ps.b], writes=[S["rinv"].b])
        k.op("dve", lambda e: e.tensor_scalar(out=S["rinv"][:, 0:nb], in0=S["rinv"][:, 0:nb], scalar1=1e-12, scalar2=None, op0=ALU.max),
             reads=[S["rinv"].b], writes=[S["rinv"].b])
        k.op("dve", lambda e: e.reciprocal(out=S["rinv"][:, 0:nb], in_=S["rinv"][:, 0:nb]), reads=[S["rinv"].b], writes=[S["rinv"].b])
        k.op("dve", lambda e: e.scalar_tensor_tensor(out=S["kap"][:, 0:nb], in0=kS[:, 0:nb], scalar=pp[:, PP_KK + pr:PP_KK + pr + 1],
                                                      in1=S["rinv"][:, 0:nb], op0=ALU.mult, op1=ALU.mult),
             reads=[kS.b, pp.b, S["rinv"].b], writes=[S["kap"].b])
        yield
        lora_a(S["a"], S["ad"], pr, d, nb)
        yield
        if final:
            lora_a(S["a2"], S["ad2"], pr, 1 - d, nb)
            yield
        yield
        k.op("act", lambda e: e.activation(out=S["tw"][0:96, 0:nb], in_=S["wd"][0:96, 0:nb], func=AF.Tanh), reads=[S["wd"].b], writes=[S["tw"].b])
        pl = [rb(), rb()]
        wsl = slice(d * 256 + pr * 128, d * 256 + (pr + 1) * 128)
        for c in range(nch):
            pb_ = pl[c // 4]
            k.op("pe", lambda e: e.matmul(out=pb_[0:64, (c % 4) * 128:(c % 4 + 1) * 128], lhsT=S["tw"][0:96, c * 64:(c + 1) * 64],
                                          rhs=wup[0:96, wsl], start=True, stop=False), reads=[S["tw"].b, wup.b], writes=[pb_.b])
            k.op("pe", lambda e: e.matmul(out=pb_[0:64, (c % 4) * 128:(c % 4 + 1) * 128], lhsT=ones_row[0:1, 0:64],
                                          rhs=w0row[0:1, wsl], start=False, stop=True), reads=[ones_row.b, w0row.b], writes=[pb_.b])
        for hb in range((nch + 3) // 4):
            n4 = min(4, nch - hb * 4)
            k.op("act", lambda e: e.activation(out=lw[0:64, hb * 4:hb * 4 + n4, :], in_=pl[hb][0:64, 0:n4 * 128].rearrange("p (c n) -> p c n", n=128),
                                               func=AF.Sigmoid), reads=[pl[hb].b], writes=[lw.b])
        yield
        pL, pLx = rb(), rb()
        for c in range(nch):
            k.op("pe", lambda e: e.matmul(out=pL[:, c * 64:(c + 1) * 64], lhsT=lw[0:64, c, :], rhs=tri[0:64, 2 * d, :], start=True, stop=True),
                 reads=[lw.b, tri.b], writes=[pL.b])
        for c in range(nch):
            k.op("pe", lambda e: e.matmul(out=pLx[:, c * 64:(c + 1) * 64], lhsT=lw[0:64, c, :], rhs=tri[0:64, 2 * d + 1, :], start=True, stop=True),
                 reads=[lw.b, tri.b], writes=[pLx.b])
        k.op("act", lambda e: e.activation(out=eLb[bdi][:, 0:nb], in_=pL[:, 0:nb], func=AF.Exp), reads=[pL.b], writes=[eLb[bdi].b])
        k.op("act", lambda e: e.activation(out=S["eLn"][:, 0:nb], in_=pL[:, 0:nb], func=AF.Exp, scale=-1.0), reads=[pL.b], writes=[S["eLn"].b])
        k.op("act", lambda e: e.activation(out=S["eLx"][:, 0:nb], in_=pLx[:, 0:nb], func=AF.Exp), reads=[pLx.b], writes=[S["eLx"].b])
        yield
        k.op("dve", lambda e: e.tensor_scalar(out=S["tmp"][:, 0:nb], in0=S["a"][:, 0:nb], scalar1=pp[:, PP_KA + pr:PP_KA + pr + 1],
                                               scalar2=c1[:, pr:pr + 1], op0=ALU.mult, op1=ALU.add),
             reads=[S["a"].b, pp.b, c1.b], writes=[S["tmp"].b])
        k.op("pool", lambda e: e.tensor_tensor(out=S["kd"][:, 0:nb], in0=kS[:, 0:nb], in1=S["tmp"][:, 0:nb], op=ALU.mult),
             reads=[kS.b, S["tmp"].b], writes=[S["kd"].b])
        k.op("pool", lambda e: e.tensor_tensor(out=S["bb"][:, 0:nb], in0=S["kap"][:, 0:nb], in1=S["a"][:, 0:nb], op=ALU.mult),
             reads=[S["kap"].b, S["a"].b], writes=[S["bb"].b])
        yield
        for name, x, y in (("kap", S["kap"], S["eLx"]), ("bt", S["bb"], S["eLn"]), ("kt", S["kd"], S["eLn"]), ("rt", rS, eLb[bdi]),
                           ("v", vS, None)):
            dst = bd[name][bdi]
            for h in range(2):
                hs = slice(h * 64, (h + 1) * 64)
                eng = ew()
                xv = x[hs, 0:nb].rearrange("p (c s) -> p c s", s=64)
                if y is None:
                    k.op(eng, lambda e: e.tensor_copy(out=rv(dst[hs, 0:nch, h, :]), in_=xv), reads=[x.b], writes=[dst.b])
                else:
                    yv = y[hs, 0:nb].rearrange("p (c s) -> p c s", s=64)
                    k.op(eng, lambda e: e.tensor_tensor(out=rv(dst[hs, 0:nch, h, :]), in0=xv, in1=yv, op=ALU.mult),
                         reads=[x.b, y.b], writes=[dst.b])
            yield

    def unit(pr, tok0, c, d, bdi, useq, final):
        ui = useq % WIDTH
        mk = MSK[d]
        Kap = bd["kap"][bdi][:, c, :, :].rearrange("p a b -> p (a b)")
        Bt = bd["bt"][bdi][:, c, :, :].rearrange("p a b -> p (a b)")
        Kt = bd["kt"][bdi][:, c, :, :].rearrange("p a b -> p (a b)")
        Rt = bd["rt"][bdi][:, c, :, :].rearrange("p a b -> p (a b)")
        Vf = bd["v"][bdi][:, c, :, :].rearrange("p a b -> p (a b)")
        bdb = [bd[n][bdi].b for n in bd]
        gcol = c * 64 + (63 if d == 0 else 0)
        gam = eLb[bdi][:, gcol:gcol + 1]

        def T(n):
            return ut(n)[ui]

        def mm(lhsT, rhs, rd, N=128, acc=None, full=False):
            ps = rb() if acc is None else acc[0]
            st, sp_ = (True, True) if acc is None else (acc[1], acc[2])
            if not full:
                lhsT, rhs = rv(lhsT), rv(rhs)
            k.op("pe", lambda e: e.matmul(out=ps[:, 0:N], lhsT=lhsT, rhs=rhs, start=st, stop=sp_), reads=rd, writes=[ps.b])
            return ps

        def ev_copy(dst_t, dst_ap, ps, N=128, scale=None, full=False):
            if not full:
                dst_ap = rv(dst_ap)
            if scale is None:
                k.op("act", lambda e: e.activation(out=dst_ap, in_=ps[:, 0:N], func=AF.Copy), reads=[ps.b], writes=[dst_t.b])
            else:
                k.op("act", lambda e: e.activation(out=dst_ap, in_=ps[:, 0:N], func=AF.Copy, scale=scale), reads=[ps.b, eLb[bdi].b],
                     writes=[dst_t.b])

        def ev_tt(dst_t, dst_ap, in0_ap, in0_b, ps, op, N=128, psfirst=False, full=False):
            if not full:
                dst_ap = rv(dst_ap)
            if psfirst:
                k.op("dve", lambda e: e.tensor_tensor(out=dst_ap, in0=ps[:, 0:N], in1=in0_ap, op=op), reads=[ps.b] + in0_b, writes=[dst_t.b])
            else:
                k.op("dve", lambda e: e.tensor_tensor(out=dst_ap, in0=in0_ap, in1=ps[:, 0:N], op=op), reads=[ps.b] + in0_b, writes=[dst_t.b])

        X = [T("Xa"), T("Xb")]
        idb = [identb.b]
        ps = mm(Kap, identb[:, :], bdb + idb)
        ev_copy(X[0], X[0][:, 0:128], ps)
        yield
        ps = mm(Bt, identb[:, :], bdb + idb)
        ev_copy(T("nBtT"), T("nBtT")[:, :], ps, scale=-1.0)
        yield
        ps = mm(Kt, identb[:, :], bdb + idb)
        ev_copy(T("KtT"), T("KtT")[:, :], ps)
        yield
        ps = mm(Vf, identb[:, :], bdb + idb)
        ev_copy(T("VT"), T("VT")[:, :], ps)
        yield
        for nm, l_, r_, m_ in (("N", Kap, Bt, "Ms"), ("Z", Bt, Kap, "MsT"), ("AkkT", Kt, Kap, "MsT"), ("ArkT", Kt, Rt, "MiT"),
                               ("nArbT", Bt, Rt, "nMiT")):
            ps = mm(l_, r_, bdb)
            ev_tt(T(nm), T(nm)[:, :], rmask[:, mk[m_], :], [rmask.b], ps, ALU.mult, psfirst=True)
            yield
        ps = mm(T("AkkT")[:, :], T("VT")[:, :], [T("AkkT").b, T("VT").b])
        ev_copy(X[0], X[0][:, 128:256], ps)
        yield
        Zc, Nc = T("Z"), T("N")
        Zalt, Nalt = T("Zalt"), T("Nalt")
        ps = mm(Zc[:, :], X[0][:, :], [Zc.b, X[0].b], N=256)
        ev_tt(X[1], X[1][:, :], X[0][:, :], [X[0].b], ps, ALU.subtract, N=256)
        yield
        xi = 1
        for lev in range(5):
            Zn = Zalt
            ps = mm(Nc[:, :], Zc[:, :], [Nc.b, Zc.b])
            ev_copy(Zn, Zn[:, :], ps)
            yield
            if lev < 4:
                Nn = Nalt
                ps = mm(Zc[:, :], Nc[:, :], [Nc.b, Zc.b])
                ev_copy(Nn, Nn[:, :], ps)
                Nc, Nalt = Nn, Nc
                yield
            Zc, Zalt = Zn, Zc
            ps = mm(Zc[:, :], X[xi][:, :], [Zc.b, X[xi].b], N=256)
            ev_tt(X[1 - xi], X[1 - xi][:, :], X[xi][:, :], [X[xi].b], ps, ALU.add, N=256)
            xi = 1 - xi
            yield
        Xf = X[xi]
        P = Xf[:, 0:128]
        Q = Xf[:, 128:256]
        ps = mm(P, T("nBtT")[:, :], [Xf.b, T("nBtT").b])
        ev_tt(T("ET"), T("ET")[:, :], cmat[:, 0, :], [cmat.b], ps, ALU.add, full=True)
        yield
        ph = rb()
        mm(T("KtT")[:, :], T("VT")[:, :], [T("KtT").b, T("VT").b], acc=(ph, True, False))
        mm(T("nBtT")[:, :], Q, [T("nBtT").b, Xf.b], acc=(ph, False, True))
        ev_copy(T("Hg"), T("Hg")[:, :], ph, scale=gam, full=True)
        yield
        ps = mm(P, T("nArbT")[:, :], [Xf.b, T("nArbT").b])
        ev_tt(T("YmT"), T("YmT")[:, :], Rt, bdb, ps, ALU.add, full=True)
        yield
        M0 = Mst[useq % 2]
        M1 = Mst[(useq + 1) % 2]
        py = rb()
        mm(M0[:, :], T("YmT")[:, :], [M0.b, T("YmT").b], acc=(py, True, False), full=True)
        mm(T("VT")[:, :], T("ArkT")[:, :], [T("VT").b, T("ArkT").b], acc=(py, False, False))
        mm(Q, T("nArbT")[:, :], [Xf.b, T("nArbT").b], acc=(py, False, True))
        for h in range(2):
            hs = slice(h * 64, (h + 1) * 64)
            if not final:
                k.op("act", lambda e: e.activation(out=yf[hs, tok0:tok0 + 64], in_=py[hs, h * 64:(h + 1) * 64], func=AF.Copy),
                     reads=[py.b], writes=[yf.sb(tok0 // 512)])
            else:
                lc = (tok0 - (0 if tok0 < CTX else CTX)) % 512
                k.op("dve", lambda e: e.tensor_tensor(out=S["ysum"][hs, lc:lc + 64], in0=py[hs, h * 64:(h + 1) * 64],
                                                       in1=yf[hs, tok0:tok0 + 64], op=ALU.add),
                     reads=[py.b, yf.sb(tok0 // 512)], writes=[S["ysum"].b])
        pm = mm(T("ET")[:, :], M0[:, :], [T("ET").b, M0.b], full=True)
        k.op("dve", lambda e: e.scalar_tensor_tensor(out=M1[:, :], in0=pm[:, 0:128], scalar=gam, in1=T("Hg")[:, :], op0=ALU.mult, op1=ALU.add),
             reads=[pm.b, eLb[bdi].b, T("Hg").b], writes=[M1.b])
        yield

    def run_lockstep(gens, extra=None):
        pending = list(gens)
        active = []
        rounds = 0
        while pending or active:
            if pending and len(active) < WIDTH and (not active or rounds % STAGGER == 0):
                active.append(pending.pop(0))
            for g in list(active):
                try:
                    next(g)
                except StopIteration:
                    active.remove(g)
            if extra is not None and rounds >= 4:
                try:
                    next(extra)
                except StopIteration:
                    extra = None
            rounds += 1
        if extra is not None:
            for _ in extra:
                pass

    def finalize(pr, t0, nb):
        ys = S["ysum"]
        ps = rb()
        k.op("pe", lambda e: e.matmul(out=ps[:, 0:nb], lhsT=cmat[:, 1, :], rhs=ys[:, 0:nb], start=True, stop=True),
             reads=[cmat.b, ys.b], writes=[ps.b])
        k.op("dve", lambda e: e.scalar_tensor_tensor(out=S["u1"][:, 0:nb], in0=ps[:, 0:nb], scalar=-1.0 / 64, in1=ys[:, 0:nb],
                                                      op0=ALU.mult, op1=ALU.add), reads=[ps.b, ys.b], writes=[S["u1"].b])
        k.op("act", lambda e: e.activation(out=S["u2"][:, 0:nb], in_=S["u1"][:, 0:nb], func=AF.Square), reads=[S["u1"].b], writes=[S["u2"].b])
        ps = rb()
        k.op("pe", lambda e: e.matmul(out=ps[:, 0:nb], lhsT=cmat[:, 1, :], rhs=S["u2"][:, 0:nb], start=True, stop=True),
             reads=[cmat.b, S["u2"].b], writes=[ps.b])
        emit_rsqrt(k, S["u3"], ps, 1.0 / 64, GN_EPS, nb)
        k.op("dve", lambda e: e.tensor_tensor(out=S["u1"][:, 0:nb], in0=S["u1"][:, 0:nb], in1=S["u3"][:, 0:nb], op=ALU.mult),
             reads=[S["u1"].b, S["u3"].b], writes=[S["u1"].b])
        k.op("act", lambda e: e.activation(out=S["u1"][:, 0:nb], in_=S["u1"][:, 0:nb], func=AF.Identity,
                                           bias=pp[:, PP_GNB + pr:PP_GNB + pr + 1], scale=pp[:, PP_GNG + pr:PP_GNG + pr + 1]),
             reads=[S["u1"].b, pp.b], writes=[S["u1"].b])
        k.op("pool", lambda e: e.tensor_tensor(out=S["u2"][:, 0:nb], in0=S["a"][:, 0:nb], in1=S["a2"][:, 0:nb], op=ALU.add),
             reads=[S["a"].b, S["a2"].b], writes=[S["u2"].b])
        k.op("dve", lambda e: e.tensor_scalar(out=S["u2"][:, 0:nb], in0=S["u2"][:, 0:nb], scalar1=pp[:, PP_KA + pr:PP_KA + pr + 1],
                                               scalar2=c2[:, pr:pr + 1], op0=ALU.mult, op1=ALU.add),
             reads=[S["u2"].b, pp.b, c2.b], writes=[S["u2"].b])
        k.op("pool", lambda e: e.tensor_tensor(out=S["u2"][:, 0:nb], in0=S["u2"][:, 0:nb], in1=S["k"][:, 0:nb], op=ALU.mult),
             reads=[S["u2"].b, S["k"].b], writes=[S["u2"].b])
        k.op("dve", lambda e: e.scalar_tensor_tensor(out=S["u2"][:, 0:nb], in0=S["r"][:, 0:nb], scalar=pp[:, PP_RK + pr:PP_RK + pr + 1],
                                                      in1=S["u2"][:, 0:nb], op0=ALU.mult, op1=ALU.mult),
             reads=[S["r"].b, pp.b, S["u2"].b], writes=[S["u2"].b])
        ps = rb()
        k.op("pe", lambda e: e.matmul(out=ps[:, 0:nb], lhsT=cmat[:, 1, :], rhs=S["u2"][:, 0:nb], start=True, stop=True),
             reads=[cmat.b, S["u2"].b], writes=[ps.b])
        k.op("dve", lambda e: e.tensor_tensor(out=S["u3"][:, 0:nb], in0=ps[:, 0:nb], in1=S["v"][:, 0:nb], op=ALU.mult),
             reads=[ps.b, S["v"].b], writes=[S["u3"].b])
        k.op("pool", lambda e: e.tensor_tensor(out=S["u1"][:, 0:nb], in0=S["u1"][:, 0:nb], in1=S["u3"][:, 0:nb], op=ALU.add),
             reads=[S["u1"].b, S["u3"].b], writes=[S["u1"].b])
        for gi, gn_ in enumerate(("g0", "g1")):
            k.op("act", lambda e: e.activation(out=S[gn_][:, 0:nb], in_=S[gn_][:, 0:nb], func=AF.Sigmoid), reads=[S[gn_].b], writes=[S[gn_].b])
        ps = rb()
        for gi, gn_ in enumerate(("g0", "g1")):
            k.op("pe", lambda e: e.matmul(out=ps[:, 0:nb], lhsT=gup[:, gi, pr * 128:(pr + 1) * 128], rhs=S[gn_][:, 0:nb],
                                          start=(gi == 0), stop=(gi == 1)), reads=[gup.b, S[gn_].b], writes=[ps.b])
        os_ = ostage[ecount[0] % 2]
        ecount[0] += 1
        k.op("dve", lambda e: e.tensor_tensor(out=os_[:, 0:nb], in0=S["u1"][:, 0:nb], in1=ps[:, 0:nb], op=ALU.mult),
             reads=[S["u1"].b, ps.b], writes=[os_.b])
        out_write(k, E, os_, 0, 128, nb, [(128 + pr * 128, 0)], t0, nb)

    blocks_f = MBLK
    blocks_b = [MBLK[0]] + MBLK[:0:-1]
    bdc = 0
    for pr in range(2):
        for d in range(2):
            useq = 0
            k.op("pool", lambda e: e.memset(Mst[0][:], 0.0), writes=[Mst[0].b])
            blks = blocks_f if d == 0 else blocks_b
            if d == 0:
                for _ in prep(pr, blks[0][0], blks[0][1], d, False, bdc % 2):
                    pass
            for bi_, (t0, nb) in enumerate(blks):
                extra = None
                if d == 1:
                    for _ in prep(pr, t0, nb, d, True, bdc % 2):
                        pass
                elif bi_ + 1 < len(blks):
                    extra = prep(pr, blks[bi_ + 1][0], blks[bi_ + 1][1], d, False, (bdc + 1) % 2)
                nch = nb // 64
                order = list(range(nch)) if d == 0 else list(range(nch - 1, -1, -1))
                run_lockstep([unit(pr, t0 + c * 64, c, d, bdc % 2, useq + i, d == 1) for i, c in enumerate(order)], extra)
                useq += nch
                if d == 1:
                    finalize(pr, t0, nb)
                bdc += 1


def build_F():
    k = K()
    nc = k.nc
    E = Env()

    def din(name, shape, dt=F32):
        return nc.dram_tensor(name, list(shape), dt, kind="ExternalInput").ap()

    E.hT0d = din("hT0", [D, NT])
    E.ccTd = din("ccT", [D, 2])
    E.awqd = din("awq", [D, 96 * 128])
    E.abqd = din("abq", [128, 96])
    E.indd = din("ind", [128, 4])
    E.gn1d = din("gn1T", [128, DEPTH * NCH])
    E.gn2d = din("gn2T", [128, DEPTH * NCH])
    E.wind = din("win", [DEPTH, D, NW])
    E.ppd = din("pp", [DEPTH, 128, NPP])
    E.sinkd = din("sinkrow", [DEPTH, 1, 512])
    E.w0d = din("w0row", [DEPTH, 1, 512])
    E.wupd = din("wup", [DEPTH, 96, 512])
    E.aupd = din("aup", [DEPTH, 96, 512])
    E.gupd = din("gup", [DEPTH, 128, 2, 256])
    E.cosd = din("cosT", [128, TALL])
    E.sind = din("sinT", [128, TALL])
    E.cmatd = din("cmat", [128, 3, 128])
    E.wmaskd = din("wmask", [128, 4, 512], BF16)
    E.rmaskd = din("rmask", [128, 6, 128])
    E.trid = din("tri", [64, 4, 64])
    E.woutd = din("wout", [DEPTH, D, D])
    E.w1d = din("w1", [DEPTH, D, 4 * D])
    E.w2d = din("w2", [DEPTH, 4 * D, D])
    E.hOd = nc.dram_tensor("hTo", [D, NT], F32, kind="ExternalOutput").ap()
    E.mod_in = k.dram("mod_in", [128, 192], F32)
    E.mod_all = k.dram("mod_all", [512, 192], F32)
    E.u_in = [k.dram(f"u_in{i}", [n * 128, NT], BF16) for i, (c0, n) in enumerate(UPIECES)]
    E.u_all = [k.dram(f"u_all{i}", [4 * n * 128, NT], BF16) for i, (c0, n) in enumerate(UPIECES)]
    E.rs_inAC = k.dram("rs_inAC", [4 * 1024, NT], BF16)
    E.rs_inB = k.dram("rs_inB", [4 * 1024, NT], BF16)
    E.rs_outAC = k.dram("rs_outAC", [1024, NT], BF16)
    E.rs_outB = k.dram("rs_outB", [1024, NT], BF16)
    E.h_spill = k.dram("h_spill", [D, NT], F32)
    E.pB = k.dram("pB", [PB_ROWS, PBW], F32)
    E.pbb = E.pB.b
    E.cst = emit_consts(k)
    E.modS = k.sbuf([128, 4, 96, 2], F32, "modS")
    E.mods = Mods(E.modS)
    E.ind = k.sbuf([128, 4], F32, "ind")
    E.gn1 = k.sbuf([128, DEPTH * NCH], F32, "gn1")
    E.gn2 = k.sbuf([128, DEPTH * NCH], F32, "gn2")
    E.pp = k.sbuf([128, NPP], F32, "pp")
    E.cmat = k.sbuf([128, 3, 128], F32, "cmat")
    E.wmask = k.sbuf([128, 4, 512], BF16, "wmask")
    E.rmask = k.sbuf([128, 6, 128], F32, "rmask")
    E.tri = k.sbuf([64, 4, 64], F32, "tri")
    E.m4c = [0]
    E.PS = k.psum([128, 8, 512], F32, "PSall")
    E.banks = [BankView(E.PS.t, i) for i in range(8)]
    emit_P0(k, E)
    emit_P1(k, E)
    for l in range(DEPTH):
        emit_B(k, E, l)
        k.collective("ReduceScatter", ALU.add, E.rs_inB, E.rs_outB, reads=[E.rs_inB.b], writes=[E.rs_outB.b])
        emit_C(k, E, l)
    return k.finish()


_PROG = {}


def _c(a):
    return np.ascontiguousarray(a)


def host_B_consts():
    ident = np.eye(128, dtype=np.float32)
    bo = np.zeros((128, 128), np.float32)
    bo[:64, :64] = 1
    bo[64:, 64:] = 1
    R = np.zeros((128, 128), np.float32)
    for m in range(128):
        if m % 64 < 32:
            R[m + 32, m] = -1.0
        else:
            R[m - 32, m] = 1.0
    cmat = _c(np.stack([ident, bo, R], 1))
    t = np.arange(SEQ)
    row = (t // 64).astype(np.float32)
    col = (t % 64).astype(np.float32)
    inv = (10000.0 ** (-np.arange(16, dtype=np.float32) / 16)).astype(np.float32)
    ang = np.concatenate([row[:, None] * inv, col[:, None] * inv], -1).astype(np.float32)
    cosT = np.ones((128, TALL), np.float32)
    sinT = np.zeros((128, TALL), np.float32)
    c = np.cos(ang).T.astype(np.float32)
    s = np.sin(ang).T.astype(np.float32)
    for p in range(128):
        cosT[p, CTX:] = c[p % 32]
        sinT[p, CTX:] = s[p % 32]
    wm = np.zeros((128, 4, 2, 256), np.float32)
    kl = np.arange(128)[:, None]
    ql = np.arange(256)[None, :]
    for r in range(4):
        d = ql - kl - (r - 1) * 128
        wm[:, r, :, :] = (np.abs(d) <= 128)[:, None, :]
    wmask = _c(wm.reshape(128, 4, 512).astype(ml_dtypes.bfloat16))
    i64 = np.arange(64)
    SL = (i64[:, None] > i64[None, :]).astype(np.float32)
    SU = SL.T.copy()
    UI = (i64[:, None] <= i64[None, :]).astype(np.float32)
    LI = UI.T.copy()
    rm = np.zeros((128, 6, 128), np.float32)
    for mi, mm_ in enumerate((SL, SU, UI, LI, -UI, -LI)):
        rm[:64, mi, :64] = mm_
        rm[64:, mi, 64:] = mm_
    tri = np.stack([UI, SU, LI, SL], 1).astype(np.float32) * np.float32(RW_SCALE)
    return {"cmat": cmat, "cosT": cosT, "sinT": sinT, "wmask": wmask, "rmask": _c(rm), "tri": _c(tri)}


A_IN_ = 768
B_IN_ = 3712


def host_core_inputs(I, core, consts):
    b, j = divmod(core, 4)
    kv = j // 2
    m = dict(consts)
    x, ctx = I["x"], I["ctx"]
    m["hT0"] = _c(np.concatenate([x[b, j * 1024:(j + 1) * 1024], ctx[b, j * 64:(j + 1) * 64]], 0).T)
    m["ccT"] = _c(np.stack([I["c"][b], I["c_ctx"]], 0).T)
    awq = np.empty((D, 96 * 128), np.float32)
    abq = np.empty((128, 96), np.float32)
    for l in range(DEPTH):
        for w in range(6):
            for cl in range(4):
                cc = (l * 6 + w) * 4 + cl
                g0 = w * D + (j * 4 + cl) * 128
                awq[:, cc * 128:(cc + 1) * 128] = I["ada_w"][l][:, g0:g0 + 128]
                abq[:, cc] = I["ada_b"][l][g0:g0 + 128]
    m["awq"] = awq
    m["abq"] = abq
    ind = np.zeros((128, 4), np.float32)
    ind[:, j] = 1.0
    m["ind"] = ind
    m["gn1T"] = _c(np.concatenate([I["norm1_g"][l].reshape(-1, 128).T for l in range(DEPTH)], 1))
    m["gn2T"] = _c(np.concatenate([I["norm2_g"][l].reshape(-1, 128).T for l in range(DEPTH)], 1))
    cols = []
    cols += list(range(2 * j * 64, (2 * j + 2) * 64))
    cols += list(range(512 + kv * 64, 512 + (kv + 1) * 64)) * 2
    cb = A_IN_ + B_IN_
    cols += list(range(cb + 2 * j * 64, cb + (2 * j + 2) * 64))
    cols += list(range(cb + 512 + kv * 64, cb + 512 + (kv + 1) * 64)) * 2
    cols += list(range(640 + kv * 64, 640 + (kv + 1) * 64))
    cols += list(range(cb + 640 + kv * 64, cb + 640 + (kv + 1) * 64))
    bb = A_IN_
    for part in range(3):
        cols += list(range(bb + part * 1024 + 4 * j * 64, bb + part * 1024 + (4 * j + 4) * 64))
    cols += list(range(bb + 3072, bb + 3712))
    assert len(cols) == NW
    m["win"] = _c(I["w_in"][:, :, cols])
    bcols = [c_ - bb for c_ in cols[640:]]
    hc = slice(4 * j * 64, (4 * j + 4) * 64)
    pp = np.zeros((DEPTH, 128, NPP), np.float32)
    for l in range(DEPTH):
        pp[l, :, PP_QKG + 0] = np.tile(I["a_q_norm"][l], 2)
        pp[l, :, PP_QKG + 1] = np.tile(I["a_k_norm"][l], 2)
        pp[l, :, PP_QKG + 2] = np.tile(I["c_q_norm"][l], 2)
        pp[l, :, PP_QKG + 3] = np.tile(I["c_k_norm"][l], 2)
        mu = I["shift_mu"][l]
        pos = 0
        for ci in range(5, 17):
            M = CHUNKS[ci][1]
            idx = bcols[pos:pos + M]
            pos += M
            pp[l, :M, PP_MU + (ci - 5) * 2 + 0] = mu[0][idx]
            pp[l, :M, PP_MU + (ci - 5) * 2 + 1] = mu[1][idx]
        for pr in range(2):
            sl = slice(4 * j * 64 + pr * 128, 4 * j * 64 + (pr + 1) * 128)
            for d in range(2):
                pp[l, :, PP_A0 + pr * 2 + d] = I["iclr_a0"][l][d][sl]
            pp[l, :, PP_KK + pr] = I["k_k"][l][sl]
            pp[l, :, PP_KA + pr] = I["k_a"][l][sl]
            pp[l, :, PP_RK + pr] = I["r_k"][l][sl]
            pp[l, :, PP_GNG + pr] = I["gn_g"][l][sl]
            pp[l, :, PP_GNB + pr] = I["gn_b"][l][sl]
    m["pp"] = pp
    m["sinkrow"] = _c(np.stack([np.repeat(I["a_sink"][l][2 * j:2 * j + 2], 256)[None, :] for l in range(DEPTH)]).astype(np.float32))
    m["w0row"] = _c(np.stack([np.concatenate([I["decay_w0"][l][d][hc] for d in range(2)])[None, :] for l in range(DEPTH)]))
    m["wup"] = _c(np.stack([np.concatenate([I["decay_up"][l][d][:, hc] for d in range(2)], 1) for l in range(DEPTH)]))
    m["aup"] = _c(np.stack([np.concatenate([I["iclr_up"][l][d][:, hc] for d in range(2)], 1) for l in range(DEPTH)]))
    m["gup"] = _c(np.stack([I["gate_up"][l][:, hc].reshape(2, 128, 256).transpose(1, 0, 2) for l in range(DEPTH)]))
    return m


def kernel(x, c, ctx, c_ctx, ada_w, ada_b, norm1_g, norm2_g, w_in, a_q_norm, a_k_norm, a_sink, c_q_norm, c_k_norm, shift_mu,
           decay_w0, decay_up, iclr_a0, iclr_up, gate_up, k_k, k_a, r_k, gn_g, gn_b, w_out, mlp_w1, mlp_w2):
    I = dict(x=x, c=c, ctx=ctx, c_ctx=c_ctx, ada_w=ada_w, ada_b=ada_b, norm1_g=norm1_g, norm2_g=norm2_g, w_in=w_in,
             a_q_norm=a_q_norm, a_k_norm=a_k_norm, a_sink=a_sink, c_q_norm=c_q_norm, c_k_norm=c_k_norm, shift_mu=shift_mu,
             decay_w0=decay_w0, decay_up=decay_up, iclr_a0=iclr_a0, iclr_up=iclr_up, gate_up=gate_up, k_k=k_k, k_a=k_a, r_k=r_k,
             gn_g=gn_g, gn_b=gn_b, w_out=w_out, mlp_w1=mlp_w1, mlp_w2=mlp_w2)
    I = {k_: np.asarray(v, dtype=np.float32) for k_, v in I.items()}
    consts = host_B_consts()
    perm = []
    for r in range(4):
        perm += list(range(2 * r * 64, 2 * r * 64 + 128)) + list(range(1536 + 2 * r * 64, 1536 + 2 * r * 64 + 128))
    for r in range(4):
        perm += list(range(512 + 4 * r * 64, 512 + 4 * r * 64 + 256))
    shared = {"wout": _c(I["w_out"][:, perm, :]), "w1": I["mlp_w1"], "w2": I["mlp_w2"]}
    maps = []
    for core in range(8):
        m = host_core_inputs(I, core, consts)
        m.update(shared)
        maps.append(m)
    if "F" not in _PROG:
        _PROG["F"] = build_F()
    res = run_bass_kernel_spmd(_PROG["F"], maps, core_ids=list(range(8))).results
    out = np.empty((2, SEQ, D), np.float32)
    for core in range(8):
        b, j = divmod(core, 4)
        out[b, j * 1024:(j + 1) * 1024] = res[core]["hTo"][:, 0:1024].T
    return out
```
